# Optimizing a Trainium2 kernel written in Bass

```python
import math
import jax, jax.numpy as jnp
from jax import lax
import numpy as np

D_MODEL = 1024
BATCH = 2
SEQ = 8192
DEPTH = 1

MEM_LEN = 256
EPS = 1e-6
LRU_WIDTH = 512
LRU_BLOCKS = 8
LRU_BLOCK = LRU_WIDTH // LRU_BLOCKS
CONV_WIDTH = 4
LRU_C = 8.0
MLA_HEADS = 8
QK_NOPE = 64
QK_ROPE = 32
QK_HEAD = QK_NOPE + QK_ROPE
V_DIM = 64
Q_LORA = 256
KV_LORA = 128
MLA_WIDTH = MLA_HEADS * V_DIM
ROPE_THETA = 10000.0
Q_BLOCK = 128
MIX_WIDTH = LRU_WIDTH + MLA_WIDTH
OFF_Y = LRU_WIDTH
OFF_CQ = 2 * LRU_WIDTH
OFF_CKV = OFF_CQ + Q_LORA
OFF_KR = OFF_CKV + KV_LORA
IN_COLS = OFF_KR + QK_ROPE
MEM_HEADS = 4
MEM_HEAD_DIM = 128
MEM_WIDTH = MEM_HEADS * MEM_HEAD_DIM
D_FF = 2816
FFN_CONV = 3

kernel_name = "hybrid_rglru_mla_memxattn_convffn_encoder"


def rms_norm(x, g):
    xf = x.astype(jnp.float32)
    y = xf * lax.rsqrt(jnp.mean(xf * xf, axis=-1, keepdims=True) + EPS)
    return (y * g.astype(jnp.float32)).astype(x.dtype)


def depthwise_conv(x, w, b, left, right):
    S = x.shape[1]
    xp = jnp.pad(x, ((0, 0), (left, right), (0, 0)))
    out = xp[:, 0:S] * w[0] + b
    for k in range(1, w.shape[0]):
        out = out + xp[:, k:k + S] * w[k]
    return out


def rope_tables(positions):
    inv = ROPE_THETA ** (-jnp.arange(0, QK_ROPE, 2, dtype=jnp.float32) / QK_ROPE)
    ang = positions.astype(jnp.float32)[..., None] * inv
    return jnp.cos(ang), jnp.sin(ang)


def apply_rope(t, cos, sin):
    half = QK_ROPE // 2
    c = cos[:, :, None, :].astype(t.dtype)
    s = sin[:, :, None, :].astype(t.dtype)
    t1, t2 = t[..., :half], t[..., half:]
    return jnp.concatenate([t1 * c - t2 * s, t1 * s + t2 * c], axis=-1)


def block_diag(x, w):
    B_, S_, _ = x.shape
    xb = x.reshape(B_, S_, LRU_BLOCKS, LRU_BLOCK)
    return jnp.einsum('bsnc,ncd->bsnd', xb, w).reshape(B_, S_, LRU_WIDTH)


def rg_lru(x, w_a, b_a, w_i, b_i, lam, reverse):
    r = jax.nn.sigmoid((block_diag(x, w_a) + b_a).astype(jnp.float32))
    i = jax.nn.sigmoid((block_diag(x, w_i) + b_i).astype(jnp.float32))
    log_a = -LRU_C * r * jax.nn.softplus(-lam.astype(jnp.float32))
    a = jnp.exp(log_a)
    mult = jnp.sqrt(-jnp.expm1(2.0 * log_a))
    b = mult * (i * x.astype(jnp.float32))

    def combine(lhs, rhs):
        a_l, b_l = lhs
        a_r, b_r = rhs
        return a_l * a_r, a_r * b_l + b_r

    _, h = lax.associative_scan(combine, (a, b), reverse=reverse, axis=1)
    return h


def mla_attention(proj, cos, sin, q_a_norm, w_uq, kv_a_norm, w_ukv, q_norm, k_norm):
    B_, S_, _ = proj.shape
    c_q = rms_norm(proj[..., OFF_CQ:OFF_CKV], q_a_norm)
    c_kv = rms_norm(proj[..., OFF_CKV:OFF_KR], kv_a_norm)
    k_rope = proj[..., OFF_KR:IN_COLS]
    q = (c_q @ w_uq).reshape(B_, S_, MLA_HEADS, QK_HEAD)
    kv = (c_kv @ w_ukv).reshape(B_, S_, MLA_HEADS, QK_NOPE + V_DIM)
    k_nope, v = kv[..., :QK_NOPE], kv[..., QK_NOPE:]
    k_rope_h = jnp.broadcast_to(k_rope[:, :, None, :], (B_, S_, MLA_HEADS, QK_ROPE))
    k = jnp.concatenate([k_nope, k_rope_h], axis=-1)
    q = rms_norm(q, q_norm)
    k = rms_norm(k, k_norm)
    q = jnp.concatenate([q[..., :QK_NOPE], apply_rope(q[..., QK_NOPE:], cos, sin)], axis=-1)
    k = jnp.concatenate([k[..., :QK_NOPE], apply_rope(k[..., QK_NOPE:], cos, sin)], axis=-1)
    scale = QK_HEAD ** -0.5
    kh = k.transpose(0, 2, 1, 3)
    vh = v.transpose(0, 2, 1, 3)
    nb = S_ // Q_BLOCK
    qb = q.transpose(0, 2, 1, 3).reshape(B_, MLA_HEADS, nb, Q_BLOCK, QK_HEAD).transpose(2, 0, 1, 3, 4)

    def attend(q_blk):
        s = jnp.einsum('bhqd,bhkd->bhqk', q_blk, kh).astype(jnp.float32) * scale
        p = jax.nn.softmax(s, axis=-1)
        return jnp.einsum('bhqk,bhkd->bhqd', p.astype(vh.dtype), vh)

    o = lax.map(attend, qb)
    return o.transpose(1, 0, 3, 2, 4).reshape(B_, S_, MLA_WIDTH)


def memory_cross_attention(h, mem_n, w_q, w_kv, q_norm, k_norm, w_o):
    B_, S_, _ = h.shape
    M = mem_n.shape[1]
    q = (h @ w_q).reshape(B_, S_, MEM_HEADS, MEM_HEAD_DIM)
    kv = mem_n @ w_kv
    k = kv[..., :MEM_WIDTH].reshape(B_, M, MEM_HEADS, MEM_HEAD_DIM)
    v = kv[..., MEM_WIDTH:].reshape(B_, M, MEM_HEADS, MEM_HEAD_DIM)
    q = rms_norm(q, q_norm)
    k = rms_norm(k, k_norm)
    s = jnp.einsum('bqhd,bkhd->bhqk', q, k).astype(jnp.float32) * (MEM_HEAD_DIM ** -0.5)
    p = jax.nn.softmax(s, axis=-1)
    o = jnp.einsum('bhqk,bkhd->bqhd', p.astype(v.dtype), v).reshape(B_, S_, MEM_WIDTH)
    return o @ w_o


def hybrid_layer(x, mem, cos, sin, attn_norm, w_in, lru_conv_w, lru_conv_b, lru_w_a, lru_b_a,
                 lru_w_i, lru_b_i, lru_lambda, q_a_norm, w_uq, kv_a_norm, w_ukv, mla_q_norm,
                 mla_k_norm, lru_out_norm, mla_out_norm, w_out, mem_attn_norm, mem_norm, w_mem_q,
                 w_mem_kv, mem_q_norm, mem_k_norm, w_mem_o, ffn_norm, w_up, ffn_conv_w, ffn_conv_b,
                 w_down):
    h = rms_norm(x, attn_norm)
    proj = h @ w_in
    xr = proj[..., :OFF_Y]
    yg = proj[..., OFF_Y:OFF_CQ]
    xf = depthwise_conv(xr, lru_conv_w[0], lru_conv_b[0], CONV_WIDTH - 1, 0)
    xb = depthwise_conv(xr, lru_conv_w[1], lru_conv_b[1], 0, CONV_WIDTH - 1)
    hf = rg_lru(xf, lru_w_a[0], lru_b_a[0], lru_w_i[0], lru_b_i[0], lru_lambda[0], False)
    hb = rg_lru(xb, lru_w_a[1], lru_b_a[1], lru_w_i[1], lru_b_i[1], lru_lambda[1], True)
    lru_out = (hf + hb).astype(x.dtype) * jax.nn.gelu(yg)
    mla_out = mla_attention(proj, cos, sin, q_a_norm, w_uq, kv_a_norm, w_ukv, mla_q_norm, mla_k_norm)
    mixed = jnp.concatenate([rms_norm(lru_out, lru_out_norm), rms_norm(mla_out, mla_out_norm)], axis=-1)
    x = x + mixed @ w_out
    x = x + memory_cross_attention(rms_norm(x, mem_attn_norm), rms_norm(mem, mem_norm),
                                   w_mem_q, w_mem_kv, mem_q_norm, mem_k_norm, w_mem_o)
    gu = rms_norm(x, ffn_norm) @ w_up
    gu = depthwise_conv(gu, ffn_conv_w, ffn_conv_b, FFN_CONV // 2, FFN_CONV // 2)
    g, u = gu[..., :D_FF], gu[..., D_FF:]
    x = x + (jax.nn.silu(g) * u) @ w_down
    return x


def setup_inputs(seed: int = 0) -> dict:
    key = jax.random.key(seed)
    ks = iter(jax.random.split(key, 40))
    f32 = jnp.float32

    def w(shape, fan_in):
        return jax.random.normal(next(ks), (DEPTH,) + shape, f32) * (fan_in ** -0.5)

    def gain(shape):
        return 1.0 + 0.05 * jax.random.normal(next(ks), (DEPTH,) + shape, f32)

    def bias(shape):
        return 0.01 * jax.random.normal(next(ks), (DEPTH,) + shape, f32)

    x = jax.random.normal(next(ks), (BATCH, SEQ, D_MODEL), f32)
    mem = jax.random.normal(next(ks), (BATCH, MEM_LEN, D_MODEL), f32)
    positions = jnp.broadcast_to(jnp.arange(SEQ, dtype=jnp.int32)[None, :], (BATCH, SEQ))
    u = jax.random.uniform(next(ks), (DEPTH, 2, LRU_WIDTH), f32, 0.9, 0.999)
    s = u ** (1.0 / LRU_C)
    lru_lambda = jnp.log(s) - jnp.log1p(-s)
    return {
        "x": x,
        "mem": mem,
        "positions": positions,
        "attn_norm": gain((D_MODEL,)),
        "w_in": w((D_MODEL, IN_COLS), D_MODEL),
        "lru_conv_w": w((2, CONV_WIDTH, LRU_WIDTH), CONV_WIDTH),
        "lru_conv_b": bias((2, LRU_WIDTH)),
        "lru_w_a": w((2, LRU_BLOCKS, LRU_BLOCK, LRU_BLOCK), LRU_BLOCK),
        "lru_b_a": bias((2, LRU_WIDTH)),
        "lru_w_i": w((2, LRU_BLOCKS, LRU_BLOCK, LRU_BLOCK), LRU_BLOCK),
        "lru_b_i": bias((2, LRU_WIDTH)),
        "lru_lambda": lru_lambda,
        "q_a_norm": gain((Q_LORA,)),
        "w_uq": w((Q_LORA, MLA_HEADS * QK_HEAD), Q_LORA),
        "kv_a_norm": gain((KV_LORA,)),
        "w_ukv": w((KV_LORA, MLA_HEADS * (QK_NOPE + V_DIM)), KV_LORA),
        "mla_q_norm": gain((QK_HEAD,)),
        "mla_k_norm": gain((QK_HEAD,)),
        "lru_out_norm": gain((LRU_WIDTH,)),
        "mla_out_norm": gain((MLA_WIDTH,)),
        "w_out": w((MIX_WIDTH, D_MODEL), MIX_WIDTH),
        "mem_attn_norm": gain((D_MODEL,)),
        "mem_norm": gain((D_MODEL,)),
        "w_mem_q": w((D_MODEL, MEM_WIDTH), D_MODEL),
        "w_mem_kv": w((D_MODEL, 2 * MEM_WIDTH), D_MODEL),
        "mem_q_norm": gain((MEM_HEAD_DIM,)),
        "mem_k_norm": gain((MEM_HEAD_DIM,)),
        "w_mem_o": w((MEM_WIDTH, D_MODEL), MEM_WIDTH),
        "ffn_norm": gain((D_MODEL,)),
        "w_up": w((D_MODEL, 2 * D_FF), D_MODEL),
        "ffn_conv_w": w((FFN_CONV, 2 * D_FF), FFN_CONV),
        "ffn_conv_b": bias((2 * D_FF,)),
        "w_down": w((D_FF, D_MODEL), D_FF),
    }


def reference(x, mem, positions, attn_norm, w_in, lru_conv_w, lru_conv_b, lru_w_a, lru_b_a,
              lru_w_i, lru_b_i, lru_lambda, q_a_norm, w_uq, kv_a_norm, w_ukv, mla_q_norm,
              mla_k_norm, lru_out_norm, mla_out_norm, w_out, mem_attn_norm, mem_norm, w_mem_q,
              w_mem_kv, mem_q_norm, mem_k_norm, w_mem_o, ffn_norm, w_up, ffn_conv_w, ffn_conv_b,
              w_down):
    cos, sin = rope_tables(positions)
    for l in range(DEPTH):
        x = hybrid_layer(x, mem, cos, sin, attn_norm[l], w_in[l], lru_conv_w[l], lru_conv_b[l],
                         lru_w_a[l], lru_b_a[l], lru_w_i[l], lru_b_i[l], lru_lambda[l],
                         q_a_norm[l], w_uq[l], kv_a_norm[l], w_ukv[l], mla_q_norm[l],
                         mla_k_norm[l], lru_out_norm[l], mla_out_norm[l], w_out[l],
                         mem_attn_norm[l], mem_norm[l], w_mem_q[l], w_mem_kv[l], mem_q_norm[l],
                         mem_k_norm[l], w_mem_o[l], ffn_norm[l], w_up[l], ffn_conv_w[l],
                         ffn_conv_b[l], w_down[l])
    return x
```

```python
import numpy as np
from contextlib import ExitStack
import concourse.bass as bass
import concourse.mybir as mybir
from concourse.bass_utils import run_bass_kernel_spmd

F32 = mybir.dt.float32
BF16 = mybir.dt.bfloat16
I32 = mybir.dt.int32
AF = mybir.ActivationFunctionType
ALU = mybir.AluOpType

import os
SAME_ENGINE_SYNC = os.environ.get('SAME_ENGINE_SYNC', '1') == '1'
EPS = 1e-6
D = 1024
DFF = 2816
NCH = 44
MEM = 256


class Res:
    __slots__ = ("name", "w", "r")

    def __init__(self, name=""):
        self.name = name
        self.w = None
        self.r = {}


class Sched:
    ENGS = ("pe", "act", "dve", "pool", "sp")

    def __init__(self):
        self.ops = {e: [] for e in self.ENGS}
        self.cnt = {e: 0 for e in self.ENGS}
        self.dcnt = {}
        self.seen = {e: {} for e in self.ENGS}

    def _need(self, eng, tok, waits):
        if tok is None:
            return
        key, val = tok
        if self.seen[eng].get(key, 0) >= val:
            return
        if val > waits.get(key, 0):
            waits[key] = val

    def op(self, eng, fn, reads=(), writes=(), dma=None):
        waits = {}
        for r in reads:
            self._need(eng, r.w, waits)
        for w in writes:
            self._need(eng, w.w, waits)
            for k, v in w.r.items():
                self._need(eng, (k, v), waits)
        if not SAME_ENGINE_SYNC:
            waits.pop(eng, None)
        for k, v in waits.items():
            self.seen[eng][k] = v
        if dma is None:
            self.cnt[eng] += 1
            tok = (eng, self.cnt[eng])
        else:
            self.dcnt[dma] = self.dcnt.get(dma, 0) + 16
            tok = (dma, self.dcnt[dma])
        self.ops[eng].append((list(waits.items()), fn, tok))
        for r in reads:
            if r.r.get(tok[0], 0) < tok[1]:
                r.r[tok[0]] = tok[1]
        for w in writes:
            w.w = tok
            w.r = {}
        return tok

    def barrier_all(self):
        for e in self.ENGS:
            waits = {}
            for e2 in self.ENGS:
                if e2 != e and self.cnt[e2] > self.seen[e].get(e2, 0):
                    waits[e2] = self.cnt[e2]
            for k, v in self.dcnt.items():
                if v > self.seen[e].get(k, 0):
                    waits[k] = v
            for k, v in waits.items():
                self.seen[e][k] = v
            if waits:
                self.ops[e].append((list(waits.items()), None, None))

    def emit(self, nc):
        keys = list(self.ENGS) + sorted(self.dcnt.keys())
        with ExitStack() as st:
            sems = {k: st.enter_context(nc.semaphore("s_" + k)) for k in keys}
            block = st.enter_context(nc.Block())
            engmap = {"pe": block.tensor, "act": block.scalar, "dve": block.vector,
                      "pool": block.gpsimd, "sp": block.sync}
            fin = {}
            for k in keys:
                v = self.cnt[k] if k in self.cnt else self.dcnt[k]
                if v > 0:
                    fin[k] = v
            self.ops["sp"].append((list(fin.items()), None, None))
            for e in self.ENGS:
                ops = self.ops[e]

                def body(engobj, ops=ops, e=e):
                    for waits, fn, tok in ops:
                        for k, v in waits:
                            engobj.wait_ge(sems[k], v)
                        if fn is None:
                            continue
                        ins = fn(engobj)
                        if tok[0] == e:
                            ins.then_inc(sems[e], 1)
                        else:
                            ins.then_inc(sems[tok[0]], 16)
                engmap[e](body)


class Buf:
    __slots__ = ("ap", "r")

    def __init__(self, ap, r):
        self.ap = ap
        self.r = r


class Arena:
    def __init__(self, t, size):
        self.t = t
        self.size = size
        self.regs = [[0, size]]

    def _take(self, n, name):
        for r in self.regs:
            if r[1] - r[0] >= n:
                o = r[0]
                r[0] += n
                return o
        raise AssertionError(("SBUF arena overflow", name, n, self.regs))

    def f32(self, n, name=""):
        o = self._take(n, name)
        return Buf(self.t[:, o:o + n], Res(name))

    def bf(self, n, name=""):
        m = (n + 1) // 2
        o = self._take(m, name)
        return Buf(self.t[:, o:o + m].bitcast(BF16)[:, 0:n], Res(name))

    def mark(self):
        return [list(r) for r in self.regs]

    def release(self, m):
        self.regs = [list(r) for r in m]

    def top(self):
        return self.regs[0][0]


def ffn_windows(CH):
    nw = -(-CH // 510)
    if CH % 512 == 0 and CH >= 512:
        nw = max(nw, 1)
    base = CH // nw
    rem = CH - base * nw
    sizes = [base + (1 if i < rem else 0) for i in range(nw)]
    starts = [sum(sizes[:i]) for i in range(nw)]
    return list(zip(starts, sizes))


def pp_layout():
    o = {}
    c = 0

    def add(name, n):
        nonlocal c
        o[name] = (c, n)
        c += n
    add("flg", 20)
    add("msk", 2)
    add("cw", 80)
    add("cb", 20)
    add("ba", 20)
    add("bi", 20)
    add("lam", 20)
    add("gqa", 2)
    add("gkv", 1)
    add("gq", 1)
    add("gqr", 1)
    add("gk", 1)
    add("gkr", 1)
    add("glru", 4)
    add("gmla", 8)
    add("gmq", 1)
    add("gmk", 1)
    add("fw", 132)
    add("fb", 44)
    add("invf", 1)
    o["_n"] = c
    return o


PPL = pp_layout()


class _Stop(Exception):
    pass


def build(CH, dbg=None, stop=None):
    holder = {}
    try:
        return _build(CH, dbg, stop, holder)
    except _Stop:
        return holder['nc']


def _build(CH, dbg, stop, holder):
    NB = CH // 512
    NKEY = 4 * CH
    NKT = NKEY // 128
    NKB = NKEY // 512
    NOWN = CH + 2
    WINS = ffn_windows(CH)
    nc = bass.Bass("TRN2", target_bir_lowering=False)
    holder["nc"] = nc

    def din(name, shape, dt=F32):
        return nc.dram_tensor(name, list(shape), dt, kind="ExternalInput").ap()

    xs = din("xs", [5, CH, D])
    xh = din("xh", [2, D])
    posk = din("posk", [4, 32, CH], I32)
    poso = din("poso", [32, NOWN], I32)
    pp_d = din("pp", [128, PPL["_n"]])
    wa_d = din("wa", [5, 4, 128, 128])
    wi_d = din("wi", [5, 4, 128, 128])
    mem_d = din("mem", [MEM, D])
    gvec = din("gvec", [4, D])
    w_in = din("w_in", [D, 1440])
    w_uq = din("w_uq", [256, 768])
    w_ukv = din("w_ukv", [128, 1024])
    w_out = din("w_out", [1024, D])
    w_mq = din("w_mem_q", [D, 512])
    w_mkv = din("w_mem_kv", [D, 1024])
    w_mo = din("w_mem_o", [512, D])
    w_up = din("w_up", [D, 2 * DFF])
    w_dn = din("w_down", [DFF, D])
    out_d = nc.dram_tensor("out", [CH, D], F32, kind="ExternalOutput").ap()
    dbg_d = {}
    if dbg:
        for name, shape in dbg.items():
            dbg_d[name] = nc.dram_tensor("dbg_" + name, list(shape), F32, kind="ExternalOutput").ap()

    S = Sched()
    with ExitStack() as st:
        ASZ = 52500
        arena_t = st.enter_context(nc.sbuf_tensor("arena", [128, ASZ], F32))
        A = Arena(arena_t, ASZ)
        psum_t = [st.enter_context(nc.psum_tensor("ps%d" % i, [128, 512], F32)) for i in range(8)]
        psum = [Buf(t[:, :], Res("ps%d" % i)) for i, t in enumerate(psum_t)]
        psi = [0]

        ps_pool = [[0, 1, 2, 3, 4]]

        def PS():
            pool = ps_pool[0]
            b = psum[pool[psi[0] % len(pool)]]
            psi[0] += 1
            return b
        psq = [0]

        def PSQ():
            b = psum[psq[0] % 3]
            psq[0] += 1
            return b

        def with_pool(pool, fn):
            if pool is None:
                return fn()
            old_ = ps_pool[0]
            ps_pool[0] = pool
            try:
                return fn()
            finally:
                ps_pool[0] = old_
        psa = [0]

        def PSACC():
            b = psum[6 + psa[0] % 2]
            psa[0] += 1
            return b

        def chk(tag):
            if stop == tag:
                S.emit(nc)
                raise _Stop()

        def op(eng, fn, reads=(), writes=(), dma=None):
            return S.op(eng, fn, [b.r for b in reads], [b.r for b in writes], dma)

        dkeys = {}

        def dkey(buf, pre="k"):
            k = (pre, id(buf.r))
            if k not in dkeys:
                dkeys[k] = "%s%02d" % (pre, len(dkeys))
            return dkeys[k]

        def load(dst, dst_ap, src_ap, eng=None, key=None):
            if eng is None:
                eng = "pool" if dst_ap.dtype != src_ap.dtype else "sp"
            op(eng, lambda e: e.dma_start(out=dst_ap, in_=src_ap), reads=[], writes=[dst], dma=dkey(dst))

        def store(dst_ap, buf, src_ap):
            op("sp", lambda e: e.dma_start(out=dst_ap, in_=src_ap), reads=[buf], dma=dkey(buf, "s"))

        def dump(name, buf, ap, rows=None):
            if name in dbg_d:
                store(dbg_d[name], buf, ap)

        PP = A.f32(PPL["_n"], "pp")
        load(PP, PP.ap, pp_d)

        def ppc(name, i=0, rows=slice(0, 128)):
            o, n = PPL[name]
            return PP.ap[rows, o + i:o + i + 1]

        CONST = A.f32(8, "const")
        op("pool", lambda e: e.memset(CONST.ap[:, 0:1], EPS), writes=[CONST])
        op("pool", lambda e: e.memset(CONST.ap[:, 1:2], 1.0), writes=[CONST])
        c_eps = CONST.ap[:, 0:1]
        c_one = CONST.ap[:, 1:2]
        NEGH = A.f32(8, "negh")
        POSH = A.f32(8, "posh")
        op("pool", lambda e: e.memset(NEGH.ap, -0.5), writes=[NEGH])
        op("pool", lambda e: e.memset(POSH.ap, 0.5), writes=[POSH])
        IDF = A.f32(128, "identf")
        IDENT = A.bf(128, "ident")
        ONESB = A.bf(128, "onesb")
        ONESF = A.f32(128, "onesf")
        op("pool", lambda e: e.iota(IDF.ap, [[1, 128]], base=0, channel_multiplier=-1,
                                    allow_small_or_imprecise_dtypes=True), writes=[IDF])
        op("dve", lambda e: e.tensor_scalar(out=IDENT.ap, in0=IDF.ap, scalar1=0.0, scalar2=None,
                                            op0=ALU.is_equal), reads=[IDF], writes=[IDENT])
        op("pool", lambda e: e.memset(ONESB.ap, 1.0), writes=[ONESB])
        op("pool", lambda e: e.memset(ONESF.ap, 1.0), writes=[ONESF])
        LP = A.f32(192, "lruparams")
        lam_o = PPL["lam"][0]
        flg_o = PPL["flg"][0]
        op("act", lambda e: e.activation(out=LP.ap[:, 0:20], in_=PP.ap[:, lam_o:lam_o + 20], func=AF.Exp, scale=-1.0),
           reads=[PP], writes=[LP])
        op("act", lambda e: e.activation(out=LP.ap[:, 0:20], in_=LP.ap[:, 0:20], func=AF.Ln, scale=1.0, bias=c_one),
           reads=[LP, CONST], writes=[LP])
        op("dve", lambda e: e.tensor_scalar(out=LP.ap[:, 20:40], in0=LP.ap[:, 0:20], scalar1=-4.0, scalar2=None,
                                            op0=ALU.mult), reads=[LP], writes=[LP])
        op("dve", lambda e: e.tensor_scalar(out=LP.ap[:, 0:20], in0=LP.ap[:, 0:20], scalar1=-8.0, scalar2=None,
                                            op0=ALU.mult), reads=[LP], writes=[LP])
        op("dve", lambda e: e.tensor_scalar(out=LP.ap[:, 40:60], in0=PP.ap[:, flg_o:flg_o + 20], scalar1=-1.0,
                                            scalar2=1.0, op0=ALU.mult, op1=ALU.add), reads=[PP], writes=[LP])

        ba_o = PPL["ba"][0]
        bi_o = PPL["bi"][0]
        fw_o = PPL["fw"][0]
        fb_o = PPL["fb"][0]
        op("dve", lambda e: e.tensor_scalar(out=LP.ap[:, 60:80], in0=PP.ap[:, ba_o:ba_o + 20], scalar1=0.5, scalar2=None,
                                            op0=ALU.mult), reads=[PP], writes=[LP])
        op("dve", lambda e: e.tensor_scalar(out=LP.ap[:, 80:100], in0=PP.ap[:, bi_o:bi_o + 20], scalar1=0.5, scalar2=None,
                                            op0=ALU.mult), reads=[PP], writes=[LP])
        op("dve", lambda e: e.tensor_scalar(out=LP.ap[:, 100:166], in0=PP.ap[:, fw_o + 66:fw_o + 132], scalar1=0.5, scalar2=None,
                                            op0=ALU.mult), reads=[PP], writes=[LP])
        op("dve", lambda e: e.tensor_scalar(out=LP.ap[:, 166:188], in0=PP.ap[:, fb_o + 22:fb_o + 44], scalar1=0.5, scalar2=None,
                                            op0=ALU.mult), reads=[PP], writes=[LP])

        def flg(k, i):
            return PP.ap[:, flg_o + 4 * k + i:flg_o + 4 * k + i + 1]

        def nflg(k, i):
            return LP.ap[:, 40 + 4 * k + i:40 + 4 * k + i + 1]

        GBC = A.f32(D, "gbc")

        NT_X = [A.f32(D, "xt%d" % i) for i in range(2)]
        NT_J = A.bf(D, "junk")
        NT_H = [A.bf(D, "hb%d" % i) for i in range(2)]
        NT_Sl = [A.f32(2, "nstat%d" % i) for i in range(2)]
        nt_i = [0]

        def norm_T(xbuf, x_ap, n, dstT, dstT_ap3, col0, evac_eng="dve", gb=None):
            gb = gb or GBC
            i = nt_i[0] % 2
            nt_i[0] += 1
            hb = NT_H[i]
            NT_S = NT_Sl[i]
            ss = NT_S.ap[0:n, 0:1]
            rs = NT_S.ap[0:n, 1:2]
            op("act", lambda e: e.activation(out=NT_J.ap[0:n], in_=x_ap, func=AF.Square, accum_out=ss),
               reads=[xbuf], writes=[NT_J, NT_S])
            op("dve", lambda e: e.tensor_scalar(out=rs, in0=ss, scalar1=1.0 / D, scalar2=EPS, op0=ALU.mult, op1=ALU.add),
               reads=[NT_S], writes=[NT_S])
            op("pool", lambda e: e.tensor_tensor(out=rs, in0=rs, in1=NEGH.ap[0:n, 0:1], op=ALU.pow),
               reads=[NT_S, NEGH], writes=[NT_S])
            op("dve", lambda e: e.scalar_tensor_tensor(out=hb.ap[0:n], in0=x_ap, scalar=rs, in1=gb.ap[0:n],
                                                       op0=ALU.mult, op1=ALU.mult),
               reads=[xbuf, NT_S, gb], writes=[hb])
            pb = PS()
            pbf = pb.ap.bitcast(BF16)

            def tr(e):
                ins = None
                for kc in range(8):
                    ins = e.transpose(pbf[:, kc * 128:kc * 128 + n], hb.ap[0:n, kc * 128:(kc + 1) * 128],
                                      IDENT.ap[0:n, 0:n])
                return ins
            op("pe", tr, reads=[hb, IDENT], writes=[pb])
            src = pbf.rearrange("p (k t) -> p k t", k=8)[:, :, 0:n]
            dst = dstT_ap3[:, :, col0:col0 + n]
            if evac_eng == "act":
                op("act", lambda e: e.activation(out=dst, in_=src, func=AF.Copy), reads=[pb], writes=[dstT])
            else:
                op("dve", lambda e: e.tensor_copy(out=dst, in_=src), reads=[pb], writes=[dstT])

        def fm_rstd(sq_list, rows_out, n, dim, SCR, pbuf=None):
            pb = pbuf if pbuf is not None else PS()

            def mm(e):
                ins = None
                for i, (b, ap, rk, base) in enumerate(sq_list):
                    ins = e.matmul(pb.ap[0:rows_out, 0:n], lhsT=ONESB.ap[base:base + rk, 0:rows_out], rhs=ap,
                                   start=(i == 0), stop=(i == len(sq_list) - 1))
                return ins
            op("pe", mm, reads=[b for b, _, _, _ in sq_list] + [ONESB], writes=[pb])
            rb = SCR()
            op("act", lambda e: e.activation(out=rb.ap[0:rows_out, 0:n], in_=pb.ap[0:rows_out, 0:n], func=AF.Ln,
                                             scale=1.0 / dim, bias=c_eps[0:rows_out]),
               reads=[pb, CONST], writes=[rb])
            op("act", lambda e: e.activation(out=rb.ap[0:rows_out, 0:n], in_=rb.ap[0:rows_out, 0:n], func=AF.Exp,
                                             scale=-0.5), reads=[rb], writes=[rb])
            return rb

        POSB = [A.f32(512, "posb%d" % i) for i in range(2)]
        posb_i = [0]

        def rope_tables(pos_src_ap, n, SCR, out_buf=None, out_s=None, out_c=None):
            pi = POSB[posb_i[0] % 2]
            posb_i[0] += 1
            pi_ap = pi.ap.bitcast(I32)
            load(pi, pi_ap[64:96, 0:n], pos_src_ap, eng="sp")
            y = SCR()
            op("dve", lambda e: e.tensor_copy(out=y.ap[64:96, 0:n], in_=pi_ap[64:96, 0:n]), reads=[pi], writes=[y])
            y2 = SCR()
            yc = SCR()
            op("dve", lambda e: e.tensor_scalar(out=y2.ap[64:96, 0:n], in0=y.ap[64:96, 0:n],
                                                scalar1=ppc("invf", 0, slice(64, 96)), scalar2=None, op0=ALU.mult),
               reads=[y, PP], writes=[y2])
            op("dve", lambda e: e.tensor_scalar(out=yc.ap[64:96, 0:n], in0=y2.ap[64:96, 0:n], scalar1=0.25,
                                                scalar2=None, op0=ALU.add), reads=[y2], writes=[yc])
            res = []
            for yy, dst in ((y2, out_s), (yc, out_c)):
                ti = SCR()
                ti_ap = ti.ap.bitcast(I32)
                op("dve", lambda e, yy=yy, ti_ap=ti_ap: e.tensor_copy(out=ti_ap[64:96, 0:n], in_=yy.ap[64:96, 0:n]),
                   reads=[yy], writes=[ti])
                tf = SCR()
                op("dve", lambda e, ti_ap=ti_ap, tf=tf: e.tensor_copy(out=tf.ap[64:96, 0:n], in_=ti_ap[64:96, 0:n]),
                   reads=[ti], writes=[tf])
                op("dve", lambda e, yy=yy, tf=tf: e.tensor_tensor(out=yy.ap[64:96, 0:n], in0=yy.ap[64:96, 0:n],
                                                                  in1=tf.ap[64:96, 0:n], op=ALU.subtract),
                   reads=[yy, tf], writes=[yy])
                if dst is None:
                    o_b, o_ap = yy, yy.ap[64:96, 0:n]
                else:
                    o_b, o_ap = out_buf, dst
                op("act", lambda e, yy=yy, o_ap=o_ap: e.activation(out=o_ap, in_=yy.ap[64:96, 0:n], func=AF.Sin,
                                                                   scale=2.0 * np.pi * 0.999999),
                   reads=[yy], writes=[o_b])
                res.append((o_b, o_ap))
            return res

        KM = A.bf(4 * MEM, "km")
        KM3 = KM.ap.rearrange("p (h t) -> p h t", h=4)
        VM = A.bf(2 * 512, "vm")
        VM3 = VM.ap.rearrange("p (t c) -> p t c", t=2)
        m0 = A.mark()
        scr0 = [A.f32(512, "scr0%d" % i) for i in range(4)]
        s0i = [0]

        def SCRD():
            b = scr0[s0i[0] % len(scr0)]
            s0i[0] += 1
            return b
        if stop == '0a':
            S.emit(nc)
            return nc
        load(GBC, GBC.ap, gvec[2, :].partition_broadcast(128))
        mW = A.mark()
        WMKV = A.bf(8 * 1024, "wmkv")
        WMKV3 = WMKV.ap.rearrange("p (k c) -> p k c", k=8)
        load(WMKV, WMKV3, w_mkv.rearrange("(k p) c -> p k c", p=128))
        MEMT = A.bf(8 * MEM, "memt")
        MEMT3 = MEMT.ap.rearrange("p (k t) -> p k t", k=8)
        for mt in range(2):
            xb = NT_X[nt_i[0] % 2]
            load(xb, xb.ap, mem_d[mt * 128:(mt + 1) * 128, :], eng="sp")
            norm_T(xb, xb.ap, 128, MEMT, MEMT3, mt * 128)
        if stop == '0b':
            S.emit(nc)
            return nc
        for h in range(4):
            pk = PS()

            def mmk(e, pk=pk, h=h):
                ins = None
                for kc in range(8):
                    ins = e.matmul(pk.ap[:, 0:MEM], lhsT=WMKV3[:, kc, h * 128:(h + 1) * 128], rhs=MEMT3[:, kc, :],
                                   start=(kc == 0), stop=(kc == 7))
                return ins
            op("pe", mmk, reads=[WMKV, MEMT], writes=[pk])
            sq = SCRD()
            sq_ap = sq.ap.bitcast(BF16)
            op("act", lambda e, sq_ap=sq_ap, pk=pk: e.activation(out=sq_ap[:, 0:MEM], in_=pk.ap[:, 0:MEM], func=AF.Square),
               reads=[pk], writes=[sq])
            rb = fm_rstd([(sq, sq_ap[:, 0:MEM], 128, 0)], 128, MEM, 128.0, SCRD)
            op("dve", lambda e, pk=pk, rb=rb, h=h: e.scalar_tensor_tensor(
                out=KM3[:, h, :], in0=pk.ap[:, 0:MEM], scalar=ppc("gmk"), in1=rb.ap[:, 0:MEM], op0=ALU.mult, op1=ALU.mult),
               reads=[pk, rb, PP], writes=[KM])
        if stop == '0c':
            S.emit(nc)
            return nc
        for mt in range(2):
            pv = PS()

            def mmv2(e, pv=pv, mt=mt):
                ins = None
                for kc in range(8):
                    ins = e.matmul(pv.ap[:, :], lhsT=MEMT3[:, kc, mt * 128:(mt + 1) * 128], rhs=WMKV3[:, kc, 512:1024],
                                   start=(kc == 0), stop=(kc == 7))
                return ins
            op("pe", mmv2, reads=[WMKV, MEMT], writes=[pv])
            op("act", lambda e, pv=pv, mt=mt: e.activation(out=VM3[:, mt, :], in_=pv.ap[:, :], func=AF.Copy),
               reads=[pv], writes=[VM])
        if stop == '0d':
            S.emit(nc)
            return nc
        S.barrier_all()
        A.release(m0)
        mP = A.top()
        if stop == '0':
            S.emit(nc)
            return nc

        MIXL = A.bf(4 * NOWN, "mixl")
        MIXL3 = MIXL.ap.rearrange("p (c t) -> p c t", c=4)
        CQN = A.bf(2 * NOWN, "cqn")
        CQN3 = CQN.ap.rearrange("p (c t) -> p c t", c=2)
        ckvn_start = A.top()
        CKVN = A.bf(NKEY, "ckvn")
        ETS = A.bf(NKEY, "e_tsq")
        ets_end = A.top()
        mA = A.mark()

        WIN = A.bf(8 * 1440, "win")
        WIN3 = WIN.ap.rearrange("p (k c) -> p k c", k=8)
        for kc in range(8):
            load(WIN, WIN3[:, kc, :], w_in[kc * 128:(kc + 1) * 128, :])
        WKR = A.bf(8 * 192, "wkrpad")
        WKR3 = WKR.ap.rearrange("p (k c) -> p k c", k=8)
        op("pool", lambda e: e.memset(WKR.ap, 0.0), writes=[WKR])
        op("dve", lambda e: e.tensor_copy(out=WKR3[:, :, 64:96], in_=WIN3[:, :, 1408:1440]), reads=[WIN], writes=[WKR])
        op("dve", lambda e: e.tensor_scalar(out=WKR3[:, :, 160:176], in0=WIN3[:, :, 1424:1440], scalar1=-1.0,
                                            scalar2=None, op0=ALU.mult), reads=[WIN], writes=[WKR])
        op("dve", lambda e: e.tensor_copy(out=WKR3[:, :, 176:192], in_=WIN3[:, :, 1408:1424]), reads=[WIN], writes=[WKR])
        WGL = [A.bf(4 * 2 * 128, "wgates%d" % i) for i in range(2)]
        WGL4 = [w.ap.rearrange("p (c g o) -> p c g o", c=4, g=2) for w in WGL]
        HTL = [A.bf(8 * 512, "ht%d" % i) for i in range(2)]
        HTL3 = [h_.ap.rearrange("p (k t) -> p k t", k=8) for h_ in HTL]
        XR = A.f32(4 * 515, "xr")
        XR3 = XR.ap.rearrange("p (c t) -> p c t", c=4)
        HF = A.bf(4 * (CH + 1), "hf")
        HF3 = HF.ap.rearrange("p (c t) -> p c t", c=4)
        GG = A.bf(4 * NOWN, "gelu")
        GG3 = GG.ap.rearrange("p (c t) -> p c t", c=4)
        CAR = A.f32(64, "carry")
        op("pool", lambda e: e.memset(CAR.ap, 0.0), writes=[CAR])
        HSC = A.f32(4 * 512, "hscan")
        HSC3 = HSC.ap.rearrange("p (c t) -> p c t", c=4)
        HE = A.f32(16, "hextra")
        scrA = [A.f32(512, "scrA%d" % i) for i in range(12)]
        sai = [0]

        def SCRA():
            b = scrA[sai[0] % len(scrA)]
            sai[0] += 1
            return b

        load(GBC, GBC.ap, gvec[0, :].partition_broadcast(128))

        chk('A0')

        def lru_cols(k, ct, name):
            o, n = PPL[name]
            return PP.ap[:, o + 4 * k + ct:o + 4 * k + ct + 1]

        def phaseA_front(k, j, n, mini, hb):
            HT = HTL[hb]
            HT3 = HTL3[hb]
            if j == 0 and not mini:
                load(WGL[k % 2], WGL4[k % 2][:, :, 0, :], wa_d[k].rearrange("c i o -> i c o"))
                load(WGL[k % 2], WGL4[k % 2][:, :, 1, :], wi_d[k].rearrange("c i o -> i c o"))
            ntile = (n + 127) // 128
            for i in range(ntile):
                rows = min(128, n - i * 128)
                xb = NT_X[nt_i[0] % 2]
                if mini:
                    src = xh[0:1, :] if k == 3 else xh[1:2, :]
                else:
                    src = xs[k, j * 512 + i * 128:j * 512 + i * 128 + rows, :]
                load(xb, xb.ap[0:rows], src, eng="sp")
                norm_T(xb, xb.ap[0:rows], rows, HT, HT3, i * 128, evac_eng="dve")

        def phaseA_block(k, j, n, mini, hb):
            HT = HTL[hb]
            HT3 = HTL3[hb]
            WG = WGL[k % 2]
            WG4 = WGL4[k % 2]
            do_kv = (k <= 3) and not mini
            do_own = (k == 3) or (k == 4 and mini)
            if mini:
                own0 = CH if k == 3 else CH + 1
            else:
                own0 = j * 512
            if j == 0 and not mini:
                op("dve", lambda e: e.tensor_scalar(out=CAR.ap[:, 36:48], in0=CAR.ap[:, 8:20], scalar1=flg(k, 0),
                                                    scalar2=None, op0=ALU.mult), reads=[CAR, PP], writes=[CAR])
                op("dve", lambda e: e.scalar_tensor_tensor(out=CAR.ap[:, 36:48], in0=CAR.ap[:, 20:32], scalar=flg(k, 1),
                                                           in1=CAR.ap[:, 36:48], op0=ALU.mult, op1=ALU.add),
                   reads=[CAR, PP], writes=[CAR])
                op("dve", lambda e: e.tensor_scalar(out=CAR.ap[:, 32:36], in0=CAR.ap[:, 0:4], scalar1=flg(k, 0),
                                                    scalar2=None, op0=ALU.mult), reads=[CAR, PP], writes=[CAR])
                op("dve", lambda e: e.scalar_tensor_tensor(out=CAR.ap[:, 32:36], in0=CAR.ap[:, 4:8], scalar=flg(k, 1),
                                                           in1=CAR.ap[:, 32:36], op0=ALU.mult, op1=ALU.add),
                   reads=[CAR, PP], writes=[CAR])
                if k == 3:
                    op("dve", lambda e: e.tensor_copy(out=CAR.ap[:, 48:52], in_=CAR.ap[:, 32:36]), reads=[CAR], writes=[CAR])
                if k == 4:
                    op("dve", lambda e: e.tensor_copy(out=CAR.ap[:, 52:56], in_=CAR.ap[:, 32:36]), reads=[CAR], writes=[CAR])
                op("dve", lambda e: e.tensor_copy(out=XR3[:, :, 0:3],
                                                  in_=CAR.ap[:, 36:48].rearrange("p (c t) -> p c t", c=4)),
                   reads=[CAR], writes=[XR])
            else:
                op("dve", lambda e: e.tensor_copy(out=XR3[:, :, 0:3], in_=XR3[:, :, 512:515]), reads=[XR], writes=[XR])
            for ct in range(4):
                pb = PS()

                def mm(e, pb=pb, ct=ct):
                    ins = None
                    for kc in range(8):
                        ins = e.matmul(pb.ap[:, 0:n], lhsT=WIN3[:, kc, ct * 128:(ct + 1) * 128], rhs=HT3[:, kc, 0:n],
                                       start=(kc == 0), stop=(kc == 7))
                    return ins
                op("pe", mm, reads=[WIN, HT], writes=[pb])
                op("act", lambda e, pb=pb, ct=ct: e.activation(out=XR3[:, ct, 3:3 + n], in_=pb.ap[:, 0:n], func=AF.Copy),
                   reads=[pb], writes=[XR])
            chk('A1')
            for pr in range(2):
                cts = (2 * pr, 2 * pr + 1)
                B_ = {}
                for ct in cts:
                    cwo = PPL["cw"][0] + (k * 4 + ct) * 4
                    xc = SCRA()
                    op("act", lambda e, xc=xc, ct=ct, cwo=cwo: e.activation(
                        out=xc.ap[:, 0:n], in_=XR3[:, ct, 3:3 + n], func=AF.Identity,
                        scale=PP.ap[:, cwo + 3:cwo + 4], bias=lru_cols(k, ct, "cb")), reads=[XR, PP], writes=[xc])
                    for tp in range(3):
                        op("dve", lambda e, xc=xc, ct=ct, cwo=cwo, tp=tp: e.scalar_tensor_tensor(
                            out=xc.ap[:, 0:n], in0=XR3[:, ct, tp:tp + n], scalar=PP.ap[:, cwo + tp:cwo + tp + 1],
                            in1=xc.ap[:, 0:n], op0=ALU.mult, op1=ALU.add), reads=[XR, PP, xc], writes=[xc])
                    xcb = SCRA()
                    xcb_ap = xcb.ap.bitcast(BF16)
                    op("dve", lambda e, xc=xc, xcb_ap=xcb_ap: e.tensor_copy(out=xcb_ap[:, 0:n], in_=xc.ap[:, 0:n]),
                       reads=[xc], writes=[xcb])
                    B_[ct] = dict(xc=xc, xcb=xcb, xcb_ap=xcb_ap)
                for ct in cts:
                    d = B_[ct]
                    pa = PS()
                    op("pe", lambda e, pa=pa, ct=ct, xcb_ap=d["xcb_ap"]: e.matmul(
                        pa.ap[:, 0:n], lhsT=WG4[:, ct, 0, :], rhs=xcb_ap[:, 0:n], start=True, stop=True),
                       reads=[WG, d["xcb"]], writes=[pa])
                    pi_ = PS()
                    op("pe", lambda e, pi_=pi_, ct=ct, xcb_ap=d["xcb_ap"]: e.matmul(
                        pi_.ap[:, 0:n], lhsT=WG4[:, ct, 1, :], rhs=xcb_ap[:, 0:n], start=True, stop=True),
                       reads=[WG, d["xcb"]], writes=[pi_])
                    d["pa"] = pa
                    d["pi"] = pi_
                for ct in cts:
                    d = B_[ct]
                    rr = SCRA()
                    ig = SCRA()
                    op("act", lambda e, rr=rr, pa=d["pa"], ct=ct: e.activation(out=rr.ap[:, 0:n], in_=pa.ap[:, 0:n], func=AF.Tanh,
                                                                          bias=LP.ap[:, 60 + 4 * k + ct:61 + 4 * k + ct], scale=0.5),
                       reads=[d["pa"], LP], writes=[rr])
                    op("act", lambda e, ig=ig, pi_=d["pi"], ct=ct: e.activation(out=ig.ap[:, 0:n], in_=pi_.ap[:, 0:n], func=AF.Tanh,
                                                                            bias=LP.ap[:, 80 + 4 * k + ct:81 + 4 * k + ct], scale=0.5),
                       reads=[d["pi"], LP], writes=[ig])
                    d["rr"] = rr
                    d["ig"] = ig
                for ct in cts:
                    d = B_[ct]
                    aa = SCRA()
                    mm_ = SCRA()
                    cs = LP.ap[:, 4 * k + ct:4 * k + ct + 1]
                    hcs = LP.ap[:, 20 + 4 * k + ct:20 + 4 * k + ct + 1]
                    op("act", lambda e, aa=aa, rr=d["rr"], hcs=hcs: e.activation(out=aa.ap[:, 0:n], in_=rr.ap[:, 0:n], func=AF.Exp,
                                                                             scale=hcs, bias=hcs), reads=[d["rr"], LP], writes=[aa])
                    op("act", lambda e, mm_=mm_, rr=d["rr"], cs=cs: e.activation(out=mm_.ap[:, 0:n], in_=rr.ap[:, 0:n], func=AF.Exp,
                                                                             scale=cs, bias=cs), reads=[d["rr"], LP], writes=[mm_])
                    op("dve", lambda e, mm_=mm_: e.tensor_scalar(out=mm_.ap[:, 0:n], in0=mm_.ap[:, 0:n], scalar1=-0.25, scalar2=0.25,
                                                                 op0=ALU.mult, op1=ALU.add), reads=[mm_], writes=[mm_])
                    op("dve", lambda e, ig=d["ig"], xc=d["xc"]: e.scalar_tensor_tensor(out=ig.ap[:, 0:n], in0=ig.ap[:, 0:n], scalar=1.0,
                                                                                   in1=xc.ap[:, 0:n], op0=ALU.add, op1=ALU.mult),
                       reads=[d["ig"], d["xc"]], writes=[d["ig"]])
                    d["aa"] = aa
                    d["mm"] = mm_
                for ct in cts:
                    d = B_[ct]
                    op("act", lambda e, mm_=d["mm"]: e.activation(out=mm_.ap[:, 0:n], in_=mm_.ap[:, 0:n], func=AF.Sqrt),
                       reads=[d["mm"]], writes=[d["mm"]])
                for ct in cts:
                    d = B_[ct]
                    op("dve", lambda e, ig=d["ig"], mm_=d["mm"]: e.tensor_tensor(out=ig.ap[:, 0:n], in0=ig.ap[:, 0:n], in1=mm_.ap[:, 0:n],
                                                                              op=ALU.mult), reads=[d["ig"], d["mm"]], writes=[d["ig"]])
                    if mini or j > 0:
                        init = CAR.ap[:, 56 + ct:57 + ct]
                    else:
                        init = CAR.ap[:, 32 + ct:33 + ct]
                    op("dve", lambda e, aa=d["aa"], ig=d["ig"], ct=ct, init=init: e.tensor_tensor_scan(
                        out=HSC3[:, ct, 0:n], data0=aa.ap[:, 0:n], data1=ig.ap[:, 0:n], initial=init,
                        op0=ALU.mult, op1=ALU.add), reads=[d["aa"], d["ig"], CAR], writes=[HSC])
            if not mini:
                op("dve", lambda e: e.tensor_copy(out=CAR.ap[:, 56:60], in_=HSC3[:, :, n - 1]), reads=[HSC], writes=[CAR])
            chk('A2')
            if k == 3:
                op("dve", lambda e: e.tensor_copy(out=HF3[:, :, own0 if not mini else CH:(own0 if not mini else CH) + n],
                                                   in_=HSC3[:, :, 0:n]), reads=[HSC], writes=[HF])
            if k <= 2 and (not mini) and j == NB - 1:
                for (st_o, u_i) in ((0, 2), (4, 3)):
                    op("dve", lambda e, st_o=st_o, u_i=u_i: e.tensor_scalar(
                        out=CAR.ap[:, st_o:st_o + 4], in0=CAR.ap[:, st_o:st_o + 4], scalar1=nflg(k, u_i), scalar2=None,
                        op0=ALU.mult), reads=[CAR, LP], writes=[CAR])
                    op("dve", lambda e, st_o=st_o, u_i=u_i: e.scalar_tensor_tensor(
                        out=CAR.ap[:, st_o:st_o + 4], in0=HSC3[:, :, n - 1], scalar=flg(k, u_i), in1=CAR.ap[:, st_o:st_o + 4],
                        op0=ALU.mult, op1=ALU.add), reads=[CAR, HSC, PP], writes=[CAR])
                for (h_o, u_i) in ((8, 2), (20, 3)):
                    hv = CAR.ap[:, h_o:h_o + 12].rearrange("p (c t) -> p c t", c=4)
                    op("dve", lambda e, hv=hv, u_i=u_i: e.tensor_scalar(
                        out=hv, in0=hv, scalar1=nflg(k, u_i), scalar2=None, op0=ALU.mult), reads=[CAR, LP], writes=[CAR])
                    op("dve", lambda e, hv=hv, u_i=u_i: e.scalar_tensor_tensor(
                        out=hv, in0=XR3[:, :, 512:515], scalar=flg(k, u_i), in1=hv, op0=ALU.mult, op1=ALU.add),
                       reads=[CAR, XR, PP], writes=[CAR])
            chk('A3')
            if do_kv:
                key0 = k * CH + j * 512
                pb = PS()

                def mmkv(e, pb=pb):
                    ins = None
                    for kc in range(8):
                        ins = e.matmul(pb.ap[:, 0:n], lhsT=WIN3[:, kc, 1280:1408], rhs=HT3[:, kc, 0:n],
                                       start=(kc == 0), stop=(kc == 7))
                    return ins
                op("pe", mmkv, reads=[WIN, HT], writes=[pb])
                cf = SCRA()
                sq = SCRA()
                sq_ap = sq.ap.bitcast(BF16)
                op("act", lambda e, cf=cf, pb=pb: e.activation(out=cf.ap[:, 0:n], in_=pb.ap[:, 0:n], func=AF.Copy),
                   reads=[pb], writes=[cf])
                op("act", lambda e, sq_ap=sq_ap, pb=pb: e.activation(out=sq_ap[:, 0:n], in_=pb.ap[:, 0:n], func=AF.Square),
                   reads=[pb], writes=[sq])
                rb = fm_rstd([(sq, sq_ap[:, 0:n], 128, 0)], 128, n, 128.0, SCRA)
                op("dve", lambda e, cf=cf, rb=rb: e.scalar_tensor_tensor(
                    out=CKVN.ap[:, key0:key0 + n], in0=cf.ap[:, 0:n], scalar=ppc("gkv"), in1=rb.ap[:, 0:n],
                    op0=ALU.mult, op1=ALU.mult), reads=[cf, rb, PP], writes=[CKVN])
                chk('A3a')
                pt = PS()
                prt = PS()

                def mmt(e, pt=pt, o=0):
                    ins = None
                    for kc in range(8):
                        ins = e.matmul(pt.ap[0:96, 0:n], lhsT=WKR3[:, kc, o:o + 96], rhs=HT3[:, kc, 0:n],
                                       start=(kc == 0), stop=(kc == 7))
                    return ins
                op("pe", lambda e: mmt(e, pt, 0), reads=[WKR, HT], writes=[pt])
                op("pe", lambda e: mmt(e, prt, 96), reads=[WKR, HT], writes=[prt])
                chk('A3b')
                (sb, s_ap), (cb_, c_ap) = rope_tables(posk[k, :, j * 512:j * 512 + n], n, SCRA)
                chk('A3c')
                tq = SCRA()
                tq_ap = tq.ap.bitcast(BF16)
                op("act", lambda e, pt=pt, tq_ap=tq_ap: e.activation(out=tq_ap[64:96, 0:n], in_=pt.ap[64:96, 0:n], func=AF.Square),
                   reads=[pt], writes=[tq])
                op("dve", lambda e, tq_ap=tq_ap: e.tensor_copy(out=ETS.ap[0:32, key0:key0 + n], in_=tq_ap[64:96, 0:n]),
                   reads=[tq], writes=[ETS])
                e1 = SCRA()
                e2 = SCRA()
                op("dve", lambda e, e1=e1, pt=pt, c_ap=c_ap: e.scalar_tensor_tensor(
                    out=e1.ap[64:96, 0:n], in0=pt.ap[64:96, 0:n], scalar=ppc("gk", 0, slice(64, 96)), in1=c_ap,
                    op0=ALU.mult, op1=ALU.mult), reads=[pt, cb_, PP, tq], writes=[e1])
                op("dve", lambda e, e2=e2, prt=prt, s_ap=s_ap: e.scalar_tensor_tensor(
                    out=e2.ap[64:96, 0:n], in0=prt.ap[64:96, 0:n], scalar=ppc("gkr", 0, slice(64, 96)), in1=s_ap,
                    op0=ALU.mult, op1=ALU.mult), reads=[prt, sb, PP], writes=[e2])
                op("dve", lambda e, e1=e1, e2=e2: e.tensor_tensor(out=ETS.ap[64:96, key0:key0 + n], in0=e1.ap[64:96, 0:n],
                                                                 in1=e2.ap[64:96, 0:n], op=ALU.add),
                   reads=[e1, e2], writes=[ETS])
            chk('A4')
            if do_own:
                for ct in range(4):
                    pb = PS()

                    def mmy(e, pb=pb, ct=ct):
                        ins = None
                        for kc in range(8):
                            ins = e.matmul(pb.ap[:, 0:n], lhsT=WIN3[:, kc, 512 + ct * 128:512 + (ct + 1) * 128],
                                           rhs=HT3[:, kc, 0:n], start=(kc == 0), stop=(kc == 7))
                        return ins
                    op("pe", mmy, reads=[WIN, HT], writes=[pb])
                    u = SCRA()
                    w = SCRA()
                    op("act", lambda e, u=u, pb=pb: e.activation(out=u.ap[:, 0:n], in_=pb.ap[:, 0:n], func=AF.Copy, scale=0.5),
                       reads=[pb], writes=[u])
                    op("act", lambda e, w=w, pb=pb: e.activation(out=w.ap[:, 0:n], in_=pb.ap[:, 0:n], func=AF.Square),
                       reads=[pb], writes=[w])
                    op("dve", lambda e, w=w: e.tensor_scalar(out=w.ap[:, 0:n], in0=w.ap[:, 0:n], scalar1=2.0 * 0.044715, scalar2=2.0,
                                                             op0=ALU.mult, op1=ALU.add), reads=[w], writes=[w])
                    op("dve", lambda e, w=w, u=u: e.tensor_tensor(out=w.ap[:, 0:n], in0=w.ap[:, 0:n], in1=u.ap[:, 0:n], op=ALU.mult),
                       reads=[w, u], writes=[w])
                    op("act", lambda e, w=w: e.activation(out=w.ap[:, 0:n], in_=w.ap[:, 0:n], func=AF.Tanh,
                                                          scale=0.7978845608028654), reads=[w], writes=[w])
                    op("dve", lambda e, w=w, u=u, ct=ct: e.scalar_tensor_tensor(out=GG3[:, ct, own0:own0 + n], in0=w.ap[:, 0:n],
                                                                             scalar=1.0, in1=u.ap[:, 0:n], op0=ALU.add, op1=ALU.mult),
                       reads=[w, u], writes=[GG])
                cfs = []
                sqs = []
                for c2 in range(2):
                    pb = PS()

                    def mmq(e, pb=pb, c2=c2):
                        ins = None
                        for kc in range(8):
                            ins = e.matmul(pb.ap[:, 0:n], lhsT=WIN3[:, kc, 1024 + c2 * 128:1024 + (c2 + 1) * 128],
                                           rhs=HT3[:, kc, 0:n], start=(kc == 0), stop=(kc == 7))
                        return ins
                    op("pe", mmq, reads=[WIN, HT], writes=[pb])
                    cf = SCRA()
                    sq = SCRA()
                    sq_ap = sq.ap.bitcast(BF16)
                    op("act", lambda e, cf=cf, pb=pb: e.activation(out=cf.ap[:, 0:n], in_=pb.ap[:, 0:n], func=AF.Copy),
                       reads=[pb], writes=[cf])
                    op("act", lambda e, sq_ap=sq_ap, pb=pb: e.activation(out=sq_ap[:, 0:n], in_=pb.ap[:, 0:n], func=AF.Square),
                       reads=[pb], writes=[sq])
                    cfs.append(cf)
                    sqs.append((sq, sq_ap[:, 0:n], 128, 0))
                rb = fm_rstd(sqs, 128, n, 256.0, SCRA)
                for c2 in range(2):
                    op("dve", lambda e, c2=c2, cf=cfs[c2], rb=rb: e.scalar_tensor_tensor(
                        out=CQN3[:, c2, own0:own0 + n], in0=cf.ap[:, 0:n], scalar=ppc("gqa", c2), in1=rb.ap[:, 0:n],
                        op0=ALU.mult, op1=ALU.mult), reads=[cf, rb, PP], writes=[CQN])
            if k == 4 and not mini:
                lo = CH - 512 * (j + 1)
                lru_combine(lambda ct: HF3[:, ct, lo:lo + n], HF, lambda ct: HSC3[:, ct, 0:n][:, ::-1], HSC, lo, n)

        def lru_combine(hf_ap, hf_buf, hb_ap, hb_buf, own0, n):
            los = []
            sqs = []
            for ct in range(4):
                lo_ = SCRA()
                op("dve", lambda e, lo_=lo_, ct=ct: e.tensor_tensor(out=lo_.ap[:, 0:n], in0=hf_ap(ct), in1=hb_ap(ct), op=ALU.add),
                   reads=[hf_buf, hb_buf], writes=[lo_])
                op("dve", lambda e, lo_=lo_, ct=ct: e.tensor_tensor(out=lo_.ap[:, 0:n], in0=lo_.ap[:, 0:n],
                                                                  in1=GG3[:, ct, own0:own0 + n], op=ALU.mult),
                   reads=[lo_, GG], writes=[lo_])
                sq = SCRA()
                sq_ap = sq.ap.bitcast(BF16)
                op("act", lambda e, sq_ap=sq_ap, lo_=lo_: e.activation(out=sq_ap[:, 0:n], in_=lo_.ap[:, 0:n], func=AF.Square),
                   reads=[lo_], writes=[sq])
                los.append(lo_)
                sqs.append((sq, sq_ap[:, 0:n], 128, 0))
            rb = fm_rstd(sqs, 128, n, 512.0, SCRA)
            for ct in range(4):
                op("dve", lambda e, ct=ct, lo_=los[ct], rb=rb: e.scalar_tensor_tensor(
                    out=MIXL3[:, ct, own0:own0 + n], in0=lo_.ap[:, 0:n], scalar=ppc("glru", ct), in1=rb.ap[:, 0:n],
                    op0=ALU.mult, op1=ALU.mult), reads=[lo_, rb, PP], writes=[MIXL])

        blks = []
        for k in range(5):
            for j in range(NB):
                blks.append((k, j, 512, False))
            if k >= 3:
                blks.append((k, NB, 1, True))
        phaseA_front(*blks[0], 0)
        for bi, (k, j, n, mini) in enumerate(blks):
            if bi + 1 < len(blks):
                phaseA_front(*blks[bi + 1], (bi + 1) % 2)
            phaseA_block(k, j, n, mini, bi % 2)
            if mini and k == 3:
                op("dve", lambda e: e.tensor_copy(out=HE.ap[:, 0:4], in_=HSC3[:, :, 0]), reads=[HSC], writes=[HE])
            if mini and k == 4:
                op("dve", lambda e: e.tensor_copy(out=HE.ap[:, 4:8], in_=HSC3[:, :, 0]), reads=[HSC], writes=[HE])
        chk('A6')
        HE3 = HE.ap[:, 8:16].rearrange("p (c t) -> p c t", c=4)
        op("dve", lambda e: e.tensor_tensor(out=HE3[:, :, 0], in0=HE.ap[:, 0:4], in1=CAR.ap[:, 52:56], op=ALU.add),
           reads=[HE, CAR], writes=[HE])
        op("dve", lambda e: e.tensor_tensor(out=HE3[:, :, 1], in0=HE.ap[:, 4:8], in1=CAR.ap[:, 48:52], op=ALU.add),
           reads=[HE, CAR], writes=[HE])
        ZERO = SCRA()
        op("pool", lambda e: e.memset(ZERO.ap[:, 0:8], 0.0), writes=[ZERO])
        lru_combine(lambda ct: HE3[:, ct, :], HE, lambda ct: ZERO.ap[:, 0:2], ZERO, CH, 2)
        if "mixl" in dbg_d:
            TMPD = A.f32(4 * NOWN, "tmpd")
            op("dve", lambda e: e.tensor_copy(out=TMPD.ap, in_=MIXL.ap), reads=[MIXL], writes=[TMPD])
            dump("mixl", TMPD, TMPD.ap)
        if "ckvn" in dbg_d:
            TMPD2 = A.f32(NKEY, "tmpd2")
            op("dve", lambda e: e.tensor_copy(out=TMPD2.ap, in_=CKVN.ap), reads=[CKVN], writes=[TMPD2])
            dump("ckvn", TMPD2, TMPD2.ap)
        if "ets" in dbg_d:
            TMPD3 = A.f32(NKEY, "tmpd3")
            op("dve", lambda e: e.tensor_copy(out=TMPD3.ap[0:96], in_=ETS.ap[0:96]), reads=[ETS], writes=[TMPD3])
            dump("ets", TMPD3, TMPD3.ap[0:96])

        S.barrier_all()
        A.release(mA)
        if stop == 'A':
            S.emit(nc)
            return nc

        WUQ = A.bf(2 * 768, "wuq")
        WUQ3 = WUQ.ap.rearrange("p (k c) -> p k c", k=2)
        for c2 in range(2):
            load(WUQ, WUQ3[:, c2, :], w_uq[c2 * 128:(c2 + 1) * 128, :])
        WUQR = A.bf(2 * 8 * 96, "wuqr")
        WUQR4 = WUQR.ap.rearrange("p (k h c) -> p k h c", k=2, h=8)
        WUQ4 = WUQ.ap.rearrange("p (k h c) -> p k h c", k=2, h=8)
        op("pool", lambda e: e.memset(WUQR.ap, 0.0), writes=[WUQR])
        for c2 in range(2):
            op("dve", lambda e, c2=c2: e.tensor_scalar(out=WUQR4[:, c2, :, 64:80], in0=WUQ4[:, c2, :, 80:96], scalar1=-1.0,
                                                       scalar2=None, op0=ALU.mult), reads=[WUQ], writes=[WUQR])
            op("dve", lambda e, c2=c2: e.tensor_copy(out=WUQR4[:, c2, :, 80:96], in_=WUQ4[:, c2, :, 64:80]),
               reads=[WUQ], writes=[WUQR])
        WUKV = A.bf(1024, "wukv")
        load(WUKV, WUKV.ap, w_ukv)
        ON = A.bf(8 * NOWN, "on")
        ON3 = ON.ap.rearrange("p (h t) -> p h t", h=8)
        TAB = A.bf(2 * NOWN, "qtab")
        mB = A.mark()
        KT = [A.bf(NKEY, "kt%d" % i) for i in range(2)]
        VV = [A.bf(NKT * 65, "v%d" % i) for i in range(2)]
        VV3 = [v.ap.rearrange("p (t c) -> p t c", c=65) for v in VV]
        QT = [A.bf(NOWN, "qt%d" % i) for i in range(2)]
        PT = [Buf(NT_H[i].ap[:, 0:512], NT_H[i].r) for i in range(2)] + [A.bf(512, "pt%d" % i) for i in range(2)]
        scrB = [Buf(NT_X[i].ap[:, 0:512], NT_X[i].r) for i in range(2)] + [A.f32(512, "scrB%d" % i) for i in range(6)]
        sbi = [0]

        def SCRB():
            b = scrB[sbi[0] % len(scrB)]
            sbi[0] += 1
            return b
        for v3, v in zip(VV3, VV):
            op("pool", lambda e, v3=v3: e.memset(v3[:, :, 64:65], 1.0), writes=[v])
        nqb = -(-NOWN // 512)
        half = NOWN // 2
        qb_base = half // nqb
        qb_sizes = [2 * (qb_base + (1 if i < half - qb_base * nqb else 0)) for i in range(nqb)]
        qblocks = [(sum(qb_sizes[:i]), qb_sizes[i]) for i in range(nqb)]
        for (q0, n) in qblocks:
            rope_tables(poso[:, q0:q0 + n], n, SCRB, out_buf=TAB, out_s=TAB.ap[64:96, q0:q0 + n],
                        out_c=TAB.ap[64:96, NOWN + q0:NOWN + q0 + n])
        pti = [0]

        RK = [A.f32(NKT, "rk%d" % i) for i in range(2)]
        SSK = psum[5]

        for kt_ in KT:
            op("dve", lambda e, kt_=kt_: e.tensor_copy(out=kt_.ap[64:96, :], in_=ETS.ap[64:96, :]), reads=[ETS], writes=[kt_])

        def gen_units(h):
            kt = KT[h % 2]
            rk = RK[h % 2]
            vv = VV[h % len(VV)]
            vv3 = VV3[h % len(VV)]
            qt = QT[h % 2]
            st = {}

            def kgenA(kb):
                c0 = kb * 512
                pk = PS()
                op("pe", lambda e: e.matmul(pk.ap[0:64, :], lhsT=WUKV.ap[:, h * 128:h * 128 + 64],
                                            rhs=CKVN.ap[:, c0:c0 + 512], start=True, stop=True),
                   reads=[WUKV, CKVN], writes=[pk])
                sq = SCRB()
                sq_ap = sq.ap.bitcast(BF16)
                op("act", lambda e: e.activation(out=sq_ap[0:64, 0:512], in_=pk.ap[0:64, :], func=AF.Square),
                   reads=[pk], writes=[sq])
                op("dve", lambda e: e.tensor_scalar(out=kt.ap[0:64, c0:c0 + 512], in0=pk.ap[0:64, :],
                                                    scalar1=ppc("gk", 0, slice(0, 64)), scalar2=None, op0=ALU.mult),
                   reads=[pk, PP, sq], writes=[kt])
                st[("k", kb)] = (sq, sq_ap)

            def kgenB(kb):
                c0 = kb * 512
                sq, sq_ap = st[("k", kb)]

                def mms(e):
                    ins = None
                    for i in range(4):
                        t = kb * 4 + i
                        e.matmul(SSK.ap[:, t:t + 1], lhsT=sq_ap[0:64, i * 128:(i + 1) * 128], rhs=ONESB.ap[0:64, 0:1],
                                 start=True, stop=False)
                        ins = e.matmul(SSK.ap[:, t:t + 1], lhsT=ETS.ap[0:32, c0 + i * 128:c0 + (i + 1) * 128],
                                       rhs=ONESB.ap[0:32, 0:1], start=False, stop=True)
                    return ins
                op("pe", mms, reads=[sq, ETS, ONESB], writes=[SSK])

            def kfinal():
                op("dve", lambda e: e.tensor_scalar(out=rk.ap[:, 0:NKT], in0=SSK.ap[:, 0:NKT], scalar1=1.0, scalar2=96.0 * EPS,
                                                    op0=ALU.mult, op1=ALU.add), reads=[SSK], writes=[rk])
                op("act", lambda e: e.activation(out=rk.ap[:, 0:NKT], in_=rk.ap[:, 0:NKT], func=AF.Ln), reads=[rk], writes=[rk])
                op("act", lambda e: e.activation(out=rk.ap[:, 0:NKT], in_=rk.ap[:, 0:NKT], func=AF.Exp, scale=-0.5),
                   reads=[rk], writes=[rk])

            def vgen(kb):
                c0 = kb * 512
                pv = PS()

                def mmv(e):
                    ins = None
                    for i in range(4):
                        ins = e.matmul(pv.ap[:, i * 64:(i + 1) * 64], lhsT=CKVN.ap[:, c0 + i * 128:c0 + (i + 1) * 128],
                                       rhs=WUKV.ap[:, h * 128 + 64:h * 128 + 128], start=True, stop=True)
                    return ins
                op("pe", mmv, reads=[CKVN, WUKV], writes=[pv])
                op("dve", lambda e: e.tensor_copy(out=vv3[:, kb * 4:(kb + 1) * 4, 0:64],
                                                  in_=pv.ap[:, 0:256].rearrange("p (t c) -> p t c", c=64)),
                   reads=[pv], writes=[vv])

            def qgenA(q0, n):
                pq = PS()
                pr = PS()

                def mmq(e):
                    ins = None
                    for c2 in range(2):
                        ins = e.matmul(pq.ap[0:96, 0:n], lhsT=WUQ3[:, c2, h * 96:(h + 1) * 96], rhs=CQN3[:, c2, q0:q0 + n],
                                       start=(c2 == 0), stop=(c2 == 1))
                    return ins

                def mmr(e):
                    ins = None
                    for c2 in range(2):
                        ins = e.matmul(pr.ap[0:96, 0:n], lhsT=WUQR4[:, c2, h, :], rhs=CQN3[:, c2, q0:q0 + n],
                                       start=(c2 == 0), stop=(c2 == 1))
                    return ins
                op("pe", mmq, reads=[WUQ, CQN], writes=[pq])
                op("pe", mmr, reads=[WUQR, CQN], writes=[pr])
                sq = SCRB()
                sq_ap = sq.ap.bitcast(BF16)
                op("act", lambda e: e.activation(out=sq_ap[0:96, 0:n], in_=pq.ap[0:96, 0:n], func=AF.Square),
                   reads=[pq], writes=[sq])
                e2 = SCRB()
                op("dve", lambda e: e.scalar_tensor_tensor(
                    out=e2.ap[64:96, 0:n], in0=pr.ap[64:96, 0:n], scalar=ppc("gqr", 0, slice(64, 96)),
                    in1=TAB.ap[64:96, q0:q0 + n], op0=ALU.mult, op1=ALU.mult), reads=[pr, TAB, PP], writes=[e2])
                st[("q", q0)] = (pq, pr, sq, sq_ap, e2)

            def qgenB(q0, n):
                pq, pr, sq, sq_ap, e2 = st[("q", q0)]
                rb = fm_rstd([(sq, sq_ap[0:96, 0:n], 96, 0)], 96, n, 96.0, SCRB, pbuf=pr)
                op("dve", lambda e: e.scalar_tensor_tensor(
                    out=qt.ap[0:64, q0:q0 + n], in0=pq.ap[0:64, 0:n], scalar=ppc("gq", 0, slice(0, 64)), in1=rb.ap[0:64, 0:n],
                    op0=ALU.mult, op1=ALU.mult), reads=[pq, rb, PP], writes=[qt])
                e1 = SCRB()
                op("dve", lambda e: e.scalar_tensor_tensor(
                    out=e1.ap[64:96, 0:n], in0=pq.ap[64:96, 0:n], scalar=ppc("gq", 0, slice(64, 96)),
                    in1=TAB.ap[64:96, NOWN + q0:NOWN + q0 + n], op0=ALU.mult, op1=ALU.mult), reads=[pq, TAB, PP, sq], writes=[e1])
                op("dve", lambda e: e.tensor_tensor(out=e1.ap[64:96, 0:n], in0=e1.ap[64:96, 0:n],
                                                    in1=e2.ap[64:96, 0:n], op=ALU.add),
                   reads=[e1, e2], writes=[e1])
                op("dve", lambda e: e.tensor_tensor(out=qt.ap[64:96, q0:q0 + n], in0=e1.ap[64:96, 0:n],
                                                    in1=rb.ap[64:96, 0:n], op=ALU.mult),
                   reads=[e1, rb], writes=[qt])

            slots = []

            def kslot(i):
                if i >= 1:
                    kgenB(i - 1)
                if i < NKB:
                    kgenA(i)
                    vgen(i)
            for i in range(NKB + 1):
                slots.append((lambda i=i: kslot(i), [3, 4]))
            slots.append((kfinal, [3, 4]))
            if len(slots) % 2:
                slots.append((lambda: None, [3, 4]))
            for (q0, n) in qblocks:
                slots.append((lambda q0=q0, n=n: qgenA(q0, n), [3, 4]))
                slots.append((lambda q0=q0, n=n: qgenB(q0, n), [3, 4]))
            return slots

        def do_head(h, bg):
            kt = KT[h % 2]
            rk = RK[h % 2]
            vv = VV[h % len(VV)]
            vv3 = VV3[h % len(VV)]
            qt = QT[h % 2]
            fin_pending = []
            for (q0, n) in qblocks:
                po = PSACC()
                pend = []

                def qk(t, q0=q0, n=n):
                    pb = PSQ()
                    op("pe", lambda e, pb=pb, t=t: e.matmul(pb.ap[:, 0:n], lhsT=kt.ap[0:96, t * 128:(t + 1) * 128],
                                                        rhs=qt.ap[0:96, q0:q0 + n], start=True, stop=True),
                       reads=[kt, qt], writes=[pb])
                    return pb

                def expv(t, pb, q0=q0, n=n, po=po):
                    pt_ = PT[pti[0] % 4]
                    pti[0] += 1
                    op("act", lambda e, pb=pb, pt_=pt_, t=t: e.activation(out=pt_.ap[:, 0:n], in_=pb.ap[:, 0:n], func=AF.Exp,
                                                                     scale=rk.ap[:, t:t + 1]), reads=[pb, rk], writes=[pt_])
                    op("pe", lambda e, pt_=pt_, t=t: e.matmul(po.ap[0:65, 0:n], lhsT=vv3[:, t, :], rhs=pt_.ap[:, 0:n],
                                                          start=(t == 0), stop=(t == NKT - 1)),
                       reads=[vv, pt_], writes=[po])
                LOOK = 2
                DEFER = 4
                for t in range(NKT + LOOK):
                    if t < NKT:
                        pend.append((t, qk(t)))
                    if t >= LOOK:
                        tt, pb = pend.pop(0)
                        expv(tt, pb)
                    if t == DEFER and fin_pending:
                        with_pool([3, 4], fin_pending.pop(0))
                    if t >= 10 and (t - 10) % 7 == 0 and bg:
                        u, pool = bg.pop(0)
                        with_pool(pool, u)
                rd = SCRB()
                op("dve", lambda e, rd=rd, po=po, n=n: e.reciprocal(out=rd.ap[64:65, 0:n], in_=po.ap[64:65, 0:n]),
                   reads=[po], writes=[rd])

                def fin(rd=rd, po=po, q0=q0, n=n):
                    pbc = PS()
                    op("pe", lambda e, pbc=pbc: e.matmul(pbc.ap[0:64, 0:n], lhsT=ONESF.ap[64:65, 0:64], rhs=rd.ap[64:65, 0:n],
                                                         start=True, stop=True), reads=[ONESF, rd], writes=[pbc])
                    oc = SCRB()
                    op("act", lambda e, oc=oc: e.activation(out=oc.ap[0:64, 0:n], in_=po.ap[0:64, 0:n], func=AF.Copy),
                       reads=[po, rd], writes=[oc])
                    op("dve", lambda e, oc=oc, pbc=pbc: e.tensor_tensor(out=ON3[0:64, h, q0:q0 + n], in0=oc.ap[0:64, 0:n],
                                                                         in1=pbc.ap[0:64, 0:n], op=ALU.mult),
                       reads=[oc, pbc], writes=[ON])
                fin_pending.append(fin)
            while fin_pending:
                with_pool([3, 4], fin_pending.pop(0))
            while bg:
                u, pool = bg.pop(0)
                with_pool(pool, u)
        chk('B0')
        for u_, _pool in gen_units(0):
            u_()
        chk('B2')
        for h_ in range(8):
            do_head(h_, gen_units(h_ + 1) if h_ < 7 else [])
        for (q0, n) in qblocks:
            sqs = []
            for h in range(8):
                sq = SCRB()
                sq_ap = sq.ap.bitcast(BF16)
                op("act", lambda e, sq_ap=sq_ap, h=h, q0=q0, n=n: e.activation(out=sq_ap[0:64, 0:n], in_=ON3[0:64, h, q0:q0 + n],
                                                                            func=AF.Square), reads=[ON], writes=[sq])
                sqs.append((sq, sq_ap[0:64, 0:n], 64, 0))
            rb = fm_rstd(sqs, 64, n, 512.0, SCRB)
            for h in range(8):
                op("dve", lambda e, h=h, rb=rb, q0=q0, n=n: e.scalar_tensor_tensor(
                    out=ON3[0:64, h, q0:q0 + n], in0=ON3[0:64, h, q0:q0 + n], scalar=ppc("gmla", h, slice(0, 64)),
                    in1=rb.ap[0:64, 0:n], op0=ALU.mult, op1=ALU.mult), reads=[ON, rb, PP], writes=[ON])
        if "on" in dbg_d:
            TMPD4 = A.f32(8 * NOWN, "tmpd4")
            op("dve", lambda e: e.tensor_copy(out=TMPD4.ap[0:64], in_=ON.ap[0:64]), reads=[ON], writes=[TMPD4])
            dump("on", TMPD4, TMPD4.ap[0:64])
        S.barrier_all()
        A.release(mB)
        if stop == 'B':
            S.emit(nc)
            return nc
        NTL = CH // 128
        tiles = [(i * 128, 128) for i in range(NTL)] + [(CH, 2)]
        xres_start = A.top()
        XRES = A.f32((NTL + 1) * D, "xres")
        xres_end = A.top()
        XRES3 = XRES.ap.rearrange("p (t d) -> p t d", d=D)
        XR_res = [Res("xres%d" % i) for i in range(NTL + 1)]
        XT_ = [Buf(XRES3[:, i, :], XR_res[i]) for i in range(NTL + 1)]
        mC = A.mark()
        A.regs = [[ckvn_start, ets_end], [xres_end, ASZ]]
        WOL = A.bf(4 * D, "wol")
        WOL3 = WOL.ap.rearrange("p (c d) -> p c d", c=4)
        WOM = A.bf(8 * D, "wom")
        WOM3 = WOM.ap.rearrange("p (h d) -> p h d", h=8)
        load(WOL, WOL3, w_out[0:512, :].rearrange("(c p) d -> p c d", p=128))
        load(WOM, WOM3[0:64], w_out[512:1024, :].rearrange("(h p) d -> p h d", p=64))
        for ti, (o0, rows) in enumerate(tiles):
            xt = XT_[ti]
            src = xs[3, o0:o0 + rows, :] if ti < NTL else xh[0:2, :]
            load(xt, xt.ap[0:rows], src, eng="sp")
            for half in range(2):
                pb = PS()

                def mmo(e, pb=pb, o0=o0, rows=rows, half=half):
                    ins = None
                    for ct in range(4):
                        ins = e.matmul(pb.ap[0:rows, :], lhsT=MIXL3[:, ct, o0:o0 + rows],
                                       rhs=WOL3[:, ct, half * 512:(half + 1) * 512], start=(ct == 0), stop=False)
                    for h in range(8):
                        ins = e.matmul(pb.ap[0:rows, :], lhsT=ON3[0:64, h, o0:o0 + rows],
                                       rhs=WOM3[0:64, h, half * 512:(half + 1) * 512], start=False, stop=(h == 7))
                    return ins
                op("pe", mmo, reads=[MIXL, ON, WOL, WOM], writes=[pb])
                op("dve", lambda e, pb=pb, xt=xt, rows=rows, half=half: e.tensor_tensor(
                    out=xt.ap[0:rows, half * 512:(half + 1) * 512], in0=xt.ap[0:rows, half * 512:(half + 1) * 512],
                    in1=pb.ap[0:rows, :], op=ALU.add), reads=[xt, pb], writes=[xt])
        if "x1" in dbg_d:
            for ti in range(NTL):
                store(dbg_d["x1"][ti * 128:(ti + 1) * 128, :], XT_[ti], XT_[ti].ap)
        S.barrier_all()
        A.regs = [[mP, xres_start], [xres_end, ASZ]]
        if stop == 'C':
            S.emit(nc)
            return nc

        GB2 = A.f32(D, "gbc2")
        GB3 = A.f32(D, "gbc3")
        load(GB2, GB2.ap, gvec[1, :].partition_broadcast(128))
        load(GB3, GB3.ap, gvec[3, :].partition_broadcast(128))
        HNT = A.bf(8 * NOWN, "hnt")
        HNT3 = HNT.ap.rearrange("p (k t) -> p k t", k=8)
        mD = A.mark()
        WMQ = A.bf(8 * 512, "wmq")
        WMQ3 = WMQ.ap.rearrange("p (k c) -> p k c", k=8)
        load(WMQ, WMQ3, w_mq.rearrange("(k p) c -> p k c", p=128))
        WMO = A.bf(4 * D, "wmo")
        WMO3 = WMO.ap.rearrange("p (h d) -> p h d", h=4)
        load(WMO, WMO3, w_mo.rearrange("(h p) d -> p h d", p=128))
        H1T = A.bf(8 * 512, "h1t")
        H1T3 = H1T.ap.rearrange("p (k t) -> p k t", k=8)
        OMT = A.bf(4 * 512, "omt")
        OMT3 = OMT.ap.rearrange("p (h t) -> p h t", h=4)
        HNH = A.bf(8 * 2, "hnh")
        HNH3 = HNH.ap.rearrange("p (k t) -> p k t", k=8)
        scrD = [A.f32(512, "scrD%d" % i) for i in range(10)]
        sdi = [0]

        def SCRD():
            b = scrD[sdi[0] % len(scrD)]
            sdi[0] += 1
            return b
        mscale = 128.0 ** -0.5
        qbl = [(q0, min(512, CH - q0)) for q0 in range(0, CH, 512)] + [(CH, 2)]
        for (q0, n) in qbl:
            tl = [ti for ti, (o0, rows) in enumerate(tiles) if q0 <= o0 < q0 + n]
            for ti in tl:
                o0, rows = tiles[ti]
                norm_T(XT_[ti], XT_[ti].ap[0:rows], rows, H1T, H1T3, o0 - q0, gb=GB2)
            for h in range(4):
                pq = PS()

                def mmq2(e, pq=pq, h=h, n=n):
                    ins = None
                    for kc in range(8):
                        ins = e.matmul(pq.ap[:, 0:n], lhsT=WMQ3[:, kc, h * 128:(h + 1) * 128], rhs=H1T3[:, kc, 0:n],
                                       start=(kc == 0), stop=(kc == 7))
                    return ins
                op("pe", mmq2, reads=[WMQ, H1T], writes=[pq])
                sq = SCRD()
                sq_ap = sq.ap.bitcast(BF16)
                op("act", lambda e, sq_ap=sq_ap, pq=pq, n=n: e.activation(out=sq_ap[:, 0:n], in_=pq.ap[:, 0:n], func=AF.Square),
                   reads=[pq], writes=[sq])
                rb = fm_rstd([(sq, sq_ap[:, 0:n], 128, 0)], 128, n, 128.0, SCRD)
                qm = SCRD()
                qm_ap = qm.ap.bitcast(BF16)
                op("dve", lambda e, pq=pq, rb=rb, qm_ap=qm_ap, n=n: e.scalar_tensor_tensor(
                    out=qm_ap[:, 0:n], in0=pq.ap[:, 0:n], scalar=ppc("gmq"), in1=rb.ap[:, 0:n], op0=ALU.mult, op1=ALU.mult),
                   reads=[pq, rb, PP], writes=[qm])
                po = PS()
                pd = PS()
                for mt in range(2):
                    ps_ = PS()
                    op("pe", lambda e, ps_=ps_, mt=mt, h=h, qm_ap=qm_ap, n=n: e.matmul(
                        ps_.ap[:, 0:n], lhsT=KM3[:, h, mt * 128:(mt + 1) * 128], rhs=qm_ap[:, 0:n], start=True, stop=True),
                       reads=[KM, qm], writes=[ps_])
                    pt_ = SCRD()
                    pt_ap = pt_.ap.bitcast(BF16)
                    op("act", lambda e, ps_=ps_, pt_ap=pt_ap, n=n: e.activation(out=pt_ap[:, 0:n], in_=ps_.ap[:, 0:n], func=AF.Exp,
                                                                         scale=mscale), reads=[ps_], writes=[pt_])
                    op("pe", lambda e, po=po, mt=mt, h=h, pt_ap=pt_ap, n=n: e.matmul(
                        po.ap[:, 0:n], lhsT=VM3[:, mt, h * 128:(h + 1) * 128], rhs=pt_ap[:, 0:n], start=(mt == 0), stop=(mt == 1)),
                       reads=[VM, pt_], writes=[po])
                    op("pe", lambda e, pd=pd, mt=mt, pt_ap=pt_ap, n=n: e.matmul(
                        pd.ap[:, 0:n], lhsT=ONESB.ap[:, 0:128], rhs=pt_ap[:, 0:n], start=(mt == 0), stop=(mt == 1)),
                       reads=[ONESB, pt_], writes=[pd])
                rd = SCRD()
                op("dve", lambda e, rd=rd, pd=pd, n=n: e.reciprocal(out=rd.ap[:, 0:n], in_=pd.ap[:, 0:n]),
                   reads=[pd], writes=[rd])
                op("dve", lambda e, po=po, rd=rd, h=h, n=n: e.tensor_tensor(out=OMT3[:, h, 0:n], in0=po.ap[:, 0:n], in1=rd.ap[:, 0:n],
                                                                         op=ALU.mult), reads=[po, rd], writes=[OMT])
            for ti in tl:
                o0, rows = tiles[ti]
                xt = XT_[ti]
                c0 = o0 - q0
                for half in range(2):
                    pb = PS()

                    def mmo2(e, pb=pb, c0=c0, rows=rows, half=half):
                        ins = None
                        for h in range(4):
                            ins = e.matmul(pb.ap[0:rows, :], lhsT=OMT3[:, h, c0:c0 + rows],
                                           rhs=WMO3[:, h, half * 512:(half + 1) * 512], start=(h == 0), stop=(h == 3))
                        return ins
                    op("pe", mmo2, reads=[OMT, WMO], writes=[pb])
                    op("dve", lambda e, pb=pb, xt=xt, rows=rows, half=half: e.tensor_tensor(
                        out=xt.ap[0:rows, half * 512:(half + 1) * 512], in0=xt.ap[0:rows, half * 512:(half + 1) * 512],
                        in1=pb.ap[0:rows, :], op=ALU.add), reads=[xt, pb], writes=[xt])
                if ti < NTL:
                    norm_T(xt, xt.ap[0:rows], rows, HNT, HNT3, o0 + 1, gb=GB3, evac_eng="act")
                else:
                    norm_T(xt, xt.ap[0:rows], rows, HNH, HNH3, 0, gb=GB3)
                    op("dve", lambda e: e.tensor_scalar(out=HNT3[:, :, CH + 1:CH + 2], in0=HNH3[:, :, 0:1], scalar1=ppc("msk", 0),
                                                        scalar2=None, op0=ALU.mult), reads=[HNH, PP], writes=[HNT])
                    op("dve", lambda e: e.tensor_scalar(out=HNT3[:, :, 0:1], in0=HNH3[:, :, 1:2], scalar1=ppc("msk", 1),
                                                        scalar2=None, op0=ALU.mult), reads=[HNH, PP], writes=[HNT])
        if "x2" in dbg_d:
            for ti in range(NTL):
                store(dbg_d["x2"][ti * 128:(ti + 1) * 128, :], XT_[ti], XT_[ti].ap)
        S.barrier_all()
        A.release(mD)
        if stop == 'D':
            S.emit(nc)
            return nc

        GP = 2
        NR = 22 // GP
        WUPG = [A.bf(8 * 2 * GP * 128, "wupg%d" % i) for i in range(2)]
        WUPG4 = [w.ap.rearrange("p (k s c) -> p k s c", k=8, s=2) for w in WUPG]
        WDNG = [A.bf(GP * D, "wdng%d" % i) for i in range(2)]
        WDNG3 = [w.ap.rearrange("p (j d) -> p j d", j=GP) for w in WDNG]
        ACTT = [A.bf(GP * CH, "actt%d" % i) for i in range(2)]
        ACTT3 = [a.ap.rearrange("p (j t) -> p j t", j=GP) for a in ACTT]
        scrE = [A.f32(512, "scrE%d" % i) for i in range(8)]
        sei = [0]

        def SCRE():
            b = scrE[sei[0] % len(scrE)]
            sei[0] += 1
            return b
        fwo = PPL["fw"][0]
        fbo = PPL["fb"][0]
        def do_round(r):
            wu = WUPG[r % 2]
            wu4 = WUPG4[r % 2]
            wd = WDNG[r % 2]
            wd3 = WDNG3[r % 2]
            at = ACTT[r % 2]
            at3 = ACTT3[r % 2]
            j0 = r * GP
            load(wu, wu4[:, :, 0, :], w_up[:, j0 * 128:(j0 + GP) * 128].rearrange("(k p) c -> p k c", p=128))
            load(wu, wu4[:, :, 1, :], w_up[:, DFF + j0 * 128:DFF + (j0 + GP) * 128].rearrange("(k p) c -> p k c", p=128))
            load(wd, wd3, w_dn[j0 * 128:(j0 + GP) * 128, :].rearrange("(j p) d -> p j d", p=128))
            for (t0, n) in WINS:
                for jj in range(GP):
                    cv = []
                    for s in range(2):
                        chn = (j0 + jj) + s * 22
                        pg = PS()

                        def mmu(e, pg=pg, s=s, jj=jj, t0=t0, n=n):
                            ins = None
                            for kc in range(8):
                                ins = e.matmul(pg.ap[:, 0:n + 2], lhsT=wu4[:, kc, s, jj * 128:(jj + 1) * 128],
                                               rhs=HNT3[:, kc, t0:t0 + n + 2], start=(kc == 0), stop=(kc == 7))
                            return ins
                        op("pe", mmu, reads=[wu, HNT], writes=[pg])
                        c_ = SCRE()
                        wsrc, wb = PP, fwo + 3 * chn
                        bsrc, bb = PP, fbo + chn
                        op("act", lambda e, c_=c_, pg=pg, wsrc=wsrc, wb=wb, bsrc=bsrc, bb=bb, n=n: e.activation(
                            out=c_.ap[:, 0:n], in_=pg.ap[:, 1:n + 1], func=AF.Identity,
                            scale=wsrc.ap[:, wb + 1:wb + 2], bias=bsrc.ap[:, bb:bb + 1]),
                           reads=[pg, wsrc, bsrc], writes=[c_])
                        for tp in (0, 2):
                            op("dve", lambda e, c_=c_, pg=pg, wsrc=wsrc, wb=wb, tp=tp, n=n: e.scalar_tensor_tensor(
                                out=c_.ap[:, 0:n], in0=pg.ap[:, tp:tp + n], scalar=wsrc.ap[:, wb + tp:wb + tp + 1],
                                in1=c_.ap[:, 0:n], op0=ALU.mult, op1=ALU.add), reads=[pg, wsrc, c_], writes=[c_])
                        cv.append(c_)
                    sg = SCRE()
                    op("act", lambda e, sg=sg, g_=cv[0], n=n: e.activation(out=sg.ap[:, 0:n], in_=g_.ap[:, 0:n], func=AF.Tanh, scale=0.5),
                       reads=[cv[0]], writes=[sg])
                    op("act", lambda e, sg=sg, n=n: e.activation(out=sg.ap[:, 0:n], in_=sg.ap[:, 0:n], func=AF.Identity, scale=0.5,
                                                                bias=POSH.ap[:, 0:1]), reads=[sg, POSH], writes=[sg])
                    op("dve", lambda e, g_=cv[0], u_=cv[1], n=n: e.tensor_tensor(out=u_.ap[:, 0:n], in0=g_.ap[:, 0:n], in1=u_.ap[:, 0:n],
                                                                                op=ALU.mult), reads=[cv[0], cv[1]], writes=[cv[1]])
                    op("dve", lambda e, sg=sg, u_=cv[1], jj=jj, t0=t0, n=n: e.tensor_tensor(
                        out=at3[:, jj, t0:t0 + n], in0=sg.ap[:, 0:n], in1=u_.ap[:, 0:n], op=ALU.mult),
                       reads=[sg, cv[1]], writes=[at])
            for ti in range(NTL):
                o0, rows = tiles[ti]
                xt = XT_[ti]
                for half in range(2):
                    pb = PS()

                    def mmd(e, pb=pb, o0=o0, half=half):
                        ins = None
                        for jj in range(GP):
                            ins = e.matmul(pb.ap[:, :], lhsT=at3[:, jj, o0:o0 + 128], rhs=wd3[:, jj, half * 512:(half + 1) * 512],
                                           start=(jj == 0), stop=(jj == GP - 1))
                        return ins
                    op("pe", mmd, reads=[at, wd], writes=[pb])
                    op("dve", lambda e, pb=pb, xt=xt, half=half: e.tensor_tensor(
                        out=xt.ap[:, half * 512:(half + 1) * 512], in0=xt.ap[:, half * 512:(half + 1) * 512],
                        in1=pb.ap[:, :], op=ALU.add), reads=[xt, pb], writes=[xt])
        for r_ in range(NR):
            do_round(r_)
        for ti in range(NTL):
            store(out_d[ti * 128:(ti + 1) * 128, :], XT_[ti], XT_[ti].ap)
        S.emit(nc)
    return nc


def _cols(v, rows=128):
    v = np.asarray(v, np.float32)
    return np.ascontiguousarray(v.reshape(-1, rows).T)


def prep_core(inp, b, c, CH):
    f32 = np.float32
    x = np.asarray(inp["x"][b], f32)
    pos = np.asarray(inp["positions"][b]).astype(np.int32)
    s0 = c * CH
    slots = [(k, False) for k in range(c)] + [(k, True) for k in range(3, c, -1)] + [(c, False), (c, True)]
    assert len(slots) == 5
    xs = np.stack([x[k * CH:(k + 1) * CH][::-1] if rev else x[k * CH:(k + 1) * CH] for k, rev in slots])
    pk = np.stack([pos[k * CH:(k + 1) * CH][::-1] if rev else pos[k * CH:(k + 1) * CH] for k, rev in slots[:4]])
    posk = np.ascontiguousarray(np.broadcast_to(pk[:, None, :], (4, 32, CH))).astype(np.int32)
    xh = np.zeros((2, D), f32)
    ph = np.zeros((2,), np.int32)
    msk = np.zeros((2,), f32)
    if c < 3:
        xh[0] = x[s0 + CH]; ph[0] = pos[s0 + CH]; msk[0] = 1.0
    if c > 0:
        xh[1] = x[s0 - 1]; ph[1] = pos[s0 - 1]; msk[1] = 1.0
    po = np.concatenate([pos[s0:s0 + CH], ph])
    poso = np.ascontiguousarray(np.broadcast_to(po[None, :], (32, CH + 2))).astype(np.int32)
    flg = np.zeros((5, 4), f32)
    for k in range(3):
        if k < c:
            flg[k] = [1, 0, 1, 0]
        else:
            flg[k] = [0, 1, 0, 1]
    flg[3] = [1, 0, 0, 0]
    flg[4] = [0, 1, 0, 0]
    pp = np.zeros((128, PPL["_n"]), f32)

    def put(name, arr):
        o, n = PPL[name]
        arr = np.asarray(arr, f32)
        if arr.ndim == 1:
            arr = np.broadcast_to(arr[None, :], (128, arr.shape[0]))
        assert arr.shape[1] == n, (name, arr.shape, n)
        pp[:arr.shape[0], o:o + n] = arr
    put("flg", flg.reshape(-1))
    put("msk", msk)
    cw = np.zeros((128, 5, 4, 4), f32)
    cb = np.zeros((128, 5, 4), f32); ba = np.zeros((128, 5, 4), f32); bi = np.zeros((128, 5, 4), f32); lam = np.zeros((128, 5, 4), f32)
    wa = np.zeros((5, 4, 128, 128), f32); wi = np.zeros((5, 4, 128, 128), f32)
    for k, (ck, rev) in enumerate(slots):
        d = 1 if rev else 0
        w = np.asarray(inp["lru_conv_w"][0, d], f32)
        if rev:
            w = w[::-1]
        for tap in range(4):
            cw[:, k, :, tap] = _cols(w[tap])
        cb[:, k, :] = _cols(inp["lru_conv_b"][0, d])
        ba[:, k, :] = _cols(inp["lru_b_a"][0, d])
        bi[:, k, :] = _cols(inp["lru_b_i"][0, d])
        lam[:, k, :] = _cols(inp["lru_lambda"][0, d])
        for ct in range(4):
            for half in range(2):
                blk = 2 * ct + half
                wa[k, ct, half * 64:(half + 1) * 64, half * 64:(half + 1) * 64] = inp["lru_w_a"][0, d, blk]
                wi[k, ct, half * 64:(half + 1) * 64, half * 64:(half + 1) * 64] = inp["lru_w_i"][0, d, blk]
    put("cw", cw.reshape(128, -1)); put("cb", cb.reshape(128, -1)); put("ba", ba.reshape(128, -1))
    put("bi", bi.reshape(128, -1)); put("lam", lam.reshape(128, -1))
    put("gqa", _cols(inp["q_a_norm"][0]))
    put("gkv", _cols(inp["kv_a_norm"][0]))

    def rotperm(g):
        g = np.asarray(g, f32)
        r = g.copy()
        r[64:80] = g[80:96]
        r[80:96] = g[64:80]
        return r
    gq = np.asarray(inp["mla_q_norm"][0], f32); gk = np.asarray(inp["mla_k_norm"][0], f32)
    for name, v in (("gq", gq), ("gqr", rotperm(gq)), ("gk", gk), ("gkr", rotperm(gk))):
        a = np.zeros((128, 1), f32); a[:96, 0] = v
        put(name, a)
    put("glru", _cols(inp["lru_out_norm"][0]))
    gm = np.zeros((128, 8), f32); gm[:64] = _cols(inp["mla_out_norm"][0], 64)
    put("gmla", gm)
    put("gmq", _cols(inp["mem_q_norm"][0])); put("gmk", _cols(inp["mem_k_norm"][0]))
    fw = np.zeros((128, 44, 3), f32)
    fcw = np.asarray(inp["ffn_conv_w"][0], f32)
    for tap in range(3):
        fw[:, :, tap] = _cols(fcw[tap])
    put("fw", fw.reshape(128, -1))
    put("fb", _cols(inp["ffn_conv_b"][0]))
    invf = np.zeros((128, 1), f32)
    inv = (10000.0 ** (-np.arange(0, 32, 2, dtype=np.float64) / 32.0)) / (2.0 * np.pi)
    for p in range(64, 96):
        invf[p, 0] = inv[(p - 64) % 16]
    put("invf", invf)
    gvec = np.stack([inp["attn_norm"][0], inp["mem_attn_norm"][0], inp["mem_norm"][0], inp["ffn_norm"][0]]).astype(f32)
    m = {
        "xs": np.ascontiguousarray(xs), "xh": xh, "posk": posk, "poso": poso, "pp": pp, "wa": wa, "wi": wi,
        "mem": np.ascontiguousarray(np.asarray(inp["mem"][b], f32)), "gvec": np.ascontiguousarray(gvec),
        "w_in": np.ascontiguousarray(inp["w_in"][0], dtype=f32), "w_uq": np.ascontiguousarray(inp["w_uq"][0], dtype=f32),
        "w_ukv": np.ascontiguousarray(inp["w_ukv"][0], dtype=f32), "w_out": np.ascontiguousarray(inp["w_out"][0], dtype=f32),
        "w_mem_q": np.ascontiguousarray(inp["w_mem_q"][0], dtype=f32),
        "w_mem_kv": np.ascontiguousarray(inp["w_mem_kv"][0], dtype=f32),
        "w_mem_o": np.ascontiguousarray(inp["w_mem_o"][0], dtype=f32),
        "w_up": np.ascontiguousarray(inp["w_up"][0], dtype=f32), "w_down": np.ascontiguousarray(inp["w_down"][0], dtype=f32),
    }
    return m


_NC_CACHE = {}


def run(inputs, dbg=None, cores=None, stop=None):
    inputs = {k: np.asarray(v) for k, v in inputs.items()}
    B, SEQ, _ = inputs["x"].shape
    CH = SEQ // 4
    key = (CH, repr(dbg), stop)
    if key not in _NC_CACHE:
        _NC_CACHE[key] = build(CH, dbg, stop)
    nc = _NC_CACHE[key]
    core_list = cores if cores is not None else [(b, c) for b in range(B) for c in range(4)]
    in_maps = [prep_core(inputs, b, c, CH) for (b, c) in core_list]
    res = run_bass_kernel_spmd(nc, in_maps, core_ids=list(range(len(core_list))), trace=bool(os.environ.get('KTRACE')))
    return res, core_list, CH


def kernel(**inputs):
    res, core_list, CH = run(inputs)
    B, SEQ, _ = np.asarray(inputs["x"]).shape
    out = np.zeros((B, SEQ, D), np.float32)
    for (b, c), r in zip(core_list, res.results):
        out[b, c * CH:(c + 1) * CH] = r["out"]
    return out
```

```python
import numpy as np
from contextlib import ExitStack
import concourse.bass as bass
import concourse.mybir as mybir
from concourse.bass_utils import run_bass_kernel_spmd

F32 = mybir.dt.float32
BF16 = mybir.dt.bfloat16
I32 = mybir.dt.int32
AF = mybir.ActivationFunctionType
ALU = mybir.AluOpType

import os
SAME_ENGINE_SYNC = os.environ.get('SAME_ENGINE_SYNC', '1') == '1'
EPS = 1e-6
D = 1024
DFF = 2816
NCH = 44
MEM = 256


class Res:
    __slots__ = ("name", "w", "r")

    def __init__(self, name=""):
        self.name = name
        self.w = None
        self.r = {}


class Sched:
    ENGS = ("pe", "act", "dve", "pool", "sp")

    def __init__(self):
        self.ops = {e: [] for e in self.ENGS}
        self.cnt = {e: 0 for e in self.ENGS}
        self.dcnt = {}
        self.seen = {e: {} for e in self.ENGS}

    def _need(self, eng, tok, waits):
        if tok is None:
            return
        key, val = tok
        if self.seen[eng].get(key, 0) >= val:
            return
        if val > waits.get(key, 0):
            waits[key] = val

    def op(self, eng, fn, reads=(), writes=(), dma=None):
        waits = {}
        for r in reads:
            self._need(eng, r.w, waits)
        for w in writes:
            self._need(eng, w.w, waits)
            for k, v in w.r.items():
                self._need(eng, (k, v), waits)
        if not SAME_ENGINE_SYNC:
            waits.pop(eng, None)
        for k, v in waits.items():
            self.seen[eng][k] = v
        if dma is None:
            self.cnt[eng] += 1
            tok = (eng, self.cnt[eng])
        else:
            self.dcnt[dma] = self.dcnt.get(dma, 0) + 16
            tok = (dma, self.dcnt[dma])
        self.ops[eng].append((list(waits.items()), fn, tok))
        for r in reads:
            if r.r.get(tok[0], 0) < tok[1]:
                r.r[tok[0]] = tok[1]
        for w in writes:
            w.w = tok
            w.r = {}
        return tok

    def barrier_all(self):
        for e in self.ENGS:
            waits = {}
            for e2 in self.ENGS:
                if e2 != e and self.cnt[e2] > self.seen[e].get(e2, 0):
                    waits[e2] = self.cnt[e2]
            for k, v in self.dcnt.items():
                if v > self.seen[e].get(k, 0):
                    waits[k] = v
            for k, v in waits.items():
                self.seen[e][k] = v
            if waits:
                self.ops[e].append((list(waits.items()), None, None))

    def emit(self, nc):
        keys = list(self.ENGS) + sorted(self.dcnt.keys())
        with ExitStack() as st:
            sems = {k: st.enter_context(nc.semaphore("s_" + k)) for k in keys}
            block = st.enter_context(nc.Block())
            engmap = {"pe": block.tensor, "act": block.scalar, "dve": block.vector,
                      "pool": block.gpsimd, "sp": block.sync}
            fin = {}
            for k in keys:
                v = self.cnt[k] if k in self.cnt else self.dcnt[k]
                if v > 0:
                    fin[k] = v
            self.ops["sp"].append((list(fin.items()), None, None))
            for e in self.ENGS:
                ops = self.ops[e]

                def body(engobj, ops=ops, e=e):
                    for waits, fn, tok in ops:
                        for k, v in waits:
                            engobj.wait_ge(sems[k], v)
                        if fn is None:
                            continue
                        ins = fn(engobj)
                        if tok[0] == e:
                            ins.then_inc(sems[e], 1)
                        else:
                            ins.then_inc(sems[tok[0]], 16)
                engmap[e](body)


class Buf:
    __slots__ = ("ap", "r")

    def __init__(self, ap, r):
        self.ap = ap
        self.r = r


class Arena:
    def __init__(self, t, size):
        self.t = t
        self.size = size
        self.regs = [[0, size]]

    def _take(self, n, name):
        for r in self.regs:
            if r[1] - r[0] >= n:
                o = r[0]
                r[0] += n
                return o
        raise AssertionError(("SBUF arena overflow", name, n, self.regs))

    def f32(self, n, name=""):
        o = self._take(n, name)
        return Buf(self.t[:, o:o + n], Res(name))

    def bf(self, n, name=""):
        m = (n + 1) // 2
        o = self._take(m, name)
        return Buf(self.t[:, o:o + m].bitcast(BF16)[:, 0:n], Res(name))

    def mark(self):
        return [list(r) for r in self.regs]

    def release(self, m):
        self.regs = [list(r) for r in m]

    def top(self):
        return self.regs[0][0]


def ffn_windows(CH):
    nw = -(-CH // 510)
    if CH % 512 == 0 and CH >= 512:
        nw = max(nw, 1)
    base = CH // nw
    rem = CH - base * nw
    sizes = [base + (1 if i < rem else 0) for i in range(nw)]
    starts = [sum(sizes[:i]) for i in range(nw)]
    return list(zip(starts, sizes))


def pp_layout():
    o = {}
    c = 0

    def add(name, n):
        nonlocal c
        o[name] = (c, n)
        c += n
    add("flg", 20)
    add("msk", 2)
    add("cw", 80)
    add("cb", 20)
    add("ba", 20)
    add("bi", 20)
    add("lam", 20)
    add("gqa", 2)
    add("gkv", 1)
    add("gq", 1)
    add("gqr", 1)
    add("gk", 1)
    add("gkr", 1)
    add("glru", 4)
    add("gmla", 8)
    add("gmq", 1)
    add("gmk", 1)
    add("fw", 132)
    add("fb", 44)
    add("invf", 1)
    o["_n"] = c
    return o


PPL = pp_layout()


class _Stop(Exception):
    pass


def build(CH, dbg=None, stop=None):
    holder = {}
    try:
        return _build(CH, dbg, stop, holder)
    except _Stop:
        return holder['nc']


def _build(CH, dbg, stop, holder):
    NB = CH // 512
    NKEY = 4 * CH
    NKT = NKEY // 128
    NKB = NKEY // 512
    NOWN = CH + 2
    WINS = ffn_windows(CH)
    nc = bass.Bass("TRN2", target_bir_lowering=False)
    holder["nc"] = nc

    def din(name, shape, dt=F32):
        return nc.dram_tensor(name, list(shape), dt, kind="ExternalInput").ap()

    xs = din("xs", [5, CH, D])
    xh = din("xh", [2, D])
    posk = din("posk", [4, 32, CH], I32)
    poso = din("poso", [32, NOWN], I32)
    pp_d = din("pp", [128, PPL["_n"]])
    wa_d = din("wa", [5, 4, 128, 128])
    wi_d = din("wi", [5, 4, 128, 128])
    mem_d = din("mem", [MEM, D])
    gvec = din("gvec", [4, D])
    w_in = din("w_in", [D, 1440])
    w_uq = din("w_uq", [256, 768])
    w_ukv = din("w_ukv", [128, 1024])
    w_out = din("w_out", [1024, D])
    w_mq = din("w_mem_q", [D, 512])
    w_mkv = din("w_mem_kv", [D, 1024])
    w_mo = din("w_mem_o", [512, D])
    w_up = din("w_up", [D, 2 * DFF])
    w_dn = din("w_down", [DFF, D])
    out_d = nc.dram_tensor("out", [CH, D], F32, kind="ExternalOutput").ap()
    dbg_d = {}
    if dbg:
        for name, shape in dbg.items():
            dbg_d[name] = nc.dram_tensor("dbg_" + name, list(shape), F32, kind="ExternalOutput").ap()

    S = Sched()
    with ExitStack() as st:
        ASZ = 52500
        arena_t = st.enter_context(nc.sbuf_tensor("arena", [128, ASZ], F32))
        A = Arena(arena_t, ASZ)
        psum_t = [st.enter_context(nc.psum_tensor("ps%d" % i, [128, 512], F32)) for i in range(8)]
        psum = [Buf(t[:, :], Res("ps%d" % i)) for i, t in enumerate(psum_t)]
        psi = [0]

        def PS():
            b = psum[psi[0] % 5]
            psi[0] += 1
            return b
        psa = [0]

        def PSACC():
            b = psum[6 + psa[0] % 2]
            psa[0] += 1
            return b

        def chk(tag):
            if stop == tag:
                S.emit(nc)
                raise _Stop()

        def op(eng, fn, reads=(), writes=(), dma=None):
            return S.op(eng, fn, [b.r for b in reads], [b.r for b in writes], dma)

        dkeys = {}

        def dkey(buf, pre="k"):
            k = (pre, id(buf.r))
            if k not in dkeys:
                dkeys[k] = "%s%02d" % (pre, len(dkeys))
            return dkeys[k]

        def load(dst, dst_ap, src_ap, eng=None, key=None):
            if eng is None:
                eng = "pool" if dst_ap.dtype != src_ap.dtype else "sp"
            op(eng, lambda e: e.dma_start(out=dst_ap, in_=src_ap), reads=[], writes=[dst], dma=dkey(dst))

        def store(dst_ap, buf, src_ap):
            op("sp", lambda e: e.dma_start(out=dst_ap, in_=src_ap), reads=[buf], dma=dkey(buf, "s"))

        def dump(name, buf, ap, rows=None):
            if name in dbg_d:
                store(dbg_d[name], buf, ap)

        PP = A.f32(PPL["_n"], "pp")
        load(PP, PP.ap, pp_d)

        def ppc(name, i=0, rows=slice(0, 128)):
            o, n = PPL[name]
            return PP.ap[rows, o + i:o + i + 1]

        CONST = A.f32(8, "const")
        op("pool", lambda e: e.memset(CONST.ap[:, 0:1], EPS), writes=[CONST])
        op("pool", lambda e: e.memset(CONST.ap[:, 1:2], 1.0), writes=[CONST])
        c_eps = CONST.ap[:, 0:1]
        c_one = CONST.ap[:, 1:2]
        NEGH = A.f32(8, "negh")
        POSH = A.f32(8, "posh")
        op("pool", lambda e: e.memset(NEGH.ap, -0.5), writes=[NEGH])
        op("pool", lambda e: e.memset(POSH.ap, 0.5), writes=[POSH])
        IDF = A.f32(128, "identf")
        IDENT = A.bf(128, "ident")
        ONESB = A.bf(128, "onesb")
        ONESF = A.f32(128, "onesf")
        op("pool", lambda e: e.iota(IDF.ap, [[1, 128]], base=0, channel_multiplier=-1,
                                    allow_small_or_imprecise_dtypes=True), writes=[IDF])
        op("dve", lambda e: e.tensor_scalar(out=IDENT.ap, in0=IDF.ap, scalar1=0.0, scalar2=None,
                                            op0=ALU.is_equal), reads=[IDF], writes=[IDENT])
        op("pool", lambda e: e.memset(ONESB.ap, 1.0), writes=[ONESB])
        op("pool", lambda e: e.memset(ONESF.ap, 1.0), writes=[ONESF])
        LP = A.f32(192, "lruparams")
        lam_o = PPL["lam"][0]
        flg_o = PPL["flg"][0]
        op("act", lambda e: e.activation(out=LP.ap[:, 0:20], in_=PP.ap[:, lam_o:lam_o + 20], func=AF.Exp, scale=-1.0),
           reads=[PP], writes=[LP])
        op("act", lambda e: e.activation(out=LP.ap[:, 0:20], in_=LP.ap[:, 0:20], func=AF.Ln, scale=1.0, bias=c_one),
           reads=[LP, CONST], writes=[LP])
        op("dve", lambda e: e.tensor_scalar(out=LP.ap[:, 20:40], in0=LP.ap[:, 0:20], scalar1=-4.0, scalar2=None,
                                            op0=ALU.mult), reads=[LP], writes=[LP])
        op("dve", lambda e: e.tensor_scalar(out=LP.ap[:, 0:20], in0=LP.ap[:, 0:20], scalar1=-8.0, scalar2=None,
                                            op0=ALU.mult), reads=[LP], writes=[LP])
        op("dve", lambda e: e.tensor_scalar(out=LP.ap[:, 40:60], in0=PP.ap[:, flg_o:flg_o + 20], scalar1=-1.0,
                                            scalar2=1.0, op0=ALU.mult, op1=ALU.add), reads=[PP], writes=[LP])

        ba_o = PPL["ba"][0]
        bi_o = PPL["bi"][0]
        fw_o = PPL["fw"][0]
        fb_o = PPL["fb"][0]
        op("dve", lambda e: e.tensor_scalar(out=LP.ap[:, 60:80], in0=PP.ap[:, ba_o:ba_o + 20], scalar1=0.5, scalar2=None,
                                            op0=ALU.mult), reads=[PP], writes=[LP])
        op("dve", lambda e: e.tensor_scalar(out=LP.ap[:, 80:100], in0=PP.ap[:, bi_o:bi_o + 20], scalar1=0.5, scalar2=None,
                                            op0=ALU.mult), reads=[PP], writes=[LP])
        op("dve", lambda e: e.tensor_scalar(out=LP.ap[:, 100:166], in0=PP.ap[:, fw_o + 66:fw_o + 132], scalar1=0.5, scalar2=None,
                                            op0=ALU.mult), reads=[PP], writes=[LP])
        op("dve", lambda e: e.tensor_scalar(out=LP.ap[:, 166:188], in0=PP.ap[:, fb_o + 22:fb_o + 44], scalar1=0.5, scalar2=None,
                                            op0=ALU.mult), reads=[PP], writes=[LP])

        def flg(k, i):
            return PP.ap[:, flg_o + 4 * k + i:flg_o + 4 * k + i + 1]

        def nflg(k, i):
            return LP.ap[:, 40 + 4 * k + i:40 + 4 * k + i + 1]

        GBC = A.f32(D, "gbc")

        NT_X = [A.f32(D, "xt%d" % i) for i in range(2)]
        NT_J = A.bf(D, "junk")
        NT_H = [A.bf(D, "hb%d" % i) for i in range(2)]
        NT_Sl = [A.f32(2, "nstat%d" % i) for i in range(2)]
        nt_i = [0]

        def norm_T(xbuf, x_ap, n, dstT, dstT_ap3, col0, evac_eng="dve", gb=None):
            gb = gb or GBC
            i = nt_i[0] % 2
            nt_i[0] += 1
            hb = NT_H[i]
            NT_S = NT_Sl[i]
            ss = NT_S.ap[0:n, 0:1]
            rs = NT_S.ap[0:n, 1:2]
            op("act", lambda e: e.activation(out=NT_J.ap[0:n], in_=x_ap, func=AF.Square, accum_out=ss),
               reads=[xbuf], writes=[NT_J, NT_S])
            op("dve", lambda e: e.tensor_scalar(out=rs, in0=ss, scalar1=1.0 / D, scalar2=EPS, op0=ALU.mult, op1=ALU.add),
               reads=[NT_S], writes=[NT_S])
            op("pool", lambda e: e.tensor_tensor(out=rs, in0=rs, in1=NEGH.ap[0:n, 0:1], op=ALU.pow),
               reads=[NT_S, NEGH], writes=[NT_S])
            op("dve", lambda e: e.scalar_tensor_tensor(out=hb.ap[0:n], in0=x_ap, scalar=rs, in1=gb.ap[0:n],
                                                       op0=ALU.mult, op1=ALU.mult),
               reads=[xbuf, NT_S, gb], writes=[hb])
            pb = PS()
            pbf = pb.ap.bitcast(BF16)

            def tr(e):
                ins = None
                for kc in range(8):
                    ins = e.transpose(pbf[:, kc * 128:kc * 128 + n], hb.ap[0:n, kc * 128:(kc + 1) * 128],
                                      IDENT.ap[0:n, 0:n])
                return ins
            op("pe", tr, reads=[hb, IDENT], writes=[pb])
            src = pbf.rearrange("p (k t) -> p k t", k=8)[:, :, 0:n]
            dst = dstT_ap3[:, :, col0:col0 + n]
            if evac_eng == "act":
                op("act", lambda e: e.activation(out=dst, in_=src, func=AF.Copy), reads=[pb], writes=[dstT])
            else:
                op("dve", lambda e: e.tensor_copy(out=dst, in_=src), reads=[pb], writes=[dstT])

        def fm_rstd(sq_list, rows_out, n, dim, SCR):
            pb = PS()

            def mm(e):
                ins = None
                for i, (b, ap, rk, base) in enumerate(sq_list):
                    ins = e.matmul(pb.ap[0:rows_out, 0:n], lhsT=ONESB.ap[base:base + rk, 0:rows_out], rhs=ap,
                                   start=(i == 0), stop=(i == len(sq_list) - 1))
                return ins
            op("pe", mm, reads=[b for b, _, _, _ in sq_list] + [ONESB], writes=[pb])
            rb = SCR()
            op("act", lambda e: e.activation(out=rb.ap[0:rows_out, 0:n], in_=pb.ap[0:rows_out, 0:n], func=AF.Ln,
                                             scale=1.0 / dim, bias=c_eps[0:rows_out]),
               reads=[pb, CONST], writes=[rb])
            op("act", lambda e: e.activation(out=rb.ap[0:rows_out, 0:n], in_=rb.ap[0:rows_out, 0:n], func=AF.Exp,
                                             scale=-0.5), reads=[rb], writes=[rb])
            return rb

        POSB = [A.f32(512, "posb%d" % i) for i in range(2)]
        posb_i = [0]

        def rope_tables(pos_src_ap, n, SCR, out_buf=None, out_s=None, out_c=None):
            pi = POSB[posb_i[0] % 2]
            posb_i[0] += 1
            pi_ap = pi.ap.bitcast(I32)
            load(pi, pi_ap[64:96, 0:n], pos_src_ap, eng="sp")
            y = SCR()
            op("dve", lambda e: e.tensor_copy(out=y.ap[64:96, 0:n], in_=pi_ap[64:96, 0:n]), reads=[pi], writes=[y])
            y2 = SCR()
            yc = SCR()
            op("dve", lambda e: e.tensor_scalar(out=y2.ap[64:96, 0:n], in0=y.ap[64:96, 0:n],
                                                scalar1=ppc("invf", 0, slice(64, 96)), scalar2=None, op0=ALU.mult),
               reads=[y, PP], writes=[y2])
            op("dve", lambda e: e.tensor_scalar(out=yc.ap[64:96, 0:n], in0=y2.ap[64:96, 0:n], scalar1=0.25,
                                                scalar2=None, op0=ALU.add), reads=[y2], writes=[yc])
            res = []
            for yy, dst in ((y2, out_s), (yc, out_c)):
                ti = SCR()
                ti_ap = ti.ap.bitcast(I32)
                op("dve", lambda e, yy=yy, ti_ap=ti_ap: e.tensor_copy(out=ti_ap[64:96, 0:n], in_=yy.ap[64:96, 0:n]),
                   reads=[yy], writes=[ti])
                tf = SCR()
                op("dve", lambda e, ti_ap=ti_ap, tf=tf: e.tensor_copy(out=tf.ap[64:96, 0:n], in_=ti_ap[64:96, 0:n]),
                   reads=[ti], writes=[tf])
                op("dve", lambda e, yy=yy, tf=tf: e.tensor_tensor(out=yy.ap[64:96, 0:n], in0=yy.ap[64:96, 0:n],
                                                                  in1=tf.ap[64:96, 0:n], op=ALU.subtract),
                   reads=[yy, tf], writes=[yy])
                if dst is None:
                    o_b, o_ap = yy, yy.ap[64:96, 0:n]
                else:
                    o_b, o_ap = out_buf, dst
                op("act", lambda e, yy=yy, o_ap=o_ap: e.activation(out=o_ap, in_=yy.ap[64:96, 0:n], func=AF.Sin,
                                                                   scale=2.0 * np.pi * 0.999999),
                   reads=[yy], writes=[o_b])
                res.append((o_b, o_ap))
            return res

        KM = A.bf(4 * MEM, "km")
        KM3 = KM.ap.rearrange("p (h t) -> p h t", h=4)
        VM = A.bf(2 * 512, "vm")
        VM3 = VM.ap.rearrange("p (t c) -> p t c", t=2)
        m0 = A.mark()
        scr0 = [A.f32(512, "scr0%d" % i) for i in range(4)]
        s0i = [0]

        def SCRD():
            b = scr0[s0i[0] % len(scr0)]
            s0i[0] += 1
            return b
        if stop == '0a':
            S.emit(nc)
            return nc
        load(GBC, GBC.ap, gvec[2, :].partition_broadcast(128))
        mW = A.mark()
        WMKV = A.bf(8 * 1024, "wmkv")
        WMKV3 = WMKV.ap.rearrange("p (k c) -> p k c", k=8)
        load(WMKV, WMKV3, w_mkv.rearrange("(k p) c -> p k c", p=128))
        MEMT = A.bf(8 * MEM, "memt")
        MEMT3 = MEMT.ap.rearrange("p (k t) -> p k t", k=8)
        for mt in range(2):
            xb = NT_X[nt_i[0] % 2]
            load(xb, xb.ap, mem_d[mt * 128:(mt + 1) * 128, :], eng="sp")
            norm_T(xb, xb.ap, 128, MEMT, MEMT3, mt * 128)
        if stop == '0b':
            S.emit(nc)
            return nc
        for h in range(4):
            pk = PS()

            def mmk(e, pk=pk, h=h):
                ins = None
                for kc in range(8):
                    ins = e.matmul(pk.ap[:, 0:MEM], lhsT=WMKV3[:, kc, h * 128:(h + 1) * 128], rhs=MEMT3[:, kc, :],
                                   start=(kc == 0), stop=(kc == 7))
                return ins
            op("pe", mmk, reads=[WMKV, MEMT], writes=[pk])
            sq = SCRD()
            sq_ap = sq.ap.bitcast(BF16)
            op("act", lambda e, sq_ap=sq_ap, pk=pk: e.activation(out=sq_ap[:, 0:MEM], in_=pk.ap[:, 0:MEM], func=AF.Square),
               reads=[pk], writes=[sq])
            rb = fm_rstd([(sq, sq_ap[:, 0:MEM], 128, 0)], 128, MEM, 128.0, SCRD)
            op("dve", lambda e, pk=pk, rb=rb, h=h: e.scalar_tensor_tensor(
                out=KM3[:, h, :], in0=pk.ap[:, 0:MEM], scalar=ppc("gmk"), in1=rb.ap[:, 0:MEM], op0=ALU.mult, op1=ALU.mult),
               reads=[pk, rb, PP], writes=[KM])
        if stop == '0c':
            S.emit(nc)
            return nc
        for mt in range(2):
            pv = PS()

            def mmv2(e, pv=pv, mt=mt):
                ins = None
                for kc in range(8):
                    ins = e.matmul(pv.ap[:, :], lhsT=MEMT3[:, kc, mt * 128:(mt + 1) * 128], rhs=WMKV3[:, kc, 512:1024],
                                   start=(kc == 0), stop=(kc == 7))
                return ins
            op("pe", mmv2, reads=[WMKV, MEMT], writes=[pv])
            op("act", lambda e, pv=pv, mt=mt: e.activation(out=VM3[:, mt, :], in_=pv.ap[:, :], func=AF.Copy),
               reads=[pv], writes=[VM])
        if stop == '0d':
            S.emit(nc)
            return nc
        S.barrier_all()
        A.release(m0)
        mP = A.top()
        if stop == '0':
            S.emit(nc)
            return nc

        MIXL = A.bf(4 * NOWN, "mixl")
        MIXL3 = MIXL.ap.rearrange("p (c t) -> p c t", c=4)
        CQN = A.bf(2 * NOWN, "cqn")
        CQN3 = CQN.ap.rearrange("p (c t) -> p c t", c=2)
        ckvn_start = A.top()
        CKVN = A.bf(NKEY, "ckvn")
        ETS = A.bf(NKEY, "e_tsq")
        ets_end = A.top()
        mA = A.mark()

        WIN = A.bf(8 * 1440, "win")
        WIN3 = WIN.ap.rearrange("p (k c) -> p k c", k=8)
        for kc in range(8):
            load(WIN, WIN3[:, kc, :], w_in[kc * 128:(kc + 1) * 128, :])
        WKR = A.bf(8 * 192, "wkrpad")
        WKR3 = WKR.ap.rearrange("p (k c) -> p k c", k=8)
        op("pool", lambda e: e.memset(WKR.ap, 0.0), writes=[WKR])
        op("dve", lambda e: e.tensor_copy(out=WKR3[:, :, 64:96], in_=WIN3[:, :, 1408:1440]), reads=[WIN], writes=[WKR])
        op("dve", lambda e: e.tensor_scalar(out=WKR3[:, :, 160:176], in0=WIN3[:, :, 1424:1440], scalar1=-1.0,
                                            scalar2=None, op0=ALU.mult), reads=[WIN], writes=[WKR])
        op("dve", lambda e: e.tensor_copy(out=WKR3[:, :, 176:192], in_=WIN3[:, :, 1408:1424]), reads=[WIN], writes=[WKR])
        WGL = [A.bf(4 * 2 * 128, "wgates%d" % i) for i in range(2)]
        WGL4 = [w.ap.rearrange("p (c g o) -> p c g o", c=4, g=2) for w in WGL]
        HTL = [A.bf(8 * 512, "ht%d" % i) for i in range(2)]
        HTL3 = [h_.ap.rearrange("p (k t) -> p k t", k=8) for h_ in HTL]
        XR = A.f32(4 * 515, "xr")
        XR3 = XR.ap.rearrange("p (c t) -> p c t", c=4)
        HF = A.bf(4 * (CH + 1), "hf")
        HF3 = HF.ap.rearrange("p (c t) -> p c t", c=4)
        GG = A.bf(4 * NOWN, "gelu")
        GG3 = GG.ap.rearrange("p (c t) -> p c t", c=4)
        CAR = A.f32(64, "carry")
        op("pool", lambda e: e.memset(CAR.ap, 0.0), writes=[CAR])
        HSC = A.f32(4 * 512, "hscan")
        HSC3 = HSC.ap.rearrange("p (c t) -> p c t", c=4)
        HE = A.f32(16, "hextra")
        scrA = [A.f32(512, "scrA%d" % i) for i in range(12)]
        sai = [0]

        def SCRA():
            b = scrA[sai[0] % len(scrA)]
            sai[0] += 1
            return b

        load(GBC, GBC.ap, gvec[0, :].partition_broadcast(128))

        chk('A0')

        def lru_cols(k, ct, name):
            o, n = PPL[name]
            return PP.ap[:, o + 4 * k + ct:o + 4 * k + ct + 1]

        def phaseA_front(k, j, n, mini, hb):
            HT = HTL[hb]
            HT3 = HTL3[hb]
            if j == 0 and not mini:
                load(WGL[k % 2], WGL4[k % 2][:, :, 0, :], wa_d[k].rearrange("c i o -> i c o"))
                load(WGL[k % 2], WGL4[k % 2][:, :, 1, :], wi_d[k].rearrange("c i o -> i c o"))
            ntile = (n + 127) // 128
            for i in range(ntile):
                rows = min(128, n - i * 128)
                xb = NT_X[nt_i[0] % 2]
                if mini:
                    src = xh[0:1, :] if k == 3 else xh[1:2, :]
                else:
                    src = xs[k, j * 512 + i * 128:j * 512 + i * 128 + rows, :]
                load(xb, xb.ap[0:rows], src, eng="sp")
                norm_T(xb, xb.ap[0:rows], rows, HT, HT3, i * 128, evac_eng="dve")

        def phaseA_block(k, j, n, mini, hb):
            HT = HTL[hb]
            HT3 = HTL3[hb]
            WG = WGL[k % 2]
            WG4 = WGL4[k % 2]
            do_kv = (k <= 3) and not mini
            do_own = (k == 3) or (k == 4 and mini)
            if mini:
                own0 = CH if k == 3 else CH + 1
            else:
                own0 = j * 512
            if j == 0 and not mini:
                op("dve", lambda e: e.tensor_scalar(out=CAR.ap[:, 36:48], in0=CAR.ap[:, 8:20], scalar1=flg(k, 0),
                                                    scalar2=None, op0=ALU.mult), reads=[CAR, PP], writes=[CAR])
                op("dve", lambda e: e.scalar_tensor_tensor(out=CAR.ap[:, 36:48], in0=CAR.ap[:, 20:32], scalar=flg(k, 1),
                                                           in1=CAR.ap[:, 36:48], op0=ALU.mult, op1=ALU.add),
                   reads=[CAR, PP], writes=[CAR])
                op("dve", lambda e: e.tensor_scalar(out=CAR.ap[:, 32:36], in0=CAR.ap[:, 0:4], scalar1=flg(k, 0),
                                                    scalar2=None, op0=ALU.mult), reads=[CAR, PP], writes=[CAR])
                op("dve", lambda e: e.scalar_tensor_tensor(out=CAR.ap[:, 32:36], in0=CAR.ap[:, 4:8], scalar=flg(k, 1),
                                                           in1=CAR.ap[:, 32:36], op0=ALU.mult, op1=ALU.add),
                   reads=[CAR, PP], writes=[CAR])
                if k == 3:
                    op("dve", lambda e: e.tensor_copy(out=CAR.ap[:, 48:52], in_=CAR.ap[:, 32:36]), reads=[CAR], writes=[CAR])
                if k == 4:
                    op("dve", lambda e: e.tensor_copy(out=CAR.ap[:, 52:56], in_=CAR.ap[:, 32:36]), reads=[CAR], writes=[CAR])
                op("dve", lambda e: e.tensor_copy(out=XR3[:, :, 0:3],
                                                  in_=CAR.ap[:, 36:48].rearrange("p (c t) -> p c t", c=4)),
                   reads=[CAR], writes=[XR])
            else:
                op("dve", lambda e: e.tensor_copy(out=XR3[:, :, 0:3], in_=XR3[:, :, 512:515]), reads=[XR], writes=[XR])
            for ct in range(4):
                pb = PS()

                def mm(e, pb=pb, ct=ct):
                    ins = None
                    for kc in range(8):
                        ins = e.matmul(pb.ap[:, 0:n], lhsT=WIN3[:, kc, ct * 128:(ct + 1) * 128], rhs=HT3[:, kc, 0:n],
                                       start=(kc == 0), stop=(kc == 7))
                    return ins
                op("pe", mm, reads=[WIN, HT], writes=[pb])
                op("act", lambda e, pb=pb, ct=ct: e.activation(out=XR3[:, ct, 3:3 + n], in_=pb.ap[:, 0:n], func=AF.Copy),
                   reads=[pb], writes=[XR])
            chk('A1')
            for pr in range(2):
                cts = (2 * pr, 2 * pr + 1)
                B_ = {}
                for ct in cts:
                    cwo = PPL["cw"][0] + (k * 4 + ct) * 4
                    xc = SCRA()
                    op("act", lambda e, xc=xc, ct=ct, cwo=cwo: e.activation(
                        out=xc.ap[:, 0:n], in_=XR3[:, ct, 3:3 + n], func=AF.Identity,
                        scale=PP.ap[:, cwo + 3:cwo + 4], bias=lru_cols(k, ct, "cb")), reads=[XR, PP], writes=[xc])
                    for tp in range(3):
                        op("dve", lambda e, xc=xc, ct=ct, cwo=cwo, tp=tp: e.scalar_tensor_tensor(
                            out=xc.ap[:, 0:n], in0=XR3[:, ct, tp:tp + n], scalar=PP.ap[:, cwo + tp:cwo + tp + 1],
                            in1=xc.ap[:, 0:n], op0=ALU.mult, op1=ALU.add), reads=[XR, PP, xc], writes=[xc])
                    xcb = SCRA()
                    xcb_ap = xcb.ap.bitcast(BF16)
                    op("dve", lambda e, xc=xc, xcb_ap=xcb_ap: e.tensor_copy(out=xcb_ap[:, 0:n], in_=xc.ap[:, 0:n]),
                       reads=[xc], writes=[xcb])
                    B_[ct] = dict(xc=xc, xcb=xcb, xcb_ap=xcb_ap)
                for ct in cts:
                    d = B_[ct]
                    pa = PS()
                    op("pe", lambda e, pa=pa, ct=ct, xcb_ap=d["xcb_ap"]: e.matmul(
                        pa.ap[:, 0:n], lhsT=WG4[:, ct, 0, :], rhs=xcb_ap[:, 0:n], start=True, stop=True),
                       reads=[WG, d["xcb"]], writes=[pa])
                    pi_ = PS()
                    op("pe", lambda e, pi_=pi_, ct=ct, xcb_ap=d["xcb_ap"]: e.matmul(
                        pi_.ap[:, 0:n], lhsT=WG4[:, ct, 1, :], rhs=xcb_ap[:, 0:n], start=True, stop=True),
                       reads=[WG, d["xcb"]], writes=[pi_])
                    d["pa"] = pa
                    d["pi"] = pi_
                for ct in cts:
                    d = B_[ct]
                    rr = SCRA()
                    ig = SCRA()
                    op("act", lambda e, rr=rr, pa=d["pa"], ct=ct: e.activation(out=rr.ap[:, 0:n], in_=pa.ap[:, 0:n], func=AF.Tanh,
                                                                          bias=LP.ap[:, 60 + 4 * k + ct:61 + 4 * k + ct], scale=0.5),
                       reads=[d["pa"], LP], writes=[rr])
                    op("act", lambda e, ig=ig, pi_=d["pi"], ct=ct: e.activation(out=ig.ap[:, 0:n], in_=pi_.ap[:, 0:n], func=AF.Tanh,
                                                                            bias=LP.ap[:, 80 + 4 * k + ct:81 + 4 * k + ct], scale=0.5),
                       reads=[d["pi"], LP], writes=[ig])
                    d["rr"] = rr
                    d["ig"] = ig
                for ct in cts:
                    d = B_[ct]
                    aa = SCRA()
                    mm_ = SCRA()
                    cs = LP.ap[:, 4 * k + ct:4 * k + ct + 1]
                    hcs = LP.ap[:, 20 + 4 * k + ct:20 + 4 * k + ct + 1]
                    op("act", lambda e, aa=aa, rr=d["rr"], hcs=hcs: e.activation(out=aa.ap[:, 0:n], in_=rr.ap[:, 0:n], func=AF.Exp,
                                                                             scale=hcs, bias=hcs), reads=[d["rr"], LP], writes=[aa])
                    op("act", lambda e, mm_=mm_, rr=d["rr"], cs=cs: e.activation(out=mm_.ap[:, 0:n], in_=rr.ap[:, 0:n], func=AF.Exp,
                                                                             scale=cs, bias=cs), reads=[d["rr"], LP], writes=[mm_])
                    op("dve", lambda e, mm_=mm_: e.tensor_scalar(out=mm_.ap[:, 0:n], in0=mm_.ap[:, 0:n], scalar1=-0.25, scalar2=0.25,
                                                                 op0=ALU.mult, op1=ALU.add), reads=[mm_], writes=[mm_])
                    op("dve", lambda e, ig=d["ig"], xc=d["xc"]: e.scalar_tensor_tensor(out=ig.ap[:, 0:n], in0=ig.ap[:, 0:n], scalar=1.0,
                                                                                   in1=xc.ap[:, 0:n], op0=ALU.add, op1=ALU.mult),
                       reads=[d["ig"], d["xc"]], writes=[d["ig"]])
                    d["aa"] = aa
                    d["mm"] = mm_
                for ct in cts:
                    d = B_[ct]
                    op("act", lambda e, mm_=d["mm"]: e.activation(out=mm_.ap[:, 0:n], in_=mm_.ap[:, 0:n], func=AF.Sqrt),
                       reads=[d["mm"]], writes=[d["mm"]])
                for ct in cts:
                    d = B_[ct]
                    op("dve", lambda e, ig=d["ig"], mm_=d["mm"]: e.tensor_tensor(out=ig.ap[:, 0:n], in0=ig.ap[:, 0:n], in1=mm_.ap[:, 0:n],
                                                                              op=ALU.mult), reads=[d["ig"], d["mm"]], writes=[d["ig"]])
                    if mini or j > 0:
                        init = CAR.ap[:, 56 + ct:57 + ct]
                    else:
                        init = CAR.ap[:, 32 + ct:33 + ct]
                    op("dve", lambda e, aa=d["aa"], ig=d["ig"], ct=ct, init=init: e.tensor_tensor_scan(
                        out=HSC3[:, ct, 0:n], data0=aa.ap[:, 0:n], data1=ig.ap[:, 0:n], initial=init,
                        op0=ALU.mult, op1=ALU.add), reads=[d["aa"], d["ig"], CAR], writes=[HSC])
            if not mini:
                op("dve", lambda e: e.tensor_copy(out=CAR.ap[:, 56:60], in_=HSC3[:, :, n - 1]), reads=[HSC], writes=[CAR])
            chk('A2')
            if k == 3:
                op("dve", lambda e: e.tensor_copy(out=HF3[:, :, own0 if not mini else CH:(own0 if not mini else CH) + n],
                                                   in_=HSC3[:, :, 0:n]), reads=[HSC], writes=[HF])
            if k <= 2 and (not mini) and j == NB - 1:
                for (st_o, u_i) in ((0, 2), (4, 3)):
                    op("dve", lambda e, st_o=st_o, u_i=u_i: e.tensor_scalar(
                        out=CAR.ap[:, st_o:st_o + 4], in0=CAR.ap[:, st_o:st_o + 4], scalar1=nflg(k, u_i), scalar2=None,
                        op0=ALU.mult), reads=[CAR, LP], writes=[CAR])
                    op("dve", lambda e, st_o=st_o, u_i=u_i: e.scalar_tensor_tensor(
                        out=CAR.ap[:, st_o:st_o + 4], in0=HSC3[:, :, n - 1], scalar=flg(k, u_i), in1=CAR.ap[:, st_o:st_o + 4],
                        op0=ALU.mult, op1=ALU.add), reads=[CAR, HSC, PP], writes=[CAR])
                for (h_o, u_i) in ((8, 2), (20, 3)):
                    hv = CAR.ap[:, h_o:h_o + 12].rearrange("p (c t) -> p c t", c=4)
                    op("dve", lambda e, hv=hv, u_i=u_i: e.tensor_scalar(
                        out=hv, in0=hv, scalar1=nflg(k, u_i), scalar2=None, op0=ALU.mult), reads=[CAR, LP], writes=[CAR])
                    op("dve", lambda e, hv=hv, u_i=u_i: e.scalar_tensor_tensor(
                        out=hv, in0=XR3[:, :, 512:515], scalar=flg(k, u_i), in1=hv, op0=ALU.mult, op1=ALU.add),
                       reads=[CAR, XR, PP], writes=[CAR])
            chk('A3')
            if do_kv:
                key0 = k * CH + j * 512
                pb = PS()

                def mmkv(e, pb=pb):
                    ins = None
                    for kc in range(8):
                        ins = e.matmul(pb.ap[:, 0:n], lhsT=WIN3[:, kc, 1280:1408], rhs=HT3[:, kc, 0:n],
                                       start=(kc == 0), stop=(kc == 7))
                    return ins
                op("pe", mmkv, reads=[WIN, HT], writes=[pb])
                cf = SCRA()
                sq = SCRA()
                sq_ap = sq.ap.bitcast(BF16)
                op("act", lambda e, cf=cf, pb=pb: e.activation(out=cf.ap[:, 0:n], in_=pb.ap[:, 0:n], func=AF.Copy),
                   reads=[pb], writes=[cf])
                op("act", lambda e, sq_ap=sq_ap, pb=pb: e.activation(out=sq_ap[:, 0:n], in_=pb.ap[:, 0:n], func=AF.Square),
                   reads=[pb], writes=[sq])
                rb = fm_rstd([(sq, sq_ap[:, 0:n], 128, 0)], 128, n, 128.0, SCRA)
                op("dve", lambda e, cf=cf, rb=rb: e.scalar_tensor_tensor(
                    out=CKVN.ap[:, key0:key0 + n], in0=cf.ap[:, 0:n], scalar=ppc("gkv"), in1=rb.ap[:, 0:n],
                    op0=ALU.mult, op1=ALU.mult), reads=[cf, rb, PP], writes=[CKVN])
                chk('A3a')
                pt = PS()
                prt = PS()

                def mmt(e, pt=pt, o=0):
                    ins = None
                    for kc in range(8):
                        ins = e.matmul(pt.ap[0:96, 0:n], lhsT=WKR3[:, kc, o:o + 96], rhs=HT3[:, kc, 0:n],
                                       start=(kc == 0), stop=(kc == 7))
                    return ins
                op("pe", lambda e: mmt(e, pt, 0), reads=[WKR, HT], writes=[pt])
                op("pe", lambda e: mmt(e, prt, 96), reads=[WKR, HT], writes=[prt])
                chk('A3b')
                (sb, s_ap), (cb_, c_ap) = rope_tables(posk[k, :, j * 512:j * 512 + n], n, SCRA)
                chk('A3c')
                tq = SCRA()
                tq_ap = tq.ap.bitcast(BF16)
                op("act", lambda e, pt=pt, tq_ap=tq_ap: e.activation(out=tq_ap[64:96, 0:n], in_=pt.ap[64:96, 0:n], func=AF.Square),
                   reads=[pt], writes=[tq])
                op("dve", lambda e, tq_ap=tq_ap: e.tensor_copy(out=ETS.ap[0:32, key0:key0 + n], in_=tq_ap[64:96, 0:n]),
                   reads=[tq], writes=[ETS])
                e1 = SCRA()
                e2 = SCRA()
                op("dve", lambda e, e1=e1, pt=pt, c_ap=c_ap: e.scalar_tensor_tensor(
                    out=e1.ap[64:96, 0:n], in0=pt.ap[64:96, 0:n], scalar=ppc("gk", 0, slice(64, 96)), in1=c_ap,
                    op0=ALU.mult, op1=ALU.mult), reads=[pt, cb_, PP, tq], writes=[e1])
                op("dve", lambda e, e2=e2, prt=prt, s_ap=s_ap: e.scalar_tensor_tensor(
                    out=e2.ap[64:96, 0:n], in0=prt.ap[64:96, 0:n], scalar=ppc("gkr", 0, slice(64, 96)), in1=s_ap,
                    op0=ALU.mult, op1=ALU.mult), reads=[prt, sb, PP], writes=[e2])
                op("dve", lambda e, e1=e1, e2=e2: e.tensor_tensor(out=ETS.ap[64:96, key0:key0 + n], in0=e1.ap[64:96, 0:n],
                                                                 in1=e2.ap[64:96, 0:n], op=ALU.add),
                   reads=[e1, e2], writes=[ETS])
            chk('A4')
            if do_own:
                for ct in range(4):
                    pb = PS()

                    def mmy(e, pb=pb, ct=ct):
                        ins = None
                        for kc in range(8):
                            ins = e.matmul(pb.ap[:, 0:n], lhsT=WIN3[:, kc, 512 + ct * 128:512 + (ct + 1) * 128],
                                           rhs=HT3[:, kc, 0:n], start=(kc == 0), stop=(kc == 7))
                        return ins
                    op("pe", mmy, reads=[WIN, HT], writes=[pb])
                    u = SCRA()
                    w = SCRA()
                    op("act", lambda e, u=u, pb=pb: e.activation(out=u.ap[:, 0:n], in_=pb.ap[:, 0:n], func=AF.Copy, scale=0.5),
                       reads=[pb], writes=[u])
                    op("act", lambda e, w=w, pb=pb: e.activation(out=w.ap[:, 0:n], in_=pb.ap[:, 0:n], func=AF.Square),
                       reads=[pb], writes=[w])
                    op("dve", lambda e, w=w: e.tensor_scalar(out=w.ap[:, 0:n], in0=w.ap[:, 0:n], scalar1=2.0 * 0.044715, scalar2=2.0,
                                                             op0=ALU.mult, op1=ALU.add), reads=[w], writes=[w])
                    op("dve", lambda e, w=w, u=u: e.tensor_tensor(out=w.ap[:, 0:n], in0=w.ap[:, 0:n], in1=u.ap[:, 0:n], op=ALU.mult),
                       reads=[w, u], writes=[w])
                    op("act", lambda e, w=w: e.activation(out=w.ap[:, 0:n], in_=w.ap[:, 0:n], func=AF.Tanh,
                                                          scale=0.7978845608028654), reads=[w], writes=[w])
                    op("dve", lambda e, w=w, u=u, ct=ct: e.scalar_tensor_tensor(out=GG3[:, ct, own0:own0 + n], in0=w.ap[:, 0:n],
                                                                             scalar=1.0, in1=u.ap[:, 0:n], op0=ALU.add, op1=ALU.mult),
                       reads=[w, u], writes=[GG])
                cfs = []
                sqs = []
                for c2 in range(2):
                    pb = PS()

                    def mmq(e, pb=pb, c2=c2):
                        ins = None
                        for kc in range(8):
                            ins = e.matmul(pb.ap[:, 0:n], lhsT=WIN3[:, kc, 1024 + c2 * 128:1024 + (c2 + 1) * 128],
                                           rhs=HT3[:, kc, 0:n], start=(kc == 0), stop=(kc == 7))
                        return ins
                    op("pe", mmq, reads=[WIN, HT], writes=[pb])
                    cf = SCRA()
                    sq = SCRA()
                    sq_ap = sq.ap.bitcast(BF16)
                    op("act", lambda e, cf=cf, pb=pb: e.activation(out=cf.ap[:, 0:n], in_=pb.ap[:, 0:n], func=AF.Copy),
                       reads=[pb], writes=[cf])
                    op("act", lambda e, sq_ap=sq_ap, pb=pb: e.activation(out=sq_ap[:, 0:n], in_=pb.ap[:, 0:n], func=AF.Square),
                       reads=[pb], writes=[sq])
                    cfs.append(cf)
                    sqs.append((sq, sq_ap[:, 0:n], 128, 0))
                rb = fm_rstd(sqs, 128, n, 256.0, SCRA)
                for c2 in range(2):
                    op("dve", lambda e, c2=c2, cf=cfs[c2], rb=rb: e.scalar_tensor_tensor(
                        out=CQN3[:, c2, own0:own0 + n], in0=cf.ap[:, 0:n], scalar=ppc("gqa", c2), in1=rb.ap[:, 0:n],
                        op0=ALU.mult, op1=ALU.mult), reads=[cf, rb, PP], writes=[CQN])
            if k == 4 and not mini:
                lo = CH - 512 * (j + 1)
                lru_combine(lambda ct: HF3[:, ct, lo:lo + n], HF, lambda ct: HSC3[:, ct, 0:n][:, ::-1], HSC, lo, n)

        def lru_combine(hf_ap, hf_buf, hb_ap, hb_buf, own0, n):
            los = []
            sqs = []
            for ct in range(4):
                lo_ = SCRA()
                op("dve", lambda e, lo_=lo_, ct=ct: e.tensor_tensor(out=lo_.ap[:, 0:n], in0=hf_ap(ct), in1=hb_ap(ct), op=ALU.add),
                   reads=[hf_buf, hb_buf], writes=[lo_])
                op("dve", lambda e, lo_=lo_, ct=ct: e.tensor_tensor(out=lo_.ap[:, 0:n], in0=lo_.ap[:, 0:n],
                                                                  in1=GG3[:, ct, own0:own0 + n], op=ALU.mult),
                   reads=[lo_, GG], writes=[lo_])
                sq = SCRA()
                sq_ap = sq.ap.bitcast(BF16)
                op("act", lambda e, sq_ap=sq_ap, lo_=lo_: e.activation(out=sq_ap[:, 0:n], in_=lo_.ap[:, 0:n], func=AF.Square),
                   reads=[lo_], writes=[sq])
                los.append(lo_)
                sqs.append((sq, sq_ap[:, 0:n], 128, 0))
            rb = fm_rstd(sqs, 128, n, 512.0, SCRA)
            for ct in range(4):
                op("dve", lambda e, ct=ct, lo_=los[ct], rb=rb: e.scalar_tensor_tensor(
                    out=MIXL3[:, ct, own0:own0 + n], in0=lo_.ap[:, 0:n], scalar=ppc("glru", ct), in1=rb.ap[:, 0:n],
                    op0=ALU.mult, op1=ALU.mult), reads=[lo_, rb, PP], writes=[MIXL])

        blks = []
        for k in range(5):
            for j in range(NB):
                blks.append((k, j, 512, False))
            if k >= 3:
                blks.append((k, NB, 1, True))
        phaseA_front(*blks[0], 0)
        for bi, (k, j, n, mini) in enumerate(blks):
            if bi + 1 < len(blks):
                phaseA_front(*blks[bi + 1], (bi + 1) % 2)
            phaseA_block(k, j, n, mini, bi % 2)
            if mini and k == 3:
                op("dve", lambda e: e.tensor_copy(out=HE.ap[:, 0:4], in_=HSC3[:, :, 0]), reads=[HSC], writes=[HE])
            if mini and k == 4:
                op("dve", lambda e: e.tensor_copy(out=HE.ap[:, 4:8], in_=HSC3[:, :, 0]), reads=[HSC], writes=[HE])
        chk('A6')
        HE3 = HE.ap[:, 8:16].rearrange("p (c t) -> p c t", c=4)
        op("dve", lambda e: e.tensor_tensor(out=HE3[:, :, 0], in0=HE.ap[:, 0:4], in1=CAR.ap[:, 52:56], op=ALU.add),
           reads=[HE, CAR], writes=[HE])
        op("dve", lambda e: e.tensor_tensor(out=HE3[:, :, 1], in0=HE.ap[:, 4:8], in1=CAR.ap[:, 48:52], op=ALU.add),
           reads=[HE, CAR], writes=[HE])
        ZERO = SCRA()
        op("pool", lambda e: e.memset(ZERO.ap[:, 0:8], 0.0), writes=[ZERO])
        lru_combine(lambda ct: HE3[:, ct, :], HE, lambda ct: ZERO.ap[:, 0:2], ZERO, CH, 2)
        if "mixl" in dbg_d:
            TMPD = A.f32(4 * NOWN, "tmpd")
            op("dve", lambda e: e.tensor_copy(out=TMPD.ap, in_=MIXL.ap), reads=[MIXL], writes=[TMPD])
            dump("mixl", TMPD, TMPD.ap)
        if "ckvn" in dbg_d:
            TMPD2 = A.f32(NKEY, "tmpd2")
            op("dve", lambda e: e.tensor_copy(out=TMPD2.ap, in_=CKVN.ap), reads=[CKVN], writes=[TMPD2])
            dump("ckvn", TMPD2, TMPD2.ap)
        if "ets" in dbg_d:
            TMPD3 = A.f32(NKEY, "tmpd3")
            op("dve", lambda e: e.tensor_copy(out=TMPD3.ap[0:96], in_=ETS.ap[0:96]), reads=[ETS], writes=[TMPD3])
            dump("ets", TMPD3, TMPD3.ap[0:96])

        S.barrier_all()
        A.release(mA)
        if stop == 'A':
            S.emit(nc)
            return nc

        WUQ = A.bf(2 * 768, "wuq")
        WUQ3 = WUQ.ap.rearrange("p (k c) -> p k c", k=2)
        for c2 in range(2):
            load(WUQ, WUQ3[:, c2, :], w_uq[c2 * 128:(c2 + 1) * 128, :])
        WUQR = A.bf(2 * 8 * 96, "wuqr")
        WUQR4 = WUQR.ap.rearrange("p (k h c) -> p k h c", k=2, h=8)
        WUQ4 = WUQ.ap.rearrange("p (k h c) -> p k h c", k=2, h=8)
        op("pool", lambda e: e.memset(WUQR.ap, 0.0), writes=[WUQR])
        for c2 in range(2):
            op("dve", lambda e, c2=c2: e.tensor_scalar(out=WUQR4[:, c2, :, 64:80], in0=WUQ4[:, c2, :, 80:96], scalar1=-1.0,
                                                       scalar2=None, op0=ALU.mult), reads=[WUQ], writes=[WUQR])
            op("dve", lambda e, c2=c2: e.tensor_copy(out=WUQR4[:, c2, :, 80:96], in_=WUQ4[:, c2, :, 64:80]),
               reads=[WUQ], writes=[WUQR])
        WUKV = A.bf(1024, "wukv")
        load(WUKV, WUKV.ap, w_ukv)
        ON = A.bf(8 * NOWN, "on")
        ON3 = ON.ap.rearrange("p (h t) -> p h t", h=8)
        TAB = A.bf(2 * NOWN, "qtab")
        mB = A.mark()
        KT = [A.bf(NKEY, "kt%d" % i) for i in range(2)]
        VV = [A.bf(NKT * 65, "v%d" % i) for i in range(2)]
        VV3 = [v.ap.rearrange("p (t c) -> p t c", c=65) for v in VV]
        QT = [A.bf(NOWN, "qt%d" % i) for i in range(2)]
        PT = [Buf(NT_H[i].ap[:, 0:512], NT_H[i].r) for i in range(2)] + [A.bf(512, "pt%d" % i) for i in range(2)]
        scrB = [Buf(NT_X[i].ap[:, 0:512], NT_X[i].r) for i in range(2)] + [A.f32(512, "scrB%d" % i) for i in range(6)]
        sbi = [0]

        def SCRB():
            b = scrB[sbi[0] % len(scrB)]
            sbi[0] += 1
            return b
        for v3, v in zip(VV3, VV):
            op("pool", lambda e, v3=v3: e.memset(v3[:, :, 64:65], 1.0), writes=[v])
        nqb = -(-NOWN // 512)
        half = NOWN // 2
        qb_base = half // nqb
        qb_sizes = [2 * (qb_base + (1 if i < half - qb_base * nqb else 0)) for i in range(nqb)]
        qblocks = [(sum(qb_sizes[:i]), qb_sizes[i]) for i in range(nqb)]
        for (q0, n) in qblocks:
            rope_tables(poso[:, q0:q0 + n], n, SCRB, out_buf=TAB, out_s=TAB.ap[64:96, q0:q0 + n],
                        out_c=TAB.ap[64:96, NOWN + q0:NOWN + q0 + n])
        pti = [0]

        RK = [A.f32(NKT, "rk%d" % i) for i in range(2)]
        SSK = psum[5]

        for kt_ in KT:
            op("dve", lambda e, kt_=kt_: e.tensor_copy(out=kt_.ap[64:96, :], in_=ETS.ap[64:96, :]), reads=[ETS], writes=[kt_])

        def kgen(h):
            kt = KT[h % 2]
            rk = RK[h % 2]
            pend_mms = []
            for kb in range(NKB):
                c0 = kb * 512
                pk = PS()
                op("pe", lambda e, pk=pk, c0=c0: e.matmul(pk.ap[0:64, :], lhsT=WUKV.ap[:, h * 128:h * 128 + 64],
                                                      rhs=CKVN.ap[:, c0:c0 + 512], start=True, stop=True),
                   reads=[WUKV, CKVN], writes=[pk])
                sq = SCRB()
                sq_ap = sq.ap.bitcast(BF16)
                op("act", lambda e, sq_ap=sq_ap, pk=pk: e.activation(out=sq_ap[0:64, 0:512], in_=pk.ap[0:64, :], func=AF.Square),
                   reads=[pk], writes=[sq])
                op("dve", lambda e, pk=pk, c0=c0: e.tensor_scalar(out=kt.ap[0:64, c0:c0 + 512], in0=pk.ap[0:64, :],
                                                                 scalar1=ppc("gk", 0, slice(0, 64)), scalar2=None, op0=ALU.mult),
                   reads=[pk, PP, sq], writes=[kt])

                def mms(e, sq_ap=sq_ap, kb=kb, c0=c0):
                    ins = None
                    for i in range(4):
                        t = kb * 4 + i
                        e.matmul(SSK.ap[:, t:t + 1], lhsT=sq_ap[0:64, i * 128:(i + 1) * 128], rhs=ONESB.ap[0:64, 0:1],
                                 start=True, stop=False)
                        ins = e.matmul(SSK.ap[:, t:t + 1], lhsT=ETS.ap[0:32, c0 + i * 128:c0 + (i + 1) * 128],
                                       rhs=ONESB.ap[0:32, 0:1], start=False, stop=True)
                    return ins
                if pend_mms:
                    pm, psq = pend_mms.pop(0)
                    op("pe", pm, reads=[psq, ETS, ONESB], writes=[SSK])
                pend_mms.append((mms, sq))
            while pend_mms:
                pm, psq = pend_mms.pop(0)
                op("pe", pm, reads=[psq, ETS, ONESB], writes=[SSK])
            op("dve", lambda e: e.tensor_scalar(out=rk.ap[:, 0:NKT], in0=SSK.ap[:, 0:NKT], scalar1=1.0, scalar2=96.0 * EPS,
                                                op0=ALU.mult, op1=ALU.add), reads=[SSK], writes=[rk])
            op("act", lambda e: e.activation(out=rk.ap[:, 0:NKT], in_=rk.ap[:, 0:NKT], func=AF.Ln), reads=[rk], writes=[rk])
            op("act", lambda e: e.activation(out=rk.ap[:, 0:NKT], in_=rk.ap[:, 0:NKT], func=AF.Exp, scale=-0.5), reads=[rk], writes=[rk])

        def vgen(h):
            vv = VV[h % len(VV)]
            vv3 = VV3[h % len(VV)]
            for kb in range(NKB):
                c0 = kb * 512
                pv = PS()

                def mmv(e, pv=pv, c0=c0):
                    ins = None
                    for i in range(4):
                        ins = e.matmul(pv.ap[:, i * 64:(i + 1) * 64], lhsT=CKVN.ap[:, c0 + i * 128:c0 + (i + 1) * 128],
                                       rhs=WUKV.ap[:, h * 128 + 64:h * 128 + 128], start=True, stop=True)
                    return ins
                op("pe", mmv, reads=[CKVN, WUKV], writes=[pv])
                op("dve", lambda e, pv=pv, kb=kb: e.tensor_copy(out=vv3[:, kb * 4:(kb + 1) * 4, 0:64],
                                                             in_=pv.ap[:, 0:256].rearrange("p (t c) -> p t c", c=64)),
                   reads=[pv], writes=[vv])

        def do_head(h):
            kt = KT[h % 2]
            rk = RK[h % 2]
            vv = VV[h % len(VV)]
            vv3 = VV3[h % len(VV)]
            qt = QT[h % 2]
            for (q0, n) in qblocks:
                pq = PS()
                pr = PS()

                def mmq(e, pq=pq, q0=q0, n=n):
                    ins = None
                    for c2 in range(2):
                        ins = e.matmul(pq.ap[0:96, 0:n], lhsT=WUQ3[:, c2, h * 96:(h + 1) * 96], rhs=CQN3[:, c2, q0:q0 + n],
                                       start=(c2 == 0), stop=(c2 == 1))
                    return ins

                def mmr(e, pr=pr, q0=q0, n=n):
                    ins = None
                    for c2 in range(2):
                        ins = e.matmul(pr.ap[0:96, 0:n], lhsT=WUQR4[:, c2, h, :], rhs=CQN3[:, c2, q0:q0 + n],
                                       start=(c2 == 0), stop=(c2 == 1))
                    return ins
                op("pe", mmq, reads=[WUQ, CQN], writes=[pq])
                op("pe", mmr, reads=[WUQR, CQN], writes=[pr])
                sq = SCRB()
                sq_ap = sq.ap.bitcast(BF16)
                op("act", lambda e, sq_ap=sq_ap, pq=pq, n=n: e.activation(out=sq_ap[0:96, 0:n], in_=pq.ap[0:96, 0:n], func=AF.Square),
                   reads=[pq], writes=[sq])
                rb = fm_rstd([(sq, sq_ap[0:96, 0:n], 96, 0)], 96, n, 96.0, SCRB)
                op("dve", lambda e, pq=pq, rb=rb, q0=q0, n=n: e.scalar_tensor_tensor(
                    out=qt.ap[0:64, q0:q0 + n], in0=pq.ap[0:64, 0:n], scalar=ppc("gq", 0, slice(0, 64)), in1=rb.ap[0:64, 0:n],
                    op0=ALU.mult, op1=ALU.mult), reads=[pq, rb, PP], writes=[qt])
                e1 = SCRB()
                e2 = SCRB()
                op("dve", lambda e, e1=e1, pq=pq, q0=q0, n=n: e.scalar_tensor_tensor(
                    out=e1.ap[64:96, 0:n], in0=pq.ap[64:96, 0:n], scalar=ppc("gq", 0, slice(64, 96)),
                    in1=TAB.ap[64:96, NOWN + q0:NOWN + q0 + n], op0=ALU.mult, op1=ALU.mult), reads=[pq, TAB, PP, sq], writes=[e1])
                op("dve", lambda e, e2=e2, pr=pr, q0=q0, n=n: e.scalar_tensor_tensor(
                    out=e2.ap[64:96, 0:n], in0=pr.ap[64:96, 0:n], scalar=ppc("gqr", 0, slice(64, 96)),
                    in1=TAB.ap[64:96, q0:q0 + n], op0=ALU.mult, op1=ALU.mult), reads=[pr, TAB, PP], writes=[e2])
                op("dve", lambda e, e1=e1, e2=e2, n=n: e.tensor_tensor(out=e1.ap[64:96, 0:n], in0=e1.ap[64:96, 0:n],
                                                                     in1=e2.ap[64:96, 0:n], op=ALU.add),
                   reads=[e1, e2], writes=[e1])
                op("dve", lambda e, e1=e1, rb=rb, q0=q0, n=n: e.tensor_tensor(out=qt.ap[64:96, q0:q0 + n], in0=e1.ap[64:96, 0:n],
                                                                           in1=rb.ap[64:96, 0:n], op=ALU.mult),
                   reads=[e1, rb], writes=[qt])
            scale = 96.0 ** -0.5
            fin_pending = []
            for (q0, n) in qblocks:
                po = PSACC()
                pend = []

                def qk(t, q0=q0, n=n):
                    pb = PS()
                    op("pe", lambda e, pb=pb, t=t: e.matmul(pb.ap[:, 0:n], lhsT=kt.ap[0:96, t * 128:(t + 1) * 128],
                                                        rhs=qt.ap[0:96, q0:q0 + n], start=True, stop=True),
                       reads=[kt, qt], writes=[pb])
                    return pb

                def expv(t, pb, q0=q0, n=n, po=po):
                    pt_ = PT[pti[0] % 4]
                    pti[0] += 1
                    op("act", lambda e, pb=pb, pt_=pt_, t=t: e.activation(out=pt_.ap[:, 0:n], in_=pb.ap[:, 0:n], func=AF.Exp,
                                                                     scale=rk.ap[:, t:t + 1]), reads=[pb, rk], writes=[pt_])
                    op("pe", lambda e, pt_=pt_, t=t: e.matmul(po.ap[0:65, 0:n], lhsT=vv3[:, t, :], rhs=pt_.ap[:, 0:n],
                                                          start=(t == 0), stop=(t == NKT - 1)),
                       reads=[vv, pt_], writes=[po])
                LOOK = 2
                DEFER = 8
                for t in range(NKT + LOOK):
                    if t < NKT:
                        pend.append((t, qk(t)))
                    if t >= LOOK:
                        tt, pb = pend.pop(0)
                        expv(tt, pb)
                    if t == DEFER and fin_pending:
                        fin_pending.pop(0)()
                rd = SCRB()
                op("dve", lambda e, rd=rd, po=po, n=n: e.reciprocal(out=rd.ap[64:65, 0:n], in_=po.ap[64:65, 0:n]),
                   reads=[po], writes=[rd])

                def fin(rd=rd, po=po, q0=q0, n=n):
                    pbc = PS()
                    op("pe", lambda e, pbc=pbc: e.matmul(pbc.ap[0:64, 0:n], lhsT=ONESF.ap[64:65, 0:64], rhs=rd.ap[64:65, 0:n],
                                                         start=True, stop=True), reads=[ONESF, rd], writes=[pbc])
                    oc = SCRB()
                    op("act", lambda e, oc=oc: e.activation(out=oc.ap[0:64, 0:n], in_=po.ap[0:64, 0:n], func=AF.Copy),
                       reads=[po, rd], writes=[oc])
                    op("dve", lambda e, oc=oc, pbc=pbc: e.tensor_tensor(out=ON3[0:64, h, q0:q0 + n], in0=oc.ap[0:64, 0:n],
                                                                         in1=pbc.ap[0:64, 0:n], op=ALU.mult),
                       reads=[oc, pbc], writes=[ON])
                fin_pending.append(fin)
            while fin_pending:
                fin_pending.pop(0)()
        chk('B0')
        kgen(0)
        chk('B1')
        vgen(0)
        chk('B2')
        for h_ in range(8):
            if h_ < 7:
                kgen(h_ + 1)
                if len(VV) > 1:
                    vgen(h_ + 1)
            do_head(h_)
            if h_ < 7 and len(VV) == 1:
                vgen(h_ + 1)
        for (q0, n) in qblocks:
            sqs = []
            for h in range(8):
                sq = SCRB()
                sq_ap = sq.ap.bitcast(BF16)
                op("act", lambda e, sq_ap=sq_ap, h=h, q0=q0, n=n: e.activation(out=sq_ap[0:64, 0:n], in_=ON3[0:64, h, q0:q0 + n],
                                                                            func=AF.Square), reads=[ON], writes=[sq])
                sqs.append((sq, sq_ap[0:64, 0:n], 64, 0))
            rb = fm_rstd(sqs, 64, n, 512.0, SCRB)
            for h in range(8):
                op("dve", lambda e, h=h, rb=rb, q0=q0, n=n: e.scalar_tensor_tensor(
                    out=ON3[0:64, h, q0:q0 + n], in0=ON3[0:64, h, q0:q0 + n], scalar=ppc("gmla", h, slice(0, 64)),
                    in1=rb.ap[0:64, 0:n], op0=ALU.mult, op1=ALU.mult), reads=[ON, rb, PP], writes=[ON])
        if "on" in dbg_d:
            TMPD4 = A.f32(8 * NOWN, "tmpd4")
            op("dve", lambda e: e.tensor_copy(out=TMPD4.ap[0:64], in_=ON.ap[0:64]), reads=[ON], writes=[TMPD4])
            dump("on", TMPD4, TMPD4.ap[0:64])
        S.barrier_all()
        A.release(mB)
        if stop == 'B':
            S.emit(nc)
            return nc
        NTL = CH // 128
        tiles = [(i * 128, 128) for i in range(NTL)] + [(CH, 2)]
        xres_start = A.top()
        XRES = A.f32((NTL + 1) * D, "xres")
        xres_end = A.top()
        XRES3 = XRES.ap.rearrange("p (t d) -> p t d", d=D)
        XR_res = [Res("xres%d" % i) for i in range(NTL + 1)]
        XT_ = [Buf(XRES3[:, i, :], XR_res[i]) for i in range(NTL + 1)]
        mC = A.mark()
        A.regs = [[ckvn_start, ets_end], [xres_end, ASZ]]
        WOL = A.bf(4 * D, "wol")
        WOL3 = WOL.ap.rearrange("p (c d) -> p c d", c=4)
        WOM = A.bf(8 * D, "wom")
        WOM3 = WOM.ap.rearrange("p (h d) -> p h d", h=8)
        load(WOL, WOL3, w_out[0:512, :].rearrange("(c p) d -> p c d", p=128))
        load(WOM, WOM3[0:64], w_out[512:1024, :].rearrange("(h p) d -> p h d", p=64))
        for ti, (o0, rows) in enumerate(tiles):
            xt = XT_[ti]
            src = xs[3, o0:o0 + rows, :] if ti < NTL else xh[0:2, :]
            load(xt, xt.ap[0:rows], src, eng="sp")
            for half in range(2):
                pb = PS()

                def mmo(e, pb=pb, o0=o0, rows=rows, half=half):
                    ins = None
                    for ct in range(4):
                        ins = e.matmul(pb.ap[0:rows, :], lhsT=MIXL3[:, ct, o0:o0 + rows],
                                       rhs=WOL3[:, ct, half * 512:(half + 1) * 512], start=(ct == 0), stop=False)
                    for h in range(8):
                        ins = e.matmul(pb.ap[0:rows, :], lhsT=ON3[0:64, h, o0:o0 + rows],
                                       rhs=WOM3[0:64, h, half * 512:(half + 1) * 512], start=False, stop=(h == 7))
                    return ins
                op("pe", mmo, reads=[MIXL, ON, WOL, WOM], writes=[pb])
                op("dve", lambda e, pb=pb, xt=xt, rows=rows, half=half: e.tensor_tensor(
                    out=xt.ap[0:rows, half * 512:(half + 1) * 512], in0=xt.ap[0:rows, half * 512:(half + 1) * 512],
                    in1=pb.ap[0:rows, :], op=ALU.add), reads=[xt, pb], writes=[xt])
        if "x1" in dbg_d:
            for ti in range(NTL):
                store(dbg_d["x1"][ti * 128:(ti + 1) * 128, :], XT_[ti], XT_[ti].ap)
        S.barrier_all()
        A.regs = [[mP, xres_start], [xres_end, ASZ]]
        if stop == 'C':
            S.emit(nc)
            return nc

        GB2 = A.f32(D, "gbc2")
        GB3 = A.f32(D, "gbc3")
        load(GB2, GB2.ap, gvec[1, :].partition_broadcast(128))
        load(GB3, GB3.ap, gvec[3, :].partition_broadcast(128))
        HNT = A.bf(8 * NOWN, "hnt")
        HNT3 = HNT.ap.rearrange("p (k t) -> p k t", k=8)
        mD = A.mark()
        WMQ = A.bf(8 * 512, "wmq")
        WMQ3 = WMQ.ap.rearrange("p (k c) -> p k c", k=8)
        load(WMQ, WMQ3, w_mq.rearrange("(k p) c -> p k c", p=128))
        WMO = A.bf(4 * D, "wmo")
        WMO3 = WMO.ap.rearrange("p (h d) -> p h d", h=4)
        load(WMO, WMO3, w_mo.rearrange("(h p) d -> p h d", p=128))
        H1T = A.bf(8 * 512, "h1t")
        H1T3 = H1T.ap.rearrange("p (k t) -> p k t", k=8)
        OMT = A.bf(4 * 512, "omt")
        OMT3 = OMT.ap.rearrange("p (h t) -> p h t", h=4)
        HNH = A.bf(8 * 2, "hnh")
        HNH3 = HNH.ap.rearrange("p (k t) -> p k t", k=8)
        scrD = [A.f32(512, "scrD%d" % i) for i in range(10)]
        sdi = [0]

        def SCRD():
            b = scrD[sdi[0] % len(scrD)]
            sdi[0] += 1
            return b
        mscale = 128.0 ** -0.5
        qbl = [(q0, min(512, CH - q0)) for q0 in range(0, CH, 512)] + [(CH, 2)]
        for (q0, n) in qbl:
            tl = [ti for ti, (o0, rows) in enumerate(tiles) if q0 <= o0 < q0 + n]
            for ti in tl:
                o0, rows = tiles[ti]
                norm_T(XT_[ti], XT_[ti].ap[0:rows], rows, H1T, H1T3, o0 - q0, gb=GB2)
            for h in range(4):
                pq = PS()

                def mmq2(e, pq=pq, h=h, n=n):
                    ins = None
                    for kc in range(8):
                        ins = e.matmul(pq.ap[:, 0:n], lhsT=WMQ3[:, kc, h * 128:(h + 1) * 128], rhs=H1T3[:, kc, 0:n],
                                       start=(kc == 0), stop=(kc == 7))
                    return ins
                op("pe", mmq2, reads=[WMQ, H1T], writes=[pq])
                sq = SCRD()
                sq_ap = sq.ap.bitcast(BF16)
                op("act", lambda e, sq_ap=sq_ap, pq=pq, n=n: e.activation(out=sq_ap[:, 0:n], in_=pq.ap[:, 0:n], func=AF.Square),
                   reads=[pq], writes=[sq])
                rb = fm_rstd([(sq, sq_ap[:, 0:n], 128, 0)], 128, n, 128.0, SCRD)
                qm = SCRD()
                qm_ap = qm.ap.bitcast(BF16)
                op("dve", lambda e, pq=pq, rb=rb, qm_ap=qm_ap, n=n: e.scalar_tensor_tensor(
                    out=qm_ap[:, 0:n], in0=pq.ap[:, 0:n], scalar=ppc("gmq"), in1=rb.ap[:, 0:n], op0=ALU.mult, op1=ALU.mult),
                   reads=[pq, rb, PP], writes=[qm])
                po = PS()
                pd = PS()
                for mt in range(2):
                    ps_ = PS()
                    op("pe", lambda e, ps_=ps_, mt=mt, h=h, qm_ap=qm_ap, n=n: e.matmul(
                        ps_.ap[:, 0:n], lhsT=KM3[:, h, mt * 128:(mt + 1) * 128], rhs=qm_ap[:, 0:n], start=True, stop=True),
                       reads=[KM, qm], writes=[ps_])
                    pt_ = SCRD()
                    pt_ap = pt_.ap.bitcast(BF16)
                    op("act", lambda e, ps_=ps_, pt_ap=pt_ap, n=n: e.activation(out=pt_ap[:, 0:n], in_=ps_.ap[:, 0:n], func=AF.Exp,
                                                                         scale=mscale), reads=[ps_], writes=[pt_])
                    op("pe", lambda e, po=po, mt=mt, h=h, pt_ap=pt_ap, n=n: e.matmul(
                        po.ap[:, 0:n], lhsT=VM3[:, mt, h * 128:(h + 1) * 128], rhs=pt_ap[:, 0:n], start=(mt == 0), stop=(mt == 1)),
                       reads=[VM, pt_], writes=[po])
                    op("pe", lambda e, pd=pd, mt=mt, pt_ap=pt_ap, n=n: e.matmul(
                        pd.ap[:, 0:n], lhsT=ONESB.ap[:, 0:128], rhs=pt_ap[:, 0:n], start=(mt == 0), stop=(mt == 1)),
                       reads=[ONESB, pt_], writes=[pd])
                rd = SCRD()
                op("dve", lambda e, rd=rd, pd=pd, n=n: e.reciprocal(out=rd.ap[:, 0:n], in_=pd.ap[:, 0:n]),
                   reads=[pd], writes=[rd])
                op("dve", lambda e, po=po, rd=rd, h=h, n=n: e.tensor_tensor(out=OMT3[:, h, 0:n], in0=po.ap[:, 0:n], in1=rd.ap[:, 0:n],
                                                                         op=ALU.mult), reads=[po, rd], writes=[OMT])
            for ti in tl:
                o0, rows = tiles[ti]
                xt = XT_[ti]
                c0 = o0 - q0
                for half in range(2):
                    pb = PS()

                    def mmo2(e, pb=pb, c0=c0, rows=rows, half=half):
                        ins = None
                        for h in range(4):
                            ins = e.matmul(pb.ap[0:rows, :], lhsT=OMT3[:, h, c0:c0 + rows],
                                           rhs=WMO3[:, h, half * 512:(half + 1) * 512], start=(h == 0), stop=(h == 3))
                        return ins
                    op("pe", mmo2, reads=[OMT, WMO], writes=[pb])
                    op("dve", lambda e, pb=pb, xt=xt, rows=rows, half=half: e.tensor_tensor(
                        out=xt.ap[0:rows, half * 512:(half + 1) * 512], in0=xt.ap[0:rows, half * 512:(half + 1) * 512],
                        in1=pb.ap[0:rows, :], op=ALU.add), reads=[xt, pb], writes=[xt])
                if ti < NTL:
                    norm_T(xt, xt.ap[0:rows], rows, HNT, HNT3, o0 + 1, gb=GB3, evac_eng="act")
                else:
                    norm_T(xt, xt.ap[0:rows], rows, HNH, HNH3, 0, gb=GB3)
                    op("dve", lambda e: e.tensor_scalar(out=HNT3[:, :, CH + 1:CH + 2], in0=HNH3[:, :, 0:1], scalar1=ppc("msk", 0),
                                                        scalar2=None, op0=ALU.mult), reads=[HNH, PP], writes=[HNT])
                    op("dve", lambda e: e.tensor_scalar(out=HNT3[:, :, 0:1], in0=HNH3[:, :, 1:2], scalar1=ppc("msk", 1),
                                                        scalar2=None, op0=ALU.mult), reads=[HNH, PP], writes=[HNT])
        if "x2" in dbg_d:
            for ti in range(NTL):
                store(dbg_d["x2"][ti * 128:(ti + 1) * 128, :], XT_[ti], XT_[ti].ap)
        S.barrier_all()
        A.release(mD)
        if stop == 'D':
            S.emit(nc)
            return nc

        GP = 2
        NR = 22 // GP
        WUPG = [A.bf(8 * 2 * GP * 128, "wupg%d" % i) for i in range(2)]
        WUPG4 = [w.ap.rearrange("p (k s c) -> p k s c", k=8, s=2) for w in WUPG]
        WDNG = [A.bf(GP * D, "wdng%d" % i) for i in range(2)]
        WDNG3 = [w.ap.rearrange("p (j d) -> p j d", j=GP) for w in WDNG]
        ACTT = [A.bf(GP * CH, "actt%d" % i) for i in range(2)]
        ACTT3 = [a.ap.rearrange("p (j t) -> p j t", j=GP) for a in ACTT]
        scrE = [A.f32(512, "scrE%d" % i) for i in range(8)]
        sei = [0]

        def SCRE():
            b = scrE[sei[0] % len(scrE)]
            sei[0] += 1
            return b
        fwo = PPL["fw"][0]
        fbo = PPL["fb"][0]
        def do_round(r):
            wu = WUPG[r % 2]
            wu4 = WUPG4[r % 2]
            wd = WDNG[r % 2]
            wd3 = WDNG3[r % 2]
            at = ACTT[r % 2]
            at3 = ACTT3[r % 2]
            j0 = r * GP
            load(wu, wu4[:, :, 0, :], w_up[:, j0 * 128:(j0 + GP) * 128].rearrange("(k p) c -> p k c", p=128))
            load(wu, wu4[:, :, 1, :], w_up[:, DFF + j0 * 128:DFF + (j0 + GP) * 128].rearrange("(k p) c -> p k c", p=128))
            load(wd, wd3, w_dn[j0 * 128:(j0 + GP) * 128, :].rearrange("(j p) d -> p j d", p=128))
            for (t0, n) in WINS:
                for jj in range(GP):
                    cv = []
                    for s in range(2):
                        chn = (j0 + jj) + s * 22
                        pg = PS()

                        def mmu(e, pg=pg, s=s, jj=jj, t0=t0, n=n):
                            ins = None
                            for kc in range(8):
                                ins = e.matmul(pg.ap[:, 0:n + 2], lhsT=wu4[:, kc, s, jj * 128:(jj + 1) * 128],
                                               rhs=HNT3[:, kc, t0:t0 + n + 2], start=(kc == 0), stop=(kc == 7))
                            return ins
                        op("pe", mmu, reads=[wu, HNT], writes=[pg])
                        c_ = SCRE()
                        wsrc, wb = PP, fwo + 3 * chn
                        bsrc, bb = PP, fbo + chn
                        op("act", lambda e, c_=c_, pg=pg, wsrc=wsrc, wb=wb, bsrc=bsrc, bb=bb, n=n: e.activation(
                            out=c_.ap[:, 0:n], in_=pg.ap[:, 1:n + 1], func=AF.Identity,
                            scale=wsrc.ap[:, wb + 1:wb + 2], bias=bsrc.ap[:, bb:bb + 1]),
                           reads=[pg, wsrc, bsrc], writes=[c_])
                        for tp in (0, 2):
                            op("dve", lambda e, c_=c_, pg=pg, wsrc=wsrc, wb=wb, tp=tp, n=n: e.scalar_tensor_tensor(
                                out=c_.ap[:, 0:n], in0=pg.ap[:, tp:tp + n], scalar=wsrc.ap[:, wb + tp:wb + tp + 1],
                                in1=c_.ap[:, 0:n], op0=ALU.mult, op1=ALU.add), reads=[pg, wsrc, c_], writes=[c_])
                        cv.append(c_)
                    sg = SCRE()
                    op("act", lambda e, sg=sg, g_=cv[0], n=n: e.activation(out=sg.ap[:, 0:n], in_=g_.ap[:, 0:n], func=AF.Tanh, scale=0.5),
                       reads=[cv[0]], writes=[sg])
                    op("act", lambda e, sg=sg, n=n: e.activation(out=sg.ap[:, 0:n], in_=sg.ap[:, 0:n], func=AF.Identity, scale=0.5,
                                                                bias=POSH.ap[:, 0:1]), reads=[sg, POSH], writes=[sg])
                    op("dve", lambda e, g_=cv[0], u_=cv[1], n=n: e.tensor_tensor(out=u_.ap[:, 0:n], in0=g_.ap[:, 0:n], in1=u_.ap[:, 0:n],
                                                                                op=ALU.mult), reads=[cv[0], cv[1]], writes=[cv[1]])
                    op("dve", lambda e, sg=sg, u_=cv[1], jj=jj, t0=t0, n=n: e.tensor_tensor(
                        out=at3[:, jj, t0:t0 + n], in0=sg.ap[:, 0:n], in1=u_.ap[:, 0:n], op=ALU.mult),
                       reads=[sg, cv[1]], writes=[at])
            for ti in range(NTL):
                o0, rows = tiles[ti]
                xt = XT_[ti]
                for half in range(2):
                    pb = PS()

                    def mmd(e, pb=pb, o0=o0, half=half):
                        ins = None
                        for jj in range(GP):
                            ins = e.matmul(pb.ap[:, :], lhsT=at3[:, jj, o0:o0 + 128], rhs=wd3[:, jj, half * 512:(half + 1) * 512],
                                           start=(jj == 0), stop=(jj == GP - 1))
                        return ins
                    op("pe", mmd, reads=[at, wd], writes=[pb])
                    op("dve", lambda e, pb=pb, xt=xt, half=half: e.tensor_tensor(
                        out=xt.ap[:, half * 512:(half + 1) * 512], in0=xt.ap[:, half * 512:(half + 1) * 512],
                        in1=pb.ap[:, :], op=ALU.add), reads=[xt, pb], writes=[xt])
        for r_ in range(NR):
            do_round(r_)
        for ti in range(NTL):
            store(out_d[ti * 128:(ti + 1) * 128, :], XT_[ti], XT_[ti].ap)
        S.emit(nc)
    return nc


def _cols(v, rows=128):
    v = np.asarray(v, np.float32)
    return np.ascontiguousarray(v.reshape(-1, rows).T)


def prep_core(inp, b, c, CH):
    f32 = np.float32
    x = np.asarray(inp["x"][b], f32)
    pos = np.asarray(inp["positions"][b]).astype(np.int32)
    s0 = c * CH
    slots = [(k, False) for k in range(c)] + [(k, True) for k in range(3, c, -1)] + [(c, False), (c, True)]
    assert len(slots) == 5
    xs = np.stack([x[k * CH:(k + 1) * CH][::-1] if rev else x[k * CH:(k + 1) * CH] for k, rev in slots])
    pk = np.stack([pos[k * CH:(k + 1) * CH][::-1] if rev else pos[k * CH:(k + 1) * CH] for k, rev in slots[:4]])
    posk = np.ascontiguousarray(np.broadcast_to(pk[:, None, :], (4, 32, CH))).astype(np.int32)
    xh = np.zeros((2, D), f32)
    ph = np.zeros((2,), np.int32)
    msk = np.zeros((2,), f32)
    if c < 3:
        xh[0] = x[s0 + CH]; ph[0] = pos[s0 + CH]; msk[0] = 1.0
    if c > 0:
        xh[1] = x[s0 - 1]; ph[1] = pos[s0 - 1]; msk[1] = 1.0
    po = np.concatenate([pos[s0:s0 + CH], ph])
    poso = np.ascontiguousarray(np.broadcast_to(po[None, :], (32, CH + 2))).astype(np.int32)
    flg = np.zeros((5, 4), f32)
    for k in range(3):
        if k < c:
            flg[k] = [1, 0, 1, 0]
        else:
            flg[k] = [0, 1, 0, 1]
    flg[3] = [1, 0, 0, 0]
    flg[4] = [0, 1, 0, 0]
    pp = np.zeros((128, PPL["_n"]), f32)

    def put(name, arr):
        o, n = PPL[name]
        arr = np.asarray(arr, f32)
        if arr.ndim == 1:
            arr = np.broadcast_to(arr[None, :], (128, arr.shape[0]))
        assert arr.shape[1] == n, (name, arr.shape, n)
        pp[:arr.shape[0], o:o + n] = arr
    put("flg", flg.reshape(-1))
    put("msk", msk)
    cw = np.zeros((128, 5, 4, 4), f32)
    cb = np.zeros((128, 5, 4), f32); ba = np.zeros((128, 5, 4), f32); bi = np.zeros((128, 5, 4), f32); lam = np.zeros((128, 5, 4), f32)
    wa = np.zeros((5, 4, 128, 128), f32); wi = np.zeros((5, 4, 128, 128), f32)
    for k, (ck, rev) in enumerate(slots):
        d = 1 if rev else 0
        w = np.asarray(inp["lru_conv_w"][0, d], f32)
        if rev:
            w = w[::-1]
        for tap in range(4):
            cw[:, k, :, tap] = _cols(w[tap])
        cb[:, k, :] = _cols(inp["lru_conv_b"][0, d])
        ba[:, k, :] = _cols(inp["lru_b_a"][0, d])
        bi[:, k, :] = _cols(inp["lru_b_i"][0, d])
        lam[:, k, :] = _cols(inp["lru_lambda"][0, d])
        for ct in range(4):
            for half in range(2):
                blk = 2 * ct + half
                wa[k, ct, half * 64:(half + 1) * 64, half * 64:(half + 1) * 64] = inp["lru_w_a"][0, d, blk]
                wi[k, ct, half * 64:(half + 1) * 64, half * 64:(half + 1) * 64] = inp["lru_w_i"][0, d, blk]
    put("cw", cw.reshape(128, -1)); put("cb", cb.reshape(128, -1)); put("ba", ba.reshape(128, -1))
    put("bi", bi.reshape(128, -1)); put("lam", lam.reshape(128, -1))
    put("gqa", _cols(inp["q_a_norm"][0]))
    put("gkv", _cols(inp["kv_a_norm"][0]))

    def rotperm(g):
        g = np.asarray(g, f32)
        r = g.copy()
        r[64:80] = g[80:96]
        r[80:96] = g[64:80]
        return r
    gq = np.asarray(inp["mla_q_norm"][0], f32); gk = np.asarray(inp["mla_k_norm"][0], f32)
    for name, v in (("gq", gq), ("gqr", rotperm(gq)), ("gk", gk), ("gkr", rotperm(gk))):
        a = np.zeros((128, 1), f32); a[:96, 0] = v
        put(name, a)
    put("glru", _cols(inp["lru_out_norm"][0]))
    gm = np.zeros((128, 8), f32); gm[:64] = _cols(inp["mla_out_norm"][0], 64)
    put("gmla", gm)
    put("gmq", _cols(inp["mem_q_norm"][0])); put("gmk", _cols(inp["mem_k_norm"][0]))
    fw = np.zeros((128, 44, 3), f32)
    fcw = np.asarray(inp["ffn_conv_w"][0], f32)
    for tap in range(3):
        fw[:, :, tap] = _cols(fcw[tap])
    put("fw", fw.reshape(128, -1))
    put("fb", _cols(inp["ffn_conv_b"][0]))
    invf = np.zeros((128, 1), f32)
    inv = (10000.0 ** (-np.arange(0, 32, 2, dtype=np.float64) / 32.0)) / (2.0 * np.pi)
    for p in range(64, 96):
        invf[p, 0] = inv[(p - 64) % 16]
    put("invf", invf)
    gvec = np.stack([inp["attn_norm"][0], inp["mem_attn_norm"][0], inp["mem_norm"][0], inp["ffn_norm"][0]]).astype(f32)
    m = {
        "xs": np.ascontiguousarray(xs), "xh": xh, "posk": posk, "poso": poso, "pp": pp, "wa": wa, "wi": wi,
        "mem": np.ascontiguousarray(np.asarray(inp["mem"][b], f32)), "gvec": np.ascontiguousarray(gvec),
        "w_in": np.ascontiguousarray(inp["w_in"][0], dtype=f32), "w_uq": np.ascontiguousarray(inp["w_uq"][0], dtype=f32),
        "w_ukv": np.ascontiguousarray(inp["w_ukv"][0], dtype=f32), "w_out": np.ascontiguousarray(inp["w_out"][0], dtype=f32),
        "w_mem_q": np.ascontiguousarray(inp["w_mem_q"][0], dtype=f32),
        "w_mem_kv": np.ascontiguousarray(inp["w_mem_kv"][0], dtype=f32),
        "w_mem_o": np.ascontiguousarray(inp["w_mem_o"][0], dtype=f32),
        "w_up": np.ascontiguousarray(inp["w_up"][0], dtype=f32), "w_down": np.ascontiguousarray(inp["w_down"][0], dtype=f32),
    }
    return m


_NC_CACHE = {}


def run(inputs, dbg=None, cores=None, stop=None):
    inputs = {k: np.asarray(v) for k, v in inputs.items()}
    B, SEQ, _ = inputs["x"].shape
    CH = SEQ // 4
    key = (CH, repr(dbg), stop)
    if key not in _NC_CACHE:
        _NC_CACHE[key] = build(CH, dbg, stop)
    nc = _NC_CACHE[key]
    core_list = cores if cores is not None else [(b, c) for b in range(B) for c in range(4)]
    in_maps = [prep_core(inputs, b, c, CH) for (b, c) in core_list]
    res = run_bass_kernel_spmd(nc, in_maps, core_ids=list(range(len(core_list))), trace=bool(os.environ.get('KTRACE')))
    return res, core_list, CH


def kernel(**inputs):
    res, core_list, CH = run(inputs)
    B, SEQ, _ = np.asarray(inputs["x"]).shape
    out = np.zeros((B, SEQ, D), np.float32)
    for (b, c), r in zip(core_list, res.results):
        out[b, c * CH:(c + 1) * CH] = r["out"]
    return out
```

```python
import numpy as np
from contextlib import ExitStack
import concourse.bass as bass
import concourse.mybir as mybir
from concourse.bass_utils import run_bass_kernel_spmd

F32 = mybir.dt.float32
BF16 = mybir.dt.bfloat16
I32 = mybir.dt.int32
AF = mybir.ActivationFunctionType
ALU = mybir.AluOpType

import os
SAME_ENGINE_SYNC = os.environ.get('SAME_ENGINE_SYNC', '1') == '1'
EPS = 1e-6
D = 1024
DFF = 2816
NCH = 44
MEM = 256


class Res:
    __slots__ = ("name", "w", "r")

    def __init__(self, name=""):
        self.name = name
        self.w = None
        self.r = {}


class Sched:
    ENGS = ("pe", "act", "dve", "pool", "sp")

    def __init__(self):
        self.ops = {e: [] for e in self.ENGS}
        self.cnt = {e: 0 for e in self.ENGS}
        self.dcnt = {}
        self.seen = {e: {} for e in self.ENGS}

    def _need(self, eng, tok, waits):
        if tok is None:
            return
        key, val = tok
        if self.seen[eng].get(key, 0) >= val:
            return
        if val > waits.get(key, 0):
            waits[key] = val

    def op(self, eng, fn, reads=(), writes=(), dma=None):
        waits = {}
        for r in reads:
            self._need(eng, r.w, waits)
        for w in writes:
            self._need(eng, w.w, waits)
            for k, v in w.r.items():
                self._need(eng, (k, v), waits)
        if not SAME_ENGINE_SYNC:
            waits.pop(eng, None)
        for k, v in waits.items():
            self.seen[eng][k] = v
        if dma is None:
            self.cnt[eng] += 1
            tok = (eng, self.cnt[eng])
        else:
            self.dcnt[dma] = self.dcnt.get(dma, 0) + 16
            tok = (dma, self.dcnt[dma])
        self.ops[eng].append((list(waits.items()), fn, tok))
        for r in reads:
            if r.r.get(tok[0], 0) < tok[1]:
                r.r[tok[0]] = tok[1]
        for w in writes:
            w.w = tok
            w.r = {}
        return tok

    def barrier_all(self):
        for e in self.ENGS:
            waits = {}
            for e2 in self.ENGS:
                if e2 != e and self.cnt[e2] > self.seen[e].get(e2, 0):
                    waits[e2] = self.cnt[e2]
            for k, v in self.dcnt.items():
                if v > self.seen[e].get(k, 0):
                    waits[k] = v
            for k, v in waits.items():
                self.seen[e][k] = v
            if waits:
                self.ops[e].append((list(waits.items()), None, None))

    def emit(self, nc):
        keys = list(self.ENGS) + sorted(self.dcnt.keys())
        with ExitStack() as st:
            sems = {k: st.enter_context(nc.semaphore("s_" + k)) for k in keys}
            block = st.enter_context(nc.Block())
            engmap = {"pe": block.tensor, "act": block.scalar, "dve": block.vector,
                      "pool": block.gpsimd, "sp": block.sync}
            fin = {}
            for k in keys:
                v = self.cnt[k] if k in self.cnt else self.dcnt[k]
                if v > 0:
                    fin[k] = v
            self.ops["sp"].append((list(fin.items()), None, None))
            for e in self.ENGS:
                ops = self.ops[e]

                def body(engobj, ops=ops, e=e):
                    for waits, fn, tok in ops:
                        for k, v in waits:
                            engobj.wait_ge(sems[k], v)
                        if fn is None:
                            continue
                        ins = fn(engobj)
                        if tok[0] == e:
                            ins.then_inc(sems[e], 1)
                        else:
                            ins.then_inc(sems[tok[0]], 16)
                engmap[e](body)


class Buf:
    __slots__ = ("ap", "r")

    def __init__(self, ap, r):
        self.ap = ap
        self.r = r


class Arena:
    def __init__(self, t, size):
        self.t = t
        self.size = size
        self.regs = [[0, size]]

    def _take(self, n, name):
        for r in self.regs:
            if r[1] - r[0] >= n:
                o = r[0]
                r[0] += n
                return o
        raise AssertionError(("SBUF arena overflow", name, n, self.regs))

    def f32(self, n, name=""):
        o = self._take(n, name)
        return Buf(self.t[:, o:o + n], Res(name))

    def bf(self, n, name=""):
        m = (n + 1) // 2
        o = self._take(m, name)
        return Buf(self.t[:, o:o + m].bitcast(BF16)[:, 0:n], Res(name))

    def mark(self):
        return [list(r) for r in self.regs]

    def release(self, m):
        self.regs = [list(r) for r in m]

    def top(self):
        return self.regs[0][0]


def ffn_windows(CH):
    nw = -(-CH // 510)
    if CH % 512 == 0 and CH >= 512:
        nw = max(nw, 1)
    base = CH // nw
    rem = CH - base * nw
    sizes = [base + (1 if i < rem else 0) for i in range(nw)]
    starts = [sum(sizes[:i]) for i in range(nw)]
    return list(zip(starts, sizes))


def pp_layout():
    o = {}
    c = 0

    def add(name, n):
        nonlocal c
        o[name] = (c, n)
        c += n
    add("flg", 20)
    add("msk", 2)
    add("cw", 80)
    add("cb", 20)
    add("ba", 20)
    add("bi", 20)
    add("lam", 20)
    add("gqa", 2)
    add("gkv", 1)
    add("gq", 1)
    add("gqr", 1)
    add("gk", 1)
    add("gkr", 1)
    add("glru", 4)
    add("gmla", 8)
    add("gmq", 1)
    add("gmk", 1)
    add("fw", 132)
    add("fb", 44)
    add("invf", 1)
    o["_n"] = c
    return o


PPL = pp_layout()


class _Stop(Exception):
    pass


def build(CH, dbg=None, stop=None):
    holder = {}
    try:
        return _build(CH, dbg, stop, holder)
    except _Stop:
        return holder['nc']


def _build(CH, dbg, stop, holder):
    NB = CH // 512
    NKEY = 4 * CH
    NKT = NKEY // 128
    NKB = NKEY // 512
    NOWN = CH + 2
    WINS = ffn_windows(CH)
    nc = bass.Bass("TRN2", target_bir_lowering=False)
    holder["nc"] = nc

    def din(name, shape, dt=F32):
        return nc.dram_tensor(name, list(shape), dt, kind="ExternalInput").ap()

    xs = din("xs", [5, CH, D])
    xh = din("xh", [2, D])
    posk = din("posk", [4, 32, CH], I32)
    poso = din("poso", [32, NOWN], I32)
    pp_d = din("pp", [128, PPL["_n"]])
    wa_d = din("wa", [5, 4, 128, 128])
    wi_d = din("wi", [5, 4, 128, 128])
    mem_d = din("mem", [MEM, D])
    gvec = din("gvec", [4, D])
    w_in = din("w_in", [D, 1440])
    w_uq = din("w_uq", [256, 768])
    w_ukv = din("w_ukv", [128, 1024])
    w_out = din("w_out", [1024, D])
    w_mq = din("w_mem_q", [D, 512])
    w_mkv = din("w_mem_kv", [D, 1024])
    w_mo = din("w_mem_o", [512, D])
    w_up = din("w_up", [D, 2 * DFF])
    w_dn = din("w_down", [DFF, D])
    out_d = nc.dram_tensor("out", [CH, D], F32, kind="ExternalOutput").ap()
    dbg_d = {}
    if dbg:
        for name, shape in dbg.items():
            dbg_d[name] = nc.dram_tensor("dbg_" + name, list(shape), F32, kind="ExternalOutput").ap()

    S = Sched()
    with ExitStack() as st:
        ASZ = 52500
        arena_t = st.enter_context(nc.sbuf_tensor("arena", [128, ASZ], F32))
        A = Arena(arena_t, ASZ)
        psum_t = [st.enter_context(nc.psum_tensor("ps%d" % i, [128, 512], F32)) for i in range(8)]
        psum = [Buf(t[:, :], Res("ps%d" % i)) for i, t in enumerate(psum_t)]
        psi = [0]

        def PS():
            b = psum[psi[0] % 5]
            psi[0] += 1
            return b
        psa = [0]

        def PSACC():
            b = psum[6 + psa[0] % 2]
            psa[0] += 1
            return b

        def chk(tag):
            if stop == tag:
                S.emit(nc)
                raise _Stop()

        def op(eng, fn, reads=(), writes=(), dma=None):
            return S.op(eng, fn, [b.r for b in reads], [b.r for b in writes], dma)

        dkeys = {}

        def dkey(buf, pre="k"):
            k = (pre, id(buf.r))
            if k not in dkeys:
                dkeys[k] = "%s%02d" % (pre, len(dkeys))
            return dkeys[k]

        def load(dst, dst_ap, src_ap, eng=None, key=None):
            if eng is None:
                eng = "pool" if dst_ap.dtype != src_ap.dtype else "sp"
            op(eng, lambda e: e.dma_start(out=dst_ap, in_=src_ap), reads=[], writes=[dst], dma=dkey(dst))

        def store(dst_ap, buf, src_ap):
            op("sp", lambda e: e.dma_start(out=dst_ap, in_=src_ap), reads=[buf], dma=dkey(buf, "s"))

        def dump(name, buf, ap, rows=None):
            if name in dbg_d:
                store(dbg_d[name], buf, ap)

        PP = A.f32(PPL["_n"], "pp")
        load(PP, PP.ap, pp_d)

        def ppc(name, i=0, rows=slice(0, 128)):
            o, n = PPL[name]
            return PP.ap[rows, o + i:o + i + 1]

        CONST = A.f32(8, "const")
        op("pool", lambda e: e.memset(CONST.ap[:, 0:1], EPS), writes=[CONST])
        op("pool", lambda e: e.memset(CONST.ap[:, 1:2], 1.0), writes=[CONST])
        c_eps = CONST.ap[:, 0:1]
        c_one = CONST.ap[:, 1:2]
        NEGH = A.f32(8, "negh")
        POSH = A.f32(8, "posh")
        op("pool", lambda e: e.memset(NEGH.ap, -0.5), writes=[NEGH])
        op("pool", lambda e: e.memset(POSH.ap, 0.5), writes=[POSH])
        IDF = A.f32(128, "identf")
        IDENT = A.bf(128, "ident")
        ONESB = A.bf(128, "onesb")
        ONESF = A.f32(128, "onesf")
        op("pool", lambda e: e.iota(IDF.ap, [[1, 128]], base=0, channel_multiplier=-1,
                                    allow_small_or_imprecise_dtypes=True), writes=[IDF])
        op("dve", lambda e: e.tensor_scalar(out=IDENT.ap, in0=IDF.ap, scalar1=0.0, scalar2=None,
                                            op0=ALU.is_equal), reads=[IDF], writes=[IDENT])
        op("pool", lambda e: e.memset(ONESB.ap, 1.0), writes=[ONESB])
        op("pool", lambda e: e.memset(ONESF.ap, 1.0), writes=[ONESF])
        LP = A.f32(192, "lruparams")
        lam_o = PPL["lam"][0]
        flg_o = PPL["flg"][0]
        op("act", lambda e: e.activation(out=LP.ap[:, 0:20], in_=PP.ap[:, lam_o:lam_o + 20], func=AF.Exp, scale=-1.0),
           reads=[PP], writes=[LP])
        op("act", lambda e: e.activation(out=LP.ap[:, 0:20], in_=LP.ap[:, 0:20], func=AF.Ln, scale=1.0, bias=c_one),
           reads=[LP, CONST], writes=[LP])
        op("dve", lambda e: e.tensor_scalar(out=LP.ap[:, 20:40], in0=LP.ap[:, 0:20], scalar1=-4.0, scalar2=None,
                                            op0=ALU.mult), reads=[LP], writes=[LP])
        op("dve", lambda e: e.tensor_scalar(out=LP.ap[:, 0:20], in0=LP.ap[:, 0:20], scalar1=-8.0, scalar2=None,
                                            op0=ALU.mult), reads=[LP], writes=[LP])
        op("dve", lambda e: e.tensor_scalar(out=LP.ap[:, 40:60], in0=PP.ap[:, flg_o:flg_o + 20], scalar1=-1.0,
                                            scalar2=1.0, op0=ALU.mult, op1=ALU.add), reads=[PP], writes=[LP])

        ba_o = PPL["ba"][0]
        bi_o = PPL["bi"][0]
        fw_o = PPL["fw"][0]
        fb_o = PPL["fb"][0]
        op("dve", lambda e: e.tensor_scalar(out=LP.ap[:, 60:80], in0=PP.ap[:, ba_o:ba_o + 20], scalar1=0.5, scalar2=None,
                                            op0=ALU.mult), reads=[PP], writes=[LP])
        op("dve", lambda e: e.tensor_scalar(out=LP.ap[:, 80:100], in0=PP.ap[:, bi_o:bi_o + 20], scalar1=0.5, scalar2=None,
                                            op0=ALU.mult), reads=[PP], writes=[LP])
        op("dve", lambda e: e.tensor_scalar(out=LP.ap[:, 100:166], in0=PP.ap[:, fw_o + 66:fw_o + 132], scalar1=0.5, scalar2=None,
                                            op0=ALU.mult), reads=[PP], writes=[LP])
        op("dve", lambda e: e.tensor_scalar(out=LP.ap[:, 166:188], in0=PP.ap[:, fb_o + 22:fb_o + 44], scalar1=0.5, scalar2=None,
                                            op0=ALU.mult), reads=[PP], writes=[LP])

        def flg(k, i):
            return PP.ap[:, flg_o + 4 * k + i:flg_o + 4 * k + i + 1]

        def nflg(k, i):
            return LP.ap[:, 40 + 4 * k + i:40 + 4 * k + i + 1]

        GBC = A.f32(D, "gbc")

        NT_X = [A.f32(D, "xt%d" % i) for i in range(2)]
        NT_J = A.bf(D, "junk")
        NT_H = [A.bf(D, "hb%d" % i) for i in range(2)]
        NT_Sl = [A.f32(2, "nstat%d" % i) for i in range(2)]
        nt_i = [0]

        def norm_T(xbuf, x_ap, n, dstT, dstT_ap3, col0, evac_eng="dve", gb=None):
            gb = gb or GBC
            i = nt_i[0] % 2
            nt_i[0] += 1
            hb = NT_H[i]
            NT_S = NT_Sl[i]
            ss = NT_S.ap[0:n, 0:1]
            rs = NT_S.ap[0:n, 1:2]
            op("act", lambda e: e.activation(out=NT_J.ap[0:n], in_=x_ap, func=AF.Square, accum_out=ss),
               reads=[xbuf], writes=[NT_J, NT_S])
            op("dve", lambda e: e.tensor_scalar(out=rs, in0=ss, scalar1=1.0 / D, scalar2=EPS, op0=ALU.mult, op1=ALU.add),
               reads=[NT_S], writes=[NT_S])
            op("pool", lambda e: e.tensor_tensor(out=rs, in0=rs, in1=NEGH.ap[0:n, 0:1], op=ALU.pow),
               reads=[NT_S, NEGH], writes=[NT_S])
            op("dve", lambda e: e.scalar_tensor_tensor(out=hb.ap[0:n], in0=x_ap, scalar=rs, in1=gb.ap[0:n],
                                                       op0=ALU.mult, op1=ALU.mult),
               reads=[xbuf, NT_S, gb], writes=[hb])
            pb = PS()
            pbf = pb.ap.bitcast(BF16)

            def tr(e):
                ins = None
                for kc in range(8):
                    ins = e.transpose(pbf[:, kc * 128:kc * 128 + n], hb.ap[0:n, kc * 128:(kc + 1) * 128],
                                      IDENT.ap[0:n, 0:n])
                return ins
            op("pe", tr, reads=[hb, IDENT], writes=[pb])
            src = pbf.rearrange("p (k t) -> p k t", k=8)[:, :, 0:n]
            dst = dstT_ap3[:, :, col0:col0 + n]
            if evac_eng == "act":
                op("act", lambda e: e.activation(out=dst, in_=src, func=AF.Copy), reads=[pb], writes=[dstT])
            else:
                op("dve", lambda e: e.tensor_copy(out=dst, in_=src), reads=[pb], writes=[dstT])

        def fm_rstd(sq_list, rows_out, n, dim, SCR, pbuf=None):
            pb = pbuf if pbuf is not None else PS()

            def mm(e):
                ins = None
                for i, (b, ap, rk, base) in enumerate(sq_list):
                    ins = e.matmul(pb.ap[0:rows_out, 0:n], lhsT=ONESB.ap[base:base + rk, 0:rows_out], rhs=ap,
                                   start=(i == 0), stop=(i == len(sq_list) - 1))
                return ins
            op("pe", mm, reads=[b for b, _, _, _ in sq_list] + [ONESB], writes=[pb])
            rb = SCR()
            op("act", lambda e: e.activation(out=rb.ap[0:rows_out, 0:n], in_=pb.ap[0:rows_out, 0:n], func=AF.Ln,
                                             scale=1.0 / dim, bias=c_eps[0:rows_out]),
               reads=[pb, CONST], writes=[rb])
            op("act", lambda e: e.activation(out=rb.ap[0:rows_out, 0:n], in_=rb.ap[0:rows_out, 0:n], func=AF.Exp,
                                             scale=-0.5), reads=[rb], writes=[rb])
            return rb

        POSB = [A.f32(512, "posb%d" % i) for i in range(2)]
        posb_i = [0]

        def rope_tables(pos_src_ap, n, SCR, out_buf=None, out_s=None, out_c=None):
            pi = POSB[posb_i[0] % 2]
            posb_i[0] += 1
            pi_ap = pi.ap.bitcast(I32)
            load(pi, pi_ap[64:96, 0:n], pos_src_ap, eng="sp")
            y = SCR()
            op("dve", lambda e: e.tensor_copy(out=y.ap[64:96, 0:n], in_=pi_ap[64:96, 0:n]), reads=[pi], writes=[y])
            y2 = SCR()
            yc = SCR()
            op("dve", lambda e: e.tensor_scalar(out=y2.ap[64:96, 0:n], in0=y.ap[64:96, 0:n],
                                                scalar1=ppc("invf", 0, slice(64, 96)), scalar2=None, op0=ALU.mult),
               reads=[y, PP], writes=[y2])
            op("dve", lambda e: e.tensor_scalar(out=yc.ap[64:96, 0:n], in0=y2.ap[64:96, 0:n], scalar1=0.25,
                                                scalar2=None, op0=ALU.add), reads=[y2], writes=[yc])
            res = []
            for yy, dst in ((y2, out_s), (yc, out_c)):
                ti = SCR()
                ti_ap = ti.ap.bitcast(I32)
                op("dve", lambda e, yy=yy, ti_ap=ti_ap: e.tensor_copy(out=ti_ap[64:96, 0:n], in_=yy.ap[64:96, 0:n]),
                   reads=[yy], writes=[ti])
                tf = SCR()
                op("dve", lambda e, ti_ap=ti_ap, tf=tf: e.tensor_copy(out=tf.ap[64:96, 0:n], in_=ti_ap[64:96, 0:n]),
                   reads=[ti], writes=[tf])
                op("dve", lambda e, yy=yy, tf=tf: e.tensor_tensor(out=yy.ap[64:96, 0:n], in0=yy.ap[64:96, 0:n],
                                                                  in1=tf.ap[64:96, 0:n], op=ALU.subtract),
                   reads=[yy, tf], writes=[yy])
                if dst is None:
                    o_b, o_ap = yy, yy.ap[64:96, 0:n]
                else:
                    o_b, o_ap = out_buf, dst
                op("act", lambda e, yy=yy, o_ap=o_ap: e.activation(out=o_ap, in_=yy.ap[64:96, 0:n], func=AF.Sin,
                                                                   scale=2.0 * np.pi * 0.999999),
                   reads=[yy], writes=[o_b])
                res.append((o_b, o_ap))
            return res

        KM = A.bf(4 * MEM, "km")
        KM3 = KM.ap.rearrange("p (h t) -> p h t", h=4)
        VM = A.bf(2 * 512, "vm")
        VM3 = VM.ap.rearrange("p (t c) -> p t c", t=2)
        m0 = A.mark()
        scr0 = [A.f32(512, "scr0%d" % i) for i in range(4)]
        s0i = [0]

        def SCRD():
            b = scr0[s0i[0] % len(scr0)]
            s0i[0] += 1
            return b
        if stop == '0a':
            S.emit(nc)
            return nc
        load(GBC, GBC.ap, gvec[2, :].partition_broadcast(128))
        mW = A.mark()
        WMKV = A.bf(8 * 1024, "wmkv")
        WMKV3 = WMKV.ap.rearrange("p (k c) -> p k c", k=8)
        load(WMKV, WMKV3, w_mkv.rearrange("(k p) c -> p k c", p=128))
        MEMT = A.bf(8 * MEM, "memt")
        MEMT3 = MEMT.ap.rearrange("p (k t) -> p k t", k=8)
        for mt in range(2):
            xb = NT_X[nt_i[0] % 2]
            load(xb, xb.ap, mem_d[mt * 128:(mt + 1) * 128, :], eng="sp")
            norm_T(xb, xb.ap, 128, MEMT, MEMT3, mt * 128)
        if stop == '0b':
            S.emit(nc)
            return nc
        for h in range(4):
            pk = PS()

            def mmk(e, pk=pk, h=h):
                ins = None
                for kc in range(8):
                    ins = e.matmul(pk.ap[:, 0:MEM], lhsT=WMKV3[:, kc, h * 128:(h + 1) * 128], rhs=MEMT3[:, kc, :],
                                   start=(kc == 0), stop=(kc == 7))
                return ins
            op("pe", mmk, reads=[WMKV, MEMT], writes=[pk])
            sq = SCRD()
            sq_ap = sq.ap.bitcast(BF16)
            op("act", lambda e, sq_ap=sq_ap, pk=pk: e.activation(out=sq_ap[:, 0:MEM], in_=pk.ap[:, 0:MEM], func=AF.Square),
               reads=[pk], writes=[sq])
            rb = fm_rstd([(sq, sq_ap[:, 0:MEM], 128, 0)], 128, MEM, 128.0, SCRD)
            op("dve", lambda e, pk=pk, rb=rb, h=h: e.scalar_tensor_tensor(
                out=KM3[:, h, :], in0=pk.ap[:, 0:MEM], scalar=ppc("gmk"), in1=rb.ap[:, 0:MEM], op0=ALU.mult, op1=ALU.mult),
               reads=[pk, rb, PP], writes=[KM])
        if stop == '0c':
            S.emit(nc)
            return nc
        for mt in range(2):
            pv = PS()

            def mmv2(e, pv=pv, mt=mt):
                ins = None
                for kc in range(8):
                    ins = e.matmul(pv.ap[:, :], lhsT=MEMT3[:, kc, mt * 128:(mt + 1) * 128], rhs=WMKV3[:, kc, 512:1024],
                                   start=(kc == 0), stop=(kc == 7))
                return ins
            op("pe", mmv2, reads=[WMKV, MEMT], writes=[pv])
            op("act", lambda e, pv=pv, mt=mt: e.activation(out=VM3[:, mt, :], in_=pv.ap[:, :], func=AF.Copy),
               reads=[pv], writes=[VM])
        if stop == '0d':
            S.emit(nc)
            return nc
        S.barrier_all()
        A.release(m0)
        mP = A.top()
        if stop == '0':
            S.emit(nc)
            return nc

        MIXL = A.bf(4 * NOWN, "mixl")
        MIXL3 = MIXL.ap.rearrange("p (c t) -> p c t", c=4)
        CQN = A.bf(2 * NOWN, "cqn")
        CQN3 = CQN.ap.rearrange("p (c t) -> p c t", c=2)
        ckvn_start = A.top()
        CKVN = A.bf(NKEY, "ckvn")
        ETS = A.bf(NKEY, "e_tsq")
        ets_end = A.top()
        mA = A.mark()

        WIN = A.bf(8 * 1440, "win")
        WIN3 = WIN.ap.rearrange("p (k c) -> p k c", k=8)
        for kc in range(8):
            load(WIN, WIN3[:, kc, :], w_in[kc * 128:(kc + 1) * 128, :])
        WKR = A.bf(8 * 192, "wkrpad")
        WKR3 = WKR.ap.rearrange("p (k c) -> p k c", k=8)
        op("pool", lambda e: e.memset(WKR.ap, 0.0), writes=[WKR])
        op("dve", lambda e: e.tensor_copy(out=WKR3[:, :, 64:96], in_=WIN3[:, :, 1408:1440]), reads=[WIN], writes=[WKR])
        op("dve", lambda e: e.tensor_scalar(out=WKR3[:, :, 160:176], in0=WIN3[:, :, 1424:1440], scalar1=-1.0,
                                            scalar2=None, op0=ALU.mult), reads=[WIN], writes=[WKR])
        op("dve", lambda e: e.tensor_copy(out=WKR3[:, :, 176:192], in_=WIN3[:, :, 1408:1424]), reads=[WIN], writes=[WKR])
        WGL = [A.bf(4 * 2 * 128, "wgates%d" % i) for i in range(2)]
        WGL4 = [w.ap.rearrange("p (c g o) -> p c g o", c=4, g=2) for w in WGL]
        HTL = [A.bf(8 * 512, "ht%d" % i) for i in range(2)]
        HTL3 = [h_.ap.rearrange("p (k t) -> p k t", k=8) for h_ in HTL]
        XR = A.f32(4 * 515, "xr")
        XR3 = XR.ap.rearrange("p (c t) -> p c t", c=4)
        HF = A.bf(4 * (CH + 1), "hf")
        HF3 = HF.ap.rearrange("p (c t) -> p c t", c=4)
        GG = A.bf(4 * NOWN, "gelu")
        GG3 = GG.ap.rearrange("p (c t) -> p c t", c=4)
        CAR = A.f32(64, "carry")
        op("pool", lambda e: e.memset(CAR.ap, 0.0), writes=[CAR])
        HSC = A.f32(4 * 512, "hscan")
        HSC3 = HSC.ap.rearrange("p (c t) -> p c t", c=4)
        HE = A.f32(16, "hextra")
        scrA = [A.f32(512, "scrA%d" % i) for i in range(12)]
        sai = [0]

        def SCRA():
            b = scrA[sai[0] % len(scrA)]
            sai[0] += 1
            return b

        load(GBC, GBC.ap, gvec[0, :].partition_broadcast(128))

        chk('A0')

        def lru_cols(k, ct, name):
            o, n = PPL[name]
            return PP.ap[:, o + 4 * k + ct:o + 4 * k + ct + 1]

        def phaseA_front(k, j, n, mini, hb):
            HT = HTL[hb]
            HT3 = HTL3[hb]
            if j == 0 and not mini:
                load(WGL[k % 2], WGL4[k % 2][:, :, 0, :], wa_d[k].rearrange("c i o -> i c o"))
                load(WGL[k % 2], WGL4[k % 2][:, :, 1, :], wi_d[k].rearrange("c i o -> i c o"))
            ntile = (n + 127) // 128
            for i in range(ntile):
                rows = min(128, n - i * 128)
                xb = NT_X[nt_i[0] % 2]
                if mini:
                    src = xh[0:1, :] if k == 3 else xh[1:2, :]
                else:
                    src = xs[k, j * 512 + i * 128:j * 512 + i * 128 + rows, :]
                load(xb, xb.ap[0:rows], src, eng="sp")
                norm_T(xb, xb.ap[0:rows], rows, HT, HT3, i * 128, evac_eng="dve")

        def phaseA_block(k, j, n, mini, hb):
            HT = HTL[hb]
            HT3 = HTL3[hb]
            WG = WGL[k % 2]
            WG4 = WGL4[k % 2]
            do_kv = (k <= 3) and not mini
            do_own = (k == 3) or (k == 4 and mini)
            if mini:
                own0 = CH if k == 3 else CH + 1
            else:
                own0 = j * 512
            if j == 0 and not mini:
                op("dve", lambda e: e.tensor_scalar(out=CAR.ap[:, 36:48], in0=CAR.ap[:, 8:20], scalar1=flg(k, 0),
                                                    scalar2=None, op0=ALU.mult), reads=[CAR, PP], writes=[CAR])
                op("dve", lambda e: e.scalar_tensor_tensor(out=CAR.ap[:, 36:48], in0=CAR.ap[:, 20:32], scalar=flg(k, 1),
                                                           in1=CAR.ap[:, 36:48], op0=ALU.mult, op1=ALU.add),
                   reads=[CAR, PP], writes=[CAR])
                op("dve", lambda e: e.tensor_scalar(out=CAR.ap[:, 32:36], in0=CAR.ap[:, 0:4], scalar1=flg(k, 0),
                                                    scalar2=None, op0=ALU.mult), reads=[CAR, PP], writes=[CAR])
                op("dve", lambda e: e.scalar_tensor_tensor(out=CAR.ap[:, 32:36], in0=CAR.ap[:, 4:8], scalar=flg(k, 1),
                                                           in1=CAR.ap[:, 32:36], op0=ALU.mult, op1=ALU.add),
                   reads=[CAR, PP], writes=[CAR])
                if k == 3:
                    op("dve", lambda e: e.tensor_copy(out=CAR.ap[:, 48:52], in_=CAR.ap[:, 32:36]), reads=[CAR], writes=[CAR])
                if k == 4:
                    op("dve", lambda e: e.tensor_copy(out=CAR.ap[:, 52:56], in_=CAR.ap[:, 32:36]), reads=[CAR], writes=[CAR])
                op("dve", lambda e: e.tensor_copy(out=XR3[:, :, 0:3],
                                                  in_=CAR.ap[:, 36:48].rearrange("p (c t) -> p c t", c=4)),
                   reads=[CAR], writes=[XR])
            else:
                op("dve", lambda e: e.tensor_copy(out=XR3[:, :, 0:3], in_=XR3[:, :, 512:515]), reads=[XR], writes=[XR])
            for ct in range(4):
                pb = PS()

                def mm(e, pb=pb, ct=ct):
                    ins = None
                    for kc in range(8):
                        ins = e.matmul(pb.ap[:, 0:n], lhsT=WIN3[:, kc, ct * 128:(ct + 1) * 128], rhs=HT3[:, kc, 0:n],
                                       start=(kc == 0), stop=(kc == 7))
                    return ins
                op("pe", mm, reads=[WIN, HT], writes=[pb])
                op("act", lambda e, pb=pb, ct=ct: e.activation(out=XR3[:, ct, 3:3 + n], in_=pb.ap[:, 0:n], func=AF.Copy),
                   reads=[pb], writes=[XR])
            chk('A1')
            for pr in range(2):
                cts = (2 * pr, 2 * pr + 1)
                B_ = {}
                for ct in cts:
                    cwo = PPL["cw"][0] + (k * 4 + ct) * 4
                    xc = SCRA()
                    op("act", lambda e, xc=xc, ct=ct, cwo=cwo: e.activation(
                        out=xc.ap[:, 0:n], in_=XR3[:, ct, 3:3 + n], func=AF.Identity,
                        scale=PP.ap[:, cwo + 3:cwo + 4], bias=lru_cols(k, ct, "cb")), reads=[XR, PP], writes=[xc])
                    for tp in range(3):
                        op("dve", lambda e, xc=xc, ct=ct, cwo=cwo, tp=tp: e.scalar_tensor_tensor(
                            out=xc.ap[:, 0:n], in0=XR3[:, ct, tp:tp + n], scalar=PP.ap[:, cwo + tp:cwo + tp + 1],
                            in1=xc.ap[:, 0:n], op0=ALU.mult, op1=ALU.add), reads=[XR, PP, xc], writes=[xc])
                    xcb = SCRA()
                    xcb_ap = xcb.ap.bitcast(BF16)
                    op("dve", lambda e, xc=xc, xcb_ap=xcb_ap: e.tensor_copy(out=xcb_ap[:, 0:n], in_=xc.ap[:, 0:n]),
                       reads=[xc], writes=[xcb])
                    B_[ct] = dict(xc=xc, xcb=xcb, xcb_ap=xcb_ap)
                for ct in cts:
                    d = B_[ct]
                    pa = PS()
                    op("pe", lambda e, pa=pa, ct=ct, xcb_ap=d["xcb_ap"]: e.matmul(
                        pa.ap[:, 0:n], lhsT=WG4[:, ct, 0, :], rhs=xcb_ap[:, 0:n], start=True, stop=True),
                       reads=[WG, d["xcb"]], writes=[pa])
                    pi_ = PS()
                    op("pe", lambda e, pi_=pi_, ct=ct, xcb_ap=d["xcb_ap"]: e.matmul(
                        pi_.ap[:, 0:n], lhsT=WG4[:, ct, 1, :], rhs=xcb_ap[:, 0:n], start=True, stop=True),
                       reads=[WG, d["xcb"]], writes=[pi_])
                    d["pa"] = pa
                    d["pi"] = pi_
                for ct in cts:
                    d = B_[ct]
                    rr = SCRA()
                    ig = SCRA()
                    op("act", lambda e, rr=rr, pa=d["pa"], ct=ct: e.activation(out=rr.ap[:, 0:n], in_=pa.ap[:, 0:n], func=AF.Tanh,
                                                                          bias=LP.ap[:, 60 + 4 * k + ct:61 + 4 * k + ct], scale=0.5),
                       reads=[d["pa"], LP], writes=[rr])
                    op("act", lambda e, ig=ig, pi_=d["pi"], ct=ct: e.activation(out=ig.ap[:, 0:n], in_=pi_.ap[:, 0:n], func=AF.Tanh,
                                                                            bias=LP.ap[:, 80 + 4 * k + ct:81 + 4 * k + ct], scale=0.5),
                       reads=[d["pi"], LP], writes=[ig])
                    d["rr"] = rr
                    d["ig"] = ig
                for ct in cts:
                    d = B_[ct]
                    aa = SCRA()
                    mm_ = SCRA()
                    cs = LP.ap[:, 4 * k + ct:4 * k + ct + 1]
                    hcs = LP.ap[:, 20 + 4 * k + ct:20 + 4 * k + ct + 1]
                    op("act", lambda e, aa=aa, rr=d["rr"], hcs=hcs: e.activation(out=aa.ap[:, 0:n], in_=rr.ap[:, 0:n], func=AF.Exp,
                                                                             scale=hcs, bias=hcs), reads=[d["rr"], LP], writes=[aa])
                    op("act", lambda e, mm_=mm_, rr=d["rr"], cs=cs: e.activation(out=mm_.ap[:, 0:n], in_=rr.ap[:, 0:n], func=AF.Exp,
                                                                             scale=cs, bias=cs), reads=[d["rr"], LP], writes=[mm_])
                    op("dve", lambda e, mm_=mm_: e.tensor_scalar(out=mm_.ap[:, 0:n], in0=mm_.ap[:, 0:n], scalar1=-0.25, scalar2=0.25,
                                                                 op0=ALU.mult, op1=ALU.add), reads=[mm_], writes=[mm_])
                    op("dve", lambda e, ig=d["ig"], xc=d["xc"]: e.scalar_tensor_tensor(out=ig.ap[:, 0:n], in0=ig.ap[:, 0:n], scalar=1.0,
                                                                                   in1=xc.ap[:, 0:n], op0=ALU.add, op1=ALU.mult),
                       reads=[d["ig"], d["xc"]], writes=[d["ig"]])
                    d["aa"] = aa
                    d["mm"] = mm_
                for ct in cts:
                    d = B_[ct]
                    op("act", lambda e, mm_=d["mm"]: e.activation(out=mm_.ap[:, 0:n], in_=mm_.ap[:, 0:n], func=AF.Sqrt),
                       reads=[d["mm"]], writes=[d["mm"]])
                for ct in cts:
                    d = B_[ct]
                    op("dve", lambda e, ig=d["ig"], mm_=d["mm"]: e.tensor_tensor(out=ig.ap[:, 0:n], in0=ig.ap[:, 0:n], in1=mm_.ap[:, 0:n],
                                                                              op=ALU.mult), reads=[d["ig"], d["mm"]], writes=[d["ig"]])
                    if mini or j > 0:
                        init = CAR.ap[:, 56 + ct:57 + ct]
                    else:
                        init = CAR.ap[:, 32 + ct:33 + ct]
                    op("dve", lambda e, aa=d["aa"], ig=d["ig"], ct=ct, init=init: e.tensor_tensor_scan(
                        out=HSC3[:, ct, 0:n], data0=aa.ap[:, 0:n], data1=ig.ap[:, 0:n], initial=init,
                        op0=ALU.mult, op1=ALU.add), reads=[d["aa"], d["ig"], CAR], writes=[HSC])
            if not mini:
                op("dve", lambda e: e.tensor_copy(out=CAR.ap[:, 56:60], in_=HSC3[:, :, n - 1]), reads=[HSC], writes=[CAR])
            chk('A2')
            if k == 3:
                op("dve", lambda e: e.tensor_copy(out=HF3[:, :, own0 if not mini else CH:(own0 if not mini else CH) + n],
                                                   in_=HSC3[:, :, 0:n]), reads=[HSC], writes=[HF])
            if k <= 2 and (not mini) and j == NB - 1:
                for (st_o, u_i) in ((0, 2), (4, 3)):
                    op("dve", lambda e, st_o=st_o, u_i=u_i: e.tensor_scalar(
                        out=CAR.ap[:, st_o:st_o + 4], in0=CAR.ap[:, st_o:st_o + 4], scalar1=nflg(k, u_i), scalar2=None,
                        op0=ALU.mult), reads=[CAR, LP], writes=[CAR])
                    op("dve", lambda e, st_o=st_o, u_i=u_i: e.scalar_tensor_tensor(
                        out=CAR.ap[:, st_o:st_o + 4], in0=HSC3[:, :, n - 1], scalar=flg(k, u_i), in1=CAR.ap[:, st_o:st_o + 4],
                        op0=ALU.mult, op1=ALU.add), reads=[CAR, HSC, PP], writes=[CAR])
                for (h_o, u_i) in ((8, 2), (20, 3)):
                    hv = CAR.ap[:, h_o:h_o + 12].rearrange("p (c t) -> p c t", c=4)
                    op("dve", lambda e, hv=hv, u_i=u_i: e.tensor_scalar(
                        out=hv, in0=hv, scalar1=nflg(k, u_i), scalar2=None, op0=ALU.mult), reads=[CAR, LP], writes=[CAR])
                    op("dve", lambda e, hv=hv, u_i=u_i: e.scalar_tensor_tensor(
                        out=hv, in0=XR3[:, :, 512:515], scalar=flg(k, u_i), in1=hv, op0=ALU.mult, op1=ALU.add),
                       reads=[CAR, XR, PP], writes=[CAR])
            chk('A3')
            if do_kv:
                key0 = k * CH + j * 512
                pb = PS()

                def mmkv(e, pb=pb):
                    ins = None
                    for kc in range(8):
                        ins = e.matmul(pb.ap[:, 0:n], lhsT=WIN3[:, kc, 1280:1408], rhs=HT3[:, kc, 0:n],
                                       start=(kc == 0), stop=(kc == 7))
                    return ins
                op("pe", mmkv, reads=[WIN, HT], writes=[pb])
                cf = SCRA()
                sq = SCRA()
                sq_ap = sq.ap.bitcast(BF16)
                op("act", lambda e, cf=cf, pb=pb: e.activation(out=cf.ap[:, 0:n], in_=pb.ap[:, 0:n], func=AF.Copy),
                   reads=[pb], writes=[cf])
                op("act", lambda e, sq_ap=sq_ap, pb=pb: e.activation(out=sq_ap[:, 0:n], in_=pb.ap[:, 0:n], func=AF.Square),
                   reads=[pb], writes=[sq])
                rb = fm_rstd([(sq, sq_ap[:, 0:n], 128, 0)], 128, n, 128.0, SCRA)
                op("dve", lambda e, cf=cf, rb=rb: e.scalar_tensor_tensor(
                    out=CKVN.ap[:, key0:key0 + n], in0=cf.ap[:, 0:n], scalar=ppc("gkv"), in1=rb.ap[:, 0:n],
                    op0=ALU.mult, op1=ALU.mult), reads=[cf, rb, PP], writes=[CKVN])
                chk('A3a')
                pt = PS()
                prt = PS()

                def mmt(e, pt=pt, o=0):
                    ins = None
                    for kc in range(8):
                        ins = e.matmul(pt.ap[0:96, 0:n], lhsT=WKR3[:, kc, o:o + 96], rhs=HT3[:, kc, 0:n],
                                       start=(kc == 0), stop=(kc == 7))
                    return ins
                op("pe", lambda e: mmt(e, pt, 0), reads=[WKR, HT], writes=[pt])
                op("pe", lambda e: mmt(e, prt, 96), reads=[WKR, HT], writes=[prt])
                chk('A3b')
                (sb, s_ap), (cb_, c_ap) = rope_tables(posk[k, :, j * 512:j * 512 + n], n, SCRA)
                chk('A3c')
                tq = SCRA()
                tq_ap = tq.ap.bitcast(BF16)
                op("act", lambda e, pt=pt, tq_ap=tq_ap: e.activation(out=tq_ap[64:96, 0:n], in_=pt.ap[64:96, 0:n], func=AF.Square),
                   reads=[pt], writes=[tq])
                op("dve", lambda e, tq_ap=tq_ap: e.tensor_copy(out=ETS.ap[0:32, key0:key0 + n], in_=tq_ap[64:96, 0:n]),
                   reads=[tq], writes=[ETS])
                e1 = SCRA()
                e2 = SCRA()
                op("dve", lambda e, e1=e1, pt=pt, c_ap=c_ap: e.scalar_tensor_tensor(
                    out=e1.ap[64:96, 0:n], in0=pt.ap[64:96, 0:n], scalar=ppc("gk", 0, slice(64, 96)), in1=c_ap,
                    op0=ALU.mult, op1=ALU.mult), reads=[pt, cb_, PP, tq], writes=[e1])
                op("dve", lambda e, e2=e2, prt=prt, s_ap=s_ap: e.scalar_tensor_tensor(
                    out=e2.ap[64:96, 0:n], in0=prt.ap[64:96, 0:n], scalar=ppc("gkr", 0, slice(64, 96)), in1=s_ap,
                    op0=ALU.mult, op1=ALU.mult), reads=[prt, sb, PP], writes=[e2])
                op("dve", lambda e, e1=e1, e2=e2: e.tensor_tensor(out=ETS.ap[64:96, key0:key0 + n], in0=e1.ap[64:96, 0:n],
                                                                 in1=e2.ap[64:96, 0:n], op=ALU.add),
                   reads=[e1, e2], writes=[ETS])
            chk('A4')
            if do_own:
                for ct in range(4):
                    pb = PS()

                    def mmy(e, pb=pb, ct=ct):
                        ins = None
                        for kc in range(8):
                            ins = e.matmul(pb.ap[:, 0:n], lhsT=WIN3[:, kc, 512 + ct * 128:512 + (ct + 1) * 128],
                                           rhs=HT3[:, kc, 0:n], start=(kc == 0), stop=(kc == 7))
                        return ins
                    op("pe", mmy, reads=[WIN, HT], writes=[pb])
                    u = SCRA()
                    w = SCRA()
                    op("act", lambda e, u=u, pb=pb: e.activation(out=u.ap[:, 0:n], in_=pb.ap[:, 0:n], func=AF.Copy, scale=0.5),
                       reads=[pb], writes=[u])
                    op("act", lambda e, w=w, pb=pb: e.activation(out=w.ap[:, 0:n], in_=pb.ap[:, 0:n], func=AF.Square),
                       reads=[pb], writes=[w])
                    op("dve", lambda e, w=w: e.tensor_scalar(out=w.ap[:, 0:n], in0=w.ap[:, 0:n], scalar1=2.0 * 0.044715, scalar2=2.0,
                                                             op0=ALU.mult, op1=ALU.add), reads=[w], writes=[w])
                    op("dve", lambda e, w=w, u=u: e.tensor_tensor(out=w.ap[:, 0:n], in0=w.ap[:, 0:n], in1=u.ap[:, 0:n], op=ALU.mult),
                       reads=[w, u], writes=[w])
                    op("act", lambda e, w=w: e.activation(out=w.ap[:, 0:n], in_=w.ap[:, 0:n], func=AF.Tanh,
                                                          scale=0.7978845608028654), reads=[w], writes=[w])
                    op("dve", lambda e, w=w, u=u, ct=ct: e.scalar_tensor_tensor(out=GG3[:, ct, own0:own0 + n], in0=w.ap[:, 0:n],
                                                                             scalar=1.0, in1=u.ap[:, 0:n], op0=ALU.add, op1=ALU.mult),
                       reads=[w, u], writes=[GG])
                cfs = []
                sqs = []
                for c2 in range(2):
                    pb = PS()

                    def mmq(e, pb=pb, c2=c2):
                        ins = None
                        for kc in range(8):
                            ins = e.matmul(pb.ap[:, 0:n], lhsT=WIN3[:, kc, 1024 + c2 * 128:1024 + (c2 + 1) * 128],
                                           rhs=HT3[:, kc, 0:n], start=(kc == 0), stop=(kc == 7))
                        return ins
                    op("pe", mmq, reads=[WIN, HT], writes=[pb])
                    cf = SCRA()
                    sq = SCRA()
                    sq_ap = sq.ap.bitcast(BF16)
                    op("act", lambda e, cf=cf, pb=pb: e.activation(out=cf.ap[:, 0:n], in_=pb.ap[:, 0:n], func=AF.Copy),
                       reads=[pb], writes=[cf])
                    op("act", lambda e, sq_ap=sq_ap, pb=pb: e.activation(out=sq_ap[:, 0:n], in_=pb.ap[:, 0:n], func=AF.Square),
                       reads=[pb], writes=[sq])
                    cfs.append(cf)
                    sqs.append((sq, sq_ap[:, 0:n], 128, 0))
                rb = fm_rstd(sqs, 128, n, 256.0, SCRA)
                for c2 in range(2):
                    op("dve", lambda e, c2=c2, cf=cfs[c2], rb=rb: e.scalar_tensor_tensor(
                        out=CQN3[:, c2, own0:own0 + n], in0=cf.ap[:, 0:n], scalar=ppc("gqa", c2), in1=rb.ap[:, 0:n],
                        op0=ALU.mult, op1=ALU.mult), reads=[cf, rb, PP], writes=[CQN])
            if k == 4 and not mini:
                lo = CH - 512 * (j + 1)
                lru_combine(lambda ct: HF3[:, ct, lo:lo + n], HF, lambda ct: HSC3[:, ct, 0:n][:, ::-1], HSC, lo, n)

        def lru_combine(hf_ap, hf_buf, hb_ap, hb_buf, own0, n):
            los = []
            sqs = []
            for ct in range(4):
                lo_ = SCRA()
                op("dve", lambda e, lo_=lo_, ct=ct: e.tensor_tensor(out=lo_.ap[:, 0:n], in0=hf_ap(ct), in1=hb_ap(ct), op=ALU.add),
                   reads=[hf_buf, hb_buf], writes=[lo_])
                op("dve", lambda e, lo_=lo_, ct=ct: e.tensor_tensor(out=lo_.ap[:, 0:n], in0=lo_.ap[:, 0:n],
                                                                  in1=GG3[:, ct, own0:own0 + n], op=ALU.mult),
                   reads=[lo_, GG], writes=[lo_])
                sq = SCRA()
                sq_ap = sq.ap.bitcast(BF16)
                op("act", lambda e, sq_ap=sq_ap, lo_=lo_: e.activation(out=sq_ap[:, 0:n], in_=lo_.ap[:, 0:n], func=AF.Square),
                   reads=[lo_], writes=[sq])
                los.append(lo_)
                sqs.append((sq, sq_ap[:, 0:n], 128, 0))
            rb = fm_rstd(sqs, 128, n, 512.0, SCRA)
            for ct in range(4):
                op("dve", lambda e, ct=ct, lo_=los[ct], rb=rb: e.scalar_tensor_tensor(
                    out=MIXL3[:, ct, own0:own0 + n], in0=lo_.ap[:, 0:n], scalar=ppc("glru", ct), in1=rb.ap[:, 0:n],
                    op0=ALU.mult, op1=ALU.mult), reads=[lo_, rb, PP], writes=[MIXL])

        blks = []
        for k in range(5):
            for j in range(NB):
                blks.append((k, j, 512, False))
            if k >= 3:
                blks.append((k, NB, 1, True))
        phaseA_front(*blks[0], 0)
        for bi, (k, j, n, mini) in enumerate(blks):
            if bi + 1 < len(blks):
                phaseA_front(*blks[bi + 1], (bi + 1) % 2)
            phaseA_block(k, j, n, mini, bi % 2)
            if mini and k == 3:
                op("dve", lambda e: e.tensor_copy(out=HE.ap[:, 0:4], in_=HSC3[:, :, 0]), reads=[HSC], writes=[HE])
            if mini and k == 4:
                op("dve", lambda e: e.tensor_copy(out=HE.ap[:, 4:8], in_=HSC3[:, :, 0]), reads=[HSC], writes=[HE])
        chk('A6')
        HE3 = HE.ap[:, 8:16].rearrange("p (c t) -> p c t", c=4)
        op("dve", lambda e: e.tensor_tensor(out=HE3[:, :, 0], in0=HE.ap[:, 0:4], in1=CAR.ap[:, 52:56], op=ALU.add),
           reads=[HE, CAR], writes=[HE])
        op("dve", lambda e: e.tensor_tensor(out=HE3[:, :, 1], in0=HE.ap[:, 4:8], in1=CAR.ap[:, 48:52], op=ALU.add),
           reads=[HE, CAR], writes=[HE])
        ZERO = SCRA()
        op("pool", lambda e: e.memset(ZERO.ap[:, 0:8], 0.0), writes=[ZERO])
        lru_combine(lambda ct: HE3[:, ct, :], HE, lambda ct: ZERO.ap[:, 0:2], ZERO, CH, 2)
        if "mixl" in dbg_d:
            TMPD = A.f32(4 * NOWN, "tmpd")
            op("dve", lambda e: e.tensor_copy(out=TMPD.ap, in_=MIXL.ap), reads=[MIXL], writes=[TMPD])
            dump("mixl", TMPD, TMPD.ap)
        if "ckvn" in dbg_d:
            TMPD2 = A.f32(NKEY, "tmpd2")
            op("dve", lambda e: e.tensor_copy(out=TMPD2.ap, in_=CKVN.ap), reads=[CKVN], writes=[TMPD2])
            dump("ckvn", TMPD2, TMPD2.ap)
        if "ets" in dbg_d:
            TMPD3 = A.f32(NKEY, "tmpd3")
            op("dve", lambda e: e.tensor_copy(out=TMPD3.ap[0:96], in_=ETS.ap[0:96]), reads=[ETS], writes=[TMPD3])
            dump("ets", TMPD3, TMPD3.ap[0:96])

        S.barrier_all()
        A.release(mA)
        if stop == 'A':
            S.emit(nc)
            return nc

        WUQ = A.bf(2 * 768, "wuq")
        WUQ3 = WUQ.ap.rearrange("p (k c) -> p k c", k=2)
        for c2 in range(2):
            load(WUQ, WUQ3[:, c2, :], w_uq[c2 * 128:(c2 + 1) * 128, :])
        WUQR = A.bf(2 * 8 * 96, "wuqr")
        WUQR4 = WUQR.ap.rearrange("p (k h c) -> p k h c", k=2, h=8)
        WUQ4 = WUQ.ap.rearrange("p (k h c) -> p k h c", k=2, h=8)
        op("pool", lambda e: e.memset(WUQR.ap, 0.0), writes=[WUQR])
        for c2 in range(2):
            op("dve", lambda e, c2=c2: e.tensor_scalar(out=WUQR4[:, c2, :, 64:80], in0=WUQ4[:, c2, :, 80:96], scalar1=-1.0,
                                                       scalar2=None, op0=ALU.mult), reads=[WUQ], writes=[WUQR])
            op("dve", lambda e, c2=c2: e.tensor_copy(out=WUQR4[:, c2, :, 80:96], in_=WUQ4[:, c2, :, 64:80]),
               reads=[WUQ], writes=[WUQR])
        WUKV = A.bf(1024, "wukv")
        load(WUKV, WUKV.ap, w_ukv)
        ON = A.bf(8 * NOWN, "on")
        ON3 = ON.ap.rearrange("p (h t) -> p h t", h=8)
        TAB = A.bf(2 * NOWN, "qtab")
        mB = A.mark()
        KT = [A.bf(NKEY, "kt%d" % i) for i in range(2)]
        VV = [A.bf(NKT * 65, "v%d" % i) for i in range(2)]
        VV3 = [v.ap.rearrange("p (t c) -> p t c", c=65) for v in VV]
        QT = [A.bf(NOWN, "qt%d" % i) for i in range(2)]
        PT = [Buf(NT_H[i].ap[:, 0:512], NT_H[i].r) for i in range(2)] + [A.bf(512, "pt%d" % i) for i in range(2)]
        scrB = [Buf(NT_X[i].ap[:, 0:512], NT_X[i].r) for i in range(2)] + [A.f32(512, "scrB%d" % i) for i in range(6)]
        sbi = [0]

        def SCRB():
            b = scrB[sbi[0] % len(scrB)]
            sbi[0] += 1
            return b
        for v3, v in zip(VV3, VV):
            op("pool", lambda e, v3=v3: e.memset(v3[:, :, 64:65], 1.0), writes=[v])
        nqb = -(-NOWN // 512)
        half = NOWN // 2
        qb_base = half // nqb
        qb_sizes = [2 * (qb_base + (1 if i < half - qb_base * nqb else 0)) for i in range(nqb)]
        qblocks = [(sum(qb_sizes[:i]), qb_sizes[i]) for i in range(nqb)]
        for (q0, n) in qblocks:
            rope_tables(poso[:, q0:q0 + n], n, SCRB, out_buf=TAB, out_s=TAB.ap[64:96, q0:q0 + n],
                        out_c=TAB.ap[64:96, NOWN + q0:NOWN + q0 + n])
        pti = [0]

        RK = [A.f32(NKT, "rk%d" % i) for i in range(2)]
        SSK = psum[5]

        for kt_ in KT:
            op("dve", lambda e, kt_=kt_: e.tensor_copy(out=kt_.ap[64:96, :], in_=ETS.ap[64:96, :]), reads=[ETS], writes=[kt_])

        def kgen(h):
            kt = KT[h % 2]
            rk = RK[h % 2]
            pend_mms = []
            for kb in range(NKB):
                c0 = kb * 512
                pk = PS()
                op("pe", lambda e, pk=pk, c0=c0: e.matmul(pk.ap[0:64, :], lhsT=WUKV.ap[:, h * 128:h * 128 + 64],
                                                      rhs=CKVN.ap[:, c0:c0 + 512], start=True, stop=True),
                   reads=[WUKV, CKVN], writes=[pk])
                sq = SCRB()
                sq_ap = sq.ap.bitcast(BF16)
                op("act", lambda e, sq_ap=sq_ap, pk=pk: e.activation(out=sq_ap[0:64, 0:512], in_=pk.ap[0:64, :], func=AF.Square),
                   reads=[pk], writes=[sq])
                op("dve", lambda e, pk=pk, c0=c0: e.tensor_scalar(out=kt.ap[0:64, c0:c0 + 512], in0=pk.ap[0:64, :],
                                                                 scalar1=ppc("gk", 0, slice(0, 64)), scalar2=None, op0=ALU.mult),
                   reads=[pk, PP, sq], writes=[kt])

                def mms(e, sq_ap=sq_ap, kb=kb, c0=c0):
                    ins = None
                    for i in range(4):
                        t = kb * 4 + i
                        e.matmul(SSK.ap[:, t:t + 1], lhsT=sq_ap[0:64, i * 128:(i + 1) * 128], rhs=ONESB.ap[0:64, 0:1],
                                 start=True, stop=False)
                        ins = e.matmul(SSK.ap[:, t:t + 1], lhsT=ETS.ap[0:32, c0 + i * 128:c0 + (i + 1) * 128],
                                       rhs=ONESB.ap[0:32, 0:1], start=False, stop=True)
                    return ins
                if pend_mms:
                    pm, psq = pend_mms.pop(0)
                    op("pe", pm, reads=[psq, ETS, ONESB], writes=[SSK])
                pend_mms.append((mms, sq))
            while pend_mms:
                pm, psq = pend_mms.pop(0)
                op("pe", pm, reads=[psq, ETS, ONESB], writes=[SSK])
            op("dve", lambda e: e.tensor_scalar(out=rk.ap[:, 0:NKT], in0=SSK.ap[:, 0:NKT], scalar1=1.0, scalar2=96.0 * EPS,
                                                op0=ALU.mult, op1=ALU.add), reads=[SSK], writes=[rk])
            op("act", lambda e: e.activation(out=rk.ap[:, 0:NKT], in_=rk.ap[:, 0:NKT], func=AF.Ln), reads=[rk], writes=[rk])
            op("act", lambda e: e.activation(out=rk.ap[:, 0:NKT], in_=rk.ap[:, 0:NKT], func=AF.Exp, scale=-0.5), reads=[rk], writes=[rk])

        def vgen(h):
            vv = VV[h % len(VV)]
            vv3 = VV3[h % len(VV)]
            for kb in range(NKB):
                c0 = kb * 512
                pv = PS()

                def mmv(e, pv=pv, c0=c0):
                    ins = None
                    for i in range(4):
                        ins = e.matmul(pv.ap[:, i * 64:(i + 1) * 64], lhsT=CKVN.ap[:, c0 + i * 128:c0 + (i + 1) * 128],
                                       rhs=WUKV.ap[:, h * 128 + 64:h * 128 + 128], start=True, stop=True)
                    return ins
                op("pe", mmv, reads=[CKVN, WUKV], writes=[pv])
                op("dve", lambda e, pv=pv, kb=kb: e.tensor_copy(out=vv3[:, kb * 4:(kb + 1) * 4, 0:64],
                                                             in_=pv.ap[:, 0:256].rearrange("p (t c) -> p t c", c=64)),
                   reads=[pv], writes=[vv])

        def do_head(h):
            kt = KT[h % 2]
            rk = RK[h % 2]
            vv = VV[h % len(VV)]
            vv3 = VV3[h % len(VV)]
            qt = QT[h % 2]
            qst = {}

            def qgenA(q0, n):
                pq = PS()
                pr = PS()

                def mmq(e):
                    ins = None
                    for c2 in range(2):
                        ins = e.matmul(pq.ap[0:96, 0:n], lhsT=WUQ3[:, c2, h * 96:(h + 1) * 96], rhs=CQN3[:, c2, q0:q0 + n],
                                       start=(c2 == 0), stop=(c2 == 1))
                    return ins

                def mmr(e):
                    ins = None
                    for c2 in range(2):
                        ins = e.matmul(pr.ap[0:96, 0:n], lhsT=WUQR4[:, c2, h, :], rhs=CQN3[:, c2, q0:q0 + n],
                                       start=(c2 == 0), stop=(c2 == 1))
                    return ins
                op("pe", mmq, reads=[WUQ, CQN], writes=[pq])
                op("pe", mmr, reads=[WUQR, CQN], writes=[pr])
                sq = SCRB()
                sq_ap = sq.ap.bitcast(BF16)
                op("act", lambda e: e.activation(out=sq_ap[0:96, 0:n], in_=pq.ap[0:96, 0:n], func=AF.Square),
                   reads=[pq], writes=[sq])
                e2 = SCRB()
                op("dve", lambda e: e.scalar_tensor_tensor(
                    out=e2.ap[64:96, 0:n], in0=pr.ap[64:96, 0:n], scalar=ppc("gqr", 0, slice(64, 96)),
                    in1=TAB.ap[64:96, q0:q0 + n], op0=ALU.mult, op1=ALU.mult), reads=[pr, TAB, PP], writes=[e2])
                qst[q0] = (pq, pr, sq, sq_ap, e2)

            def qgenB(q0, n):
                pq, pr, sq, sq_ap, e2 = qst[q0]
                rb = fm_rstd([(sq, sq_ap[0:96, 0:n], 96, 0)], 96, n, 96.0, SCRB, pbuf=pr)
                op("dve", lambda e: e.scalar_tensor_tensor(
                    out=qt.ap[0:64, q0:q0 + n], in0=pq.ap[0:64, 0:n], scalar=ppc("gq", 0, slice(0, 64)), in1=rb.ap[0:64, 0:n],
                    op0=ALU.mult, op1=ALU.mult), reads=[pq, rb, PP], writes=[qt])
                e1 = SCRB()
                op("dve", lambda e: e.scalar_tensor_tensor(
                    out=e1.ap[64:96, 0:n], in0=pq.ap[64:96, 0:n], scalar=ppc("gq", 0, slice(64, 96)),
                    in1=TAB.ap[64:96, NOWN + q0:NOWN + q0 + n], op0=ALU.mult, op1=ALU.mult), reads=[pq, TAB, PP, sq], writes=[e1])
                op("dve", lambda e: e.tensor_tensor(out=e1.ap[64:96, 0:n], in0=e1.ap[64:96, 0:n],
                                                    in1=e2.ap[64:96, 0:n], op=ALU.add),
                   reads=[e1, e2], writes=[e1])
                op("dve", lambda e: e.tensor_tensor(out=qt.ap[64:96, q0:q0 + n], in0=e1.ap[64:96, 0:n],
                                                    in1=rb.ap[64:96, 0:n], op=ALU.mult),
                   reads=[e1, rb], writes=[qt])
            qgenA(*qblocks[0])
            for qi in range(len(qblocks)):
                if qi + 1 < len(qblocks):
                    qgenA(*qblocks[qi + 1])
                qgenB(*qblocks[qi])
            scale = 96.0 ** -0.5
            fin_pending = []
            for (q0, n) in qblocks:
                po = PSACC()
                pend = []

                def qk(t, q0=q0, n=n):
                    pb = PS()
                    op("pe", lambda e, pb=pb, t=t: e.matmul(pb.ap[:, 0:n], lhsT=kt.ap[0:96, t * 128:(t + 1) * 128],
                                                        rhs=qt.ap[0:96, q0:q0 + n], start=True, stop=True),
                       reads=[kt, qt], writes=[pb])
                    return pb

                def expv(t, pb, q0=q0, n=n, po=po):
                    pt_ = PT[pti[0] % 4]
                    pti[0] += 1
                    op("act", lambda e, pb=pb, pt_=pt_, t=t: e.activation(out=pt_.ap[:, 0:n], in_=pb.ap[:, 0:n], func=AF.Exp,
                                                                     scale=rk.ap[:, t:t + 1]), reads=[pb, rk], writes=[pt_])
                    op("pe", lambda e, pt_=pt_, t=t: e.matmul(po.ap[0:65, 0:n], lhsT=vv3[:, t, :], rhs=pt_.ap[:, 0:n],
                                                          start=(t == 0), stop=(t == NKT - 1)),
                       reads=[vv, pt_], writes=[po])
                LOOK = 2
                DEFER = 8
                for t in range(NKT + LOOK):
                    if t < NKT:
                        pend.append((t, qk(t)))
                    if t >= LOOK:
                        tt, pb = pend.pop(0)
                        expv(tt, pb)
                    if t == DEFER and fin_pending:
                        fin_pending.pop(0)()
                rd = SCRB()
                op("dve", lambda e, rd=rd, po=po, n=n: e.reciprocal(out=rd.ap[64:65, 0:n], in_=po.ap[64:65, 0:n]),
                   reads=[po], writes=[rd])

                def fin(rd=rd, po=po, q0=q0, n=n):
                    pbc = PS()
                    op("pe", lambda e, pbc=pbc: e.matmul(pbc.ap[0:64, 0:n], lhsT=ONESF.ap[64:65, 0:64], rhs=rd.ap[64:65, 0:n],
                                                         start=True, stop=True), reads=[ONESF, rd], writes=[pbc])
                    oc = SCRB()
                    op("act", lambda e, oc=oc: e.activation(out=oc.ap[0:64, 0:n], in_=po.ap[0:64, 0:n], func=AF.Copy),
                       reads=[po, rd], writes=[oc])
                    op("dve", lambda e, oc=oc, pbc=pbc: e.tensor_tensor(out=ON3[0:64, h, q0:q0 + n], in0=oc.ap[0:64, 0:n],
                                                                         in1=pbc.ap[0:64, 0:n], op=ALU.mult),
                       reads=[oc, pbc], writes=[ON])
                fin_pending.append(fin)
            while fin_pending:
                fin_pending.pop(0)()
        chk('B0')
        kgen(0)
        chk('B1')
        vgen(0)
        chk('B2')
        for h_ in range(8):
            if h_ < 7:
                kgen(h_ + 1)
                if len(VV) > 1:
                    vgen(h_ + 1)
            do_head(h_)
            if h_ < 7 and len(VV) == 1:
                vgen(h_ + 1)
        for (q0, n) in qblocks:
            sqs = []
            for h in range(8):
                sq = SCRB()
                sq_ap = sq.ap.bitcast(BF16)
                op("act", lambda e, sq_ap=sq_ap, h=h, q0=q0, n=n: e.activation(out=sq_ap[0:64, 0:n], in_=ON3[0:64, h, q0:q0 + n],
                                                                            func=AF.Square), reads=[ON], writes=[sq])
                sqs.append((sq, sq_ap[0:64, 0:n], 64, 0))
            rb = fm_rstd(sqs, 64, n, 512.0, SCRB)
            for h in range(8):
                op("dve", lambda e, h=h, rb=rb, q0=q0, n=n: e.scalar_tensor_tensor(
                    out=ON3[0:64, h, q0:q0 + n], in0=ON3[0:64, h, q0:q0 + n], scalar=ppc("gmla", h, slice(0, 64)),
                    in1=rb.ap[0:64, 0:n], op0=ALU.mult, op1=ALU.mult), reads=[ON, rb, PP], writes=[ON])
        if "on" in dbg_d:
            TMPD4 = A.f32(8 * NOWN, "tmpd4")
            op("dve", lambda e: e.tensor_copy(out=TMPD4.ap[0:64], in_=ON.ap[0:64]), reads=[ON], writes=[TMPD4])
            dump("on", TMPD4, TMPD4.ap[0:64])
        S.barrier_all()
        A.release(mB)
        if stop == 'B':
            S.emit(nc)
            return nc
        NTL = CH // 128
        tiles = [(i * 128, 128) for i in range(NTL)] + [(CH, 2)]
        xres_start = A.top()
        XRES = A.f32((NTL + 1) * D, "xres")
        xres_end = A.top()
        XRES3 = XRES.ap.rearrange("p (t d) -> p t d", d=D)
        XR_res = [Res("xres%d" % i) for i in range(NTL + 1)]
        XT_ = [Buf(XRES3[:, i, :], XR_res[i]) for i in range(NTL + 1)]
        mC = A.mark()
        A.regs = [[ckvn_start, ets_end], [xres_end, ASZ]]
        WOL = A.bf(4 * D, "wol")
        WOL3 = WOL.ap.rearrange("p (c d) -> p c d", c=4)
        WOM = A.bf(8 * D, "wom")
        WOM3 = WOM.ap.rearrange("p (h d) -> p h d", h=8)
        load(WOL, WOL3, w_out[0:512, :].rearrange("(c p) d -> p c d", p=128))
        load(WOM, WOM3[0:64], w_out[512:1024, :].rearrange("(h p) d -> p h d", p=64))
        for ti, (o0, rows) in enumerate(tiles):
            xt = XT_[ti]
            src = xs[3, o0:o0 + rows, :] if ti < NTL else xh[0:2, :]
            load(xt, xt.ap[0:rows], src, eng="sp")
            for half in range(2):
                pb = PS()

                def mmo(e, pb=pb, o0=o0, rows=rows, half=half):
                    ins = None
                    for ct in range(4):
                        ins = e.matmul(pb.ap[0:rows, :], lhsT=MIXL3[:, ct, o0:o0 + rows],
                                       rhs=WOL3[:, ct, half * 512:(half + 1) * 512], start=(ct == 0), stop=False)
                    for h in range(8):
                        ins = e.matmul(pb.ap[0:rows, :], lhsT=ON3[0:64, h, o0:o0 + rows],
                                       rhs=WOM3[0:64, h, half * 512:(half + 1) * 512], start=False, stop=(h == 7))
                    return ins
                op("pe", mmo, reads=[MIXL, ON, WOL, WOM], writes=[pb])
                op("dve", lambda e, pb=pb, xt=xt, rows=rows, half=half: e.tensor_tensor(
                    out=xt.ap[0:rows, half * 512:(half + 1) * 512], in0=xt.ap[0:rows, half * 512:(half + 1) * 512],
                    in1=pb.ap[0:rows, :], op=ALU.add), reads=[xt, pb], writes=[xt])
        if "x1" in dbg_d:
            for ti in range(NTL):
                store(dbg_d["x1"][ti * 128:(ti + 1) * 128, :], XT_[ti], XT_[ti].ap)
        S.barrier_all()
        A.regs = [[mP, xres_start], [xres_end, ASZ]]
        if stop == 'C':
            S.emit(nc)
            return nc

        GB2 = A.f32(D, "gbc2")
        GB3 = A.f32(D, "gbc3")
        load(GB2, GB2.ap, gvec[1, :].partition_broadcast(128))
        load(GB3, GB3.ap, gvec[3, :].partition_broadcast(128))
        HNT = A.bf(8 * NOWN, "hnt")
        HNT3 = HNT.ap.rearrange("p (k t) -> p k t", k=8)
        mD = A.mark()
        WMQ = A.bf(8 * 512, "wmq")
        WMQ3 = WMQ.ap.rearrange("p (k c) -> p k c", k=8)
        load(WMQ, WMQ3, w_mq.rearrange("(k p) c -> p k c", p=128))
        WMO = A.bf(4 * D, "wmo")
        WMO3 = WMO.ap.rearrange("p (h d) -> p h d", h=4)
        load(WMO, WMO3, w_mo.rearrange("(h p) d -> p h d", p=128))
        H1T = A.bf(8 * 512, "h1t")
        H1T3 = H1T.ap.rearrange("p (k t) -> p k t", k=8)
        OMT = A.bf(4 * 512, "omt")
        OMT3 = OMT.ap.rearrange("p (h t) -> p h t", h=4)
        HNH = A.bf(8 * 2, "hnh")
        HNH3 = HNH.ap.rearrange("p (k t) -> p k t", k=8)
        scrD = [A.f32(512, "scrD%d" % i) for i in range(10)]
        sdi = [0]

        def SCRD():
            b = scrD[sdi[0] % len(scrD)]
            sdi[0] += 1
            return b
        mscale = 128.0 ** -0.5
        qbl = [(q0, min(512, CH - q0)) for q0 in range(0, CH, 512)] + [(CH, 2)]
        for (q0, n) in qbl:
            tl = [ti for ti, (o0, rows) in enumerate(tiles) if q0 <= o0 < q0 + n]
            for ti in tl:
                o0, rows = tiles[ti]
                norm_T(XT_[ti], XT_[ti].ap[0:rows], rows, H1T, H1T3, o0 - q0, gb=GB2)
            for h in range(4):
                pq = PS()

                def mmq2(e, pq=pq, h=h, n=n):
                    ins = None
                    for kc in range(8):
                        ins = e.matmul(pq.ap[:, 0:n], lhsT=WMQ3[:, kc, h * 128:(h + 1) * 128], rhs=H1T3[:, kc, 0:n],
                                       start=(kc == 0), stop=(kc == 7))
                    return ins
                op("pe", mmq2, reads=[WMQ, H1T], writes=[pq])
                sq = SCRD()
                sq_ap = sq.ap.bitcast(BF16)
                op("act", lambda e, sq_ap=sq_ap, pq=pq, n=n: e.activation(out=sq_ap[:, 0:n], in_=pq.ap[:, 0:n], func=AF.Square),
                   reads=[pq], writes=[sq])
                rb = fm_rstd([(sq, sq_ap[:, 0:n], 128, 0)], 128, n, 128.0, SCRD)
                qm = SCRD()
                qm_ap = qm.ap.bitcast(BF16)
                op("dve", lambda e, pq=pq, rb=rb, qm_ap=qm_ap, n=n: e.scalar_tensor_tensor(
                    out=qm_ap[:, 0:n], in0=pq.ap[:, 0:n], scalar=ppc("gmq"), in1=rb.ap[:, 0:n], op0=ALU.mult, op1=ALU.mult),
                   reads=[pq, rb, PP], writes=[qm])
                po = PS()
                pd = PS()
                for mt in range(2):
                    ps_ = PS()
                    op("pe", lambda e, ps_=ps_, mt=mt, h=h, qm_ap=qm_ap, n=n: e.matmul(
                        ps_.ap[:, 0:n], lhsT=KM3[:, h, mt * 128:(mt + 1) * 128], rhs=qm_ap[:, 0:n], start=True, stop=True),
                       reads=[KM, qm], writes=[ps_])
                    pt_ = SCRD()
                    pt_ap = pt_.ap.bitcast(BF16)
                    op("act", lambda e, ps_=ps_, pt_ap=pt_ap, n=n: e.activation(out=pt_ap[:, 0:n], in_=ps_.ap[:, 0:n], func=AF.Exp,
                                                                         scale=mscale), reads=[ps_], writes=[pt_])
                    op("pe", lambda e, po=po, mt=mt, h=h, pt_ap=pt_ap, n=n: e.matmul(
                        po.ap[:, 0:n], lhsT=VM3[:, mt, h * 128:(h + 1) * 128], rhs=pt_ap[:, 0:n], start=(mt == 0), stop=(mt == 1)),
                       reads=[VM, pt_], writes=[po])
                    op("pe", lambda e, pd=pd, mt=mt, pt_ap=pt_ap, n=n: e.matmul(
                        pd.ap[:, 0:n], lhsT=ONESB.ap[:, 0:128], rhs=pt_ap[:, 0:n], start=(mt == 0), stop=(mt == 1)),
                       reads=[ONESB, pt_], writes=[pd])
                rd = SCRD()
                op("dve", lambda e, rd=rd, pd=pd, n=n: e.reciprocal(out=rd.ap[:, 0:n], in_=pd.ap[:, 0:n]),
                   reads=[pd], writes=[rd])
                op("dve", lambda e, po=po, rd=rd, h=h, n=n: e.tensor_tensor(out=OMT3[:, h, 0:n], in0=po.ap[:, 0:n], in1=rd.ap[:, 0:n],
                                                                         op=ALU.mult), reads=[po, rd], writes=[OMT])
            for ti in tl:
                o0, rows = tiles[ti]
                xt = XT_[ti]
                c0 = o0 - q0
                for half in range(2):
                    pb = PS()

                    def mmo2(e, pb=pb, c0=c0, rows=rows, half=half):
                        ins = None
                        for h in range(4):
                            ins = e.matmul(pb.ap[0:rows, :], lhsT=OMT3[:, h, c0:c0 + rows],
                                           rhs=WMO3[:, h, half * 512:(half + 1) * 512], start=(h == 0), stop=(h == 3))
                        return ins
                    op("pe", mmo2, reads=[OMT, WMO], writes=[pb])
                    op("dve", lambda e, pb=pb, xt=xt, rows=rows, half=half: e.tensor_tensor(
                        out=xt.ap[0:rows, half * 512:(half + 1) * 512], in0=xt.ap[0:rows, half * 512:(half + 1) * 512],
                        in1=pb.ap[0:rows, :], op=ALU.add), reads=[xt, pb], writes=[xt])
                if ti < NTL:
                    norm_T(xt, xt.ap[0:rows], rows, HNT, HNT3, o0 + 1, gb=GB3, evac_eng="act")
                else:
                    norm_T(xt, xt.ap[0:rows], rows, HNH, HNH3, 0, gb=GB3)
                    op("dve", lambda e: e.tensor_scalar(out=HNT3[:, :, CH + 1:CH + 2], in0=HNH3[:, :, 0:1], scalar1=ppc("msk", 0),
                                                        scalar2=None, op0=ALU.mult), reads=[HNH, PP], writes=[HNT])
                    op("dve", lambda e: e.tensor_scalar(out=HNT3[:, :, 0:1], in0=HNH3[:, :, 1:2], scalar1=ppc("msk", 1),
                                                        scalar2=None, op0=ALU.mult), reads=[HNH, PP], writes=[HNT])
        if "x2" in dbg_d:
            for ti in range(NTL):
                store(dbg_d["x2"][ti * 128:(ti + 1) * 128, :], XT_[ti], XT_[ti].ap)
        S.barrier_all()
        A.release(mD)
        if stop == 'D':
            S.emit(nc)
            return nc

        GP = 2
        NR = 22 // GP
        WUPG = [A.bf(8 * 2 * GP * 128, "wupg%d" % i) for i in range(2)]
        WUPG4 = [w.ap.rearrange("p (k s c) -> p k s c", k=8, s=2) for w in WUPG]
        WDNG = [A.bf(GP * D, "wdng%d" % i) for i in range(2)]
        WDNG3 = [w.ap.rearrange("p (j d) -> p j d", j=GP) for w in WDNG]
        ACTT = [A.bf(GP * CH, "actt%d" % i) for i in range(2)]
        ACTT3 = [a.ap.rearrange("p (j t) -> p j t", j=GP) for a in ACTT]
        scrE = [A.f32(512, "scrE%d" % i) for i in range(8)]
        sei = [0]

        def SCRE():
            b = scrE[sei[0] % len(scrE)]
            sei[0] += 1
            return b
        fwo = PPL["fw"][0]
        fbo = PPL["fb"][0]
        def do_round(r):
            wu = WUPG[r % 2]
            wu4 = WUPG4[r % 2]
            wd = WDNG[r % 2]
            wd3 = WDNG3[r % 2]
            at = ACTT[r % 2]
            at3 = ACTT3[r % 2]
            j0 = r * GP
            load(wu, wu4[:, :, 0, :], w_up[:, j0 * 128:(j0 + GP) * 128].rearrange("(k p) c -> p k c", p=128))
            load(wu, wu4[:, :, 1, :], w_up[:, DFF + j0 * 128:DFF + (j0 + GP) * 128].rearrange("(k p) c -> p k c", p=128))
            load(wd, wd3, w_dn[j0 * 128:(j0 + GP) * 128, :].rearrange("(j p) d -> p j d", p=128))
            for (t0, n) in WINS:
                for jj in range(GP):
                    cv = []
                    for s in range(2):
                        chn = (j0 + jj) + s * 22
                        pg = PS()

                        def mmu(e, pg=pg, s=s, jj=jj, t0=t0, n=n):
                            ins = None
                            for kc in range(8):
                                ins = e.matmul(pg.ap[:, 0:n + 2], lhsT=wu4[:, kc, s, jj * 128:(jj + 1) * 128],
                                               rhs=HNT3[:, kc, t0:t0 + n + 2], start=(kc == 0), stop=(kc == 7))
                            return ins
                        op("pe", mmu, reads=[wu, HNT], writes=[pg])
                        c_ = SCRE()
                        wsrc, wb = PP, fwo + 3 * chn
                        bsrc, bb = PP, fbo + chn
                        op("act", lambda e, c_=c_, pg=pg, wsrc=wsrc, wb=wb, bsrc=bsrc, bb=bb, n=n: e.activation(
                            out=c_.ap[:, 0:n], in_=pg.ap[:, 1:n + 1], func=AF.Identity,
                            scale=wsrc.ap[:, wb + 1:wb + 2], bias=bsrc.ap[:, bb:bb + 1]),
                           reads=[pg, wsrc, bsrc], writes=[c_])
                        for tp in (0, 2):
                            op("dve", lambda e, c_=c_, pg=pg, wsrc=wsrc, wb=wb, tp=tp, n=n: e.scalar_tensor_tensor(
                                out=c_.ap[:, 0:n], in0=pg.ap[:, tp:tp + n], scalar=wsrc.ap[:, wb + tp:wb + tp + 1],
                                in1=c_.ap[:, 0:n], op0=ALU.mult, op1=ALU.add), reads=[pg, wsrc, c_], writes=[c_])
                        cv.append(c_)
                    sg = SCRE()
                    op("act", lambda e, sg=sg, g_=cv[0], n=n: e.activation(out=sg.ap[:, 0:n], in_=g_.ap[:, 0:n], func=AF.Tanh, scale=0.5),
                       reads=[cv[0]], writes=[sg])
                    op("act", lambda e, sg=sg, n=n: e.activation(out=sg.ap[:, 0:n], in_=sg.ap[:, 0:n], func=AF.Identity, scale=0.5,
                                                                bias=POSH.ap[:, 0:1]), reads=[sg, POSH], writes=[sg])
                    op("dve", lambda e, g_=cv[0], u_=cv[1], n=n: e.tensor_tensor(out=u_.ap[:, 0:n], in0=g_.ap[:, 0:n], in1=u_.ap[:, 0:n],
                                                                                op=ALU.mult), reads=[cv[0], cv[1]], writes=[cv[1]])
                    op("dve", lambda e, sg=sg, u_=cv[1], jj=jj, t0=t0, n=n: e.tensor_tensor(
                        out=at3[:, jj, t0:t0 + n], in0=sg.ap[:, 0:n], in1=u_.ap[:, 0:n], op=ALU.mult),
                       reads=[sg, cv[1]], writes=[at])
            if not (r % 2 == 1 or r == NR - 1):
                return
            rds = [r - 1, r] if r % 2 == 1 else [r]
            srcs = [(ACTT3[rr % 2], WDNG3[rr % 2]) for rr in rds]
            rbufs = [ACTT[rr % 2] for rr in rds] + [WDNG[rr % 2] for rr in rds]
            nmm = GP * len(rds)
            for ti in range(NTL):
                o0, rows = tiles[ti]
                xt = XT_[ti]
                for half in range(2):
                    pb = PS()

                    def mmd(e, pb=pb, o0=o0, half=half):
                        ins = None
                        i_ = 0
                        for (a3_, w3_) in srcs:
                            for jj in range(GP):
                                ins = e.matmul(pb.ap[:, :], lhsT=a3_[:, jj, o0:o0 + 128], rhs=w3_[:, jj, half * 512:(half + 1) * 512],
                                               start=(i_ == 0), stop=(i_ == nmm - 1))
                                i_ += 1
                        return ins
                    op("pe", mmd, reads=rbufs, writes=[pb])
                    op("dve", lambda e, pb=pb, xt=xt, half=half: e.tensor_tensor(
                        out=xt.ap[:, half * 512:(half + 1) * 512], in0=xt.ap[:, half * 512:(half + 1) * 512],
                        in1=pb.ap[:, :], op=ALU.add), reads=[xt, pb], writes=[xt])
        for r_ in range(NR):
            do_round(r_)
        for ti in range(NTL):
            store(out_d[ti * 128:(ti + 1) * 128, :], XT_[ti], XT_[ti].ap)
        S.emit(nc)
    return nc


def _cols(v, rows=128):
    v = np.asarray(v, np.float32)
    return np.ascontiguousarray(v.reshape(-1, rows).T)


def prep_core(inp, b, c, CH):
    f32 = np.float32
    x = np.asarray(inp["x"][b], f32)
    pos = np.asarray(inp["positions"][b]).astype(np.int32)
    s0 = c * CH
    slots = [(k, False) for k in range(c)] + [(k, True) for k in range(3, c, -1)] + [(c, False), (c, True)]
    assert len(slots) == 5
    xs = np.stack([x[k * CH:(k + 1) * CH][::-1] if rev else x[k * CH:(k + 1) * CH] for k, rev in slots])
    pk = np.stack([pos[k * CH:(k + 1) * CH][::-1] if rev else pos[k * CH:(k + 1) * CH] for k, rev in slots[:4]])
    posk = np.ascontiguousarray(np.broadcast_to(pk[:, None, :], (4, 32, CH))).astype(np.int32)
    xh = np.zeros((2, D), f32)
    ph = np.zeros((2,), np.int32)
    msk = np.zeros((2,), f32)
    if c < 3:
        xh[0] = x[s0 + CH]; ph[0] = pos[s0 + CH]; msk[0] = 1.0
    if c > 0:
        xh[1] = x[s0 - 1]; ph[1] = pos[s0 - 1]; msk[1] = 1.0
    po = np.concatenate([pos[s0:s0 + CH], ph])
    poso = np.ascontiguousarray(np.broadcast_to(po[None, :], (32, CH + 2))).astype(np.int32)
    flg = np.zeros((5, 4), f32)
    for k in range(3):
        if k < c:
            flg[k] = [1, 0, 1, 0]
        else:
            flg[k] = [0, 1, 0, 1]
    flg[3] = [1, 0, 0, 0]
    flg[4] = [0, 1, 0, 0]
    pp = np.zeros((128, PPL["_n"]), f32)

    def put(name, arr):
        o, n = PPL[name]
        arr = np.asarray(arr, f32)
        if arr.ndim == 1:
            arr = np.broadcast_to(arr[None, :], (128, arr.shape[0]))
        assert arr.shape[1] == n, (name, arr.shape, n)
        pp[:arr.shape[0], o:o + n] = arr
    put("flg", flg.reshape(-1))
    put("msk", msk)
    cw = np.zeros((128, 5, 4, 4), f32)
    cb = np.zeros((128, 5, 4), f32); ba = np.zeros((128, 5, 4), f32); bi = np.zeros((128, 5, 4), f32); lam = np.zeros((128, 5, 4), f32)
    wa = np.zeros((5, 4, 128, 128), f32); wi = np.zeros((5, 4, 128, 128), f32)
    for k, (ck, rev) in enumerate(slots):
        d = 1 if rev else 0
        w = np.asarray(inp["lru_conv_w"][0, d], f32)
        if rev:
            w = w[::-1]
        for tap in range(4):
            cw[:, k, :, tap] = _cols(w[tap])
        cb[:, k, :] = _cols(inp["lru_conv_b"][0, d])
        ba[:, k, :] = _cols(inp["lru_b_a"][0, d])
        bi[:, k, :] = _cols(inp["lru_b_i"][0, d])
        lam[:, k, :] = _cols(inp["lru_lambda"][0, d])
        for ct in range(4):
            for half in range(2):
                blk = 2 * ct + half
                wa[k, ct, half * 64:(half + 1) * 64, half * 64:(half + 1) * 64] = inp["lru_w_a"][0, d, blk]
                wi[k, ct, half * 64:(half + 1) * 64, half * 64:(half + 1) * 64] = inp["lru_w_i"][0, d, blk]
    put("cw", cw.reshape(128, -1)); put("cb", cb.reshape(128, -1)); put("ba", ba.reshape(128, -1))
    put("bi", bi.reshape(128, -1)); put("lam", lam.reshape(128, -1))
    put("gqa", _cols(inp["q_a_norm"][0]))
    put("gkv", _cols(inp["kv_a_norm"][0]))

    def rotperm(g):
        g = np.asarray(g, f32)
        r = g.copy()
        r[64:80] = g[80:96]
        r[80:96] = g[64:80]
        return r
    gq = np.asarray(inp["mla_q_norm"][0], f32); gk = np.asarray(inp["mla_k_norm"][0], f32)
    for name, v in (("gq", gq), ("gqr", rotperm(gq)), ("gk", gk), ("gkr", rotperm(gk))):
        a = np.zeros((128, 1), f32); a[:96, 0] = v
        put(name, a)
    put("glru", _cols(inp["lru_out_norm"][0]))
    gm = np.zeros((128, 8), f32); gm[:64] = _cols(inp["mla_out_norm"][0], 64)
    put("gmla", gm)
    put("gmq", _cols(inp["mem_q_norm"][0])); put("gmk", _cols(inp["mem_k_norm"][0]))
    fw = np.zeros((128, 44, 3), f32)
    fcw = np.asarray(inp["ffn_conv_w"][0], f32)
    for tap in range(3):
        fw[:, :, tap] = _cols(fcw[tap])
    put("fw", fw.reshape(128, -1))
    put("fb", _cols(inp["ffn_conv_b"][0]))
    invf = np.zeros((128, 1), f32)
    inv = (10000.0 ** (-np.arange(0, 32, 2, dtype=np.float64) / 32.0)) / (2.0 * np.pi)
    for p in range(64, 96):
        invf[p, 0] = inv[(p - 64) % 16]
    put("invf", invf)
    gvec = np.stack([inp["attn_norm"][0], inp["mem_attn_norm"][0], inp["mem_norm"][0], inp["ffn_norm"][0]]).astype(f32)
    m = {
        "xs": np.ascontiguousarray(xs), "xh": xh, "posk": posk, "poso": poso, "pp": pp, "wa": wa, "wi": wi,
        "mem": np.ascontiguousarray(np.asarray(inp["mem"][b], f32)), "gvec": np.ascontiguousarray(gvec),
        "w_in": np.ascontiguousarray(inp["w_in"][0], dtype=f32), "w_uq": np.ascontiguousarray(inp["w_uq"][0], dtype=f32),
        "w_ukv": np.ascontiguousarray(inp["w_ukv"][0], dtype=f32), "w_out": np.ascontiguousarray(inp["w_out"][0], dtype=f32),
        "w_mem_q": np.ascontiguousarray(inp["w_mem_q"][0], dtype=f32),
        "w_mem_kv": np.ascontiguousarray(inp["w_mem_kv"][0], dtype=f32),
        "w_mem_o": np.ascontiguousarray(inp["w_mem_o"][0], dtype=f32),
        "w_up": np.ascontiguousarray(inp["w_up"][0], dtype=f32), "w_down": np.ascontiguousarray(inp["w_down"][0], dtype=f32),
    }
    return m


_NC_CACHE = {}


def run(inputs, dbg=None, cores=None, stop=None):
    inputs = {k: np.asarray(v) for k, v in inputs.items()}
    B, SEQ, _ = inputs["x"].shape
    CH = SEQ // 4
    key = (CH, repr(dbg), stop)
    if key not in _NC_CACHE:
        _NC_CACHE[key] = build(CH, dbg, stop)
    nc = _NC_CACHE[key]
    core_list = cores if cores is not None else [(b, c) for b in range(B) for c in range(4)]
    in_maps = [prep_core(inputs, b, c, CH) for (b, c) in core_list]
    res = run_bass_kernel_spmd(nc, in_maps, core_ids=list(range(len(core_list))), trace=bool(os.environ.get('KTRACE')))
    return res, core_list, CH


def kernel(**inputs):
    res, core_list, CH = run(inputs)
    B, SEQ, _ = np.asarray(inputs["x"]).shape
    out = np.zeros((B, SEQ, D), np.float32)
    for (b, c), r in zip(core_list, res.results):
        out[b, c * CH:(c + 1) * CH] = r["out"]
    return out
```

```python
import numpy as np
from contextlib import ExitStack
import concourse.bass as bass
import concourse.mybir as mybir
from concourse.bass_utils import run_bass_kernel_spmd

F32 = mybir.dt.float32
BF16 = mybir.dt.bfloat16
I32 = mybir.dt.int32
AF = mybir.ActivationFunctionType
ALU = mybir.AluOpType

import os
SAME_ENGINE_SYNC = os.environ.get('SAME_ENGINE_SYNC', '1') == '1'
EPS = 1e-6
D = 1024
DFF = 2816
NCH = 44
MEM = 256


class Res:
    __slots__ = ("name", "w", "r")

    def __init__(self, name=""):
        self.name = name
        self.w = None
        self.r = {}


class Sched:
    ENGS = ("pe", "act", "dve", "pool", "sp")

    def __init__(self):
        self.ops = {e: [] for e in self.ENGS}
        self.cnt = {e: 0 for e in self.ENGS}
        self.dcnt = {}
        self.seen = {e: {} for e in self.ENGS}

    def _need(self, eng, tok, waits):
        if tok is None:
            return
        key, val = tok
        if self.seen[eng].get(key, 0) >= val:
            return
        if val > waits.get(key, 0):
            waits[key] = val

    def op(self, eng, fn, reads=(), writes=(), dma=None):
        waits = {}
        for r in reads:
            self._need(eng, r.w, waits)
        for w in writes:
            self._need(eng, w.w, waits)
            for k, v in w.r.items():
                self._need(eng, (k, v), waits)
        if not SAME_ENGINE_SYNC:
            waits.pop(eng, None)
        for k, v in waits.items():
            self.seen[eng][k] = v
        if dma is None:
            self.cnt[eng] += 1
            tok = (eng, self.cnt[eng])
        else:
            self.dcnt[dma] = self.dcnt.get(dma, 0) + 16
            tok = (dma, self.dcnt[dma])
        self.ops[eng].append((list(waits.items()), fn, tok))
        for r in reads:
            if r.r.get(tok[0], 0) < tok[1]:
                r.r[tok[0]] = tok[1]
        for w in writes:
            w.w = tok
            w.r = {}
        return tok

    def barrier_all(self):
        for e in self.ENGS:
            waits = {}
            for e2 in self.ENGS:
                if e2 != e and self.cnt[e2] > self.seen[e].get(e2, 0):
                    waits[e2] = self.cnt[e2]
            for k, v in self.dcnt.items():
                if v > self.seen[e].get(k, 0):
                    waits[k] = v
            for k, v in waits.items():
                self.seen[e][k] = v
            if waits:
                self.ops[e].append((list(waits.items()), None, None))

    def emit(self, nc):
        keys = list(self.ENGS) + sorted(self.dcnt.keys())
        with ExitStack() as st:
            sems = {k: st.enter_context(nc.semaphore("s_" + k)) for k in keys}
            block = st.enter_context(nc.Block())
            engmap = {"pe": block.tensor, "act": block.scalar, "dve": block.vector,
                      "pool": block.gpsimd, "sp": block.sync}
            fin = {}
            for k in keys:
                v = self.cnt[k] if k in self.cnt else self.dcnt[k]
                if v > 0:
                    fin[k] = v
            self.ops["sp"].append((list(fin.items()), None, None))
            for e in self.ENGS:
                ops = self.ops[e]

                def body(engobj, ops=ops, e=e):
                    for waits, fn, tok in ops:
                        for k, v in waits:
                            engobj.wait_ge(sems[k], v)
                        if fn is None:
                            continue
                        ins = fn(engobj)
                        if tok[0] == e:
                            ins.then_inc(sems[e], 1)
                        else:
                            ins.then_inc(sems[tok[0]], 16)
                engmap[e](body)


class Buf:
    __slots__ = ("ap", "r")

    def __init__(self, ap, r):
        self.ap = ap
        self.r = r


class Arena:
    def __init__(self, t, size):
        self.t = t
        self.size = size
        self.regs = [[0, size]]

    def _take(self, n, name):
        for r in self.regs:
            if r[1] - r[0] >= n:
                o = r[0]
                r[0] += n
                return o
        raise AssertionError(("SBUF arena overflow", name, n, self.regs))

    def f32(self, n, name=""):
        o = self._take(n, name)
        return Buf(self.t[:, o:o + n], Res(name))

    def bf(self, n, name=""):
        m = (n + 1) // 2
        o = self._take(m, name)
        return Buf(self.t[:, o:o + m].bitcast(BF16)[:, 0:n], Res(name))

    def mark(self):
        return [list(r) for r in self.regs]

    def release(self, m):
        self.regs = [list(r) for r in m]

    def top(self):
        return self.regs[0][0]


def ffn_windows(CH):
    nw = -(-CH // 510)
    if CH % 512 == 0 and CH >= 512:
        nw = max(nw, 1)
    base = CH // nw
    rem = CH - base * nw
    sizes = [base + (1 if i < rem else 0) for i in range(nw)]
    starts = [sum(sizes[:i]) for i in range(nw)]
    return list(zip(starts, sizes))


def pp_layout():
    o = {}
    c = 0

    def add(name, n):
        nonlocal c
        o[name] = (c, n)
        c += n
    add("flg", 20)
    add("msk", 2)
    add("cw", 80)
    add("cb", 20)
    add("ba", 20)
    add("bi", 20)
    add("lam", 20)
    add("gqa", 2)
    add("gkv", 1)
    add("gq", 1)
    add("gqr", 1)
    add("gk", 1)
    add("gkr", 1)
    add("glru", 4)
    add("gmla", 8)
    add("gmq", 1)
    add("gmk", 1)
    add("fw", 132)
    add("fb", 44)
    add("invf", 1)
    o["_n"] = c
    return o


PPL = pp_layout()


class _Stop(Exception):
    pass


def build(CH, dbg=None, stop=None):
    holder = {}
    try:
        return _build(CH, dbg, stop, holder)
    except _Stop:
        return holder['nc']


def _build(CH, dbg, stop, holder):
    NB = CH // 512
    NKEY = 4 * CH
    NKT = NKEY // 128
    NKB = NKEY // 512
    NOWN = CH + 2
    WINS = ffn_windows(CH)
    nc = bass.Bass("TRN2", target_bir_lowering=False)
    holder["nc"] = nc

    def din(name, shape, dt=F32):
        return nc.dram_tensor(name, list(shape), dt, kind="ExternalInput").ap()

    xs = din("xs", [5, CH, D])
    xh = din("xh", [2, D])
    posk = din("posk", [4, 32, CH], I32)
    poso = din("poso", [32, NOWN], I32)
    pp_d = din("pp", [128, PPL["_n"]])
    wa_d = din("wa", [5, 4, 128, 128])
    wi_d = din("wi", [5, 4, 128, 128])
    mem_d = din("mem", [MEM, D])
    gvec = din("gvec", [4, D])
    w_in = din("w_in", [D, 1440])
    w_uq = din("w_uq", [256, 768])
    w_ukv = din("w_ukv", [128, 1024])
    w_out = din("w_out", [1024, D])
    w_mq = din("w_mem_q", [D, 512])
    w_mkv = din("w_mem_kv", [D, 1024])
    w_mo = din("w_mem_o", [512, D])
    w_up = din("w_up", [D, 2 * DFF])
    w_dn = din("w_down", [DFF, D])
    out_d = nc.dram_tensor("out", [CH, D], F32, kind="ExternalOutput").ap()
    dbg_d = {}
    if dbg:
        for name, shape in dbg.items():
            dbg_d[name] = nc.dram_tensor("dbg_" + name, list(shape), F32, kind="ExternalOutput").ap()

    S = Sched()
    with ExitStack() as st:
        ASZ = 52500
        arena_t = st.enter_context(nc.sbuf_tensor("arena", [128, ASZ], F32))
        A = Arena(arena_t, ASZ)
        psum_t = [st.enter_context(nc.psum_tensor("ps%d" % i, [128, 512], F32)) for i in range(8)]
        psum = [Buf(t[:, :], Res("ps%d" % i)) for i, t in enumerate(psum_t)]
        psi = [0]

        def PS():
            b = psum[psi[0] % 5]
            psi[0] += 1
            return b
        psa = [0]

        def PSACC():
            b = psum[6 + psa[0] % 2]
            psa[0] += 1
            return b

        def chk(tag):
            if stop == tag:
                S.emit(nc)
                raise _Stop()

        def op(eng, fn, reads=(), writes=(), dma=None):
            return S.op(eng, fn, [b.r for b in reads], [b.r for b in writes], dma)

        dkeys = {}

        def dkey(buf, pre="k"):
            k = (pre, id(buf.r))
            if k not in dkeys:
                dkeys[k] = "%s%02d" % (pre, len(dkeys))
            return dkeys[k]

        def load(dst, dst_ap, src_ap, eng=None, key=None):
            if eng is None:
                eng = "pool" if dst_ap.dtype != src_ap.dtype else "sp"
            op(eng, lambda e: e.dma_start(out=dst_ap, in_=src_ap), reads=[], writes=[dst], dma=dkey(dst))

        def store(dst_ap, buf, src_ap):
            op("sp", lambda e: e.dma_start(out=dst_ap, in_=src_ap), reads=[buf], dma=dkey(buf, "s"))

        def dump(name, buf, ap, rows=None):
            if name in dbg_d:
                store(dbg_d[name], buf, ap)

        PP = A.f32(PPL["_n"], "pp")
        load(PP, PP.ap, pp_d)

        def ppc(name, i=0, rows=slice(0, 128)):
            o, n = PPL[name]
            return PP.ap[rows, o + i:o + i + 1]

        CONST = A.f32(8, "const")
        op("pool", lambda e: e.memset(CONST.ap[:, 0:1], EPS), writes=[CONST])
        op("pool", lambda e: e.memset(CONST.ap[:, 1:2], 1.0), writes=[CONST])
        c_eps = CONST.ap[:, 0:1]
        c_one = CONST.ap[:, 1:2]
        NEGH = A.f32(8, "negh")
        POSH = A.f32(8, "posh")
        op("pool", lambda e: e.memset(NEGH.ap, -0.5), writes=[NEGH])
        op("pool", lambda e: e.memset(POSH.ap, 0.5), writes=[POSH])
        IDF = A.f32(128, "identf")
        IDENT = A.bf(128, "ident")
        ONESB = A.bf(128, "onesb")
        ONESF = A.f32(128, "onesf")
        op("pool", lambda e: e.iota(IDF.ap, [[1, 128]], base=0, channel_multiplier=-1,
                                    allow_small_or_imprecise_dtypes=True), writes=[IDF])
        op("dve", lambda e: e.tensor_scalar(out=IDENT.ap, in0=IDF.ap, scalar1=0.0, scalar2=None,
                                            op0=ALU.is_equal), reads=[IDF], writes=[IDENT])
        op("pool", lambda e: e.memset(ONESB.ap, 1.0), writes=[ONESB])
        op("pool", lambda e: e.memset(ONESF.ap, 1.0), writes=[ONESF])
        LP = A.f32(192, "lruparams")
        lam_o = PPL["lam"][0]
        flg_o = PPL["flg"][0]
        op("act", lambda e: e.activation(out=LP.ap[:, 0:20], in_=PP.ap[:, lam_o:lam_o + 20], func=AF.Exp, scale=-1.0),
           reads=[PP], writes=[LP])
        op("act", lambda e: e.activation(out=LP.ap[:, 0:20], in_=LP.ap[:, 0:20], func=AF.Ln, scale=1.0, bias=c_one),
           reads=[LP, CONST], writes=[LP])
        op("dve", lambda e: e.tensor_scalar(out=LP.ap[:, 20:40], in0=LP.ap[:, 0:20], scalar1=-4.0, scalar2=None,
                                            op0=ALU.mult), reads=[LP], writes=[LP])
        op("dve", lambda e: e.tensor_scalar(out=LP.ap[:, 0:20], in0=LP.ap[:, 0:20], scalar1=-8.0, scalar2=None,
                                            op0=ALU.mult), reads=[LP], writes=[LP])
        op("dve", lambda e: e.tensor_scalar(out=LP.ap[:, 40:60], in0=PP.ap[:, flg_o:flg_o + 20], scalar1=-1.0,
                                            scalar2=1.0, op0=ALU.mult, op1=ALU.add), reads=[PP], writes=[LP])

        ba_o = PPL["ba"][0]
        bi_o = PPL["bi"][0]
        fw_o = PPL["fw"][0]
        fb_o = PPL["fb"][0]
        op("dve", lambda e: e.tensor_scalar(out=LP.ap[:, 60:80], in0=PP.ap[:, ba_o:ba_o + 20], scalar1=0.5, scalar2=None,
                                            op0=ALU.mult), reads=[PP], writes=[LP])
        op("dve", lambda e: e.tensor_scalar(out=LP.ap[:, 80:100], in0=PP.ap[:, bi_o:bi_o + 20], scalar1=0.5, scalar2=None,
                                            op0=ALU.mult), reads=[PP], writes=[LP])
        op("dve", lambda e: e.tensor_scalar(out=LP.ap[:, 100:166], in0=PP.ap[:, fw_o + 66:fw_o + 132], scalar1=0.5, scalar2=None,
                                            op0=ALU.mult), reads=[PP], writes=[LP])
        op("dve", lambda e: e.tensor_scalar(out=LP.ap[:, 166:188], in0=PP.ap[:, fb_o + 22:fb_o + 44], scalar1=0.5, scalar2=None,
                                            op0=ALU.mult), reads=[PP], writes=[LP])

        def flg(k, i):
            return PP.ap[:, flg_o + 4 * k + i:flg_o + 4 * k + i + 1]

        def nflg(k, i):
            return LP.ap[:, 40 + 4 * k + i:40 + 4 * k + i + 1]

        GBC = A.f32(D, "gbc")

        NT_X = [A.f32(D, "xt%d" % i) for i in range(2)]
        NT_J = A.bf(D, "junk")
        NT_H = [A.bf(D, "hb%d" % i) for i in range(2)]
        NT_Sl = [A.f32(2, "nstat%d" % i) for i in range(2)]
        nt_i = [0]

        def norm_T(xbuf, x_ap, n, dstT, dstT_ap3, col0, evac_eng="dve", gb=None):
            gb = gb or GBC
            i = nt_i[0] % 2
            nt_i[0] += 1
            hb = NT_H[i]
            NT_S = NT_Sl[i]
            ss = NT_S.ap[0:n, 0:1]
            rs = NT_S.ap[0:n, 1:2]
            op("act", lambda e: e.activation(out=NT_J.ap[0:n], in_=x_ap, func=AF.Square, accum_out=ss),
               reads=[xbuf], writes=[NT_J, NT_S])
            op("dve", lambda e: e.tensor_scalar(out=rs, in0=ss, scalar1=1.0 / D, scalar2=EPS, op0=ALU.mult, op1=ALU.add),
               reads=[NT_S], writes=[NT_S])
            op("pool", lambda e: e.tensor_tensor(out=rs, in0=rs, in1=NEGH.ap[0:n, 0:1], op=ALU.pow),
               reads=[NT_S, NEGH], writes=[NT_S])
            op("dve", lambda e: e.scalar_tensor_tensor(out=hb.ap[0:n], in0=x_ap, scalar=rs, in1=gb.ap[0:n],
                                                       op0=ALU.mult, op1=ALU.mult),
               reads=[xbuf, NT_S, gb], writes=[hb])
            pb = PS()
            pbf = pb.ap.bitcast(BF16)

            def tr(e):
                ins = None
                for kc in range(8):
                    ins = e.transpose(pbf[:, kc * 128:kc * 128 + n], hb.ap[0:n, kc * 128:(kc + 1) * 128],
                                      IDENT.ap[0:n, 0:n])
                return ins
            op("pe", tr, reads=[hb, IDENT], writes=[pb])
            src = pbf.rearrange("p (k t) -> p k t", k=8)[:, :, 0:n]
            dst = dstT_ap3[:, :, col0:col0 + n]
            if evac_eng == "act":
                op("act", lambda e: e.activation(out=dst, in_=src, func=AF.Copy), reads=[pb], writes=[dstT])
            else:
                op("dve", lambda e: e.tensor_copy(out=dst, in_=src), reads=[pb], writes=[dstT])

        def fm_rstd(sq_list, rows_out, n, dim, SCR, pbuf=None):
            pb = pbuf if pbuf is not None else PS()

            def mm(e):
                ins = None
                for i, (b, ap, rk, base) in enumerate(sq_list):
                    ins = e.matmul(pb.ap[0:rows_out, 0:n], lhsT=ONESB.ap[base:base + rk, 0:rows_out], rhs=ap,
                                   start=(i == 0), stop=(i == len(sq_list) - 1))
                return ins
            op("pe", mm, reads=[b for b, _, _, _ in sq_list] + [ONESB], writes=[pb])
            rb = SCR()
            op("act", lambda e: e.activation(out=rb.ap[0:rows_out, 0:n], in_=pb.ap[0:rows_out, 0:n], func=AF.Ln,
                                             scale=1.0 / dim, bias=c_eps[0:rows_out]),
               reads=[pb, CONST], writes=[rb])
            op("act", lambda e: e.activation(out=rb.ap[0:rows_out, 0:n], in_=rb.ap[0:rows_out, 0:n], func=AF.Exp,
                                             scale=-0.5), reads=[rb], writes=[rb])
            return rb

        POSB = [A.f32(512, "posb%d" % i) for i in range(2)]
        posb_i = [0]

        def rope_tables(pos_src_ap, n, SCR, out_buf=None, out_s=None, out_c=None):
            pi = POSB[posb_i[0] % 2]
            posb_i[0] += 1
            pi_ap = pi.ap.bitcast(I32)
            load(pi, pi_ap[64:96, 0:n], pos_src_ap, eng="sp")
            y = SCR()
            op("dve", lambda e: e.tensor_copy(out=y.ap[64:96, 0:n], in_=pi_ap[64:96, 0:n]), reads=[pi], writes=[y])
            y2 = SCR()
            yc = SCR()
            op("dve", lambda e: e.tensor_scalar(out=y2.ap[64:96, 0:n], in0=y.ap[64:96, 0:n],
                                                scalar1=ppc("invf", 0, slice(64, 96)), scalar2=None, op0=ALU.mult),
               reads=[y, PP], writes=[y2])
            op("dve", lambda e: e.tensor_scalar(out=yc.ap[64:96, 0:n], in0=y2.ap[64:96, 0:n], scalar1=0.25,
                                                scalar2=None, op0=ALU.add), reads=[y2], writes=[yc])
            res = []
            for yy, dst in ((y2, out_s), (yc, out_c)):
                ti = SCR()
                ti_ap = ti.ap.bitcast(I32)
                op("dve", lambda e, yy=yy, ti_ap=ti_ap: e.tensor_copy(out=ti_ap[64:96, 0:n], in_=yy.ap[64:96, 0:n]),
                   reads=[yy], writes=[ti])
                tf = SCR()
                op("dve", lambda e, ti_ap=ti_ap, tf=tf: e.tensor_copy(out=tf.ap[64:96, 0:n], in_=ti_ap[64:96, 0:n]),
                   reads=[ti], writes=[tf])
                op("dve", lambda e, yy=yy, tf=tf: e.tensor_tensor(out=yy.ap[64:96, 0:n], in0=yy.ap[64:96, 0:n],
                                                                  in1=tf.ap[64:96, 0:n], op=ALU.subtract),
                   reads=[yy, tf], writes=[yy])
                if dst is None:
                    o_b, o_ap = yy, yy.ap[64:96, 0:n]
                else:
                    o_b, o_ap = out_buf, dst
                op("act", lambda e, yy=yy, o_ap=o_ap: e.activation(out=o_ap, in_=yy.ap[64:96, 0:n], func=AF.Sin,
                                                                   scale=2.0 * np.pi * 0.999999),
                   reads=[yy], writes=[o_b])
                res.append((o_b, o_ap))
            return res

        KM = A.bf(4 * MEM, "km")
        KM3 = KM.ap.rearrange("p (h t) -> p h t", h=4)
        VM = A.bf(2 * 512, "vm")
        VM3 = VM.ap.rearrange("p (t c) -> p t c", t=2)
        m0 = A.mark()
        scr0 = [A.f32(512, "scr0%d" % i) for i in range(4)]
        s0i = [0]

        def SCRD():
            b = scr0[s0i[0] % len(scr0)]
            s0i[0] += 1
            return b
        if stop == '0a':
            S.emit(nc)
            return nc
        load(GBC, GBC.ap, gvec[2, :].partition_broadcast(128))
        mW = A.mark()
        WMKV = A.bf(8 * 1024, "wmkv")
        WMKV3 = WMKV.ap.rearrange("p (k c) -> p k c", k=8)
        load(WMKV, WMKV3, w_mkv.rearrange("(k p) c -> p k c", p=128))
        MEMT = A.bf(8 * MEM, "memt")
        MEMT3 = MEMT.ap.rearrange("p (k t) -> p k t", k=8)
        for mt in range(2):
            xb = NT_X[nt_i[0] % 2]
            load(xb, xb.ap, mem_d[mt * 128:(mt + 1) * 128, :], eng="sp")
            norm_T(xb, xb.ap, 128, MEMT, MEMT3, mt * 128)
        if stop == '0b':
            S.emit(nc)
            return nc
        for h in range(4):
            pk = PS()

            def mmk(e, pk=pk, h=h):
                ins = None
                for kc in range(8):
                    ins = e.matmul(pk.ap[:, 0:MEM], lhsT=WMKV3[:, kc, h * 128:(h + 1) * 128], rhs=MEMT3[:, kc, :],
                                   start=(kc == 0), stop=(kc == 7))
                return ins
            op("pe", mmk, reads=[WMKV, MEMT], writes=[pk])
            sq = SCRD()
            sq_ap = sq.ap.bitcast(BF16)
            op("act", lambda e, sq_ap=sq_ap, pk=pk: e.activation(out=sq_ap[:, 0:MEM], in_=pk.ap[:, 0:MEM], func=AF.Square),
               reads=[pk], writes=[sq])
            rb = fm_rstd([(sq, sq_ap[:, 0:MEM], 128, 0)], 128, MEM, 128.0, SCRD)
            op("dve", lambda e, pk=pk, rb=rb, h=h: e.scalar_tensor_tensor(
                out=KM3[:, h, :], in0=pk.ap[:, 0:MEM], scalar=ppc("gmk"), in1=rb.ap[:, 0:MEM], op0=ALU.mult, op1=ALU.mult),
               reads=[pk, rb, PP], writes=[KM])
        if stop == '0c':
            S.emit(nc)
            return nc
        for mt in range(2):
            pv = PS()

            def mmv2(e, pv=pv, mt=mt):
                ins = None
                for kc in range(8):
                    ins = e.matmul(pv.ap[:, :], lhsT=MEMT3[:, kc, mt * 128:(mt + 1) * 128], rhs=WMKV3[:, kc, 512:1024],
                                   start=(kc == 0), stop=(kc == 7))
                return ins
            op("pe", mmv2, reads=[WMKV, MEMT], writes=[pv])
            op("act", lambda e, pv=pv, mt=mt: e.activation(out=VM3[:, mt, :], in_=pv.ap[:, :], func=AF.Copy),
               reads=[pv], writes=[VM])
        if stop == '0d':
            S.emit(nc)
            return nc
        S.barrier_all()
        A.release(m0)
        mP = A.top()
        if stop == '0':
            S.emit(nc)
            return nc

        MIXL = A.bf(4 * NOWN, "mixl")
        MIXL3 = MIXL.ap.rearrange("p (c t) -> p c t", c=4)
        CQN = A.bf(2 * NOWN, "cqn")
        CQN3 = CQN.ap.rearrange("p (c t) -> p c t", c=2)
        ckvn_start = A.top()
        CKVN = A.bf(NKEY, "ckvn")
        ETS = A.bf(NKEY, "e_tsq")
        ets_end = A.top()
        mA = A.mark()

        WIN = A.bf(8 * 1440, "win")
        WIN3 = WIN.ap.rearrange("p (k c) -> p k c", k=8)
        for kc in range(8):
            load(WIN, WIN3[:, kc, :], w_in[kc * 128:(kc + 1) * 128, :])
        WKR = A.bf(8 * 192, "wkrpad")
        WKR3 = WKR.ap.rearrange("p (k c) -> p k c", k=8)
        op("pool", lambda e: e.memset(WKR.ap, 0.0), writes=[WKR])
        op("dve", lambda e: e.tensor_copy(out=WKR3[:, :, 64:96], in_=WIN3[:, :, 1408:1440]), reads=[WIN], writes=[WKR])
        op("dve", lambda e: e.tensor_scalar(out=WKR3[:, :, 160:176], in0=WIN3[:, :, 1424:1440], scalar1=-1.0,
                                            scalar2=None, op0=ALU.mult), reads=[WIN], writes=[WKR])
        op("dve", lambda e: e.tensor_copy(out=WKR3[:, :, 176:192], in_=WIN3[:, :, 1408:1424]), reads=[WIN], writes=[WKR])
        WGL = [A.bf(4 * 2 * 128, "wgates%d" % i) for i in range(2)]
        WGL4 = [w.ap.rearrange("p (c g o) -> p c g o", c=4, g=2) for w in WGL]
        HTL = [A.bf(8 * 512, "ht%d" % i) for i in range(2)]
        HTL3 = [h_.ap.rearrange("p (k t) -> p k t", k=8) for h_ in HTL]
        XR = A.f32(4 * 515, "xr")
        XR3 = XR.ap.rearrange("p (c t) -> p c t", c=4)
        HF = A.bf(4 * (CH + 1), "hf")
        HF3 = HF.ap.rearrange("p (c t) -> p c t", c=4)
        GG = A.bf(4 * NOWN, "gelu")
        GG3 = GG.ap.rearrange("p (c t) -> p c t", c=4)
        CAR = A.f32(64, "carry")
        op("pool", lambda e: e.memset(CAR.ap, 0.0), writes=[CAR])
        HSC = A.f32(4 * 512, "hscan")
        HSC3 = HSC.ap.rearrange("p (c t) -> p c t", c=4)
        HE = A.f32(16, "hextra")
        scrA = [A.f32(512, "scrA%d" % i) for i in range(12)]
        sai = [0]

        def SCRA():
            b = scrA[sai[0] % len(scrA)]
            sai[0] += 1
            return b

        load(GBC, GBC.ap, gvec[0, :].partition_broadcast(128))

        chk('A0')

        def lru_cols(k, ct, name):
            o, n = PPL[name]
            return PP.ap[:, o + 4 * k + ct:o + 4 * k + ct + 1]

        def phaseA_front(k, j, n, mini, hb):
            HT = HTL[hb]
            HT3 = HTL3[hb]
            if j == 0 and not mini:
                load(WGL[k % 2], WGL4[k % 2][:, :, 0, :], wa_d[k].rearrange("c i o -> i c o"))
                load(WGL[k % 2], WGL4[k % 2][:, :, 1, :], wi_d[k].rearrange("c i o -> i c o"))
            ntile = (n + 127) // 128
            for i in range(ntile):
                rows = min(128, n - i * 128)
                xb = NT_X[nt_i[0] % 2]
                if mini:
                    src = xh[0:1, :] if k == 3 else xh[1:2, :]
                else:
                    src = xs[k, j * 512 + i * 128:j * 512 + i * 128 + rows, :]
                load(xb, xb.ap[0:rows], src, eng="sp")
                norm_T(xb, xb.ap[0:rows], rows, HT, HT3, i * 128, evac_eng="dve")

        def phaseA_block(k, j, n, mini, hb):
            HT = HTL[hb]
            HT3 = HTL3[hb]
            WG = WGL[k % 2]
            WG4 = WGL4[k % 2]
            do_kv = (k <= 3) and not mini
            do_own = (k == 3) or (k == 4 and mini)
            if mini:
                own0 = CH if k == 3 else CH + 1
            else:
                own0 = j * 512
            if j == 0 and not mini:
                op("dve", lambda e: e.tensor_scalar(out=CAR.ap[:, 36:48], in0=CAR.ap[:, 8:20], scalar1=flg(k, 0),
                                                    scalar2=None, op0=ALU.mult), reads=[CAR, PP], writes=[CAR])
                op("dve", lambda e: e.scalar_tensor_tensor(out=CAR.ap[:, 36:48], in0=CAR.ap[:, 20:32], scalar=flg(k, 1),
                                                           in1=CAR.ap[:, 36:48], op0=ALU.mult, op1=ALU.add),
                   reads=[CAR, PP], writes=[CAR])
                op("dve", lambda e: e.tensor_scalar(out=CAR.ap[:, 32:36], in0=CAR.ap[:, 0:4], scalar1=flg(k, 0),
                                                    scalar2=None, op0=ALU.mult), reads=[CAR, PP], writes=[CAR])
                op("dve", lambda e: e.scalar_tensor_tensor(out=CAR.ap[:, 32:36], in0=CAR.ap[:, 4:8], scalar=flg(k, 1),
                                                           in1=CAR.ap[:, 32:36], op0=ALU.mult, op1=ALU.add),
                   reads=[CAR, PP], writes=[CAR])
                if k == 3:
                    op("dve", lambda e: e.tensor_copy(out=CAR.ap[:, 48:52], in_=CAR.ap[:, 32:36]), reads=[CAR], writes=[CAR])
                if k == 4:
                    op("dve", lambda e: e.tensor_copy(out=CAR.ap[:, 52:56], in_=CAR.ap[:, 32:36]), reads=[CAR], writes=[CAR])
                op("dve", lambda e: e.tensor_copy(out=XR3[:, :, 0:3],
                                                  in_=CAR.ap[:, 36:48].rearrange("p (c t) -> p c t", c=4)),
                   reads=[CAR], writes=[XR])
            else:
                op("dve", lambda e: e.tensor_copy(out=XR3[:, :, 0:3], in_=XR3[:, :, 512:515]), reads=[XR], writes=[XR])
            for ct in range(4):
                pb = PS()

                def mm(e, pb=pb, ct=ct):
                    ins = None
                    for kc in range(8):
                        ins = e.matmul(pb.ap[:, 0:n], lhsT=WIN3[:, kc, ct * 128:(ct + 1) * 128], rhs=HT3[:, kc, 0:n],
                                       start=(kc == 0), stop=(kc == 7))
                    return ins
                op("pe", mm, reads=[WIN, HT], writes=[pb])
                op("act", lambda e, pb=pb, ct=ct: e.activation(out=XR3[:, ct, 3:3 + n], in_=pb.ap[:, 0:n], func=AF.Copy),
                   reads=[pb], writes=[XR])
            chk('A1')
            for pr in range(2):
                cts = (2 * pr, 2 * pr + 1)
                B_ = {}
                for ct in cts:
                    cwo = PPL["cw"][0] + (k * 4 + ct) * 4
                    xc = SCRA()
                    op("act", lambda e, xc=xc, ct=ct, cwo=cwo: e.activation(
                        out=xc.ap[:, 0:n], in_=XR3[:, ct, 3:3 + n], func=AF.Identity,
                        scale=PP.ap[:, cwo + 3:cwo + 4], bias=lru_cols(k, ct, "cb")), reads=[XR, PP], writes=[xc])
                    for tp in range(3):
                        op("dve", lambda e, xc=xc, ct=ct, cwo=cwo, tp=tp: e.scalar_tensor_tensor(
                            out=xc.ap[:, 0:n], in0=XR3[:, ct, tp:tp + n], scalar=PP.ap[:, cwo + tp:cwo + tp + 1],
                            in1=xc.ap[:, 0:n], op0=ALU.mult, op1=ALU.add), reads=[XR, PP, xc], writes=[xc])
                    xcb = SCRA()
                    xcb_ap = xcb.ap.bitcast(BF16)
                    op("dve", lambda e, xc=xc, xcb_ap=xcb_ap: e.tensor_copy(out=xcb_ap[:, 0:n], in_=xc.ap[:, 0:n]),
                       reads=[xc], writes=[xcb])
                    B_[ct] = dict(xc=xc, xcb=xcb, xcb_ap=xcb_ap)
                for ct in cts:
                    d = B_[ct]
                    pa = PS()
                    op("pe", lambda e, pa=pa, ct=ct, xcb_ap=d["xcb_ap"]: e.matmul(
                        pa.ap[:, 0:n], lhsT=WG4[:, ct, 0, :], rhs=xcb_ap[:, 0:n], start=True, stop=True),
                       reads=[WG, d["xcb"]], writes=[pa])
                    pi_ = PS()
                    op("pe", lambda e, pi_=pi_, ct=ct, xcb_ap=d["xcb_ap"]: e.matmul(
                        pi_.ap[:, 0:n], lhsT=WG4[:, ct, 1, :], rhs=xcb_ap[:, 0:n], start=True, stop=True),
                       reads=[WG, d["xcb"]], writes=[pi_])
                    d["pa"] = pa
                    d["pi"] = pi_
                for ct in cts:
                    d = B_[ct]
                    rr = SCRA()
                    ig = SCRA()
                    op("act", lambda e, rr=rr, pa=d["pa"], ct=ct: e.activation(out=rr.ap[:, 0:n], in_=pa.ap[:, 0:n], func=AF.Tanh,
                                                                          bias=LP.ap[:, 60 + 4 * k + ct:61 + 4 * k + ct], scale=0.5),
                       reads=[d["pa"], LP], writes=[rr])
                    op("act", lambda e, ig=ig, pi_=d["pi"], ct=ct: e.activation(out=ig.ap[:, 0:n], in_=pi_.ap[:, 0:n], func=AF.Tanh,
                                                                            bias=LP.ap[:, 80 + 4 * k + ct:81 + 4 * k + ct], scale=0.5),
                       reads=[d["pi"], LP], writes=[ig])
                    d["rr"] = rr
                    d["ig"] = ig
                for ct in cts:
                    d = B_[ct]
                    aa = SCRA()
                    mm_ = SCRA()
                    cs = LP.ap[:, 4 * k + ct:4 * k + ct + 1]
                    hcs = LP.ap[:, 20 + 4 * k + ct:20 + 4 * k + ct + 1]
                    op("act", lambda e, aa=aa, rr=d["rr"], hcs=hcs: e.activation(out=aa.ap[:, 0:n], in_=rr.ap[:, 0:n], func=AF.Exp,
                                                                             scale=hcs, bias=hcs), reads=[d["rr"], LP], writes=[aa])
                    op("act", lambda e, mm_=mm_, rr=d["rr"], cs=cs: e.activation(out=mm_.ap[:, 0:n], in_=rr.ap[:, 0:n], func=AF.Exp,
                                                                             scale=cs, bias=cs), reads=[d["rr"], LP], writes=[mm_])
                    op("dve", lambda e, mm_=mm_: e.tensor_scalar(out=mm_.ap[:, 0:n], in0=mm_.ap[:, 0:n], scalar1=-0.25, scalar2=0.25,
                                                                 op0=ALU.mult, op1=ALU.add), reads=[mm_], writes=[mm_])
                    op("dve", lambda e, ig=d["ig"], xc=d["xc"]: e.scalar_tensor_tensor(out=ig.ap[:, 0:n], in0=ig.ap[:, 0:n], scalar=1.0,
                                                                                   in1=xc.ap[:, 0:n], op0=ALU.add, op1=ALU.mult),
                       reads=[d["ig"], d["xc"]], writes=[d["ig"]])
                    d["aa"] = aa
                    d["mm"] = mm_
                for ct in cts:
                    d = B_[ct]
                    op("act", lambda e, mm_=d["mm"]: e.activation(out=mm_.ap[:, 0:n], in_=mm_.ap[:, 0:n], func=AF.Sqrt),
                       reads=[d["mm"]], writes=[d["mm"]])
                for ct in cts:
                    d = B_[ct]
                    op("dve", lambda e, ig=d["ig"], mm_=d["mm"]: e.tensor_tensor(out=ig.ap[:, 0:n], in0=ig.ap[:, 0:n], in1=mm_.ap[:, 0:n],
                                                                              op=ALU.mult), reads=[d["ig"], d["mm"]], writes=[d["ig"]])
                    if mini or j > 0:
                        init = CAR.ap[:, 56 + ct:57 + ct]
                    else:
                        init = CAR.ap[:, 32 + ct:33 + ct]
                    op("dve", lambda e, aa=d["aa"], ig=d["ig"], ct=ct, init=init: e.tensor_tensor_scan(
                        out=HSC3[:, ct, 0:n], data0=aa.ap[:, 0:n], data1=ig.ap[:, 0:n], initial=init,
                        op0=ALU.mult, op1=ALU.add), reads=[d["aa"], d["ig"], CAR], writes=[HSC])
            if not mini:
                op("dve", lambda e: e.tensor_copy(out=CAR.ap[:, 56:60], in_=HSC3[:, :, n - 1]), reads=[HSC], writes=[CAR])
            chk('A2')
            if k == 3:
                op("dve", lambda e: e.tensor_copy(out=HF3[:, :, own0 if not mini else CH:(own0 if not mini else CH) + n],
                                                   in_=HSC3[:, :, 0:n]), reads=[HSC], writes=[HF])
            if k <= 2 and (not mini) and j == NB - 1:
                for (st_o, u_i) in ((0, 2), (4, 3)):
                    op("dve", lambda e, st_o=st_o, u_i=u_i: e.tensor_scalar(
                        out=CAR.ap[:, st_o:st_o + 4], in0=CAR.ap[:, st_o:st_o + 4], scalar1=nflg(k, u_i), scalar2=None,
                        op0=ALU.mult), reads=[CAR, LP], writes=[CAR])
                    op("dve", lambda e, st_o=st_o, u_i=u_i: e.scalar_tensor_tensor(
                        out=CAR.ap[:, st_o:st_o + 4], in0=HSC3[:, :, n - 1], scalar=flg(k, u_i), in1=CAR.ap[:, st_o:st_o + 4],
                        op0=ALU.mult, op1=ALU.add), reads=[CAR, HSC, PP], writes=[CAR])
                for (h_o, u_i) in ((8, 2), (20, 3)):
                    hv = CAR.ap[:, h_o:h_o + 12].rearrange("p (c t) -> p c t", c=4)
                    op("dve", lambda e, hv=hv, u_i=u_i: e.tensor_scalar(
                        out=hv, in0=hv, scalar1=nflg(k, u_i), scalar2=None, op0=ALU.mult), reads=[CAR, LP], writes=[CAR])
                    op("dve", lambda e, hv=hv, u_i=u_i: e.scalar_tensor_tensor(
                        out=hv, in0=XR3[:, :, 512:515], scalar=flg(k, u_i), in1=hv, op0=ALU.mult, op1=ALU.add),
                       reads=[CAR, XR, PP], writes=[CAR])
            chk('A3')
            if do_kv:
                key0 = k * CH + j * 512
                pb = PS()

                def mmkv(e, pb=pb):
                    ins = None
                    for kc in range(8):
                        ins = e.matmul(pb.ap[:, 0:n], lhsT=WIN3[:, kc, 1280:1408], rhs=HT3[:, kc, 0:n],
                                       start=(kc == 0), stop=(kc == 7))
                    return ins
                op("pe", mmkv, reads=[WIN, HT], writes=[pb])
                cf = SCRA()
                sq = SCRA()
                sq_ap = sq.ap.bitcast(BF16)
                op("act", lambda e, cf=cf, pb=pb: e.activation(out=cf.ap[:, 0:n], in_=pb.ap[:, 0:n], func=AF.Copy),
                   reads=[pb], writes=[cf])
                op("act", lambda e, sq_ap=sq_ap, pb=pb: e.activation(out=sq_ap[:, 0:n], in_=pb.ap[:, 0:n], func=AF.Square),
                   reads=[pb], writes=[sq])
                rb = fm_rstd([(sq, sq_ap[:, 0:n], 128, 0)], 128, n, 128.0, SCRA)
                op("dve", lambda e, cf=cf, rb=rb: e.scalar_tensor_tensor(
                    out=CKVN.ap[:, key0:key0 + n], in0=cf.ap[:, 0:n], scalar=ppc("gkv"), in1=rb.ap[:, 0:n],
                    op0=ALU.mult, op1=ALU.mult), reads=[cf, rb, PP], writes=[CKVN])
                chk('A3a')
                pt = PS()
                prt = PS()

                def mmt(e, pt=pt, o=0):
                    ins = None
                    for kc in range(8):
                        ins = e.matmul(pt.ap[0:96, 0:n], lhsT=WKR3[:, kc, o:o + 96], rhs=HT3[:, kc, 0:n],
                                       start=(kc == 0), stop=(kc == 7))
                    return ins
                op("pe", lambda e: mmt(e, pt, 0), reads=[WKR, HT], writes=[pt])
                op("pe", lambda e: mmt(e, prt, 96), reads=[WKR, HT], writes=[prt])
                chk('A3b')
                (sb, s_ap), (cb_, c_ap) = rope_tables(posk[k, :, j * 512:j * 512 + n], n, SCRA)
                chk('A3c')
                tq = SCRA()
                tq_ap = tq.ap.bitcast(BF16)
                op("act", lambda e, pt=pt, tq_ap=tq_ap: e.activation(out=tq_ap[64:96, 0:n], in_=pt.ap[64:96, 0:n], func=AF.Square),
                   reads=[pt], writes=[tq])
                op("dve", lambda e, tq_ap=tq_ap: e.tensor_copy(out=ETS.ap[0:32, key0:key0 + n], in_=tq_ap[64:96, 0:n]),
                   reads=[tq], writes=[ETS])
                e1 = SCRA()
                e2 = SCRA()
                op("dve", lambda e, e1=e1, pt=pt, c_ap=c_ap: e.scalar_tensor_tensor(
                    out=e1.ap[64:96, 0:n], in0=pt.ap[64:96, 0:n], scalar=ppc("gk", 0, slice(64, 96)), in1=c_ap,
                    op0=ALU.mult, op1=ALU.mult), reads=[pt, cb_, PP, tq], writes=[e1])
                op("dve", lambda e, e2=e2, prt=prt, s_ap=s_ap: e.scalar_tensor_tensor(
                    out=e2.ap[64:96, 0:n], in0=prt.ap[64:96, 0:n], scalar=ppc("gkr", 0, slice(64, 96)), in1=s_ap,
                    op0=ALU.mult, op1=ALU.mult), reads=[prt, sb, PP], writes=[e2])
                op("dve", lambda e, e1=e1, e2=e2: e.tensor_tensor(out=ETS.ap[64:96, key0:key0 + n], in0=e1.ap[64:96, 0:n],
                                                                 in1=e2.ap[64:96, 0:n], op=ALU.add),
                   reads=[e1, e2], writes=[ETS])
            chk('A4')
            if do_own:
                for ct in range(4):
                    pb = PS()

                    def mmy(e, pb=pb, ct=ct):
                        ins = None
                        for kc in range(8):
                            ins = e.matmul(pb.ap[:, 0:n], lhsT=WIN3[:, kc, 512 + ct * 128:512 + (ct + 1) * 128],
                                           rhs=HT3[:, kc, 0:n], start=(kc == 0), stop=(kc == 7))
                        return ins
                    op("pe", mmy, reads=[WIN, HT], writes=[pb])
                    u = SCRA()
                    w = SCRA()
                    op("act", lambda e, u=u, pb=pb: e.activation(out=u.ap[:, 0:n], in_=pb.ap[:, 0:n], func=AF.Copy, scale=0.5),
                       reads=[pb], writes=[u])
                    op("act", lambda e, w=w, pb=pb: e.activation(out=w.ap[:, 0:n], in_=pb.ap[:, 0:n], func=AF.Square),
                       reads=[pb], writes=[w])
                    op("dve", lambda e, w=w: e.tensor_scalar(out=w.ap[:, 0:n], in0=w.ap[:, 0:n], scalar1=2.0 * 0.044715, scalar2=2.0,
                                                             op0=ALU.mult, op1=ALU.add), reads=[w], writes=[w])
                    op("dve", lambda e, w=w, u=u: e.tensor_tensor(out=w.ap[:, 0:n], in0=w.ap[:, 0:n], in1=u.ap[:, 0:n], op=ALU.mult),
                       reads=[w, u], writes=[w])
                    op("act", lambda e, w=w: e.activation(out=w.ap[:, 0:n], in_=w.ap[:, 0:n], func=AF.Tanh,
                                                          scale=0.7978845608028654), reads=[w], writes=[w])
                    op("dve", lambda e, w=w, u=u, ct=ct: e.scalar_tensor_tensor(out=GG3[:, ct, own0:own0 + n], in0=w.ap[:, 0:n],
                                                                             scalar=1.0, in1=u.ap[:, 0:n], op0=ALU.add, op1=ALU.mult),
                       reads=[w, u], writes=[GG])
                cfs = []
                sqs = []
                for c2 in range(2):
                    pb = PS()

                    def mmq(e, pb=pb, c2=c2):
                        ins = None
                        for kc in range(8):
                            ins = e.matmul(pb.ap[:, 0:n], lhsT=WIN3[:, kc, 1024 + c2 * 128:1024 + (c2 + 1) * 128],
                                           rhs=HT3[:, kc, 0:n], start=(kc == 0), stop=(kc == 7))
                        return ins
                    op("pe", mmq, reads=[WIN, HT], writes=[pb])
                    cf = SCRA()
                    sq = SCRA()
                    sq_ap = sq.ap.bitcast(BF16)
                    op("act", lambda e, cf=cf, pb=pb: e.activation(out=cf.ap[:, 0:n], in_=pb.ap[:, 0:n], func=AF.Copy),
                       reads=[pb], writes=[cf])
                    op("act", lambda e, sq_ap=sq_ap, pb=pb: e.activation(out=sq_ap[:, 0:n], in_=pb.ap[:, 0:n], func=AF.Square),
                       reads=[pb], writes=[sq])
                    cfs.append(cf)
                    sqs.append((sq, sq_ap[:, 0:n], 128, 0))
                rb = fm_rstd(sqs, 128, n, 256.0, SCRA)
                for c2 in range(2):
                    op("dve", lambda e, c2=c2, cf=cfs[c2], rb=rb: e.scalar_tensor_tensor(
                        out=CQN3[:, c2, own0:own0 + n], in0=cf.ap[:, 0:n], scalar=ppc("gqa", c2), in1=rb.ap[:, 0:n],
                        op0=ALU.mult, op1=ALU.mult), reads=[cf, rb, PP], writes=[CQN])
            if k == 4 and not mini:
                lo = CH - 512 * (j + 1)
                lru_combine(lambda ct: HF3[:, ct, lo:lo + n], HF, lambda ct: HSC3[:, ct, 0:n][:, ::-1], HSC, lo, n)

        def lru_combine(hf_ap, hf_buf, hb_ap, hb_buf, own0, n):
            los = []
            sqs = []
            for ct in range(4):
                lo_ = SCRA()
                op("dve", lambda e, lo_=lo_, ct=ct: e.tensor_tensor(out=lo_.ap[:, 0:n], in0=hf_ap(ct), in1=hb_ap(ct), op=ALU.add),
                   reads=[hf_buf, hb_buf], writes=[lo_])
                op("dve", lambda e, lo_=lo_, ct=ct: e.tensor_tensor(out=lo_.ap[:, 0:n], in0=lo_.ap[:, 0:n],
                                                                  in1=GG3[:, ct, own0:own0 + n], op=ALU.mult),
                   reads=[lo_, GG], writes=[lo_])
                sq = SCRA()
                sq_ap = sq.ap.bitcast(BF16)
                op("act", lambda e, sq_ap=sq_ap, lo_=lo_: e.activation(out=sq_ap[:, 0:n], in_=lo_.ap[:, 0:n], func=AF.Square),
                   reads=[lo_], writes=[sq])
                los.append(lo_)
                sqs.append((sq, sq_ap[:, 0:n], 128, 0))
            rb = fm_rstd(sqs, 128, n, 512.0, SCRA)
            for ct in range(4):
                op("dve", lambda e, ct=ct, lo_=los[ct], rb=rb: e.scalar_tensor_tensor(
                    out=MIXL3[:, ct, own0:own0 + n], in0=lo_.ap[:, 0:n], scalar=ppc("glru", ct), in1=rb.ap[:, 0:n],
                    op0=ALU.mult, op1=ALU.mult), reads=[lo_, rb, PP], writes=[MIXL])

        blks = []
        for k in range(5):
            for j in range(NB):
                blks.append((k, j, 512, False))
            if k >= 3:
                blks.append((k, NB, 1, True))
        phaseA_front(*blks[0], 0)
        for bi, (k, j, n, mini) in enumerate(blks):
            if bi + 1 < len(blks):
                phaseA_front(*blks[bi + 1], (bi + 1) % 2)
            phaseA_block(k, j, n, mini, bi % 2)
            if mini and k == 3:
                op("dve", lambda e: e.tensor_copy(out=HE.ap[:, 0:4], in_=HSC3[:, :, 0]), reads=[HSC], writes=[HE])
            if mini and k == 4:
                op("dve", lambda e: e.tensor_copy(out=HE.ap[:, 4:8], in_=HSC3[:, :, 0]), reads=[HSC], writes=[HE])
        chk('A6')
        HE3 = HE.ap[:, 8:16].rearrange("p (c t) -> p c t", c=4)
        op("dve", lambda e: e.tensor_tensor(out=HE3[:, :, 0], in0=HE.ap[:, 0:4], in1=CAR.ap[:, 52:56], op=ALU.add),
           reads=[HE, CAR], writes=[HE])
        op("dve", lambda e: e.tensor_tensor(out=HE3[:, :, 1], in0=HE.ap[:, 4:8], in1=CAR.ap[:, 48:52], op=ALU.add),
           reads=[HE, CAR], writes=[HE])
        ZERO = SCRA()
        op("pool", lambda e: e.memset(ZERO.ap[:, 0:8], 0.0), writes=[ZERO])
        lru_combine(lambda ct: HE3[:, ct, :], HE, lambda ct: ZERO.ap[:, 0:2], ZERO, CH, 2)
        if "mixl" in dbg_d:
            TMPD = A.f32(4 * NOWN, "tmpd")
            op("dve", lambda e: e.tensor_copy(out=TMPD.ap, in_=MIXL.ap), reads=[MIXL], writes=[TMPD])
            dump("mixl", TMPD, TMPD.ap)
        if "ckvn" in dbg_d:
            TMPD2 = A.f32(NKEY, "tmpd2")
            op("dve", lambda e: e.tensor_copy(out=TMPD2.ap, in_=CKVN.ap), reads=[CKVN], writes=[TMPD2])
            dump("ckvn", TMPD2, TMPD2.ap)
        if "ets" in dbg_d:
            TMPD3 = A.f32(NKEY, "tmpd3")
            op("dve", lambda e: e.tensor_copy(out=TMPD3.ap[0:96], in_=ETS.ap[0:96]), reads=[ETS], writes=[TMPD3])
            dump("ets", TMPD3, TMPD3.ap[0:96])

        S.barrier_all()
        A.release(mA)
        if stop == 'A':
            S.emit(nc)
            return nc

        WUQ = A.bf(2 * 768, "wuq")
        WUQ3 = WUQ.ap.rearrange("p (k c) -> p k c", k=2)
        for c2 in range(2):
            load(WUQ, WUQ3[:, c2, :], w_uq[c2 * 128:(c2 + 1) * 128, :])
        WUQR = A.bf(2 * 8 * 96, "wuqr")
        WUQR4 = WUQR.ap.rearrange("p (k h c) -> p k h c", k=2, h=8)
        WUQ4 = WUQ.ap.rearrange("p (k h c) -> p k h c", k=2, h=8)
        op("pool", lambda e: e.memset(WUQR.ap, 0.0), writes=[WUQR])
        for c2 in range(2):
            op("dve", lambda e, c2=c2: e.tensor_scalar(out=WUQR4[:, c2, :, 64:80], in0=WUQ4[:, c2, :, 80:96], scalar1=-1.0,
                                                       scalar2=None, op0=ALU.mult), reads=[WUQ], writes=[WUQR])
            op("dve", lambda e, c2=c2: e.tensor_copy(out=WUQR4[:, c2, :, 80:96], in_=WUQ4[:, c2, :, 64:80]),
               reads=[WUQ], writes=[WUQR])
        WUKV = A.bf(1024, "wukv")
        load(WUKV, WUKV.ap, w_ukv)
        ON = A.bf(8 * NOWN, "on")
        ON3 = ON.ap.rearrange("p (h t) -> p h t", h=8)
        TAB = A.bf(2 * NOWN, "qtab")
        mB = A.mark()
        KT = [A.bf(NKEY, "kt%d" % i) for i in range(2)]
        VV = [A.bf(NKT * 65, "v%d" % i) for i in range(2)]
        VV3 = [v.ap.rearrange("p (t c) -> p t c", c=65) for v in VV]
        QT = [A.bf(NOWN, "qt%d" % i) for i in range(2)]
        PT = [Buf(NT_H[i].ap[:, 0:512], NT_H[i].r) for i in range(2)] + [A.bf(512, "pt%d" % i) for i in range(2)]
        scrB = [Buf(NT_X[i].ap[:, 0:512], NT_X[i].r) for i in range(2)] + [A.f32(512, "scrB%d" % i) for i in range(6)]
        sbi = [0]

        def SCRB():
            b = scrB[sbi[0] % len(scrB)]
            sbi[0] += 1
            return b
        for v3, v in zip(VV3, VV):
            op("pool", lambda e, v3=v3: e.memset(v3[:, :, 64:65], 1.0), writes=[v])
        nqb = -(-NOWN // 512)
        half = NOWN // 2
        qb_base = half // nqb
        qb_sizes = [2 * (qb_base + (1 if i < half - qb_base * nqb else 0)) for i in range(nqb)]
        qblocks = [(sum(qb_sizes[:i]), qb_sizes[i]) for i in range(nqb)]
        for (q0, n) in qblocks:
            rope_tables(poso[:, q0:q0 + n], n, SCRB, out_buf=TAB, out_s=TAB.ap[64:96, q0:q0 + n],
                        out_c=TAB.ap[64:96, NOWN + q0:NOWN + q0 + n])
        pti = [0]

        RK = [A.f32(NKT, "rk%d" % i) for i in range(2)]
        SSK = psum[5]

        for kt_ in KT:
            op("dve", lambda e, kt_=kt_: e.tensor_copy(out=kt_.ap[64:96, :], in_=ETS.ap[64:96, :]), reads=[ETS], writes=[kt_])

        TSS = A.f32(NKT, "tss")

        def mm_tss(e):
            ins = None
            for t in range(NKT):
                ins = e.matmul(SSK.ap[:, t:t + 1], lhsT=ETS.ap[0:32, t * 128:(t + 1) * 128], rhs=ONESB.ap[0:32, 0:1],
                               start=True, stop=True)
            return ins
        op("pe", mm_tss, reads=[ETS, ONESB], writes=[SSK])
        op("dve", lambda e: e.tensor_copy(out=TSS.ap[:, 0:NKT], in_=SSK.ap[:, 0:NKT]), reads=[SSK], writes=[TSS])

        def kgen(h):
            kt = KT[h % 2]
            rk = RK[h % 2]
            pend_mms = []
            for kb in range(NKB):
                c0 = kb * 512
                pk = PS()
                op("pe", lambda e, pk=pk, c0=c0: e.matmul(pk.ap[0:64, :], lhsT=WUKV.ap[:, h * 128:h * 128 + 64],
                                                      rhs=CKVN.ap[:, c0:c0 + 512], start=True, stop=True),
                   reads=[WUKV, CKVN], writes=[pk])
                sq = SCRB()
                sq_ap = sq.ap.bitcast(BF16)
                op("act", lambda e, sq_ap=sq_ap, pk=pk: e.activation(out=sq_ap[0:64, 0:512], in_=pk.ap[0:64, :], func=AF.Square),
                   reads=[pk], writes=[sq])
                op("dve", lambda e, pk=pk, c0=c0: e.tensor_scalar(out=kt.ap[0:64, c0:c0 + 512], in0=pk.ap[0:64, :],
                                                                 scalar1=ppc("gk", 0, slice(0, 64)), scalar2=None, op0=ALU.mult),
                   reads=[pk, PP, sq], writes=[kt])

                def mms(e, sq_ap=sq_ap, kb=kb, c0=c0):
                    ins = None
                    for i in range(4):
                        t = kb * 4 + i
                        ins = e.matmul(SSK.ap[:, t:t + 1], lhsT=sq_ap[0:64, i * 128:(i + 1) * 128], rhs=ONESB.ap[0:64, 0:1],
                                       start=True, stop=True)
                    return ins
                if pend_mms:
                    pm, psq = pend_mms.pop(0)
                    op("pe", pm, reads=[psq, ETS, ONESB], writes=[SSK])
                pend_mms.append((mms, sq))
            while pend_mms:
                pm, psq = pend_mms.pop(0)
                op("pe", pm, reads=[psq, ETS, ONESB], writes=[SSK])
            op("dve", lambda e: e.scalar_tensor_tensor(out=rk.ap[:, 0:NKT], in0=SSK.ap[:, 0:NKT], scalar=96.0 * EPS,
                                                       in1=TSS.ap[:, 0:NKT], op0=ALU.add, op1=ALU.add),
               reads=[SSK, TSS], writes=[rk])
            op("act", lambda e: e.activation(out=rk.ap[:, 0:NKT], in_=rk.ap[:, 0:NKT], func=AF.Ln), reads=[rk], writes=[rk])
            op("act", lambda e: e.activation(out=rk.ap[:, 0:NKT], in_=rk.ap[:, 0:NKT], func=AF.Exp, scale=-0.5), reads=[rk], writes=[rk])

        def vgen(h):
            vv = VV[h % len(VV)]
            vv3 = VV3[h % len(VV)]
            for kb in range(NKB):
                c0 = kb * 512
                pv = PS()

                def mmv(e, pv=pv, c0=c0):
                    ins = None
                    for i in range(4):
                        ins = e.matmul(pv.ap[:, i * 64:(i + 1) * 64], lhsT=CKVN.ap[:, c0 + i * 128:c0 + (i + 1) * 128],
                                       rhs=WUKV.ap[:, h * 128 + 64:h * 128 + 128], start=True, stop=True)
                    return ins
                op("pe", mmv, reads=[CKVN, WUKV], writes=[pv])
                op("dve", lambda e, pv=pv, kb=kb: e.tensor_copy(out=vv3[:, kb * 4:(kb + 1) * 4, 0:64],
                                                             in_=pv.ap[:, 0:256].rearrange("p (t c) -> p t c", c=64)),
                   reads=[pv], writes=[vv])

        def do_head(h):
            kt = KT[h % 2]
            rk = RK[h % 2]
            vv = VV[h % len(VV)]
            vv3 = VV3[h % len(VV)]
            qt = QT[h % 2]
            qst = {}

            def qgenA(q0, n):
                pq = PS()
                pr = PS()

                def mmq(e):
                    ins = None
                    for c2 in range(2):
                        ins = e.matmul(pq.ap[0:96, 0:n], lhsT=WUQ3[:, c2, h * 96:(h + 1) * 96], rhs=CQN3[:, c2, q0:q0 + n],
                                       start=(c2 == 0), stop=(c2 == 1))
                    return ins

                def mmr(e):
                    ins = None
                    for c2 in range(2):
                        ins = e.matmul(pr.ap[0:96, 0:n], lhsT=WUQR4[:, c2, h, :], rhs=CQN3[:, c2, q0:q0 + n],
                                       start=(c2 == 0), stop=(c2 == 1))
                    return ins
                op("pe", mmq, reads=[WUQ, CQN], writes=[pq])
                op("pe", mmr, reads=[WUQR, CQN], writes=[pr])
                sq = SCRB()
                sq_ap = sq.ap.bitcast(BF16)
                op("act", lambda e: e.activation(out=sq_ap[0:96, 0:n], in_=pq.ap[0:96, 0:n], func=AF.Square),
                   reads=[pq], writes=[sq])
                e2 = SCRB()
                op("dve", lambda e: e.scalar_tensor_tensor(
                    out=e2.ap[64:96, 0:n], in0=pr.ap[64:96, 0:n], scalar=ppc("gqr", 0, slice(64, 96)),
                    in1=TAB.ap[64:96, q0:q0 + n], op0=ALU.mult, op1=ALU.mult), reads=[pr, TAB, PP], writes=[e2])
                qst[q0] = (pq, pr, sq, sq_ap, e2)

            def qgenB(q0, n):
                pq, pr, sq, sq_ap, e2 = qst[q0]
                rb = fm_rstd([(sq, sq_ap[0:96, 0:n], 96, 0)], 96, n, 96.0, SCRB, pbuf=pr)
                op("dve", lambda e: e.scalar_tensor_tensor(
                    out=qt.ap[0:64, q0:q0 + n], in0=pq.ap[0:64, 0:n], scalar=ppc("gq", 0, slice(0, 64)), in1=rb.ap[0:64, 0:n],
                    op0=ALU.mult, op1=ALU.mult), reads=[pq, rb, PP], writes=[qt])
                e1 = SCRB()
                op("dve", lambda e: e.scalar_tensor_tensor(
                    out=e1.ap[64:96, 0:n], in0=pq.ap[64:96, 0:n], scalar=ppc("gq", 0, slice(64, 96)),
                    in1=TAB.ap[64:96, NOWN + q0:NOWN + q0 + n], op0=ALU.mult, op1=ALU.mult), reads=[pq, TAB, PP, sq], writes=[e1])
                op("dve", lambda e: e.tensor_tensor(out=e1.ap[64:96, 0:n], in0=e1.ap[64:96, 0:n],
                                                    in1=e2.ap[64:96, 0:n], op=ALU.add),
                   reads=[e1, e2], writes=[e1])
                op("dve", lambda e: e.tensor_tensor(out=qt.ap[64:96, q0:q0 + n], in0=e1.ap[64:96, 0:n],
                                                    in1=rb.ap[64:96, 0:n], op=ALU.mult),
                   reads=[e1, rb], writes=[qt])
            qgenA(*qblocks[0])
            for qi in range(len(qblocks)):
                if qi + 1 < len(qblocks):
                    qgenA(*qblocks[qi + 1])
                qgenB(*qblocks[qi])
            scale = 96.0 ** -0.5
            fin_pending = []
            for (q0, n) in qblocks:
                po = PSACC()
                pend = []

                def qk(t, q0=q0, n=n):
                    pb = PS()
                    op("pe", lambda e, pb=pb, t=t: e.matmul(pb.ap[:, 0:n], lhsT=kt.ap[0:96, t * 128:(t + 1) * 128],
                                                        rhs=qt.ap[0:96, q0:q0 + n], start=True, stop=True),
                       reads=[kt, qt], writes=[pb])
                    return pb

                def expv(t, pb, q0=q0, n=n, po=po):
                    pt_ = PT[pti[0] % 4]
                    pti[0] += 1
                    op("act", lambda e, pb=pb, pt_=pt_, t=t: e.activation(out=pt_.ap[:, 0:n], in_=pb.ap[:, 0:n], func=AF.Exp,
                                                                     scale=rk.ap[:, t:t + 1]), reads=[pb, rk], writes=[pt_])
                    op("pe", lambda e, pt_=pt_, t=t: e.matmul(po.ap[0:65, 0:n], lhsT=vv3[:, t, :], rhs=pt_.ap[:, 0:n],
                                                          start=(t == 0), stop=(t == NKT - 1)),
                       reads=[vv, pt_], writes=[po])
                LOOK = 2
                DEFER = 8
                for t in range(NKT + LOOK):
                    if t < NKT:
                        pend.append((t, qk(t)))
                    if t >= LOOK:
                        tt, pb = pend.pop(0)
                        expv(tt, pb)
                    if t == DEFER and fin_pending:
                        fin_pending.pop(0)()
                rd = SCRB()
                op("dve", lambda e, rd=rd, po=po, n=n: e.reciprocal(out=rd.ap[64:65, 0:n], in_=po.ap[64:65, 0:n]),
                   reads=[po], writes=[rd])

                def fin(rd=rd, po=po, q0=q0, n=n):
                    pbc = PS()
                    op("pe", lambda e, pbc=pbc: e.matmul(pbc.ap[0:64, 0:n], lhsT=ONESF.ap[64:65, 0:64], rhs=rd.ap[64:65, 0:n],
                                                         start=True, stop=True), reads=[ONESF, rd], writes=[pbc])
                    oc = SCRB()
                    op("act", lambda e, oc=oc: e.activation(out=oc.ap[0:64, 0:n], in_=po.ap[0:64, 0:n], func=AF.Copy),
                       reads=[po, rd], writes=[oc])
                    op("dve", lambda e, oc=oc, pbc=pbc: e.tensor_tensor(out=ON3[0:64, h, q0:q0 + n], in0=oc.ap[0:64, 0:n],
                                                                         in1=pbc.ap[0:64, 0:n], op=ALU.mult),
                       reads=[oc, pbc], writes=[ON])
                fin_pending.append(fin)
            while fin_pending:
                fin_pending.pop(0)()
        chk('B0')
        kgen(0)
        chk('B1')
        vgen(0)
        chk('B2')
        for h_ in range(8):
            if h_ < 7:
                kgen(h_ + 1)
                if len(VV) > 1:
                    vgen(h_ + 1)
            do_head(h_)
            if h_ < 7 and len(VV) == 1:
                vgen(h_ + 1)
        for (q0, n) in qblocks:
            sqs = []
            for h in range(8):
                sq = SCRB()
                sq_ap = sq.ap.bitcast(BF16)
                op("act", lambda e, sq_ap=sq_ap, h=h, q0=q0, n=n: e.activation(out=sq_ap[0:64, 0:n], in_=ON3[0:64, h, q0:q0 + n],
                                                                            func=AF.Square), reads=[ON], writes=[sq])
                sqs.append((sq, sq_ap[0:64, 0:n], 64, 0))
            rb = fm_rstd(sqs, 64, n, 512.0, SCRB)
            for h in range(8):
                op("dve", lambda e, h=h, rb=rb, q0=q0, n=n: e.scalar_tensor_tensor(
                    out=ON3[0:64, h, q0:q0 + n], in0=ON3[0:64, h, q0:q0 + n], scalar=ppc("gmla", h, slice(0, 64)),
                    in1=rb.ap[0:64, 0:n], op0=ALU.mult, op1=ALU.mult), reads=[ON, rb, PP], writes=[ON])
        if "on" in dbg_d:
            TMPD4 = A.f32(8 * NOWN, "tmpd4")
            op("dve", lambda e: e.tensor_copy(out=TMPD4.ap[0:64], in_=ON.ap[0:64]), reads=[ON], writes=[TMPD4])
            dump("on", TMPD4, TMPD4.ap[0:64])
        S.barrier_all()
        A.release(mB)
        if stop == 'B':
            S.emit(nc)
            return nc
        NTL = CH // 128
        tiles = [(i * 128, 128) for i in range(NTL)] + [(CH, 2)]
        xres_start = A.top()
        XRES = A.f32((NTL + 1) * D, "xres")
        xres_end = A.top()
        XRES3 = XRES.ap.rearrange("p (t d) -> p t d", d=D)
        XR_res = [Res("xres%d" % i) for i in range(NTL + 1)]
        XT_ = [Buf(XRES3[:, i, :], XR_res[i]) for i in range(NTL + 1)]
        mC = A.mark()
        A.regs = [[ckvn_start, ets_end], [xres_end, ASZ]]
        WOL = A.bf(4 * D, "wol")
        WOL3 = WOL.ap.rearrange("p (c d) -> p c d", c=4)
        WOM = A.bf(8 * D, "wom")
        WOM3 = WOM.ap.rearrange("p (h d) -> p h d", h=8)
        load(WOL, WOL3, w_out[0:512, :].rearrange("(c p) d -> p c d", p=128))
        load(WOM, WOM3[0:64], w_out[512:1024, :].rearrange("(h p) d -> p h d", p=64))
        for ti, (o0, rows) in enumerate(tiles):
            xt = XT_[ti]
            src = xs[3, o0:o0 + rows, :] if ti < NTL else xh[0:2, :]
            load(xt, xt.ap[0:rows], src, eng="sp")
            for half in range(2):
                pb = PS()

                def mmo(e, pb=pb, o0=o0, rows=rows, half=half):
                    ins = None
                    for ct in range(4):
                        ins = e.matmul(pb.ap[0:rows, :], lhsT=MIXL3[:, ct, o0:o0 + rows],
                                       rhs=WOL3[:, ct, half * 512:(half + 1) * 512], start=(ct == 0), stop=False)
                    for h in range(8):
                        ins = e.matmul(pb.ap[0:rows, :], lhsT=ON3[0:64, h, o0:o0 + rows],
                                       rhs=WOM3[0:64, h, half * 512:(half + 1) * 512], start=False, stop=(h == 7))
                    return ins
                op("pe", mmo, reads=[MIXL, ON, WOL, WOM], writes=[pb])
                op("dve", lambda e, pb=pb, xt=xt, rows=rows, half=half: e.tensor_tensor(
                    out=xt.ap[0:rows, half * 512:(half + 1) * 512], in0=xt.ap[0:rows, half * 512:(half + 1) * 512],
                    in1=pb.ap[0:rows, :], op=ALU.add), reads=[xt, pb], writes=[xt])
        if "x1" in dbg_d:
            for ti in range(NTL):
                store(dbg_d["x1"][ti * 128:(ti + 1) * 128, :], XT_[ti], XT_[ti].ap)
        S.barrier_all()
        A.regs = [[mP, xres_start], [xres_end, ASZ]]
        if stop == 'C':
            S.emit(nc)
            return nc

        GB2 = A.f32(D, "gbc2")
        GB3 = A.f32(D, "gbc3")
        load(GB2, GB2.ap, gvec[1, :].partition_broadcast(128))
        load(GB3, GB3.ap, gvec[3, :].partition_broadcast(128))
        HNT = A.bf(8 * NOWN, "hnt")
        HNT3 = HNT.ap.rearrange("p (k t) -> p k t", k=8)
        mD = A.mark()
        WMQ = A.bf(8 * 512, "wmq")
        WMQ3 = WMQ.ap.rearrange("p (k c) -> p k c", k=8)
        load(WMQ, WMQ3, w_mq.rearrange("(k p) c -> p k c", p=128))
        WMO = A.bf(4 * D, "wmo")
        WMO3 = WMO.ap.rearrange("p (h d) -> p h d", h=4)
        load(WMO, WMO3, w_mo.rearrange("(h p) d -> p h d", p=128))
        H1T = A.bf(8 * 512, "h1t")
        H1T3 = H1T.ap.rearrange("p (k t) -> p k t", k=8)
        OMT = A.bf(4 * 512, "omt")
        OMT3 = OMT.ap.rearrange("p (h t) -> p h t", h=4)
        HNH = A.bf(8 * 2, "hnh")
        HNH3 = HNH.ap.rearrange("p (k t) -> p k t", k=8)
        scrD = [A.f32(512, "scrD%d" % i) for i in range(10)]
        sdi = [0]

        def SCRD():
            b = scrD[sdi[0] % len(scrD)]
            sdi[0] += 1
            return b
        mscale = 128.0 ** -0.5
        qbl = [(q0, min(512, CH - q0)) for q0 in range(0, CH, 512)] + [(CH, 2)]
        for (q0, n) in qbl:
            tl = [ti for ti, (o0, rows) in enumerate(tiles) if q0 <= o0 < q0 + n]
            for ti in tl:
                o0, rows = tiles[ti]
                norm_T(XT_[ti], XT_[ti].ap[0:rows], rows, H1T, H1T3, o0 - q0, gb=GB2)
            for h in range(4):
                pq = PS()

                def mmq2(e, pq=pq, h=h, n=n):
                    ins = None
                    for kc in range(8):
                        ins = e.matmul(pq.ap[:, 0:n], lhsT=WMQ3[:, kc, h * 128:(h + 1) * 128], rhs=H1T3[:, kc, 0:n],
                                       start=(kc == 0), stop=(kc == 7))
                    return ins
                op("pe", mmq2, reads=[WMQ, H1T], writes=[pq])
                sq = SCRD()
                sq_ap = sq.ap.bitcast(BF16)
                op("act", lambda e, sq_ap=sq_ap, pq=pq, n=n: e.activation(out=sq_ap[:, 0:n], in_=pq.ap[:, 0:n], func=AF.Square),
                   reads=[pq], writes=[sq])
                rb = fm_rstd([(sq, sq_ap[:, 0:n], 128, 0)], 128, n, 128.0, SCRD)
                qm = SCRD()
                qm_ap = qm.ap.bitcast(BF16)
                op("dve", lambda e, pq=pq, rb=rb, qm_ap=qm_ap, n=n: e.scalar_tensor_tensor(
                    out=qm_ap[:, 0:n], in0=pq.ap[:, 0:n], scalar=ppc("gmq"), in1=rb.ap[:, 0:n], op0=ALU.mult, op1=ALU.mult),
                   reads=[pq, rb, PP], writes=[qm])
                po = PS()
                pd = PS()
                for mt in range(2):
                    ps_ = PS()
                    op("pe", lambda e, ps_=ps_, mt=mt, h=h, qm_ap=qm_ap, n=n: e.matmul(
                        ps_.ap[:, 0:n], lhsT=KM3[:, h, mt * 128:(mt + 1) * 128], rhs=qm_ap[:, 0:n], start=True, stop=True),
                       reads=[KM, qm], writes=[ps_])
                    pt_ = SCRD()
                    pt_ap = pt_.ap.bitcast(BF16)
                    op("act", lambda e, ps_=ps_, pt_ap=pt_ap, n=n: e.activation(out=pt_ap[:, 0:n], in_=ps_.ap[:, 0:n], func=AF.Exp,
                                                                         scale=mscale), reads=[ps_], writes=[pt_])
                    op("pe", lambda e, po=po, mt=mt, h=h, pt_ap=pt_ap, n=n: e.matmul(
                        po.ap[:, 0:n], lhsT=VM3[:, mt, h * 128:(h + 1) * 128], rhs=pt_ap[:, 0:n], start=(mt == 0), stop=(mt == 1)),
                       reads=[VM, pt_], writes=[po])
                    op("pe", lambda e, pd=pd, mt=mt, pt_ap=pt_ap, n=n: e.matmul(
                        pd.ap[:, 0:n], lhsT=ONESB.ap[:, 0:128], rhs=pt_ap[:, 0:n], start=(mt == 0), stop=(mt == 1)),
                       reads=[ONESB, pt_], writes=[pd])
                rd = SCRD()
                op("dve", lambda e, rd=rd, pd=pd, n=n: e.reciprocal(out=rd.ap[:, 0:n], in_=pd.ap[:, 0:n]),
                   reads=[pd], writes=[rd])
                op("dve", lambda e, po=po, rd=rd, h=h, n=n: e.tensor_tensor(out=OMT3[:, h, 0:n], in0=po.ap[:, 0:n], in1=rd.ap[:, 0:n],
                                                                         op=ALU.mult), reads=[po, rd], writes=[OMT])
            for ti in tl:
                o0, rows = tiles[ti]
                xt = XT_[ti]
                c0 = o0 - q0
                for half in range(2):
                    pb = PS()

                    def mmo2(e, pb=pb, c0=c0, rows=rows, half=half):
                        ins = None
                        for h in range(4):
                            ins = e.matmul(pb.ap[0:rows, :], lhsT=OMT3[:, h, c0:c0 + rows],
                                           rhs=WMO3[:, h, half * 512:(half + 1) * 512], start=(h == 0), stop=(h == 3))
                        return ins
                    op("pe", mmo2, reads=[OMT, WMO], writes=[pb])
                    op("dve", lambda e, pb=pb, xt=xt, rows=rows, half=half: e.tensor_tensor(
                        out=xt.ap[0:rows, half * 512:(half + 1) * 512], in0=xt.ap[0:rows, half * 512:(half + 1) * 512],
                        in1=pb.ap[0:rows, :], op=ALU.add), reads=[xt, pb], writes=[xt])
                if ti < NTL:
                    norm_T(xt, xt.ap[0:rows], rows, HNT, HNT3, o0 + 1, gb=GB3, evac_eng="act")
                else:
                    norm_T(xt, xt.ap[0:rows], rows, HNH, HNH3, 0, gb=GB3)
                    op("dve", lambda e: e.tensor_scalar(out=HNT3[:, :, CH + 1:CH + 2], in0=HNH3[:, :, 0:1], scalar1=ppc("msk", 0),
                                                        scalar2=None, op0=ALU.mult), reads=[HNH, PP], writes=[HNT])
                    op("dve", lambda e: e.tensor_scalar(out=HNT3[:, :, 0:1], in0=HNH3[:, :, 1:2], scalar1=ppc("msk", 1),
                                                        scalar2=None, op0=ALU.mult), reads=[HNH, PP], writes=[HNT])
        if "x2" in dbg_d:
            for ti in range(NTL):
                store(dbg_d["x2"][ti * 128:(ti + 1) * 128, :], XT_[ti], XT_[ti].ap)
        S.barrier_all()
        A.release(mD)
        if stop == 'D':
            S.emit(nc)
            return nc

        GP = 2
        NR = 22 // GP
        WUPG = [A.bf(8 * 2 * GP * 128, "wupg%d" % i) for i in range(2)]
        WUPG4 = [w.ap.rearrange("p (k s c) -> p k s c", k=8, s=2) for w in WUPG]
        WDNG = [A.bf(GP * D, "wdng%d" % i) for i in range(2)]
        WDNG3 = [w.ap.rearrange("p (j d) -> p j d", j=GP) for w in WDNG]
        ACTT = [A.bf(GP * CH, "actt%d" % i) for i in range(2)]
        ACTT3 = [a.ap.rearrange("p (j t) -> p j t", j=GP) for a in ACTT]
        scrE = [A.f32(512, "scrE%d" % i) for i in range(8)]
        sei = [0]

        def SCRE():
            b = scrE[sei[0] % len(scrE)]
            sei[0] += 1
            return b
        fwo = PPL["fw"][0]
        fbo = PPL["fb"][0]
        def do_round(r):
            wu = WUPG[r % 2]
            wu4 = WUPG4[r % 2]
            wd = WDNG[r % 2]
            wd3 = WDNG3[r % 2]
            at = ACTT[r % 2]
            at3 = ACTT3[r % 2]
            j0 = r * GP
            load(wu, wu4[:, :, 0, :], w_up[:, j0 * 128:(j0 + GP) * 128].rearrange("(k p) c -> p k c", p=128))
            load(wu, wu4[:, :, 1, :], w_up[:, DFF + j0 * 128:DFF + (j0 + GP) * 128].rearrange("(k p) c -> p k c", p=128))
            load(wd, wd3, w_dn[j0 * 128:(j0 + GP) * 128, :].rearrange("(j p) d -> p j d", p=128))
            for (t0, n) in WINS:
                for jj in range(GP):
                    cv = []
                    for s in range(2):
                        chn = (j0 + jj) + s * 22
                        pg = PS()

                        def mmu(e, pg=pg, s=s, jj=jj, t0=t0, n=n):
                            ins = None
                            for kc in range(8):
                                ins = e.matmul(pg.ap[:, 0:n + 2], lhsT=wu4[:, kc, s, jj * 128:(jj + 1) * 128],
                                               rhs=HNT3[:, kc, t0:t0 + n + 2], start=(kc == 0), stop=(kc == 7))
                            return ins
                        op("pe", mmu, reads=[wu, HNT], writes=[pg])
                        c_ = SCRE()
                        wsrc, wb = PP, fwo + 3 * chn
                        bsrc, bb = PP, fbo + chn
                        op("act", lambda e, c_=c_, pg=pg, wsrc=wsrc, wb=wb, bsrc=bsrc, bb=bb, n=n: e.activation(
                            out=c_.ap[:, 0:n], in_=pg.ap[:, 1:n + 1], func=AF.Identity,
                            scale=wsrc.ap[:, wb + 1:wb + 2], bias=bsrc.ap[:, bb:bb + 1]),
                           reads=[pg, wsrc, bsrc], writes=[c_])
                        for tp in (0, 2):
                            op("dve", lambda e, c_=c_, pg=pg, wsrc=wsrc, wb=wb, tp=tp, n=n: e.scalar_tensor_tensor(
                                out=c_.ap[:, 0:n], in0=pg.ap[:, tp:tp + n], scalar=wsrc.ap[:, wb + tp:wb + tp + 1],
                                in1=c_.ap[:, 0:n], op0=ALU.mult, op1=ALU.add), reads=[pg, wsrc, c_], writes=[c_])
                        cv.append(c_)
                    sg = SCRE()
                    op("act", lambda e, sg=sg, g_=cv[0], n=n: e.activation(out=sg.ap[:, 0:n], in_=g_.ap[:, 0:n], func=AF.Tanh, scale=0.5),
                       reads=[cv[0]], writes=[sg])
                    op("act", lambda e, sg=sg, n=n: e.activation(out=sg.ap[:, 0:n], in_=sg.ap[:, 0:n], func=AF.Identity, scale=0.5,
                                                                bias=POSH.ap[:, 0:1]), reads=[sg, POSH], writes=[sg])
                    op("dve", lambda e, g_=cv[0], u_=cv[1], n=n: e.tensor_tensor(out=u_.ap[:, 0:n], in0=g_.ap[:, 0:n], in1=u_.ap[:, 0:n],
                                                                                op=ALU.mult), reads=[cv[0], cv[1]], writes=[cv[1]])
                    op("dve", lambda e, sg=sg, u_=cv[1], jj=jj, t0=t0, n=n: e.tensor_tensor(
                        out=at3[:, jj, t0:t0 + n], in0=sg.ap[:, 0:n], in1=u_.ap[:, 0:n], op=ALU.mult),
                       reads=[sg, cv[1]], writes=[at])
            if not (r % 2 == 1 or r == NR - 1):
                return
            rds = [r - 1, r] if r % 2 == 1 else [r]
            srcs = [(ACTT3[rr % 2], WDNG3[rr % 2]) for rr in rds]
            rbufs = [ACTT[rr % 2] for rr in rds] + [WDNG[rr % 2] for rr in rds]
            nmm = GP * len(rds)
            for ti in range(NTL):
                o0, rows = tiles[ti]
                xt = XT_[ti]
                for half in range(2):
                    pb = PS()

                    def mmd(e, pb=pb, o0=o0, half=half):
                        ins = None
                        i_ = 0
                        for (a3_, w3_) in srcs:
                            for jj in range(GP):
                                ins = e.matmul(pb.ap[:, :], lhsT=a3_[:, jj, o0:o0 + 128], rhs=w3_[:, jj, half * 512:(half + 1) * 512],
                                               start=(i_ == 0), stop=(i_ == nmm - 1))
                                i_ += 1
                        return ins
                    op("pe", mmd, reads=rbufs, writes=[pb])
                    op("dve", lambda e, pb=pb, xt=xt, half=half: e.tensor_tensor(
                        out=xt.ap[:, half * 512:(half + 1) * 512], in0=xt.ap[:, half * 512:(half + 1) * 512],
                        in1=pb.ap[:, :], op=ALU.add), reads=[xt, pb], writes=[xt])
        for r_ in range(NR):
            do_round(r_)
        for ti in range(NTL):
            store(out_d[ti * 128:(ti + 1) * 128, :], XT_[ti], XT_[ti].ap)
        S.emit(nc)
    return nc


def _cols(v, rows=128):
    v = np.asarray(v, np.float32)
    return np.ascontiguousarray(v.reshape(-1, rows).T)


def prep_core(inp, b, c, CH):
    f32 = np.float32
    x = np.asarray(inp["x"][b], f32)
    pos = np.asarray(inp["positions"][b]).astype(np.int32)
    s0 = c * CH
    slots = [(k, False) for k in range(c)] + [(k, True) for k in range(3, c, -1)] + [(c, False), (c, True)]
    assert len(slots) == 5
    xs = np.stack([x[k * CH:(k + 1) * CH][::-1] if rev else x[k * CH:(k + 1) * CH] for k, rev in slots])
    pk = np.stack([pos[k * CH:(k + 1) * CH][::-1] if rev else pos[k * CH:(k + 1) * CH] for k, rev in slots[:4]])
    posk = np.ascontiguousarray(np.broadcast_to(pk[:, None, :], (4, 32, CH))).astype(np.int32)
    xh = np.zeros((2, D), f32)
    ph = np.zeros((2,), np.int32)
    msk = np.zeros((2,), f32)
    if c < 3:
        xh[0] = x[s0 + CH]; ph[0] = pos[s0 + CH]; msk[0] = 1.0
    if c > 0:
        xh[1] = x[s0 - 1]; ph[1] = pos[s0 - 1]; msk[1] = 1.0
    po = np.concatenate([pos[s0:s0 + CH], ph])
    poso = np.ascontiguousarray(np.broadcast_to(po[None, :], (32, CH + 2))).astype(np.int32)
    flg = np.zeros((5, 4), f32)
    for k in range(3):
        if k < c:
            flg[k] = [1, 0, 1, 0]
        else:
            flg[k] = [0, 1, 0, 1]
    flg[3] = [1, 0, 0, 0]
    flg[4] = [0, 1, 0, 0]
    pp = np.zeros((128, PPL["_n"]), f32)

    def put(name, arr):
        o, n = PPL[name]
        arr = np.asarray(arr, f32)
        if arr.ndim == 1:
            arr = np.broadcast_to(arr[None, :], (128, arr.shape[0]))
        assert arr.shape[1] == n, (name, arr.shape, n)
        pp[:arr.shape[0], o:o + n] = arr
    put("flg", flg.reshape(-1))
    put("msk", msk)
    cw = np.zeros((128, 5, 4, 4), f32)
    cb = np.zeros((128, 5, 4), f32); ba = np.zeros((128, 5, 4), f32); bi = np.zeros((128, 5, 4), f32); lam = np.zeros((128, 5, 4), f32)
    wa = np.zeros((5, 4, 128, 128), f32); wi = np.zeros((5, 4, 128, 128), f32)
    for k, (ck, rev) in enumerate(slots):
        d = 1 if rev else 0
        w = np.asarray(inp["lru_conv_w"][0, d], f32)
        if rev:
            w = w[::-1]
        for tap in range(4):
            cw[:, k, :, tap] = _cols(w[tap])
        cb[:, k, :] = _cols(inp["lru_conv_b"][0, d])
        ba[:, k, :] = _cols(inp["lru_b_a"][0, d])
        bi[:, k, :] = _cols(inp["lru_b_i"][0, d])
        lam[:, k, :] = _cols(inp["lru_lambda"][0, d])
        for ct in range(4):
            for half in range(2):
                blk = 2 * ct + half
                wa[k, ct, half * 64:(half + 1) * 64, half * 64:(half + 1) * 64] = inp["lru_w_a"][0, d, blk]
                wi[k, ct, half * 64:(half + 1) * 64, half * 64:(half + 1) * 64] = inp["lru_w_i"][0, d, blk]
    put("cw", cw.reshape(128, -1)); put("cb", cb.reshape(128, -1)); put("ba", ba.reshape(128, -1))
    put("bi", bi.reshape(128, -1)); put("lam", lam.reshape(128, -1))
    put("gqa", _cols(inp["q_a_norm"][0]))
    put("gkv", _cols(inp["kv_a_norm"][0]))

    def rotperm(g):
        g = np.asarray(g, f32)
        r = g.copy()
        r[64:80] = g[80:96]
        r[80:96] = g[64:80]
        return r
    gq = np.asarray(inp["mla_q_norm"][0], f32); gk = np.asarray(inp["mla_k_norm"][0], f32)
    for name, v in (("gq", gq), ("gqr", rotperm(gq)), ("gk", gk), ("gkr", rotperm(gk))):
        a = np.zeros((128, 1), f32); a[:96, 0] = v
        put(name, a)
    put("glru", _cols(inp["lru_out_norm"][0]))
    gm = np.zeros((128, 8), f32); gm[:64] = _cols(inp["mla_out_norm"][0], 64)
    put("gmla", gm)
    put("gmq", _cols(inp["mem_q_norm"][0])); put("gmk", _cols(inp["mem_k_norm"][0]))
    fw = np.zeros((128, 44, 3), f32)
    fcw = np.asarray(inp["ffn_conv_w"][0], f32)
    for tap in range(3):
        fw[:, :, tap] = _cols(fcw[tap])
    put("fw", fw.reshape(128, -1))
    put("fb", _cols(inp["ffn_conv_b"][0]))
    invf = np.zeros((128, 1), f32)
    inv = (10000.0 ** (-np.arange(0, 32, 2, dtype=np.float64) / 32.0)) / (2.0 * np.pi)
    for p in range(64, 96):
        invf[p, 0] = inv[(p - 64) % 16]
    put("invf", invf)
    gvec = np.stack([inp["attn_norm"][0], inp["mem_attn_norm"][0], inp["mem_norm"][0], inp["ffn_norm"][0]]).astype(f32)
    m = {
        "xs": np.ascontiguousarray(xs), "xh": xh, "posk": posk, "poso": poso, "pp": pp, "wa": wa, "wi": wi,
        "mem": np.ascontiguousarray(np.asarray(inp["mem"][b], f32)), "gvec": np.ascontiguousarray(gvec),
        "w_in": np.ascontiguousarray(inp["w_in"][0], dtype=f32), "w_uq": np.ascontiguousarray(inp["w_uq"][0], dtype=f32),
        "w_ukv": np.ascontiguousarray(inp["w_ukv"][0], dtype=f32), "w_out": np.ascontiguousarray(inp["w_out"][0], dtype=f32),
        "w_mem_q": np.ascontiguousarray(inp["w_mem_q"][0], dtype=f32),
        "w_mem_kv": np.ascontiguousarray(inp["w_mem_kv"][0], dtype=f32),
        "w_mem_o": np.ascontiguousarray(inp["w_mem_o"][0], dtype=f32),
        "w_up": np.ascontiguousarray(inp["w_up"][0], dtype=f32), "w_down": np.ascontiguousarray(inp["w_down"][0], dtype=f32),
    }
    return m


_NC_CACHE = {}


def run(inputs, dbg=None, cores=None, stop=None):
    inputs = {k: np.asarray(v) for k, v in inputs.items()}
    B, SEQ, _ = inputs["x"].shape
    CH = SEQ // 4
    key = (CH, repr(dbg), stop)
    if key not in _NC_CACHE:
        _NC_CACHE[key] = build(CH, dbg, stop)
    nc = _NC_CACHE[key]
    core_list = cores if cores is not None else [(b, c) for b in range(B) for c in range(4)]
    in_maps = [prep_core(inputs, b, c, CH) for (b, c) in core_list]
    res = run_bass_kernel_spmd(nc, in_maps, core_ids=list(range(len(core_list))), trace=bool(os.environ.get('KTRACE')))
    return res, core_list, CH


def kernel(**inputs):
    res, core_list, CH = run(inputs)
    B, SEQ, _ = np.asarray(inputs["x"]).shape
    out = np.zeros((B, SEQ, D), np.float32)
    for (b, c), r in zip(core_list, res.results):
        out[b, c * CH:(c + 1) * CH] = r["out"]
    return out
```

```python
import numpy as np
from contextlib import ExitStack
import concourse.bass as bass
import concourse.mybir as mybir
from concourse.bass_utils import run_bass_kernel_spmd

F32 = mybir.dt.float32
BF16 = mybir.dt.bfloat16
I32 = mybir.dt.int32
AF = mybir.ActivationFunctionType
ALU = mybir.AluOpType

import os
SAME_ENGINE_SYNC = os.environ.get('SAME_ENGINE_SYNC', '1') == '1'
EPS = 1e-6
D = 1024
DFF = 2816
NCH = 44
MEM = 256


class Res:
    __slots__ = ("name", "w", "r")

    def __init__(self, name=""):
        self.name = name
        self.w = None
        self.r = {}


class Sched:
    ENGS = ("pe", "act", "dve", "pool", "sp")

    def __init__(self):
        self.ops = {e: [] for e in self.ENGS}
        self.cnt = {e: 0 for e in self.ENGS}
        self.dcnt = {}
        self.seen = {e: {} for e in self.ENGS}

    def _need(self, eng, tok, waits):
        if tok is None:
            return
        key, val = tok
        if self.seen[eng].get(key, 0) >= val:
            return
        if val > waits.get(key, 0):
            waits[key] = val

    def op(self, eng, fn, reads=(), writes=(), dma=None):
        waits = {}
        for r in reads:
            self._need(eng, r.w, waits)
        for w in writes:
            self._need(eng, w.w, waits)
            for k, v in w.r.items():
                self._need(eng, (k, v), waits)
        if not SAME_ENGINE_SYNC:
            waits.pop(eng, None)
        for k, v in waits.items():
            self.seen[eng][k] = v
        if dma is None:
            self.cnt[eng] += 1
            tok = (eng, self.cnt[eng])
        else:
            self.dcnt[dma] = self.dcnt.get(dma, 0) + 16
            tok = (dma, self.dcnt[dma])
        self.ops[eng].append((list(waits.items()), fn, tok))
        for r in reads:
            if r.r.get(tok[0], 0) < tok[1]:
                r.r[tok[0]] = tok[1]
        for w in writes:
            w.w = tok
            w.r = {}
        return tok

    def barrier_all(self):
        for e in self.ENGS:
            waits = {}
            for e2 in self.ENGS:
                if e2 != e and self.cnt[e2] > self.seen[e].get(e2, 0):
                    waits[e2] = self.cnt[e2]
            for k, v in self.dcnt.items():
                if v > self.seen[e].get(k, 0):
                    waits[k] = v
            for k, v in waits.items():
                self.seen[e][k] = v
            if waits:
                self.ops[e].append((list(waits.items()), None, None))

    def emit(self, nc):
        keys = list(self.ENGS) + sorted(self.dcnt.keys())
        with ExitStack() as st:
            sems = {k: st.enter_context(nc.semaphore("s_" + k)) for k in keys}
            block = st.enter_context(nc.Block())
            engmap = {"pe": block.tensor, "act": block.scalar, "dve": block.vector,
                      "pool": block.gpsimd, "sp": block.sync}
            fin = {}
            for k in keys:
                v = self.cnt[k] if k in self.cnt else self.dcnt[k]
                if v > 0:
                    fin[k] = v
            self.ops["sp"].append((list(fin.items()), None, None))
            for e in self.ENGS:
                ops = self.ops[e]

                def body(engobj, ops=ops, e=e):
                    for waits, fn, tok in ops:
                        for k, v in waits:
                            engobj.wait_ge(sems[k], v)
                        if fn is None:
                            continue
                        ins = fn(engobj)
                        if tok[0] == e:
                            ins.then_inc(sems[e], 1)
                        else:
                            ins.then_inc(sems[tok[0]], 16)
                engmap[e](body)


class Buf:
    __slots__ = ("ap", "r")

    def __init__(self, ap, r):
        self.ap = ap
        self.r = r


class Arena:
    def __init__(self, t, size):
        self.t = t
        self.size = size
        self.regs = [[0, size]]

    def _take(self, n, name):
        for r in self.regs:
            if r[1] - r[0] >= n:
                o = r[0]
                r[0] += n
                return o
        raise AssertionError(("SBUF arena overflow", name, n, self.regs))

    def f32(self, n, name=""):
        o = self._take(n, name)
        return Buf(self.t[:, o:o + n], Res(name))

    def bf(self, n, name=""):
        m = (n + 1) // 2
        o = self._take(m, name)
        return Buf(self.t[:, o:o + m].bitcast(BF16)[:, 0:n], Res(name))

    def mark(self):
        return [list(r) for r in self.regs]

    def release(self, m):
        self.regs = [list(r) for r in m]

    def top(self):
        return self.regs[0][0]


def ffn_windows(CH):
    nw = -(-CH // 510)
    if CH % 512 == 0 and CH >= 512:
        nw = max(nw, 1)
    base = CH // nw
    rem = CH - base * nw
    sizes = [base + (1 if i < rem else 0) for i in range(nw)]
    starts = [sum(sizes[:i]) for i in range(nw)]
    return list(zip(starts, sizes))


def pp_layout():
    o = {}
    c = 0

    def add(name, n):
        nonlocal c
        o[name] = (c, n)
        c += n
    add("flg", 20)
    add("msk", 2)
    add("cw", 80)
    add("cb", 20)
    add("ba", 20)
    add("bi", 20)
    add("lam", 20)
    add("gqa", 2)
    add("gkv", 1)
    add("gq", 1)
    add("gqr", 1)
    add("gk", 1)
    add("gkr", 1)
    add("glru", 4)
    add("gmla", 8)
    add("gmq", 1)
    add("gmk", 1)
    add("fw", 132)
    add("fb", 44)
    add("invf", 1)
    o["_n"] = c
    return o


PPL = pp_layout()


class _Stop(Exception):
    pass


def build(CH, dbg=None, stop=None):
    holder = {}
    try:
        return _build(CH, dbg, stop, holder)
    except _Stop:
        return holder['nc']


def _build(CH, dbg, stop, holder):
    NB = CH // 512
    NKEY = 4 * CH
    NKT = NKEY // 128
    NKB = NKEY // 512
    NOWN = CH + 2
    WINS = ffn_windows(CH)
    nc = bass.Bass("TRN2", target_bir_lowering=False)
    holder["nc"] = nc

    def din(name, shape, dt=F32):
        return nc.dram_tensor(name, list(shape), dt, kind="ExternalInput").ap()

    xs = din("xs", [5, CH, D])
    xh = din("xh", [2, D])
    posk = din("posk", [4, 32, CH], I32)
    poso = din("poso", [32, NOWN], I32)
    pp_d = din("pp", [128, PPL["_n"]])
    wa_d = din("wa", [5, 4, 128, 128])
    wi_d = din("wi", [5, 4, 128, 128])
    mem_d = din("mem", [MEM, D])
    gvec = din("gvec", [4, D])
    w_in = din("w_in", [D, 1440])
    w_uq = din("w_uq", [256, 768])
    w_ukv = din("w_ukv", [128, 1024])
    w_out = din("w_out", [1024, D])
    w_mq = din("w_mem_q", [D, 512])
    w_mkv = din("w_mem_kv", [D, 1024])
    w_mo = din("w_mem_o", [512, D])
    w_up = din("w_up", [D, 2 * DFF])
    w_dn = din("w_down", [DFF, D])
    out_d = nc.dram_tensor("out", [CH, D], F32, kind="ExternalOutput").ap()
    dbg_d = {}
    if dbg:
        for name, shape in dbg.items():
            dbg_d[name] = nc.dram_tensor("dbg_" + name, list(shape), F32, kind="ExternalOutput").ap()

    S = Sched()
    with ExitStack() as st:
        ASZ = 52500
        arena_t = st.enter_context(nc.sbuf_tensor("arena", [128, ASZ], F32))
        A = Arena(arena_t, ASZ)
        psum_t = [st.enter_context(nc.psum_tensor("ps%d" % i, [128, 512], F32)) for i in range(8)]
        psum = [Buf(t[:, :], Res("ps%d" % i)) for i, t in enumerate(psum_t)]
        psi = [0]

        def PS():
            b = psum[psi[0] % 5]
            psi[0] += 1
            return b
        psa = [0]

        def PSACC():
            b = psum[6 + psa[0] % 2]
            psa[0] += 1
            return b

        def chk(tag):
            if stop == tag:
                S.emit(nc)
                raise _Stop()

        def op(eng, fn, reads=(), writes=(), dma=None):
            return S.op(eng, fn, [b.r for b in reads], [b.r for b in writes], dma)

        dkeys = {}

        def dkey(buf, pre="k"):
            k = (pre, id(buf.r))
            if k not in dkeys:
                dkeys[k] = "%s%02d" % (pre, len(dkeys))
            return dkeys[k]

        def load(dst, dst_ap, src_ap, eng=None, key=None):
            if eng is None:
                eng = "pool" if dst_ap.dtype != src_ap.dtype else "sp"
            op(eng, lambda e: e.dma_start(out=dst_ap, in_=src_ap), reads=[], writes=[dst], dma=dkey(dst))

        def store(dst_ap, buf, src_ap):
            op("sp", lambda e: e.dma_start(out=dst_ap, in_=src_ap), reads=[buf], dma=dkey(buf, "s"))

        def dump(name, buf, ap, rows=None):
            if name in dbg_d:
                store(dbg_d[name], buf, ap)

        PP = A.f32(PPL["_n"], "pp")
        load(PP, PP.ap, pp_d)

        def ppc(name, i=0, rows=slice(0, 128)):
            o, n = PPL[name]
            return PP.ap[rows, o + i:o + i + 1]

        CONST = A.f32(8, "const")
        op("pool", lambda e: e.memset(CONST.ap[:, 0:1], EPS), writes=[CONST])
        op("pool", lambda e: e.memset(CONST.ap[:, 1:2], 1.0), writes=[CONST])
        c_eps = CONST.ap[:, 0:1]
        c_one = CONST.ap[:, 1:2]
        NEGH = A.f32(8, "negh")
        POSH = A.f32(8, "posh")
        op("pool", lambda e: e.memset(NEGH.ap, -0.5), writes=[NEGH])
        op("pool", lambda e: e.memset(POSH.ap, 0.5), writes=[POSH])
        IDF = A.f32(128, "identf")
        IDENT = A.bf(128, "ident")
        ONESB = A.bf(128, "onesb")
        ONESF = A.f32(128, "onesf")
        op("pool", lambda e: e.iota(IDF.ap, [[1, 128]], base=0, channel_multiplier=-1,
                                    allow_small_or_imprecise_dtypes=True), writes=[IDF])
        op("dve", lambda e: e.tensor_scalar(out=IDENT.ap, in0=IDF.ap, scalar1=0.0, scalar2=None,
                                            op0=ALU.is_equal), reads=[IDF], writes=[IDENT])
        op("pool", lambda e: e.memset(ONESB.ap, 1.0), writes=[ONESB])
        op("pool", lambda e: e.memset(ONESF.ap, 1.0), writes=[ONESF])
        LP = A.f32(192, "lruparams")
        lam_o = PPL["lam"][0]
        flg_o = PPL["flg"][0]
        op("act", lambda e: e.activation(out=LP.ap[:, 0:20], in_=PP.ap[:, lam_o:lam_o + 20], func=AF.Exp, scale=-1.0),
           reads=[PP], writes=[LP])
        op("act", lambda e: e.activation(out=LP.ap[:, 0:20], in_=LP.ap[:, 0:20], func=AF.Ln, scale=1.0, bias=c_one),
           reads=[LP, CONST], writes=[LP])
        op("dve", lambda e: e.tensor_scalar(out=LP.ap[:, 20:40], in0=LP.ap[:, 0:20], scalar1=-4.0, scalar2=None,
                                            op0=ALU.mult), reads=[LP], writes=[LP])
        op("dve", lambda e: e.tensor_scalar(out=LP.ap[:, 0:20], in0=LP.ap[:, 0:20], scalar1=-8.0, scalar2=None,
                                            op0=ALU.mult), reads=[LP], writes=[LP])
        op("dve", lambda e: e.tensor_scalar(out=LP.ap[:, 40:60], in0=PP.ap[:, flg_o:flg_o + 20], scalar1=-1.0,
                                            scalar2=1.0, op0=ALU.mult, op1=ALU.add), reads=[PP], writes=[LP])

        ba_o = PPL["ba"][0]
        bi_o = PPL["bi"][0]
        fw_o = PPL["fw"][0]
        fb_o = PPL["fb"][0]
        op("dve", lambda e: e.tensor_scalar(out=LP.ap[:, 60:80], in0=PP.ap[:, ba_o:ba_o + 20], scalar1=0.5, scalar2=None,
                                            op0=ALU.mult), reads=[PP], writes=[LP])
        op("dve", lambda e: e.tensor_scalar(out=LP.ap[:, 80:100], in0=PP.ap[:, bi_o:bi_o + 20], scalar1=0.5, scalar2=None,
                                            op0=ALU.mult), reads=[PP], writes=[LP])
        op("dve", lambda e: e.tensor_scalar(out=LP.ap[:, 100:166], in0=PP.ap[:, fw_o + 66:fw_o + 132], scalar1=0.5, scalar2=None,
                                            op0=ALU.mult), reads=[PP], writes=[LP])
        op("dve", lambda e: e.tensor_scalar(out=LP.ap[:, 166:188], in0=PP.ap[:, fb_o + 22:fb_o + 44], scalar1=0.5, scalar2=None,
                                            op0=ALU.mult), reads=[PP], writes=[LP])

        def flg(k, i):
            return PP.ap[:, flg_o + 4 * k + i:flg_o + 4 * k + i + 1]

        def nflg(k, i):
            return LP.ap[:, 40 + 4 * k + i:40 + 4 * k + i + 1]

        GBC = A.f32(D, "gbc")

        NT_X = [A.f32(D, "xt%d" % i) for i in range(2)]
        NT_J = A.bf(D, "junk")
        NT_H = [A.bf(D, "hb%d" % i) for i in range(2)]
        NT_Sl = [A.f32(2, "nstat%d" % i) for i in range(2)]
        nt_i = [0]

        def norm_T(xbuf, x_ap, n, dstT, dstT_ap3, col0, evac_eng="dve", gb=None):
            gb = gb or GBC
            i = nt_i[0] % 2
            nt_i[0] += 1
            hb = NT_H[i]
            NT_S = NT_Sl[i]
            ss = NT_S.ap[0:n, 0:1]
            rs = NT_S.ap[0:n, 1:2]
            op("act", lambda e: e.activation(out=NT_J.ap[0:n], in_=x_ap, func=AF.Square, accum_out=ss),
               reads=[xbuf], writes=[NT_J, NT_S])
            op("dve", lambda e: e.tensor_scalar(out=rs, in0=ss, scalar1=1.0 / D, scalar2=EPS, op0=ALU.mult, op1=ALU.add),
               reads=[NT_S], writes=[NT_S])
            op("pool", lambda e: e.tensor_tensor(out=rs, in0=rs, in1=NEGH.ap[0:n, 0:1], op=ALU.pow),
               reads=[NT_S, NEGH], writes=[NT_S])
            op("dve", lambda e: e.scalar_tensor_tensor(out=hb.ap[0:n], in0=x_ap, scalar=rs, in1=gb.ap[0:n],
                                                       op0=ALU.mult, op1=ALU.mult),
               reads=[xbuf, NT_S, gb], writes=[hb])
            pb = PS()
            pbf = pb.ap.bitcast(BF16)

            def tr(e):
                ins = None
                for kc in range(8):
                    ins = e.transpose(pbf[:, kc * 128:kc * 128 + n], hb.ap[0:n, kc * 128:(kc + 1) * 128],
                                      IDENT.ap[0:n, 0:n])
                return ins
            op("pe", tr, reads=[hb, IDENT], writes=[pb])
            src = pbf.rearrange("p (k t) -> p k t", k=8)[:, :, 0:n]
            dst = dstT_ap3[:, :, col0:col0 + n]
            if evac_eng == "act":
                op("act", lambda e: e.activation(out=dst, in_=src, func=AF.Copy), reads=[pb], writes=[dstT])
            else:
                op("dve", lambda e: e.tensor_copy(out=dst, in_=src), reads=[pb], writes=[dstT])

        def fm_rstd(sq_list, rows_out, n, dim, SCR, pbuf=None):
            pb = pbuf if pbuf is not None else PS()

            def mm(e):
                ins = None
                for i, (b, ap, rk, base) in enumerate(sq_list):
                    ins = e.matmul(pb.ap[0:rows_out, 0:n], lhsT=ONESB.ap[base:base + rk, 0:rows_out], rhs=ap,
                                   start=(i == 0), stop=(i == len(sq_list) - 1))
                return ins
            op("pe", mm, reads=[b for b, _, _, _ in sq_list] + [ONESB], writes=[pb])
            rb = SCR()
            op("act", lambda e: e.activation(out=rb.ap[0:rows_out, 0:n], in_=pb.ap[0:rows_out, 0:n], func=AF.Ln,
                                             scale=1.0 / dim, bias=c_eps[0:rows_out]),
               reads=[pb, CONST], writes=[rb])
            op("act", lambda e: e.activation(out=rb.ap[0:rows_out, 0:n], in_=rb.ap[0:rows_out, 0:n], func=AF.Exp,
                                             scale=-0.5), reads=[rb], writes=[rb])
            return rb

        POSB = [A.f32(512, "posb%d" % i) for i in range(2)]
        posb_i = [0]

        def rope_tables(pos_src_ap, n, SCR, out_buf=None, out_s=None, out_c=None):
            pi = POSB[posb_i[0] % 2]
            posb_i[0] += 1
            pi_ap = pi.ap.bitcast(I32)
            load(pi, pi_ap[64:96, 0:n], pos_src_ap, eng="sp")
            y = SCR()
            op("dve", lambda e: e.tensor_copy(out=y.ap[64:96, 0:n], in_=pi_ap[64:96, 0:n]), reads=[pi], writes=[y])
            y2 = SCR()
            yc = SCR()
            op("dve", lambda e: e.tensor_scalar(out=y2.ap[64:96, 0:n], in0=y.ap[64:96, 0:n],
                                                scalar1=ppc("invf", 0, slice(64, 96)), scalar2=None, op0=ALU.mult),
               reads=[y, PP], writes=[y2])
            op("dve", lambda e: e.tensor_scalar(out=yc.ap[64:96, 0:n], in0=y2.ap[64:96, 0:n], scalar1=0.25,
                                                scalar2=None, op0=ALU.add), reads=[y2], writes=[yc])
            res = []
            for yy, dst in ((y2, out_s), (yc, out_c)):
                ti = SCR()
                ti_ap = ti.ap.bitcast(I32)
                op("dve", lambda e, yy=yy, ti_ap=ti_ap: e.tensor_copy(out=ti_ap[64:96, 0:n], in_=yy.ap[64:96, 0:n]),
                   reads=[yy], writes=[ti])
                tf = SCR()
                op("dve", lambda e, ti_ap=ti_ap, tf=tf: e.tensor_copy(out=tf.ap[64:96, 0:n], in_=ti_ap[64:96, 0:n]),
                   reads=[ti], writes=[tf])
                op("dve", lambda e, yy=yy, tf=tf: e.tensor_tensor(out=yy.ap[64:96, 0:n], in0=yy.ap[64:96, 0:n],
                                                                  in1=tf.ap[64:96, 0:n], op=ALU.subtract),
                   reads=[yy, tf], writes=[yy])
                if dst is None:
                    o_b, o_ap = yy, yy.ap[64:96, 0:n]
                else:
                    o_b, o_ap = out_buf, dst
                op("act", lambda e, yy=yy, o_ap=o_ap: e.activation(out=o_ap, in_=yy.ap[64:96, 0:n], func=AF.Sin,
                                                                   scale=2.0 * np.pi * 0.999999),
                   reads=[yy], writes=[o_b])
                res.append((o_b, o_ap))
            return res

        KM = A.bf(4 * MEM, "km")
        KM3 = KM.ap.rearrange("p (h t) -> p h t", h=4)
        VM = A.bf(2 * 512, "vm")
        VM3 = VM.ap.rearrange("p (t c) -> p t c", t=2)
        m0 = A.mark()
        scr0 = [A.f32(512, "scr0%d" % i) for i in range(4)]
        s0i = [0]

        def SCRD():
            b = scr0[s0i[0] % len(scr0)]
            s0i[0] += 1
            return b
        if stop == '0a':
            S.emit(nc)
            return nc
        load(GBC, GBC.ap, gvec[2, :].partition_broadcast(128))
        mW = A.mark()
        WMKV = A.bf(8 * 1024, "wmkv")
        WMKV3 = WMKV.ap.rearrange("p (k c) -> p k c", k=8)
        load(WMKV, WMKV3, w_mkv.rearrange("(k p) c -> p k c", p=128))
        MEMT = A.bf(8 * MEM, "memt")
        MEMT3 = MEMT.ap.rearrange("p (k t) -> p k t", k=8)
        for mt in range(2):
            xb = NT_X[nt_i[0] % 2]
            load(xb, xb.ap, mem_d[mt * 128:(mt + 1) * 128, :], eng="sp")
            norm_T(xb, xb.ap, 128, MEMT, MEMT3, mt * 128)
        if stop == '0b':
            S.emit(nc)
            return nc
        for h in range(4):
            pk = PS()

            def mmk(e, pk=pk, h=h):
                ins = None
                for kc in range(8):
                    ins = e.matmul(pk.ap[:, 0:MEM], lhsT=WMKV3[:, kc, h * 128:(h + 1) * 128], rhs=MEMT3[:, kc, :],
                                   start=(kc == 0), stop=(kc == 7))
                return ins
            op("pe", mmk, reads=[WMKV, MEMT], writes=[pk])
            sq = SCRD()
            sq_ap = sq.ap.bitcast(BF16)
            op("act", lambda e, sq_ap=sq_ap, pk=pk: e.activation(out=sq_ap[:, 0:MEM], in_=pk.ap[:, 0:MEM], func=AF.Square),
               reads=[pk], writes=[sq])
            rb = fm_rstd([(sq, sq_ap[:, 0:MEM], 128, 0)], 128, MEM, 128.0, SCRD)
            op("dve", lambda e, pk=pk, rb=rb, h=h: e.scalar_tensor_tensor(
                out=KM3[:, h, :], in0=pk.ap[:, 0:MEM], scalar=ppc("gmk"), in1=rb.ap[:, 0:MEM], op0=ALU.mult, op1=ALU.mult),
               reads=[pk, rb, PP], writes=[KM])
        if stop == '0c':
            S.emit(nc)
            return nc
        for mt in range(2):
            pv = PS()

            def mmv2(e, pv=pv, mt=mt):
                ins = None
                for kc in range(8):
                    ins = e.matmul(pv.ap[:, :], lhsT=MEMT3[:, kc, mt * 128:(mt + 1) * 128], rhs=WMKV3[:, kc, 512:1024],
                                   start=(kc == 0), stop=(kc == 7))
                return ins
            op("pe", mmv2, reads=[WMKV, MEMT], writes=[pv])
            op("act", lambda e, pv=pv, mt=mt: e.activation(out=VM3[:, mt, :], in_=pv.ap[:, :], func=AF.Copy),
               reads=[pv], writes=[VM])
        if stop == '0d':
            S.emit(nc)
            return nc
        S.barrier_all()
        A.release(m0)
        mP = A.top()
        if stop == '0':
            S.emit(nc)
            return nc

        MIXL = A.bf(4 * NOWN, "mixl")
        MIXL3 = MIXL.ap.rearrange("p (c t) -> p c t", c=4)
        CQN = A.bf(2 * NOWN, "cqn")
        CQN3 = CQN.ap.rearrange("p (c t) -> p c t", c=2)
        ckvn_start = A.top()
        CKVN = A.bf(NKEY, "ckvn")
        ETS = A.bf(NKEY, "e_tsq")
        ets_end = A.top()
        mA = A.mark()

        WIN = A.bf(8 * 1440, "win")
        WIN3 = WIN.ap.rearrange("p (k c) -> p k c", k=8)
        for kc in range(8):
            load(WIN, WIN3[:, kc, :], w_in[kc * 128:(kc + 1) * 128, :])
        WKR = A.bf(8 * 192, "wkrpad")
        WKR3 = WKR.ap.rearrange("p (k c) -> p k c", k=8)
        op("pool", lambda e: e.memset(WKR.ap, 0.0), writes=[WKR])
        op("dve", lambda e: e.tensor_copy(out=WKR3[:, :, 64:96], in_=WIN3[:, :, 1408:1440]), reads=[WIN], writes=[WKR])
        op("dve", lambda e: e.tensor_scalar(out=WKR3[:, :, 160:176], in0=WIN3[:, :, 1424:1440], scalar1=-1.0,
                                            scalar2=None, op0=ALU.mult), reads=[WIN], writes=[WKR])
        op("dve", lambda e: e.tensor_copy(out=WKR3[:, :, 176:192], in_=WIN3[:, :, 1408:1424]), reads=[WIN], writes=[WKR])
        WGL = [A.bf(4 * 2 * 128, "wgates%d" % i) for i in range(2)]
        WGL4 = [w.ap.rearrange("p (c g o) -> p c g o", c=4, g=2) for w in WGL]
        HTL = [A.bf(8 * 512, "ht%d" % i) for i in range(2)]
        HTL3 = [h_.ap.rearrange("p (k t) -> p k t", k=8) for h_ in HTL]
        XR = A.f32(4 * 515, "xr")
        XR3 = XR.ap.rearrange("p (c t) -> p c t", c=4)
        HF = A.bf(4 * (CH + 1), "hf")
        HF3 = HF.ap.rearrange("p (c t) -> p c t", c=4)
        GG = A.bf(4 * NOWN, "gelu")
        GG3 = GG.ap.rearrange("p (c t) -> p c t", c=4)
        CAR = A.f32(64, "carry")
        op("pool", lambda e: e.memset(CAR.ap, 0.0), writes=[CAR])
        HSC = A.f32(4 * 512, "hscan")
        HSC3 = HSC.ap.rearrange("p (c t) -> p c t", c=4)
        HE = A.f32(16, "hextra")
        scrA = [A.f32(512, "scrA%d" % i) for i in range(12)]
        sai = [0]

        def SCRA():
            b = scrA[sai[0] % len(scrA)]
            sai[0] += 1
            return b

        load(GBC, GBC.ap, gvec[0, :].partition_broadcast(128))

        chk('A0')

        def lru_cols(k, ct, name):
            o, n = PPL[name]
            return PP.ap[:, o + 4 * k + ct:o + 4 * k + ct + 1]

        def phaseA_front(k, j, n, mini, hb):
            HT = HTL[hb]
            HT3 = HTL3[hb]
            if j == 0 and not mini:
                load(WGL[k % 2], WGL4[k % 2][:, :, 0, :], wa_d[k].rearrange("c i o -> i c o"))
                load(WGL[k % 2], WGL4[k % 2][:, :, 1, :], wi_d[k].rearrange("c i o -> i c o"))
            ntile = (n + 127) // 128
            for i in range(ntile):
                rows = min(128, n - i * 128)
                xb = NT_X[nt_i[0] % 2]
                if mini:
                    src = xh[0:1, :] if k == 3 else xh[1:2, :]
                else:
                    src = xs[k, j * 512 + i * 128:j * 512 + i * 128 + rows, :]
                load(xb, xb.ap[0:rows], src, eng="sp")
                norm_T(xb, xb.ap[0:rows], rows, HT, HT3, i * 128, evac_eng="dve")

        def phaseA_xr(k, j, n, mini, hb):
            HT = HTL[hb]
            HT3 = HTL3[hb]
            if j == 0 and not mini:
                op("dve", lambda e: e.tensor_scalar(out=CAR.ap[:, 36:48], in0=CAR.ap[:, 8:20], scalar1=flg(k, 0),
                                                    scalar2=None, op0=ALU.mult), reads=[CAR, PP], writes=[CAR])
                op("dve", lambda e: e.scalar_tensor_tensor(out=CAR.ap[:, 36:48], in0=CAR.ap[:, 20:32], scalar=flg(k, 1),
                                                           in1=CAR.ap[:, 36:48], op0=ALU.mult, op1=ALU.add),
                   reads=[CAR, PP], writes=[CAR])
                op("dve", lambda e: e.tensor_scalar(out=CAR.ap[:, 32:36], in0=CAR.ap[:, 0:4], scalar1=flg(k, 0),
                                                    scalar2=None, op0=ALU.mult), reads=[CAR, PP], writes=[CAR])
                op("dve", lambda e: e.scalar_tensor_tensor(out=CAR.ap[:, 32:36], in0=CAR.ap[:, 4:8], scalar=flg(k, 1),
                                                           in1=CAR.ap[:, 32:36], op0=ALU.mult, op1=ALU.add),
                   reads=[CAR, PP], writes=[CAR])
                if k == 3:
                    op("dve", lambda e: e.tensor_copy(out=CAR.ap[:, 48:52], in_=CAR.ap[:, 32:36]), reads=[CAR], writes=[CAR])
                if k == 4:
                    op("dve", lambda e: e.tensor_copy(out=CAR.ap[:, 52:56], in_=CAR.ap[:, 32:36]), reads=[CAR], writes=[CAR])
                op("dve", lambda e: e.tensor_copy(out=XR3[:, :, 0:3],
                                                  in_=CAR.ap[:, 36:48].rearrange("p (c t) -> p c t", c=4)),
                   reads=[CAR], writes=[XR])
            else:
                op("dve", lambda e: e.tensor_copy(out=XR3[:, :, 0:3], in_=XR3[:, :, 512:515]), reads=[XR], writes=[XR])
            for ct in range(4):
                pb = PS()

                def mm(e, pb=pb, ct=ct):
                    ins = None
                    for kc in range(8):
                        ins = e.matmul(pb.ap[:, 0:n], lhsT=WIN3[:, kc, ct * 128:(ct + 1) * 128], rhs=HT3[:, kc, 0:n],
                                       start=(kc == 0), stop=(kc == 7))
                    return ins
                op("pe", mm, reads=[WIN, HT], writes=[pb])
                op("act", lambda e, pb=pb, ct=ct: e.activation(out=XR3[:, ct, 3:3 + n], in_=pb.ap[:, 0:n], func=AF.Copy),
                   reads=[pb], writes=[XR])

        def phaseA_block(k, j, n, mini, hb, mid=None):
            HT = HTL[hb]
            HT3 = HTL3[hb]
            WG = WGL[k % 2]
            WG4 = WGL4[k % 2]
            do_kv = (k <= 3) and not mini
            do_own = (k == 3) or (k == 4 and mini)
            if mini:
                own0 = CH if k == 3 else CH + 1
            else:
                own0 = j * 512
            chk('A1')
            for pr in range(2):
                cts = (2 * pr, 2 * pr + 1)
                B_ = {}
                for ct in cts:
                    cwo = PPL["cw"][0] + (k * 4 + ct) * 4
                    xc = SCRA()
                    op("act", lambda e, xc=xc, ct=ct, cwo=cwo: e.activation(
                        out=xc.ap[:, 0:n], in_=XR3[:, ct, 3:3 + n], func=AF.Identity,
                        scale=PP.ap[:, cwo + 3:cwo + 4], bias=lru_cols(k, ct, "cb")), reads=[XR, PP], writes=[xc])
                    for tp in range(3):
                        op("dve", lambda e, xc=xc, ct=ct, cwo=cwo, tp=tp: e.scalar_tensor_tensor(
                            out=xc.ap[:, 0:n], in0=XR3[:, ct, tp:tp + n], scalar=PP.ap[:, cwo + tp:cwo + tp + 1],
                            in1=xc.ap[:, 0:n], op0=ALU.mult, op1=ALU.add), reads=[XR, PP, xc], writes=[xc])
                    xcb = SCRA()
                    xcb_ap = xcb.ap.bitcast(BF16)
                    op("dve", lambda e, xc=xc, xcb_ap=xcb_ap: e.tensor_copy(out=xcb_ap[:, 0:n], in_=xc.ap[:, 0:n]),
                       reads=[xc], writes=[xcb])
                    B_[ct] = dict(xc=xc, xcb=xcb, xcb_ap=xcb_ap)
                for ct in cts:
                    d = B_[ct]
                    pa = PS()
                    op("pe", lambda e, pa=pa, ct=ct, xcb_ap=d["xcb_ap"]: e.matmul(
                        pa.ap[:, 0:n], lhsT=WG4[:, ct, 0, :], rhs=xcb_ap[:, 0:n], start=True, stop=True),
                       reads=[WG, d["xcb"]], writes=[pa])
                    pi_ = PS()
                    op("pe", lambda e, pi_=pi_, ct=ct, xcb_ap=d["xcb_ap"]: e.matmul(
                        pi_.ap[:, 0:n], lhsT=WG4[:, ct, 1, :], rhs=xcb_ap[:, 0:n], start=True, stop=True),
                       reads=[WG, d["xcb"]], writes=[pi_])
                    d["pa"] = pa
                    d["pi"] = pi_
                for ct in cts:
                    d = B_[ct]
                    rr = SCRA()
                    ig = SCRA()
                    op("act", lambda e, rr=rr, pa=d["pa"], ct=ct: e.activation(out=rr.ap[:, 0:n], in_=pa.ap[:, 0:n], func=AF.Tanh,
                                                                          bias=LP.ap[:, 60 + 4 * k + ct:61 + 4 * k + ct], scale=0.5),
                       reads=[d["pa"], LP], writes=[rr])
                    op("act", lambda e, ig=ig, pi_=d["pi"], ct=ct: e.activation(out=ig.ap[:, 0:n], in_=pi_.ap[:, 0:n], func=AF.Tanh,
                                                                            bias=LP.ap[:, 80 + 4 * k + ct:81 + 4 * k + ct], scale=0.5),
                       reads=[d["pi"], LP], writes=[ig])
                    d["rr"] = rr
                    d["ig"] = ig
                for ct in cts:
                    d = B_[ct]
                    aa = SCRA()
                    mm_ = SCRA()
                    cs = LP.ap[:, 4 * k + ct:4 * k + ct + 1]
                    hcs = LP.ap[:, 20 + 4 * k + ct:20 + 4 * k + ct + 1]
                    op("act", lambda e, aa=aa, rr=d["rr"], hcs=hcs: e.activation(out=aa.ap[:, 0:n], in_=rr.ap[:, 0:n], func=AF.Exp,
                                                                             scale=hcs, bias=hcs), reads=[d["rr"], LP], writes=[aa])
                    op("act", lambda e, mm_=mm_, rr=d["rr"], cs=cs: e.activation(out=mm_.ap[:, 0:n], in_=rr.ap[:, 0:n], func=AF.Exp,
                                                                             scale=cs, bias=cs), reads=[d["rr"], LP], writes=[mm_])
                    op("dve", lambda e, mm_=mm_: e.tensor_scalar(out=mm_.ap[:, 0:n], in0=mm_.ap[:, 0:n], scalar1=-0.25, scalar2=0.25,
                                                                 op0=ALU.mult, op1=ALU.add), reads=[mm_], writes=[mm_])
                    op("dve", lambda e, ig=d["ig"], xc=d["xc"]: e.scalar_tensor_tensor(out=ig.ap[:, 0:n], in0=ig.ap[:, 0:n], scalar=1.0,
                                                                                   in1=xc.ap[:, 0:n], op0=ALU.add, op1=ALU.mult),
                       reads=[d["ig"], d["xc"]], writes=[d["ig"]])
                    d["aa"] = aa
                    d["mm"] = mm_
                for ct in cts:
                    d = B_[ct]
                    op("act", lambda e, mm_=d["mm"]: e.activation(out=mm_.ap[:, 0:n], in_=mm_.ap[:, 0:n], func=AF.Sqrt),
                       reads=[d["mm"]], writes=[d["mm"]])
                for ct in cts:
                    d = B_[ct]
                    op("dve", lambda e, ig=d["ig"], mm_=d["mm"]: e.tensor_tensor(out=ig.ap[:, 0:n], in0=ig.ap[:, 0:n], in1=mm_.ap[:, 0:n],
                                                                              op=ALU.mult), reads=[d["ig"], d["mm"]], writes=[d["ig"]])
                    if mini or j > 0:
                        init = CAR.ap[:, 56 + ct:57 + ct]
                    else:
                        init = CAR.ap[:, 32 + ct:33 + ct]
                    op("dve", lambda e, aa=d["aa"], ig=d["ig"], ct=ct, init=init: e.tensor_tensor_scan(
                        out=HSC3[:, ct, 0:n], data0=aa.ap[:, 0:n], data1=ig.ap[:, 0:n], initial=init,
                        op0=ALU.mult, op1=ALU.add), reads=[d["aa"], d["ig"], CAR], writes=[HSC])
            if not mini:
                op("dve", lambda e: e.tensor_copy(out=CAR.ap[:, 56:60], in_=HSC3[:, :, n - 1]), reads=[HSC], writes=[CAR])
            chk('A2')
            if k == 3:
                op("dve", lambda e: e.tensor_copy(out=HF3[:, :, own0 if not mini else CH:(own0 if not mini else CH) + n],
                                                   in_=HSC3[:, :, 0:n]), reads=[HSC], writes=[HF])
            if k <= 2 and (not mini) and j == NB - 1:
                for (st_o, u_i) in ((0, 2), (4, 3)):
                    op("dve", lambda e, st_o=st_o, u_i=u_i: e.tensor_scalar(
                        out=CAR.ap[:, st_o:st_o + 4], in0=CAR.ap[:, st_o:st_o + 4], scalar1=nflg(k, u_i), scalar2=None,
                        op0=ALU.mult), reads=[CAR, LP], writes=[CAR])
                    op("dve", lambda e, st_o=st_o, u_i=u_i: e.scalar_tensor_tensor(
                        out=CAR.ap[:, st_o:st_o + 4], in0=HSC3[:, :, n - 1], scalar=flg(k, u_i), in1=CAR.ap[:, st_o:st_o + 4],
                        op0=ALU.mult, op1=ALU.add), reads=[CAR, HSC, PP], writes=[CAR])
                for (h_o, u_i) in ((8, 2), (20, 3)):
                    hv = CAR.ap[:, h_o:h_o + 12].rearrange("p (c t) -> p c t", c=4)
                    op("dve", lambda e, hv=hv, u_i=u_i: e.tensor_scalar(
                        out=hv, in0=hv, scalar1=nflg(k, u_i), scalar2=None, op0=ALU.mult), reads=[CAR, LP], writes=[CAR])
                    op("dve", lambda e, hv=hv, u_i=u_i: e.scalar_tensor_tensor(
                        out=hv, in0=XR3[:, :, 512:515], scalar=flg(k, u_i), in1=hv, op0=ALU.mult, op1=ALU.add),
                       reads=[CAR, XR, PP], writes=[CAR])
            chk('A3')
            if mid is not None:
                mid()
            if do_kv:
                key0 = k * CH + j * 512
                pb = PS()

                def mmkv(e, pb=pb):
                    ins = None
                    for kc in range(8):
                        ins = e.matmul(pb.ap[:, 0:n], lhsT=WIN3[:, kc, 1280:1408], rhs=HT3[:, kc, 0:n],
                                       start=(kc == 0), stop=(kc == 7))
                    return ins
                op("pe", mmkv, reads=[WIN, HT], writes=[pb])
                cf = SCRA()
                sq = SCRA()
                sq_ap = sq.ap.bitcast(BF16)
                op("act", lambda e, cf=cf, pb=pb: e.activation(out=cf.ap[:, 0:n], in_=pb.ap[:, 0:n], func=AF.Copy),
                   reads=[pb], writes=[cf])
                op("act", lambda e, sq_ap=sq_ap, pb=pb: e.activation(out=sq_ap[:, 0:n], in_=pb.ap[:, 0:n], func=AF.Square),
                   reads=[pb], writes=[sq])
                rb = fm_rstd([(sq, sq_ap[:, 0:n], 128, 0)], 128, n, 128.0, SCRA)
                op("dve", lambda e, cf=cf, rb=rb: e.scalar_tensor_tensor(
                    out=CKVN.ap[:, key0:key0 + n], in0=cf.ap[:, 0:n], scalar=ppc("gkv"), in1=rb.ap[:, 0:n],
                    op0=ALU.mult, op1=ALU.mult), reads=[cf, rb, PP], writes=[CKVN])
                chk('A3a')
                pt = PS()
                prt = PS()

                def mmt(e, pt=pt, o=0):
                    ins = None
                    for kc in range(8):
                        ins = e.matmul(pt.ap[0:96, 0:n], lhsT=WKR3[:, kc, o:o + 96], rhs=HT3[:, kc, 0:n],
                                       start=(kc == 0), stop=(kc == 7))
                    return ins
                op("pe", lambda e: mmt(e, pt, 0), reads=[WKR, HT], writes=[pt])
                op("pe", lambda e: mmt(e, prt, 96), reads=[WKR, HT], writes=[prt])
                chk('A3b')
                (sb, s_ap), (cb_, c_ap) = rope_tables(posk[k, :, j * 512:j * 512 + n], n, SCRA)
                chk('A3c')
                tq = SCRA()
                tq_ap = tq.ap.bitcast(BF16)
                op("act", lambda e, pt=pt, tq_ap=tq_ap: e.activation(out=tq_ap[64:96, 0:n], in_=pt.ap[64:96, 0:n], func=AF.Square),
                   reads=[pt], writes=[tq])
                op("dve", lambda e, tq_ap=tq_ap: e.tensor_copy(out=ETS.ap[0:32, key0:key0 + n], in_=tq_ap[64:96, 0:n]),
                   reads=[tq], writes=[ETS])
                e1 = SCRA()
                e2 = SCRA()
                op("dve", lambda e, e1=e1, pt=pt, c_ap=c_ap: e.scalar_tensor_tensor(
                    out=e1.ap[64:96, 0:n], in0=pt.ap[64:96, 0:n], scalar=ppc("gk", 0, slice(64, 96)), in1=c_ap,
                    op0=ALU.mult, op1=ALU.mult), reads=[pt, cb_, PP, tq], writes=[e1])
                op("dve", lambda e, e2=e2, prt=prt, s_ap=s_ap: e.scalar_tensor_tensor(
                    out=e2.ap[64:96, 0:n], in0=prt.ap[64:96, 0:n], scalar=ppc("gkr", 0, slice(64, 96)), in1=s_ap,
                    op0=ALU.mult, op1=ALU.mult), reads=[prt, sb, PP], writes=[e2])
                op("dve", lambda e, e1=e1, e2=e2: e.tensor_tensor(out=ETS.ap[64:96, key0:key0 + n], in0=e1.ap[64:96, 0:n],
                                                                 in1=e2.ap[64:96, 0:n], op=ALU.add),
                   reads=[e1, e2], writes=[ETS])
            chk('A4')
            if do_own:
                for ct in range(4):
                    pb = PS()

                    def mmy(e, pb=pb, ct=ct):
                        ins = None
                        for kc in range(8):
                            ins = e.matmul(pb.ap[:, 0:n], lhsT=WIN3[:, kc, 512 + ct * 128:512 + (ct + 1) * 128],
                                           rhs=HT3[:, kc, 0:n], start=(kc == 0), stop=(kc == 7))
                        return ins
                    op("pe", mmy, reads=[WIN, HT], writes=[pb])
                    u = SCRA()
                    w = SCRA()
                    op("act", lambda e, u=u, pb=pb: e.activation(out=u.ap[:, 0:n], in_=pb.ap[:, 0:n], func=AF.Copy, scale=0.5),
                       reads=[pb], writes=[u])
                    op("act", lambda e, w=w, pb=pb: e.activation(out=w.ap[:, 0:n], in_=pb.ap[:, 0:n], func=AF.Square),
                       reads=[pb], writes=[w])
                    op("dve", lambda e, w=w: e.tensor_scalar(out=w.ap[:, 0:n], in0=w.ap[:, 0:n], scalar1=2.0 * 0.044715, scalar2=2.0,
                                                             op0=ALU.mult, op1=ALU.add), reads=[w], writes=[w])
                    op("dve", lambda e, w=w, u=u: e.tensor_tensor(out=w.ap[:, 0:n], in0=w.ap[:, 0:n], in1=u.ap[:, 0:n], op=ALU.mult),
                       reads=[w, u], writes=[w])
                    op("act", lambda e, w=w: e.activation(out=w.ap[:, 0:n], in_=w.ap[:, 0:n], func=AF.Tanh,
                                                          scale=0.7978845608028654), reads=[w], writes=[w])
                    op("dve", lambda e, w=w, u=u, ct=ct: e.scalar_tensor_tensor(out=GG3[:, ct, own0:own0 + n], in0=w.ap[:, 0:n],
                                                                             scalar=1.0, in1=u.ap[:, 0:n], op0=ALU.add, op1=ALU.mult),
                       reads=[w, u], writes=[GG])
                cfs = []
                sqs = []
                for c2 in range(2):
                    pb = PS()

                    def mmq(e, pb=pb, c2=c2):
                        ins = None
                        for kc in range(8):
                            ins = e.matmul(pb.ap[:, 0:n], lhsT=WIN3[:, kc, 1024 + c2 * 128:1024 + (c2 + 1) * 128],
                                           rhs=HT3[:, kc, 0:n], start=(kc == 0), stop=(kc == 7))
                        return ins
                    op("pe", mmq, reads=[WIN, HT], writes=[pb])
                    cf = SCRA()
                    sq = SCRA()
                    sq_ap = sq.ap.bitcast(BF16)
                    op("act", lambda e, cf=cf, pb=pb: e.activation(out=cf.ap[:, 0:n], in_=pb.ap[:, 0:n], func=AF.Copy),
                       reads=[pb], writes=[cf])
                    op("act", lambda e, sq_ap=sq_ap, pb=pb: e.activation(out=sq_ap[:, 0:n], in_=pb.ap[:, 0:n], func=AF.Square),
                       reads=[pb], writes=[sq])
                    cfs.append(cf)
                    sqs.append((sq, sq_ap[:, 0:n], 128, 0))
                rb = fm_rstd(sqs, 128, n, 256.0, SCRA)
                for c2 in range(2):
                    op("dve", lambda e, c2=c2, cf=cfs[c2], rb=rb: e.scalar_tensor_tensor(
                        out=CQN3[:, c2, own0:own0 + n], in0=cf.ap[:, 0:n], scalar=ppc("gqa", c2), in1=rb.ap[:, 0:n],
                        op0=ALU.mult, op1=ALU.mult), reads=[cf, rb, PP], writes=[CQN])
            if k == 4 and not mini:
                lo = CH - 512 * (j + 1)
                lru_combine(lambda ct: HF3[:, ct, lo:lo + n], HF, lambda ct: HSC3[:, ct, 0:n][:, ::-1], HSC, lo, n)

        def lru_combine(hf_ap, hf_buf, hb_ap, hb_buf, own0, n):
            los = []
            sqs = []
            for ct in range(4):
                lo_ = SCRA()
                op("dve", lambda e, lo_=lo_, ct=ct: e.tensor_tensor(out=lo_.ap[:, 0:n], in0=hf_ap(ct), in1=hb_ap(ct), op=ALU.add),
                   reads=[hf_buf, hb_buf], writes=[lo_])
                op("dve", lambda e, lo_=lo_, ct=ct: e.tensor_tensor(out=lo_.ap[:, 0:n], in0=lo_.ap[:, 0:n],
                                                                  in1=GG3[:, ct, own0:own0 + n], op=ALU.mult),
                   reads=[lo_, GG], writes=[lo_])
                sq = SCRA()
                sq_ap = sq.ap.bitcast(BF16)
                op("act", lambda e, sq_ap=sq_ap, lo_=lo_: e.activation(out=sq_ap[:, 0:n], in_=lo_.ap[:, 0:n], func=AF.Square),
                   reads=[lo_], writes=[sq])
                los.append(lo_)
                sqs.append((sq, sq_ap[:, 0:n], 128, 0))
            rb = fm_rstd(sqs, 128, n, 512.0, SCRA)
            for ct in range(4):
                op("dve", lambda e, ct=ct, lo_=los[ct], rb=rb: e.scalar_tensor_tensor(
                    out=MIXL3[:, ct, own0:own0 + n], in0=lo_.ap[:, 0:n], scalar=ppc("glru", ct), in1=rb.ap[:, 0:n],
                    op0=ALU.mult, op1=ALU.mult), reads=[lo_, rb, PP], writes=[MIXL])

        blks = []
        for k in range(5):
            for j in range(NB):
                blks.append((k, j, 512, False))
            if k >= 3:
                blks.append((k, NB, 1, True))
        phaseA_front(*blks[0], 0)
        phaseA_xr(*blks[0], 0)
        for bi, (k, j, n, mini) in enumerate(blks):
            mid = None
            if bi + 1 < len(blks):
                phaseA_front(*blks[bi + 1], (bi + 1) % 2)
                mid = (lambda nb=blks[bi + 1], hb_=(bi + 1) % 2: phaseA_xr(*nb, hb_))
            phaseA_block(k, j, n, mini, bi % 2, mid)
            if mini and k == 3:
                op("dve", lambda e: e.tensor_copy(out=HE.ap[:, 0:4], in_=HSC3[:, :, 0]), reads=[HSC], writes=[HE])
            if mini and k == 4:
                op("dve", lambda e: e.tensor_copy(out=HE.ap[:, 4:8], in_=HSC3[:, :, 0]), reads=[HSC], writes=[HE])
        chk('A6')
        HE3 = HE.ap[:, 8:16].rearrange("p (c t) -> p c t", c=4)
        op("dve", lambda e: e.tensor_tensor(out=HE3[:, :, 0], in0=HE.ap[:, 0:4], in1=CAR.ap[:, 52:56], op=ALU.add),
           reads=[HE, CAR], writes=[HE])
        op("dve", lambda e: e.tensor_tensor(out=HE3[:, :, 1], in0=HE.ap[:, 4:8], in1=CAR.ap[:, 48:52], op=ALU.add),
           reads=[HE, CAR], writes=[HE])
        ZERO = SCRA()
        op("pool", lambda e: e.memset(ZERO.ap[:, 0:8], 0.0), writes=[ZERO])
        lru_combine(lambda ct: HE3[:, ct, :], HE, lambda ct: ZERO.ap[:, 0:2], ZERO, CH, 2)
        if "mixl" in dbg_d:
            TMPD = A.f32(4 * NOWN, "tmpd")
            op("dve", lambda e: e.tensor_copy(out=TMPD.ap, in_=MIXL.ap), reads=[MIXL], writes=[TMPD])
            dump("mixl", TMPD, TMPD.ap)
        if "ckvn" in dbg_d:
            TMPD2 = A.f32(NKEY, "tmpd2")
            op("dve", lambda e: e.tensor_copy(out=TMPD2.ap, in_=CKVN.ap), reads=[CKVN], writes=[TMPD2])
            dump("ckvn", TMPD2, TMPD2.ap)
        if "ets" in dbg_d:
            TMPD3 = A.f32(NKEY, "tmpd3")
            op("dve", lambda e: e.tensor_copy(out=TMPD3.ap[0:96], in_=ETS.ap[0:96]), reads=[ETS], writes=[TMPD3])
            dump("ets", TMPD3, TMPD3.ap[0:96])

        S.barrier_all()
        A.release(mA)
        if stop == 'A':
            S.emit(nc)
            return nc

        WUQ = A.bf(2 * 768, "wuq")
        WUQ3 = WUQ.ap.rearrange("p (k c) -> p k c", k=2)
        for c2 in range(2):
            load(WUQ, WUQ3[:, c2, :], w_uq[c2 * 128:(c2 + 1) * 128, :])
        WUQR = A.bf(2 * 8 * 96, "wuqr")
        WUQR4 = WUQR.ap.rearrange("p (k h c) -> p k h c", k=2, h=8)
        WUQ4 = WUQ.ap.rearrange("p (k h c) -> p k h c", k=2, h=8)
        op("pool", lambda e: e.memset(WUQR.ap, 0.0), writes=[WUQR])
        for c2 in range(2):
            op("dve", lambda e, c2=c2: e.tensor_scalar(out=WUQR4[:, c2, :, 64:80], in0=WUQ4[:, c2, :, 80:96], scalar1=-1.0,
                                                       scalar2=None, op0=ALU.mult), reads=[WUQ], writes=[WUQR])
            op("dve", lambda e, c2=c2: e.tensor_copy(out=WUQR4[:, c2, :, 80:96], in_=WUQ4[:, c2, :, 64:80]),
               reads=[WUQ], writes=[WUQR])
        WUKV = A.bf(1024, "wukv")
        load(WUKV, WUKV.ap, w_ukv)
        ON = A.bf(8 * NOWN, "on")
        ON3 = ON.ap.rearrange("p (h t) -> p h t", h=8)
        TAB = A.bf(2 * NOWN, "qtab")
        mB = A.mark()
        KT = [A.bf(NKEY, "kt%d" % i) for i in range(2)]
        VV = [A.bf(NKT * 65, "v%d" % i) for i in range(2)]
        VV3 = [v.ap.rearrange("p (t c) -> p t c", c=65) for v in VV]
        QT = [A.bf(NOWN, "qt%d" % i) for i in range(2)]
        PT = [Buf(NT_H[i].ap[:, 0:512], NT_H[i].r) for i in range(2)] + [A.bf(512, "pt%d" % i) for i in range(2)]
        scrB = [Buf(NT_X[i].ap[:, 0:512], NT_X[i].r) for i in range(2)] + [A.f32(512, "scrB%d" % i) for i in range(6)]
        sbi = [0]

        def SCRB():
            b = scrB[sbi[0] % len(scrB)]
            sbi[0] += 1
            return b
        for v3, v in zip(VV3, VV):
            op("pool", lambda e, v3=v3: e.memset(v3[:, :, 64:65], 1.0), writes=[v])
        nqb = -(-NOWN // 512)
        half = NOWN // 2
        qb_base = half // nqb
        qb_sizes = [2 * (qb_base + (1 if i < half - qb_base * nqb else 0)) for i in range(nqb)]
        qblocks = [(sum(qb_sizes[:i]), qb_sizes[i]) for i in range(nqb)]
        for (q0, n) in qblocks:
            rope_tables(poso[:, q0:q0 + n], n, SCRB, out_buf=TAB, out_s=TAB.ap[64:96, q0:q0 + n],
                        out_c=TAB.ap[64:96, NOWN + q0:NOWN + q0 + n])
        pti = [0]

        RK = [A.f32(NKT, "rk%d" % i) for i in range(2)]
        SSK = psum[5]

        for kt_ in KT:
            op("dve", lambda e, kt_=kt_: e.tensor_copy(out=kt_.ap[64:96, :], in_=ETS.ap[64:96, :]), reads=[ETS], writes=[kt_])

        TSS = A.f32(NKT, "tss")

        def mm_tss(e):
            ins = None
            for t in range(NKT):
                ins = e.matmul(SSK.ap[:, t:t + 1], lhsT=ETS.ap[0:32, t * 128:(t + 1) * 128], rhs=ONESB.ap[0:32, 0:1],
                               start=True, stop=True)
            return ins
        op("pe", mm_tss, reads=[ETS, ONESB], writes=[SSK])
        op("dve", lambda e: e.tensor_copy(out=TSS.ap[:, 0:NKT], in_=SSK.ap[:, 0:NKT]), reads=[SSK], writes=[TSS])

        def kgen(h):
            kt = KT[h % 2]
            rk = RK[h % 2]
            pend_mms = []
            for kb in range(NKB):
                c0 = kb * 512
                pk = PS()
                op("pe", lambda e, pk=pk, c0=c0: e.matmul(pk.ap[0:64, :], lhsT=WUKV.ap[:, h * 128:h * 128 + 64],
                                                      rhs=CKVN.ap[:, c0:c0 + 512], start=True, stop=True),
                   reads=[WUKV, CKVN], writes=[pk])
                sq = SCRB()
                sq_ap = sq.ap.bitcast(BF16)
                op("act", lambda e, sq_ap=sq_ap, pk=pk: e.activation(out=sq_ap[0:64, 0:512], in_=pk.ap[0:64, :], func=AF.Square),
                   reads=[pk], writes=[sq])
                op("dve", lambda e, pk=pk, c0=c0: e.tensor_scalar(out=kt.ap[0:64, c0:c0 + 512], in0=pk.ap[0:64, :],
                                                                 scalar1=ppc("gk", 0, slice(0, 64)), scalar2=None, op0=ALU.mult),
                   reads=[pk, PP, sq], writes=[kt])

                def mms(e, sq_ap=sq_ap, kb=kb, c0=c0):
                    ins = None
                    for i in range(4):
                        t = kb * 4 + i
                        ins = e.matmul(SSK.ap[:, t:t + 1], lhsT=sq_ap[0:64, i * 128:(i + 1) * 128], rhs=ONESB.ap[0:64, 0:1],
                                       start=True, stop=True)
                    return ins
                if pend_mms:
                    pm, psq = pend_mms.pop(0)
                    op("pe", pm, reads=[psq, ETS, ONESB], writes=[SSK])
                pend_mms.append((mms, sq))
            while pend_mms:
                pm, psq = pend_mms.pop(0)
                op("pe", pm, reads=[psq, ETS, ONESB], writes=[SSK])
            op("dve", lambda e: e.scalar_tensor_tensor(out=rk.ap[:, 0:NKT], in0=SSK.ap[:, 0:NKT], scalar=96.0 * EPS,
                                                       in1=TSS.ap[:, 0:NKT], op0=ALU.add, op1=ALU.add),
               reads=[SSK, TSS], writes=[rk])
            op("act", lambda e: e.activation(out=rk.ap[:, 0:NKT], in_=rk.ap[:, 0:NKT], func=AF.Ln), reads=[rk], writes=[rk])
            op("act", lambda e: e.activation(out=rk.ap[:, 0:NKT], in_=rk.ap[:, 0:NKT], func=AF.Exp, scale=-0.5), reads=[rk], writes=[rk])

        def vgen(h):
            vv = VV[h % len(VV)]
            vv3 = VV3[h % len(VV)]
            for kb in range(NKB):
                c0 = kb * 512
                pv = PS()

                def mmv(e, pv=pv, c0=c0):
                    ins = None
                    for i in range(4):
                        ins = e.matmul(pv.ap[:, i * 64:(i + 1) * 64], lhsT=CKVN.ap[:, c0 + i * 128:c0 + (i + 1) * 128],
                                       rhs=WUKV.ap[:, h * 128 + 64:h * 128 + 128], start=True, stop=True)
                    return ins
                op("pe", mmv, reads=[CKVN, WUKV], writes=[pv])
                op("dve", lambda e, pv=pv, kb=kb: e.tensor_copy(out=vv3[:, kb * 4:(kb + 1) * 4, 0:64],
                                                             in_=pv.ap[:, 0:256].rearrange("p (t c) -> p t c", c=64)),
                   reads=[pv], writes=[vv])

        def do_head(h):
            kt = KT[h % 2]
            rk = RK[h % 2]
            vv = VV[h % len(VV)]
            vv3 = VV3[h % len(VV)]
            qt = QT[h % 2]
            qst = {}

            def qgenA(q0, n):
                pq = PS()
                pr = PS()

                def mmq(e):
                    ins = None
                    for c2 in range(2):
                        ins = e.matmul(pq.ap[0:96, 0:n], lhsT=WUQ3[:, c2, h * 96:(h + 1) * 96], rhs=CQN3[:, c2, q0:q0 + n],
                                       start=(c2 == 0), stop=(c2 == 1))
                    return ins

                def mmr(e):
                    ins = None
                    for c2 in range(2):
                        ins = e.matmul(pr.ap[0:96, 0:n], lhsT=WUQR4[:, c2, h, :], rhs=CQN3[:, c2, q0:q0 + n],
                                       start=(c2 == 0), stop=(c2 == 1))
                    return ins
                op("pe", mmq, reads=[WUQ, CQN], writes=[pq])
                op("pe", mmr, reads=[WUQR, CQN], writes=[pr])
                sq = SCRB()
                sq_ap = sq.ap.bitcast(BF16)
                op("act", lambda e: e.activation(out=sq_ap[0:96, 0:n], in_=pq.ap[0:96, 0:n], func=AF.Square),
                   reads=[pq], writes=[sq])
                e2 = SCRB()
                op("dve", lambda e: e.scalar_tensor_tensor(
                    out=e2.ap[64:96, 0:n], in0=pr.ap[64:96, 0:n], scalar=ppc("gqr", 0, slice(64, 96)),
                    in1=TAB.ap[64:96, q0:q0 + n], op0=ALU.mult, op1=ALU.mult), reads=[pr, TAB, PP], writes=[e2])
                qst[q0] = (pq, pr, sq, sq_ap, e2)

            def qgenB(q0, n):
                pq, pr, sq, sq_ap, e2 = qst[q0]
                rb = fm_rstd([(sq, sq_ap[0:96, 0:n], 96, 0)], 96, n, 96.0, SCRB, pbuf=pr)
                op("dve", lambda e: e.scalar_tensor_tensor(
                    out=qt.ap[0:64, q0:q0 + n], in0=pq.ap[0:64, 0:n], scalar=ppc("gq", 0, slice(0, 64)), in1=rb.ap[0:64, 0:n],
                    op0=ALU.mult, op1=ALU.mult), reads=[pq, rb, PP], writes=[qt])
                e1 = SCRB()
                op("dve", lambda e: e.scalar_tensor_tensor(
                    out=e1.ap[64:96, 0:n], in0=pq.ap[64:96, 0:n], scalar=ppc("gq", 0, slice(64, 96)),
                    in1=TAB.ap[64:96, NOWN + q0:NOWN + q0 + n], op0=ALU.mult, op1=ALU.mult), reads=[pq, TAB, PP, sq], writes=[e1])
                op("dve", lambda e: e.tensor_tensor(out=e1.ap[64:96, 0:n], in0=e1.ap[64:96, 0:n],
                                                    in1=e2.ap[64:96, 0:n], op=ALU.add),
                   reads=[e1, e2], writes=[e1])
                op("dve", lambda e: e.tensor_tensor(out=qt.ap[64:96, q0:q0 + n], in0=e1.ap[64:96, 0:n],
                                                    in1=rb.ap[64:96, 0:n], op=ALU.mult),
                   reads=[e1, rb], writes=[qt])
            qgenA(*qblocks[0])
            for qi in range(len(qblocks)):
                if qi + 1 < len(qblocks):
                    qgenA(*qblocks[qi + 1])
                qgenB(*qblocks[qi])
            scale = 96.0 ** -0.5
            fin_pending = []
            for (q0, n) in qblocks:
                po = PSACC()
                pend = []

                def qk(t, q0=q0, n=n):
                    pb = PS()
                    op("pe", lambda e, pb=pb, t=t: e.matmul(pb.ap[:, 0:n], lhsT=kt.ap[0:96, t * 128:(t + 1) * 128],
                                                        rhs=qt.ap[0:96, q0:q0 + n], start=True, stop=True),
                       reads=[kt, qt], writes=[pb])
                    return pb

                def expv(t, pb, q0=q0, n=n, po=po):
                    pt_ = PT[pti[0] % 4]
                    pti[0] += 1
                    op("act", lambda e, pb=pb, pt_=pt_, t=t: e.activation(out=pt_.ap[:, 0:n], in_=pb.ap[:, 0:n], func=AF.Exp,
                                                                     scale=rk.ap[:, t:t + 1]), reads=[pb, rk], writes=[pt_])
                    op("pe", lambda e, pt_=pt_, t=t: e.matmul(po.ap[0:65, 0:n], lhsT=vv3[:, t, :], rhs=pt_.ap[:, 0:n],
                                                          start=(t == 0), stop=(t == NKT - 1)),
                       reads=[vv, pt_], writes=[po])
                LOOK = 2
                DEFER = 8
                for t in range(NKT + LOOK):
                    if t < NKT:
                        pend.append((t, qk(t)))
                    if t >= LOOK:
                        tt, pb = pend.pop(0)
                        expv(tt, pb)
                    if t == DEFER and fin_pending:
                        fin_pending.pop(0)()
                rd = SCRB()
                op("dve", lambda e, rd=rd, po=po, n=n: e.reciprocal(out=rd.ap[64:65, 0:n], in_=po.ap[64:65, 0:n]),
                   reads=[po], writes=[rd])

                def fin(rd=rd, po=po, q0=q0, n=n):
                    pbc = PS()
                    op("pe", lambda e, pbc=pbc: e.matmul(pbc.ap[0:64, 0:n], lhsT=ONESF.ap[64:65, 0:64], rhs=rd.ap[64:65, 0:n],
                                                         start=True, stop=True), reads=[ONESF, rd], writes=[pbc])
                    oc = SCRB()
                    op("act", lambda e, oc=oc: e.activation(out=oc.ap[0:64, 0:n], in_=po.ap[0:64, 0:n], func=AF.Copy),
                       reads=[po, rd], writes=[oc])
                    op("dve", lambda e, oc=oc, pbc=pbc: e.tensor_tensor(out=ON3[0:64, h, q0:q0 + n], in0=oc.ap[0:64, 0:n],
                                                                         in1=pbc.ap[0:64, 0:n], op=ALU.mult),
                       reads=[oc, pbc], writes=[ON])
                fin_pending.append(fin)
            while fin_pending:
                fin_pending.pop(0)()
        chk('B0')
        kgen(0)
        chk('B1')
        vgen(0)
        chk('B2')
        for h_ in range(8):
            if h_ < 7:
                kgen(h_ + 1)
                if len(VV) > 1:
                    vgen(h_ + 1)
            do_head(h_)
            if h_ < 7 and len(VV) == 1:
                vgen(h_ + 1)
        for (q0, n) in qblocks:
            sqs = []
            for h in range(8):
                sq = SCRB()
                sq_ap = sq.ap.bitcast(BF16)
                op("act", lambda e, sq_ap=sq_ap, h=h, q0=q0, n=n: e.activation(out=sq_ap[0:64, 0:n], in_=ON3[0:64, h, q0:q0 + n],
                                                                            func=AF.Square), reads=[ON], writes=[sq])
                sqs.append((sq, sq_ap[0:64, 0:n], 64, 0))
            rb = fm_rstd(sqs, 64, n, 512.0, SCRB)
            for h in range(8):
                op("dve", lambda e, h=h, rb=rb, q0=q0, n=n: e.scalar_tensor_tensor(
                    out=ON3[0:64, h, q0:q0 + n], in0=ON3[0:64, h, q0:q0 + n], scalar=ppc("gmla", h, slice(0, 64)),
                    in1=rb.ap[0:64, 0:n], op0=ALU.mult, op1=ALU.mult), reads=[ON, rb, PP], writes=[ON])
        if "on" in dbg_d:
            TMPD4 = A.f32(8 * NOWN, "tmpd4")
            op("dve", lambda e: e.tensor_copy(out=TMPD4.ap[0:64], in_=ON.ap[0:64]), reads=[ON], writes=[TMPD4])
            dump("on", TMPD4, TMPD4.ap[0:64])
        S.barrier_all()
        A.release(mB)
        if stop == 'B':
            S.emit(nc)
            return nc
        NTL = CH // 128
        tiles = [(i * 128, 128) for i in range(NTL)] + [(CH, 2)]
        xres_start = A.top()
        XRES = A.f32((NTL + 1) * D, "xres")
        xres_end = A.top()
        XRES3 = XRES.ap.rearrange("p (t d) -> p t d", d=D)
        XR_res = [Res("xres%d" % i) for i in range(NTL + 1)]
        XT_ = [Buf(XRES3[:, i, :], XR_res[i]) for i in range(NTL + 1)]
        mC = A.mark()
        A.regs = [[ckvn_start, ets_end], [xres_end, ASZ]]
        WOL = A.bf(4 * D, "wol")
        WOL3 = WOL.ap.rearrange("p (c d) -> p c d", c=4)
        WOM = A.bf(8 * D, "wom")
        WOM3 = WOM.ap.rearrange("p (h d) -> p h d", h=8)
        load(WOL, WOL3, w_out[0:512, :].rearrange("(c p) d -> p c d", p=128))
        load(WOM, WOM3[0:64], w_out[512:1024, :].rearrange("(h p) d -> p h d", p=64))
        for ti, (o0, rows) in enumerate(tiles):
            xt = XT_[ti]
            src = xs[3, o0:o0 + rows, :] if ti < NTL else xh[0:2, :]
            load(xt, xt.ap[0:rows], src, eng="sp")
            for half in range(2):
                pb = PS()

                def mmo(e, pb=pb, o0=o0, rows=rows, half=half):
                    ins = None
                    for ct in range(4):
                        ins = e.matmul(pb.ap[0:rows, :], lhsT=MIXL3[:, ct, o0:o0 + rows],
                                       rhs=WOL3[:, ct, half * 512:(half + 1) * 512], start=(ct == 0), stop=False)
                    for h in range(8):
                        ins = e.matmul(pb.ap[0:rows, :], lhsT=ON3[0:64, h, o0:o0 + rows],
                                       rhs=WOM3[0:64, h, half * 512:(half + 1) * 512], start=False, stop=(h == 7))
                    return ins
                op("pe", mmo, reads=[MIXL, ON, WOL, WOM], writes=[pb])
                op("dve", lambda e, pb=pb, xt=xt, rows=rows, half=half: e.tensor_tensor(
                    out=xt.ap[0:rows, half * 512:(half + 1) * 512], in0=xt.ap[0:rows, half * 512:(half + 1) * 512],
                    in1=pb.ap[0:rows, :], op=ALU.add), reads=[xt, pb], writes=[xt])
        if "x1" in dbg_d:
            for ti in range(NTL):
                store(dbg_d["x1"][ti * 128:(ti + 1) * 128, :], XT_[ti], XT_[ti].ap)
        S.barrier_all()
        A.regs = [[mP, xres_start], [xres_end, ASZ]]
        if stop == 'C':
            S.emit(nc)
            return nc

        GB2 = A.f32(D, "gbc2")
        GB3 = A.f32(D, "gbc3")
        load(GB2, GB2.ap, gvec[1, :].partition_broadcast(128))
        load(GB3, GB3.ap, gvec[3, :].partition_broadcast(128))
        HNT = A.bf(8 * NOWN, "hnt")
        HNT3 = HNT.ap.rearrange("p (k t) -> p k t", k=8)
        mD = A.mark()
        WMQ = A.bf(8 * 512, "wmq")
        WMQ3 = WMQ.ap.rearrange("p (k c) -> p k c", k=8)
        load(WMQ, WMQ3, w_mq.rearrange("(k p) c -> p k c", p=128))
        WMO = A.bf(4 * D, "wmo")
        WMO3 = WMO.ap.rearrange("p (h d) -> p h d", h=4)
        load(WMO, WMO3, w_mo.rearrange("(h p) d -> p h d", p=128))
        H1T = A.bf(8 * 512, "h1t")
        H1T3 = H1T.ap.rearrange("p (k t) -> p k t", k=8)
        OMT = A.bf(4 * 512, "omt")
        OMT3 = OMT.ap.rearrange("p (h t) -> p h t", h=4)
        HNH = A.bf(8 * 2, "hnh")
        HNH3 = HNH.ap.rearrange("p (k t) -> p k t", k=8)
        scrD = [A.f32(512, "scrD%d" % i) for i in range(10)]
        sdi = [0]

        def SCRD():
            b = scrD[sdi[0] % len(scrD)]
            sdi[0] += 1
            return b
        mscale = 128.0 ** -0.5
        qbl = [(q0, min(512, CH - q0)) for q0 in range(0, CH, 512)] + [(CH, 2)]
        for (q0, n) in qbl:
            tl = [ti for ti, (o0, rows) in enumerate(tiles) if q0 <= o0 < q0 + n]
            for ti in tl:
                o0, rows = tiles[ti]
                norm_T(XT_[ti], XT_[ti].ap[0:rows], rows, H1T, H1T3, o0 - q0, gb=GB2)
            for h in range(4):
                pq = PS()

                def mmq2(e, pq=pq, h=h, n=n):
                    ins = None
                    for kc in range(8):
                        ins = e.matmul(pq.ap[:, 0:n], lhsT=WMQ3[:, kc, h * 128:(h + 1) * 128], rhs=H1T3[:, kc, 0:n],
                                       start=(kc == 0), stop=(kc == 7))
                    return ins
                op("pe", mmq2, reads=[WMQ, H1T], writes=[pq])
                sq = SCRD()
                sq_ap = sq.ap.bitcast(BF16)
                op("act", lambda e, sq_ap=sq_ap, pq=pq, n=n: e.activation(out=sq_ap[:, 0:n], in_=pq.ap[:, 0:n], func=AF.Square),
                   reads=[pq], writes=[sq])
                rb = fm_rstd([(sq, sq_ap[:, 0:n], 128, 0)], 128, n, 128.0, SCRD)
                qm = SCRD()
                qm_ap = qm.ap.bitcast(BF16)
                op("dve", lambda e, pq=pq, rb=rb, qm_ap=qm_ap, n=n: e.scalar_tensor_tensor(
                    out=qm_ap[:, 0:n], in0=pq.ap[:, 0:n], scalar=ppc("gmq"), in1=rb.ap[:, 0:n], op0=ALU.mult, op1=ALU.mult),
                   reads=[pq, rb, PP], writes=[qm])
                po = PS()
                pd = PS()
                for mt in range(2):
                    ps_ = PS()
                    op("pe", lambda e, ps_=ps_, mt=mt, h=h, qm_ap=qm_ap, n=n: e.matmul(
                        ps_.ap[:, 0:n], lhsT=KM3[:, h, mt * 128:(mt + 1) * 128], rhs=qm_ap[:, 0:n], start=True, stop=True),
                       reads=[KM, qm], writes=[ps_])
                    pt_ = SCRD()
                    pt_ap = pt_.ap.bitcast(BF16)
                    op("act", lambda e, ps_=ps_, pt_ap=pt_ap, n=n: e.activation(out=pt_ap[:, 0:n], in_=ps_.ap[:, 0:n], func=AF.Exp,
                                                                         scale=mscale), reads=[ps_], writes=[pt_])
                    op("pe", lambda e, po=po, mt=mt, h=h, pt_ap=pt_ap, n=n: e.matmul(
                        po.ap[:, 0:n], lhsT=VM3[:, mt, h * 128:(h + 1) * 128], rhs=pt_ap[:, 0:n], start=(mt == 0), stop=(mt == 1)),
                       reads=[VM, pt_], writes=[po])
                    op("pe", lambda e, pd=pd, mt=mt, pt_ap=pt_ap, n=n: e.matmul(
                        pd.ap[:, 0:n], lhsT=ONESB.ap[:, 0:128], rhs=pt_ap[:, 0:n], start=(mt == 0), stop=(mt == 1)),
                       reads=[ONESB, pt_], writes=[pd])
                rd = SCRD()
                op("dve", lambda e, rd=rd, pd=pd, n=n: e.reciprocal(out=rd.ap[:, 0:n], in_=pd.ap[:, 0:n]),
                   reads=[pd], writes=[rd])
                op("dve", lambda e, po=po, rd=rd, h=h, n=n: e.tensor_tensor(out=OMT3[:, h, 0:n], in0=po.ap[:, 0:n], in1=rd.ap[:, 0:n],
                                                                         op=ALU.mult), reads=[po, rd], writes=[OMT])
            for ti in tl:
                o0, rows = tiles[ti]
                xt = XT_[ti]
                c0 = o0 - q0
                for half in range(2):
                    pb = PS()

                    def mmo2(e, pb=pb, c0=c0, rows=rows, half=half):
                        ins = None
                        for h in range(4):
                            ins = e.matmul(pb.ap[0:rows, :], lhsT=OMT3[:, h, c0:c0 + rows],
                                           rhs=WMO3[:, h, half * 512:(half + 1) * 512], start=(h == 0), stop=(h == 3))
                        return ins
                    op("pe", mmo2, reads=[OMT, WMO], writes=[pb])
                    op("dve", lambda e, pb=pb, xt=xt, rows=rows, half=half: e.tensor_tensor(
                        out=xt.ap[0:rows, half * 512:(half + 1) * 512], in0=xt.ap[0:rows, half * 512:(half + 1) * 512],
                        in1=pb.ap[0:rows, :], op=ALU.add), reads=[xt, pb], writes=[xt])
                if ti < NTL:
                    norm_T(xt, xt.ap[0:rows], rows, HNT, HNT3, o0 + 1, gb=GB3, evac_eng="act")
                else:
                    norm_T(xt, xt.ap[0:rows], rows, HNH, HNH3, 0, gb=GB3)
                    op("dve", lambda e: e.tensor_scalar(out=HNT3[:, :, CH + 1:CH + 2], in0=HNH3[:, :, 0:1], scalar1=ppc("msk", 0),
                                                        scalar2=None, op0=ALU.mult), reads=[HNH, PP], writes=[HNT])
                    op("dve", lambda e: e.tensor_scalar(out=HNT3[:, :, 0:1], in0=HNH3[:, :, 1:2], scalar1=ppc("msk", 1),
                                                        scalar2=None, op0=ALU.mult), reads=[HNH, PP], writes=[HNT])
        if "x2" in dbg_d:
            for ti in range(NTL):
                store(dbg_d["x2"][ti * 128:(ti + 1) * 128, :], XT_[ti], XT_[ti].ap)
        S.barrier_all()
        A.release(mD)
        if stop == 'D':
            S.emit(nc)
            return nc

        GP = 2
        NR = 22 // GP
        WUPG = [A.bf(8 * 2 * GP * 128, "wupg%d" % i) for i in range(2)]
        WUPG4 = [w.ap.rearrange("p (k s c) -> p k s c", k=8, s=2) for w in WUPG]
        WDNG = [A.bf(GP * D, "wdng%d" % i) for i in range(2)]
        WDNG3 = [w.ap.rearrange("p (j d) -> p j d", j=GP) for w in WDNG]
        ACTT = [A.bf(GP * CH, "actt%d" % i) for i in range(2)]
        ACTT3 = [a.ap.rearrange("p (j t) -> p j t", j=GP) for a in ACTT]
        scrE = [A.f32(512, "scrE%d" % i) for i in range(8)]
        sei = [0]

        def SCRE():
            b = scrE[sei[0] % len(scrE)]
            sei[0] += 1
            return b
        fwo = PPL["fw"][0]
        fbo = PPL["fb"][0]
        def do_round(r):
            wu = WUPG[r % 2]
            wu4 = WUPG4[r % 2]
            wd = WDNG[r % 2]
            wd3 = WDNG3[r % 2]
            at = ACTT[r % 2]
            at3 = ACTT3[r % 2]
            j0 = r * GP
            load(wu, wu4[:, :, 0, :], w_up[:, j0 * 128:(j0 + GP) * 128].rearrange("(k p) c -> p k c", p=128))
            load(wu, wu4[:, :, 1, :], w_up[:, DFF + j0 * 128:DFF + (j0 + GP) * 128].rearrange("(k p) c -> p k c", p=128))
            load(wd, wd3, w_dn[j0 * 128:(j0 + GP) * 128, :].rearrange("(j p) d -> p j d", p=128))
            for (t0, n) in WINS:
                for jj in range(GP):
                    cv = []
                    for s in range(2):
                        chn = (j0 + jj) + s * 22
                        pg = PS()

                        def mmu(e, pg=pg, s=s, jj=jj, t0=t0, n=n):
                            ins = None
                            for kc in range(8):
                                ins = e.matmul(pg.ap[:, 0:n + 2], lhsT=wu4[:, kc, s, jj * 128:(jj + 1) * 128],
                                               rhs=HNT3[:, kc, t0:t0 + n + 2], start=(kc == 0), stop=(kc == 7))
                            return ins
                        op("pe", mmu, reads=[wu, HNT], writes=[pg])
                        c_ = SCRE()
                        wsrc, wb = PP, fwo + 3 * chn
                        bsrc, bb = PP, fbo + chn
                        op("act", lambda e, c_=c_, pg=pg, wsrc=wsrc, wb=wb, bsrc=bsrc, bb=bb, n=n: e.activation(
                            out=c_.ap[:, 0:n], in_=pg.ap[:, 1:n + 1], func=AF.Identity,
                            scale=wsrc.ap[:, wb + 1:wb + 2], bias=bsrc.ap[:, bb:bb + 1]),
                           reads=[pg, wsrc, bsrc], writes=[c_])
                        for tp in (0, 2):
                            op("dve", lambda e, c_=c_, pg=pg, wsrc=wsrc, wb=wb, tp=tp, n=n: e.scalar_tensor_tensor(
                                out=c_.ap[:, 0:n], in0=pg.ap[:, tp:tp + n], scalar=wsrc.ap[:, wb + tp:wb + tp + 1],
                                in1=c_.ap[:, 0:n], op0=ALU.mult, op1=ALU.add), reads=[pg, wsrc, c_], writes=[c_])
                        cv.append(c_)
                    sg = SCRE()
                    op("act", lambda e, sg=sg, g_=cv[0], n=n: e.activation(out=sg.ap[:, 0:n], in_=g_.ap[:, 0:n], func=AF.Tanh, scale=0.5),
                       reads=[cv[0]], writes=[sg])
                    op("act", lambda e, sg=sg, n=n: e.activation(out=sg.ap[:, 0:n], in_=sg.ap[:, 0:n], func=AF.Identity, scale=0.5,
                                                                bias=POSH.ap[:, 0:1]), reads=[sg, POSH], writes=[sg])
                    op("dve", lambda e, g_=cv[0], u_=cv[1], n=n: e.tensor_tensor(out=u_.ap[:, 0:n], in0=g_.ap[:, 0:n], in1=u_.ap[:, 0:n],
                                                                                op=ALU.mult), reads=[cv[0], cv[1]], writes=[cv[1]])
                    op("dve", lambda e, sg=sg, u_=cv[1], jj=jj, t0=t0, n=n: e.tensor_tensor(
                        out=at3[:, jj, t0:t0 + n], in0=sg.ap[:, 0:n], in1=u_.ap[:, 0:n], op=ALU.mult),
                       reads=[sg, cv[1]], writes=[at])
            if not (r % 2 == 1 or r == NR - 1):
                return
            rds = [r - 1, r] if r % 2 == 1 else [r]
            srcs = [(ACTT3[rr % 2], WDNG3[rr % 2]) for rr in rds]
            rbufs = [ACTT[rr % 2] for rr in rds] + [WDNG[rr % 2] for rr in rds]
            nmm = GP * len(rds)
            for ti in range(NTL):
                o0, rows = tiles[ti]
                xt = XT_[ti]
                for half in range(2):
                    pb = PS()

                    def mmd(e, pb=pb, o0=o0, half=half):
                        ins = None
                        i_ = 0
                        for (a3_, w3_) in srcs:
                            for jj in range(GP):
                                ins = e.matmul(pb.ap[:, :], lhsT=a3_[:, jj, o0:o0 + 128], rhs=w3_[:, jj, half * 512:(half + 1) * 512],
                                               start=(i_ == 0), stop=(i_ == nmm - 1))
                                i_ += 1
                        return ins
                    op("pe", mmd, reads=rbufs, writes=[pb])
                    op("dve", lambda e, pb=pb, xt=xt, half=half: e.tensor_tensor(
                        out=xt.ap[:, half * 512:(half + 1) * 512], in0=xt.ap[:, half * 512:(half + 1) * 512],
                        in1=pb.ap[:, :], op=ALU.add), reads=[xt, pb], writes=[xt])
        for r_ in range(NR):
            do_round(r_)
        for ti in range(NTL):
            store(out_d[ti * 128:(ti + 1) * 128, :], XT_[ti], XT_[ti].ap)
        S.emit(nc)
    return nc


def _cols(v, rows=128):
    v = np.asarray(v, np.float32)
    return np.ascontiguousarray(v.reshape(-1, rows).T)


def prep_core(inp, b, c, CH):
    f32 = np.float32
    x = np.asarray(inp["x"][b], f32)
    pos = np.asarray(inp["positions"][b]).astype(np.int32)
    s0 = c * CH
    slots = [(k, False) for k in range(c)] + [(k, True) for k in range(3, c, -1)] + [(c, False), (c, True)]
    assert len(slots) == 5
    xs = np.stack([x[k * CH:(k + 1) * CH][::-1] if rev else x[k * CH:(k + 1) * CH] for k, rev in slots])
    pk = np.stack([pos[k * CH:(k + 1) * CH][::-1] if rev else pos[k * CH:(k + 1) * CH] for k, rev in slots[:4]])
    posk = np.ascontiguousarray(np.broadcast_to(pk[:, None, :], (4, 32, CH))).astype(np.int32)
    xh = np.zeros((2, D), f32)
    ph = np.zeros((2,), np.int32)
    msk = np.zeros((2,), f32)
    if c < 3:
        xh[0] = x[s0 + CH]; ph[0] = pos[s0 + CH]; msk[0] = 1.0
    if c > 0:
        xh[1] = x[s0 - 1]; ph[1] = pos[s0 - 1]; msk[1] = 1.0
    po = np.concatenate([pos[s0:s0 + CH], ph])
    poso = np.ascontiguousarray(np.broadcast_to(po[None, :], (32, CH + 2))).astype(np.int32)
    flg = np.zeros((5, 4), f32)
    for k in range(3):
        if k < c:
            flg[k] = [1, 0, 1, 0]
        else:
            flg[k] = [0, 1, 0, 1]
    flg[3] = [1, 0, 0, 0]
    flg[4] = [0, 1, 0, 0]
    pp = np.zeros((128, PPL["_n"]), f32)

    def put(name, arr):
        o, n = PPL[name]
        arr = np.asarray(arr, f32)
        if arr.ndim == 1:
            arr = np.broadcast_to(arr[None, :], (128, arr.shape[0]))
        assert arr.shape[1] == n, (name, arr.shape, n)
        pp[:arr.shape[0], o:o + n] = arr
    put("flg", flg.reshape(-1))
    put("msk", msk)
    cw = np.zeros((128, 5, 4, 4), f32)
    cb = np.zeros((128, 5, 4), f32); ba = np.zeros((128, 5, 4), f32); bi = np.zeros((128, 5, 4), f32); lam = np.zeros((128, 5, 4), f32)
    wa = np.zeros((5, 4, 128, 128), f32); wi = np.zeros((5, 4, 128, 128), f32)
    for k, (ck, rev) in enumerate(slots):
        d = 1 if rev else 0
        w = np.asarray(inp["lru_conv_w"][0, d], f32)
        if rev:
            w = w[::-1]
        for tap in range(4):
            cw[:, k, :, tap] = _cols(w[tap])
        cb[:, k, :] = _cols(inp["lru_conv_b"][0, d])
        ba[:, k, :] = _cols(inp["lru_b_a"][0, d])
        bi[:, k, :] = _cols(inp["lru_b_i"][0, d])
        lam[:, k, :] = _cols(inp["lru_lambda"][0, d])
        for ct in range(4):
            for half in range(2):
                blk = 2 * ct + half
                wa[k, ct, half * 64:(half + 1) * 64, half * 64:(half + 1) * 64] = inp["lru_w_a"][0, d, blk]
                wi[k, ct, half * 64:(half + 1) * 64, half * 64:(half + 1) * 64] = inp["lru_w_i"][0, d, blk]
    put("cw", cw.reshape(128, -1)); put("cb", cb.reshape(128, -1)); put("ba", ba.reshape(128, -1))
    put("bi", bi.reshape(128, -1)); put("lam", lam.reshape(128, -1))
    put("gqa", _cols(inp["q_a_norm"][0]))
    put("gkv", _cols(inp["kv_a_norm"][0]))

    def rotperm(g):
        g = np.asarray(g, f32)
        r = g.copy()
        r[64:80] = g[80:96]
        r[80:96] = g[64:80]
        return r
    gq = np.asarray(inp["mla_q_norm"][0], f32); gk = np.asarray(inp["mla_k_norm"][0], f32)
    for name, v in (("gq", gq), ("gqr", rotperm(gq)), ("gk", gk), ("gkr", rotperm(gk))):
        a = np.zeros((128, 1), f32); a[:96, 0] = v
        put(name, a)
    put("glru", _cols(inp["lru_out_norm"][0]))
    gm = np.zeros((128, 8), f32); gm[:64] = _cols(inp["mla_out_norm"][0], 64)
    put("gmla", gm)
    put("gmq", _cols(inp["mem_q_norm"][0])); put("gmk", _cols(inp["mem_k_norm"][0]))
    fw = np.zeros((128, 44, 3), f32)
    fcw = np.asarray(inp["ffn_conv_w"][0], f32)
    for tap in range(3):
        fw[:, :, tap] = _cols(fcw[tap])
    put("fw", fw.reshape(128, -1))
    put("fb", _cols(inp["ffn_conv_b"][0]))
    invf = np.zeros((128, 1), f32)
    inv = (10000.0 ** (-np.arange(0, 32, 2, dtype=np.float64) / 32.0)) / (2.0 * np.pi)
    for p in range(64, 96):
        invf[p, 0] = inv[(p - 64) % 16]
    put("invf", invf)
    gvec = np.stack([inp["attn_norm"][0], inp["mem_attn_norm"][0], inp["mem_norm"][0], inp["ffn_norm"][0]]).astype(f32)
    m = {
        "xs": np.ascontiguousarray(xs), "xh": xh, "posk": posk, "poso": poso, "pp": pp, "wa": wa, "wi": wi,
        "mem": np.ascontiguousarray(np.asarray(inp["mem"][b], f32)), "gvec": np.ascontiguousarray(gvec),
        "w_in": np.ascontiguousarray(inp["w_in"][0], dtype=f32), "w_uq": np.ascontiguousarray(inp["w_uq"][0], dtype=f32),
        "w_ukv": np.ascontiguousarray(inp["w_ukv"][0], dtype=f32), "w_out": np.ascontiguousarray(inp["w_out"][0], dtype=f32),
        "w_mem_q": np.ascontiguousarray(inp["w_mem_q"][0], dtype=f32),
        "w_mem_kv": np.ascontiguousarray(inp["w_mem_kv"][0], dtype=f32),
        "w_mem_o": np.ascontiguousarray(inp["w_mem_o"][0], dtype=f32),
        "w_up": np.ascontiguousarray(inp["w_up"][0], dtype=f32), "w_down": np.ascontiguousarray(inp["w_down"][0], dtype=f32),
    }
    return m


_NC_CACHE = {}


def run(inputs, dbg=None, cores=None, stop=None):
    inputs = {k: np.asarray(v) for k, v in inputs.items()}
    B, SEQ, _ = inputs["x"].shape
    CH = SEQ // 4
    key = (CH, repr(dbg), stop)
    if key not in _NC_CACHE:
        _NC_CACHE[key] = build(CH, dbg, stop)
    nc = _NC_CACHE[key]
    core_list = cores if cores is not None else [(b, c) for b in range(B) for c in range(4)]
    in_maps = [prep_core(inputs, b, c, CH) for (b, c) in core_list]
    res = run_bass_kernel_spmd(nc, in_maps, core_ids=list(range(len(core_list))), trace=bool(os.environ.get('KTRACE')))
    return res, core_list, CH


def kernel(**inputs):
    res, core_list, CH = run(inputs)
    B, SEQ, _ = np.asarray(inputs["x"]).shape
    out = np.zeros((B, SEQ, D), np.float32)
    for (b, c), r in zip(core_list, res.results):
        out[b, c * CH:(c + 1) * CH] = r["out"]
    return out
```

```python
import numpy as np
from contextlib import ExitStack
import concourse.bass as bass
import concourse.mybir as mybir
from concourse.bass_utils import run_bass_kernel_spmd

F32 = mybir.dt.float32
BF16 = mybir.dt.bfloat16
I32 = mybir.dt.int32
AF = mybir.ActivationFunctionType
ALU = mybir.AluOpType

import os
SAME_ENGINE_SYNC = os.environ.get('SAME_ENGINE_SYNC', '1') == '1'
EPS = 1e-6
D = 1024
DFF = 2816
NCH = 44
MEM = 256


class Res:
    __slots__ = ("name", "w", "r")

    def __init__(self, name=""):
        self.name = name
        self.w = None
        self.r = {}


class Sched:
    ENGS = ("pe", "act", "dve", "pool", "sp")

    def __init__(self):
        self.ops = {e: [] for e in self.ENGS}
        self.cnt = {e: 0 for e in self.ENGS}
        self.dcnt = {}
        self.seen = {e: {} for e in self.ENGS}

    def _need(self, eng, tok, waits):
        if tok is None:
            return
        key, val = tok
        if self.seen[eng].get(key, 0) >= val:
            return
        if val > waits.get(key, 0):
            waits[key] = val

    def op(self, eng, fn, reads=(), writes=(), dma=None):
        waits = {}
        for r in reads:
            self._need(eng, r.w, waits)
        for w in writes:
            self._need(eng, w.w, waits)
            for k, v in w.r.items():
                self._need(eng, (k, v), waits)
        if not SAME_ENGINE_SYNC:
            waits.pop(eng, None)
        for k, v in waits.items():
            self.seen[eng][k] = v
        if dma is None:
            self.cnt[eng] += 1
            tok = (eng, self.cnt[eng])
        else:
            self.dcnt[dma] = self.dcnt.get(dma, 0) + 16
            tok = (dma, self.dcnt[dma])
        self.ops[eng].append((list(waits.items()), fn, tok))
        for r in reads:
            if r.r.get(tok[0], 0) < tok[1]:
                r.r[tok[0]] = tok[1]
        for w in writes:
            w.w = tok
            w.r = {}
        return tok

    def barrier_all(self):
        for e in self.ENGS:
            waits = {}
            for e2 in self.ENGS:
                if e2 != e and self.cnt[e2] > self.seen[e].get(e2, 0):
                    waits[e2] = self.cnt[e2]
            for k, v in self.dcnt.items():
                if v > self.seen[e].get(k, 0):
                    waits[k] = v
            for k, v in waits.items():
                self.seen[e][k] = v
            if waits:
                self.ops[e].append((list(waits.items()), None, None))

    def emit(self, nc):
        keys = list(self.ENGS) + sorted(self.dcnt.keys())
        with ExitStack() as st:
            sems = {k: st.enter_context(nc.semaphore("s_" + k)) for k in keys}
            block = st.enter_context(nc.Block())
            engmap = {"pe": block.tensor, "act": block.scalar, "dve": block.vector,
                      "pool": block.gpsimd, "sp": block.sync}
            fin = {}
            for k in keys:
                v = self.cnt[k] if k in self.cnt else self.dcnt[k]
                if v > 0:
                    fin[k] = v
            self.ops["sp"].append((list(fin.items()), None, None))
            for e in self.ENGS:
                ops = self.ops[e]

                def body(engobj, ops=ops, e=e):
                    for waits, fn, tok in ops:
                        for k, v in waits:
                            engobj.wait_ge(sems[k], v)
                        if fn is None:
                            continue
                        ins = fn(engobj)
                        if tok[0] == e:
                            ins.then_inc(sems[e], 1)
                        else:
                            ins.then_inc(sems[tok[0]], 16)
                engmap[e](body)


class Buf:
    __slots__ = ("ap", "r")

    def __init__(self, ap, r):
        self.ap = ap
        self.r = r


class Arena:
    def __init__(self, t, size):
        self.t = t
        self.size = size
        self.regs = [[0, size]]

    def _take(self, n, name):
        for r in self.regs:
            if r[1] - r[0] >= n:
                o = r[0]
                r[0] += n
                return o
        raise AssertionError(("SBUF arena overflow", name, n, self.regs))

    def f32(self, n, name=""):
        o = self._take(n, name)
        return Buf(self.t[:, o:o + n], Res(name))

    def bf(self, n, name=""):
        m = (n + 1) // 2
        o = self._take(m, name)
        return Buf(self.t[:, o:o + m].bitcast(BF16)[:, 0:n], Res(name))

    def mark(self):
        return [list(r) for r in self.regs]

    def release(self, m):
        self.regs = [list(r) for r in m]

    def top(self):
        return self.regs[0][0]


def ffn_windows(CH):
    nw = -(-CH // 510)
    if CH % 512 == 0 and CH >= 512:
        nw = max(nw, 1)
    base = CH // nw
    rem = CH - base * nw
    sizes = [base + (1 if i < rem else 0) for i in range(nw)]
    starts = [sum(sizes[:i]) for i in range(nw)]
    return list(zip(starts, sizes))


def pp_layout():
    o = {}
    c = 0

    def add(name, n):
        nonlocal c
        o[name] = (c, n)
        c += n
    add("flg", 20)
    add("msk", 2)
    add("cw", 80)
    add("cb", 20)
    add("ba", 20)
    add("bi", 20)
    add("lam", 20)
    add("gqa", 2)
    add("gkv", 1)
    add("gq", 1)
    add("gqr", 1)
    add("gk", 1)
    add("gkr", 1)
    add("glru", 4)
    add("gmla", 8)
    add("gmq", 1)
    add("gmk", 1)
    add("fw", 132)
    add("fb", 44)
    add("invf", 1)
    o["_n"] = c
    return o


PPL = pp_layout()


class _Stop(Exception):
    pass


def build(CH, dbg=None, stop=None):
    holder = {}
    try:
        return _build(CH, dbg, stop, holder)
    except _Stop:
        return holder['nc']


def _build(CH, dbg, stop, holder):
    NB = CH // 512
    NKEY = 4 * CH
    NKT = NKEY // 128
    NKB = NKEY // 512
    NOWN = CH + 2
    WINS = ffn_windows(CH)
    nc = bass.Bass("TRN2", target_bir_lowering=False)
    holder["nc"] = nc

    def din(name, shape, dt=F32):
        return nc.dram_tensor(name, list(shape), dt, kind="ExternalInput").ap()

    xs = din("xs", [5, CH, D])
    xh = din("xh", [2, D])
    posk = din("posk", [4, 32, CH], I32)
    poso = din("poso", [32, NOWN], I32)
    pp_d = din("pp", [128, PPL["_n"]])
    wa_d = din("wa", [5, 4, 128, 128])
    wi_d = din("wi", [5, 4, 128, 128])
    mem_d = din("mem", [MEM, D])
    gvec = din("gvec", [4, D])
    w_in = din("w_in", [D, 1440])
    w_uq = din("w_uq", [256, 768])
    w_ukv = din("w_ukv", [128, 1024])
    w_out = din("w_out", [1024, D])
    w_mq = din("w_mem_q", [D, 512])
    w_mkv = din("w_mem_kv", [D, 1024])
    w_mo = din("w_mem_o", [512, D])
    w_up = din("w_up", [D, 2 * DFF])
    w_dn = din("w_down", [DFF, D])
    out_d = nc.dram_tensor("out", [CH, D], F32, kind="ExternalOutput").ap()
    dbg_d = {}
    if dbg:
        for name, shape in dbg.items():
            dbg_d[name] = nc.dram_tensor("dbg_" + name, list(shape), F32, kind="ExternalOutput").ap()

    S = Sched()
    with ExitStack() as st:
        ASZ = 52500
        arena_t = st.enter_context(nc.sbuf_tensor("arena", [128, ASZ], F32))
        A = Arena(arena_t, ASZ)
        psum_t = [st.enter_context(nc.psum_tensor("ps%d" % i, [128, 512], F32)) for i in range(8)]
        psum = [Buf(t[:, :], Res("ps%d" % i)) for i, t in enumerate(psum_t)]
        psi = [0]

        def PS():
            b = psum[psi[0] % 5]
            psi[0] += 1
            return b
        psa = [0]

        def PSACC():
            b = psum[6 + psa[0] % 2]
            psa[0] += 1
            return b

        def chk(tag):
            if stop == tag:
                S.emit(nc)
                raise _Stop()

        def op(eng, fn, reads=(), writes=(), dma=None):
            return S.op(eng, fn, [b.r for b in reads], [b.r for b in writes], dma)

        dkeys = {}

        def dkey(buf, pre="k"):
            k = (pre, id(buf.r))
            if k not in dkeys:
                dkeys[k] = "%s%02d" % (pre, len(dkeys))
            return dkeys[k]

        def load(dst, dst_ap, src_ap, eng=None, key=None):
            if eng is None:
                eng = "pool" if dst_ap.dtype != src_ap.dtype else "sp"
            op(eng, lambda e: e.dma_start(out=dst_ap, in_=src_ap), reads=[], writes=[dst], dma=dkey(dst))

        def store(dst_ap, buf, src_ap):
            op("sp", lambda e: e.dma_start(out=dst_ap, in_=src_ap), reads=[buf], dma=dkey(buf, "s"))

        def dump(name, buf, ap, rows=None):
            if name in dbg_d:
                store(dbg_d[name], buf, ap)

        PP = A.f32(PPL["_n"], "pp")
        load(PP, PP.ap, pp_d)

        def ppc(name, i=0, rows=slice(0, 128)):
            o, n = PPL[name]
            return PP.ap[rows, o + i:o + i + 1]

        CONST = A.f32(8, "const")
        op("pool", lambda e: e.memset(CONST.ap[:, 0:1], EPS), writes=[CONST])
        op("pool", lambda e: e.memset(CONST.ap[:, 1:2], 1.0), writes=[CONST])
        c_eps = CONST.ap[:, 0:1]
        c_one = CONST.ap[:, 1:2]
        NEGH = A.f32(8, "negh")
        POSH = A.f32(8, "posh")
        op("pool", lambda e: e.memset(NEGH.ap, -0.5), writes=[NEGH])
        op("pool", lambda e: e.memset(POSH.ap, 0.5), writes=[POSH])
        IDF = A.f32(128, "identf")
        IDENT = A.bf(128, "ident")
        ONESB = A.bf(128, "onesb")
        ONESF = A.f32(128, "onesf")
        op("pool", lambda e: e.iota(IDF.ap, [[1, 128]], base=0, channel_multiplier=-1,
                                    allow_small_or_imprecise_dtypes=True), writes=[IDF])
        op("dve", lambda e: e.tensor_scalar(out=IDENT.ap, in0=IDF.ap, scalar1=0.0, scalar2=None,
                                            op0=ALU.is_equal), reads=[IDF], writes=[IDENT])
        op("pool", lambda e: e.memset(ONESB.ap, 1.0), writes=[ONESB])
        op("pool", lambda e: e.memset(ONESF.ap, 1.0), writes=[ONESF])
        LP = A.f32(192, "lruparams")
        lam_o = PPL["lam"][0]
        flg_o = PPL["flg"][0]
        op("act", lambda e: e.activation(out=LP.ap[:, 0:20], in_=PP.ap[:, lam_o:lam_o + 20], func=AF.Exp, scale=-1.0),
           reads=[PP], writes=[LP])
        op("act", lambda e: e.activation(out=LP.ap[:, 0:20], in_=LP.ap[:, 0:20], func=AF.Ln, scale=1.0, bias=c_one),
           reads=[LP, CONST], writes=[LP])
        op("dve", lambda e: e.tensor_scalar(out=LP.ap[:, 20:40], in0=LP.ap[:, 0:20], scalar1=-4.0, scalar2=None,
                                            op0=ALU.mult), reads=[LP], writes=[LP])
        op("dve", lambda e: e.tensor_scalar(out=LP.ap[:, 0:20], in0=LP.ap[:, 0:20], scalar1=-8.0, scalar2=None,
                                            op0=ALU.mult), reads=[LP], writes=[LP])
        op("dve", lambda e: e.tensor_scalar(out=LP.ap[:, 40:60], in0=PP.ap[:, flg_o:flg_o + 20], scalar1=-1.0,
                                            scalar2=1.0, op0=ALU.mult, op1=ALU.add), reads=[PP], writes=[LP])

        ba_o = PPL["ba"][0]
        bi_o = PPL["bi"][0]
        fw_o = PPL["fw"][0]
        fb_o = PPL["fb"][0]
        op("dve", lambda e: e.tensor_scalar(out=LP.ap[:, 60:80], in0=PP.ap[:, ba_o:ba_o + 20], scalar1=0.5, scalar2=None,
                                            op0=ALU.mult), reads=[PP], writes=[LP])
        op("dve", lambda e: e.tensor_scalar(out=LP.ap[:, 80:100], in0=PP.ap[:, bi_o:bi_o + 20], scalar1=0.5, scalar2=None,
                                            op0=ALU.mult), reads=[PP], writes=[LP])
        op("dve", lambda e: e.tensor_scalar(out=LP.ap[:, 100:166], in0=PP.ap[:, fw_o + 66:fw_o + 132], scalar1=0.5, scalar2=None,
                                            op0=ALU.mult), reads=[PP], writes=[LP])
        op("dve", lambda e: e.tensor_scalar(out=LP.ap[:, 166:188], in0=PP.ap[:, fb_o + 22:fb_o + 44], scalar1=0.5, scalar2=None,
                                            op0=ALU.mult), reads=[PP], writes=[LP])

        def flg(k, i):
            return PP.ap[:, flg_o + 4 * k + i:flg_o + 4 * k + i + 1]

        def nflg(k, i):
            return LP.ap[:, 40 + 4 * k + i:40 + 4 * k + i + 1]

        GBC = A.f32(D, "gbc")

        NT_X = [A.f32(D, "xt%d" % i) for i in range(2)]
        NT_J = A.bf(D, "junk")
        NT_H = [A.bf(D, "hb%d" % i) for i in range(2)]
        NT_Sl = [A.f32(2, "nstat%d" % i) for i in range(2)]
        nt_i = [0]

        def norm_T(xbuf, x_ap, n, dstT, dstT_ap3, col0, evac_eng="dve", gb=None):
            gb = gb or GBC
            i = nt_i[0] % 2
            nt_i[0] += 1
            hb = NT_H[i]
            NT_S = NT_Sl[i]
            ss = NT_S.ap[0:n, 0:1]
            rs = NT_S.ap[0:n, 1:2]
            op("act", lambda e: e.activation(out=NT_J.ap[0:n], in_=x_ap, func=AF.Square, accum_out=ss),
               reads=[xbuf], writes=[NT_J, NT_S])
            op("dve", lambda e: e.tensor_scalar(out=rs, in0=ss, scalar1=1.0 / D, scalar2=EPS, op0=ALU.mult, op1=ALU.add),
               reads=[NT_S], writes=[NT_S])
            op("pool", lambda e: e.tensor_tensor(out=rs, in0=rs, in1=NEGH.ap[0:n, 0:1], op=ALU.pow),
               reads=[NT_S, NEGH], writes=[NT_S])
            op("dve", lambda e: e.scalar_tensor_tensor(out=hb.ap[0:n], in0=x_ap, scalar=rs, in1=gb.ap[0:n],
                                                       op0=ALU.mult, op1=ALU.mult),
               reads=[xbuf, NT_S, gb], writes=[hb])
            pb = PS()
            pbf = pb.ap.bitcast(BF16)

            def tr(e):
                ins = None
                for kc in range(8):
                    ins = e.transpose(pbf[:, kc * 128:kc * 128 + n], hb.ap[0:n, kc * 128:(kc + 1) * 128],
                                      IDENT.ap[0:n, 0:n])
                return ins
            op("pe", tr, reads=[hb, IDENT], writes=[pb])
            src = pbf.rearrange("p (k t) -> p k t", k=8)[:, :, 0:n]
            dst = dstT_ap3[:, :, col0:col0 + n]
            if evac_eng == "act":
                op("act", lambda e: e.activation(out=dst, in_=src, func=AF.Copy), reads=[pb], writes=[dstT])
            else:
                op("dve", lambda e: e.tensor_copy(out=dst, in_=src), reads=[pb], writes=[dstT])

        def fm_rstd(sq_list, rows_out, n, dim, SCR, pbuf=None):
            pb = pbuf if pbuf is not None else PS()

            def mm(e):
                ins = None
                for i, (b, ap, rk, base) in enumerate(sq_list):
                    ins = e.matmul(pb.ap[0:rows_out, 0:n], lhsT=ONESB.ap[base:base + rk, 0:rows_out], rhs=ap,
                                   start=(i == 0), stop=(i == len(sq_list) - 1))
                return ins
            op("pe", mm, reads=[b for b, _, _, _ in sq_list] + [ONESB], writes=[pb])
            rb = SCR()
            op("act", lambda e: e.activation(out=rb.ap[0:rows_out, 0:n], in_=pb.ap[0:rows_out, 0:n], func=AF.Ln,
                                             scale=1.0 / dim, bias=c_eps[0:rows_out]),
               reads=[pb, CONST], writes=[rb])
            op("act", lambda e: e.activation(out=rb.ap[0:rows_out, 0:n], in_=rb.ap[0:rows_out, 0:n], func=AF.Exp,
                                             scale=-0.5), reads=[rb], writes=[rb])
            return rb

        POSB = [A.f32(512, "posb%d" % i) for i in range(2)]
        posb_i = [0]

        def rope_tables(pos_src_ap, n, SCR, out_buf=None, out_s=None, out_c=None):
            pi = POSB[posb_i[0] % 2]
            posb_i[0] += 1
            pi_ap = pi.ap.bitcast(I32)
            load(pi, pi_ap[64:96, 0:n], pos_src_ap, eng="sp")
            y = SCR()
            op("dve", lambda e: e.tensor_copy(out=y.ap[64:96, 0:n], in_=pi_ap[64:96, 0:n]), reads=[pi], writes=[y])
            y2 = SCR()
            yc = SCR()
            op("dve", lambda e: e.tensor_scalar(out=y2.ap[64:96, 0:n], in0=y.ap[64:96, 0:n],
                                                scalar1=ppc("invf", 0, slice(64, 96)), scalar2=None, op0=ALU.mult),
               reads=[y, PP], writes=[y2])
            op("dve", lambda e: e.tensor_scalar(out=yc.ap[64:96, 0:n], in0=y2.ap[64:96, 0:n], scalar1=0.25,
                                                scalar2=None, op0=ALU.add), reads=[y2], writes=[yc])
            res = []
            for yy, dst in ((y2, out_s), (yc, out_c)):
                ti = SCR()
                ti_ap = ti.ap.bitcast(I32)
                op("dve", lambda e, yy=yy, ti_ap=ti_ap: e.tensor_copy(out=ti_ap[64:96, 0:n], in_=yy.ap[64:96, 0:n]),
                   reads=[yy], writes=[ti])
                tf = SCR()
                op("dve", lambda e, ti_ap=ti_ap, tf=tf: e.tensor_copy(out=tf.ap[64:96, 0:n], in_=ti_ap[64:96, 0:n]),
                   reads=[ti], writes=[tf])
                op("dve", lambda e, yy=yy, tf=tf: e.tensor_tensor(out=yy.ap[64:96, 0:n], in0=yy.ap[64:96, 0:n],
                                                                  in1=tf.ap[64:96, 0:n], op=ALU.subtract),
                   reads=[yy, tf], writes=[yy])
                if dst is None:
                    o_b, o_ap = yy, yy.ap[64:96, 0:n]
                else:
                    o_b, o_ap = out_buf, dst
                op("act", lambda e, yy=yy, o_ap=o_ap: e.activation(out=o_ap, in_=yy.ap[64:96, 0:n], func=AF.Sin,
                                                                   scale=2.0 * np.pi * 0.999999),
                   reads=[yy], writes=[o_b])
                res.append((o_b, o_ap))
            return res

        KM = A.bf(4 * MEM, "km")
        KM3 = KM.ap.rearrange("p (h t) -> p h t", h=4)
        VM = A.bf(2 * 512, "vm")
        VM3 = VM.ap.rearrange("p (t c) -> p t c", t=2)
        m0 = A.mark()
        scr0 = [A.f32(512, "scr0%d" % i) for i in range(4)]
        s0i = [0]

        def SCRD():
            b = scr0[s0i[0] % len(scr0)]
            s0i[0] += 1
            return b
        if stop == '0a':
            S.emit(nc)
            return nc
        load(GBC, GBC.ap, gvec[2, :].partition_broadcast(128))
        mW = A.mark()
        WMKV = A.bf(8 * 1024, "wmkv")
        WMKV3 = WMKV.ap.rearrange("p (k c) -> p k c", k=8)
        load(WMKV, WMKV3, w_mkv.rearrange("(k p) c -> p k c", p=128))
        MEMT = A.bf(8 * MEM, "memt")
        MEMT3 = MEMT.ap.rearrange("p (k t) -> p k t", k=8)
        for mt in range(2):
            xb = NT_X[nt_i[0] % 2]
            load(xb, xb.ap, mem_d[mt * 128:(mt + 1) * 128, :], eng="sp")
            norm_T(xb, xb.ap, 128, MEMT, MEMT3, mt * 128)
        if stop == '0b':
            S.emit(nc)
            return nc
        for h in range(4):
            pk = PS()

            def mmk(e, pk=pk, h=h):
                ins = None
                for kc in range(8):
                    ins = e.matmul(pk.ap[:, 0:MEM], lhsT=WMKV3[:, kc, h * 128:(h + 1) * 128], rhs=MEMT3[:, kc, :],
                                   start=(kc == 0), stop=(kc == 7))
                return ins
            op("pe", mmk, reads=[WMKV, MEMT], writes=[pk])
            sq = SCRD()
            sq_ap = sq.ap.bitcast(BF16)
            op("act", lambda e, sq_ap=sq_ap, pk=pk: e.activation(out=sq_ap[:, 0:MEM], in_=pk.ap[:, 0:MEM], func=AF.Square),
               reads=[pk], writes=[sq])
            rb = fm_rstd([(sq, sq_ap[:, 0:MEM], 128, 0)], 128, MEM, 128.0, SCRD)
            op("dve", lambda e, pk=pk, rb=rb, h=h: e.scalar_tensor_tensor(
                out=KM3[:, h, :], in0=pk.ap[:, 0:MEM], scalar=ppc("gmk"), in1=rb.ap[:, 0:MEM], op0=ALU.mult, op1=ALU.mult),
               reads=[pk, rb, PP], writes=[KM])
        if stop == '0c':
            S.emit(nc)
            return nc
        for mt in range(2):
            pv = PS()

            def mmv2(e, pv=pv, mt=mt):
                ins = None
                for kc in range(8):
                    ins = e.matmul(pv.ap[:, :], lhsT=MEMT3[:, kc, mt * 128:(mt + 1) * 128], rhs=WMKV3[:, kc, 512:1024],
                                   start=(kc == 0), stop=(kc == 7))
                return ins
            op("pe", mmv2, reads=[WMKV, MEMT], writes=[pv])
            op("act", lambda e, pv=pv, mt=mt: e.activation(out=VM3[:, mt, :], in_=pv.ap[:, :], func=AF.Copy),
               reads=[pv], writes=[VM])
        if stop == '0d':
            S.emit(nc)
            return nc
        S.barrier_all()
        A.release(m0)
        mP = A.top()
        if stop == '0':
            S.emit(nc)
            return nc

        MIXL = A.bf(4 * NOWN, "mixl")
        MIXL3 = MIXL.ap.rearrange("p (c t) -> p c t", c=4)
        CQN = A.bf(2 * NOWN, "cqn")
        CQN3 = CQN.ap.rearrange("p (c t) -> p c t", c=2)
        ckvn_start = A.top()
        CKVN = A.bf(NKEY, "ckvn")
        ETS = A.bf(NKEY, "e_tsq")
        ets_end = A.top()
        mA = A.mark()

        WIN = A.bf(8 * 1440, "win")
        WIN3 = WIN.ap.rearrange("p (k c) -> p k c", k=8)
        for kc in range(8):
            load(WIN, WIN3[:, kc, :], w_in[kc * 128:(kc + 1) * 128, :])
        WKR = A.bf(8 * 192, "wkrpad")
        WKR3 = WKR.ap.rearrange("p (k c) -> p k c", k=8)
        op("pool", lambda e: e.memset(WKR.ap, 0.0), writes=[WKR])
        op("dve", lambda e: e.tensor_copy(out=WKR3[:, :, 64:96], in_=WIN3[:, :, 1408:1440]), reads=[WIN], writes=[WKR])
        op("dve", lambda e: e.tensor_scalar(out=WKR3[:, :, 160:176], in0=WIN3[:, :, 1424:1440], scalar1=-1.0,
                                            scalar2=None, op0=ALU.mult), reads=[WIN], writes=[WKR])
        op("dve", lambda e: e.tensor_copy(out=WKR3[:, :, 176:192], in_=WIN3[:, :, 1408:1424]), reads=[WIN], writes=[WKR])
        WGL = [A.bf(4 * 2 * 128, "wgates%d" % i) for i in range(2)]
        WGL4 = [w.ap.rearrange("p (c g o) -> p c g o", c=4, g=2) for w in WGL]
        HTL = [A.bf(8 * 512, "ht%d" % i) for i in range(2)]
        HTL3 = [h_.ap.rearrange("p (k t) -> p k t", k=8) for h_ in HTL]
        XR = A.f32(4 * 515, "xr")
        XR3 = XR.ap.rearrange("p (c t) -> p c t", c=4)
        HF = A.bf(4 * (CH + 1), "hf")
        HF3 = HF.ap.rearrange("p (c t) -> p c t", c=4)
        GG = A.bf(4 * NOWN, "gelu")
        GG3 = GG.ap.rearrange("p (c t) -> p c t", c=4)
        CAR = A.f32(64, "carry")
        op("pool", lambda e: e.memset(CAR.ap, 0.0), writes=[CAR])
        HSC = A.f32(4 * 512, "hscan")
        HSC3 = HSC.ap.rearrange("p (c t) -> p c t", c=4)
        HE = A.f32(16, "hextra")
        scrA = [A.f32(512, "scrA%d" % i) for i in range(12)]
        sai = [0]

        def SCRA():
            b = scrA[sai[0] % len(scrA)]
            sai[0] += 1
            return b

        load(GBC, GBC.ap, gvec[0, :].partition_broadcast(128))

        chk('A0')

        def lru_cols(k, ct, name):
            o, n = PPL[name]
            return PP.ap[:, o + 4 * k + ct:o + 4 * k + ct + 1]

        def phaseA_front(k, j, n, mini, hb):
            HT = HTL[hb]
            HT3 = HTL3[hb]
            if j == 0 and not mini:
                load(WGL[k % 2], WGL4[k % 2][:, :, 0, :], wa_d[k].rearrange("c i o -> i c o"))
                load(WGL[k % 2], WGL4[k % 2][:, :, 1, :], wi_d[k].rearrange("c i o -> i c o"))
            ntile = (n + 127) // 128
            for i in range(ntile):
                rows = min(128, n - i * 128)
                xb = NT_X[nt_i[0] % 2]
                if mini:
                    src = xh[0:1, :] if k == 3 else xh[1:2, :]
                else:
                    src = xs[k, j * 512 + i * 128:j * 512 + i * 128 + rows, :]
                load(xb, xb.ap[0:rows], src, eng="sp")
                norm_T(xb, xb.ap[0:rows], rows, HT, HT3, i * 128, evac_eng="dve")

        def phaseA_xr(k, j, n, mini, hb):
            HT = HTL[hb]
            HT3 = HTL3[hb]
            if j == 0 and not mini:
                op("dve", lambda e: e.tensor_scalar(out=CAR.ap[:, 36:48], in0=CAR.ap[:, 8:20], scalar1=flg(k, 0),
                                                    scalar2=None, op0=ALU.mult), reads=[CAR, PP], writes=[CAR])
                op("dve", lambda e: e.scalar_tensor_tensor(out=CAR.ap[:, 36:48], in0=CAR.ap[:, 20:32], scalar=flg(k, 1),
                                                           in1=CAR.ap[:, 36:48], op0=ALU.mult, op1=ALU.add),
                   reads=[CAR, PP], writes=[CAR])
                op("dve", lambda e: e.tensor_scalar(out=CAR.ap[:, 32:36], in0=CAR.ap[:, 0:4], scalar1=flg(k, 0),
                                                    scalar2=None, op0=ALU.mult), reads=[CAR, PP], writes=[CAR])
                op("dve", lambda e: e.scalar_tensor_tensor(out=CAR.ap[:, 32:36], in0=CAR.ap[:, 4:8], scalar=flg(k, 1),
                                                           in1=CAR.ap[:, 32:36], op0=ALU.mult, op1=ALU.add),
                   reads=[CAR, PP], writes=[CAR])
                if k == 3:
                    op("dve", lambda e: e.tensor_copy(out=CAR.ap[:, 48:52], in_=CAR.ap[:, 32:36]), reads=[CAR], writes=[CAR])
                if k == 4:
                    op("dve", lambda e: e.tensor_copy(out=CAR.ap[:, 52:56], in_=CAR.ap[:, 32:36]), reads=[CAR], writes=[CAR])
                op("dve", lambda e: e.tensor_copy(out=XR3[:, :, 0:3],
                                                  in_=CAR.ap[:, 36:48].rearrange("p (c t) -> p c t", c=4)),
                   reads=[CAR], writes=[XR])
            else:
                op("dve", lambda e: e.tensor_copy(out=XR3[:, :, 0:3], in_=XR3[:, :, 512:515]), reads=[XR], writes=[XR])
            for ct in range(4):
                pb = PS()

                def mm(e, pb=pb, ct=ct):
                    ins = None
                    for kc in range(8):
                        ins = e.matmul(pb.ap[:, 0:n], lhsT=WIN3[:, kc, ct * 128:(ct + 1) * 128], rhs=HT3[:, kc, 0:n],
                                       start=(kc == 0), stop=(kc == 7))
                    return ins
                op("pe", mm, reads=[WIN, HT], writes=[pb])
                op("act", lambda e, pb=pb, ct=ct: e.activation(out=XR3[:, ct, 3:3 + n], in_=pb.ap[:, 0:n], func=AF.Copy),
                   reads=[pb], writes=[XR])

        def phaseA_block(k, j, n, mini, hb, mid=None):
            HT = HTL[hb]
            HT3 = HTL3[hb]
            WG = WGL[k % 2]
            WG4 = WGL4[k % 2]
            do_kv = (k <= 3) and not mini
            do_own = (k == 3) or (k == 4 and mini)
            if mini:
                own0 = CH if k == 3 else CH + 1
            else:
                own0 = j * 512
            chk('A1')
            for pr in range(2):
                cts = (2 * pr, 2 * pr + 1)
                B_ = {}
                for ct in cts:
                    cwo = PPL["cw"][0] + (k * 4 + ct) * 4
                    xc = SCRA()
                    op("act", lambda e, xc=xc, ct=ct, cwo=cwo: e.activation(
                        out=xc.ap[:, 0:n], in_=XR3[:, ct, 3:3 + n], func=AF.Identity,
                        scale=PP.ap[:, cwo + 3:cwo + 4], bias=lru_cols(k, ct, "cb")), reads=[XR, PP], writes=[xc])
                    for tp in range(3):
                        op("dve", lambda e, xc=xc, ct=ct, cwo=cwo, tp=tp: e.scalar_tensor_tensor(
                            out=xc.ap[:, 0:n], in0=XR3[:, ct, tp:tp + n], scalar=PP.ap[:, cwo + tp:cwo + tp + 1],
                            in1=xc.ap[:, 0:n], op0=ALU.mult, op1=ALU.add), reads=[XR, PP, xc], writes=[xc])
                    xcb = SCRA()
                    xcb_ap = xcb.ap.bitcast(BF16)
                    op("dve", lambda e, xc=xc, xcb_ap=xcb_ap: e.tensor_copy(out=xcb_ap[:, 0:n], in_=xc.ap[:, 0:n]),
                       reads=[xc], writes=[xcb])
                    B_[ct] = dict(xc=xc, xcb=xcb, xcb_ap=xcb_ap)
                for ct in cts:
                    d = B_[ct]
                    pa = PS()
                    op("pe", lambda e, pa=pa, ct=ct, xcb_ap=d["xcb_ap"]: e.matmul(
                        pa.ap[:, 0:n], lhsT=WG4[:, ct, 0, :], rhs=xcb_ap[:, 0:n], start=True, stop=True),
                       reads=[WG, d["xcb"]], writes=[pa])
                    pi_ = PS()
                    op("pe", lambda e, pi_=pi_, ct=ct, xcb_ap=d["xcb_ap"]: e.matmul(
                        pi_.ap[:, 0:n], lhsT=WG4[:, ct, 1, :], rhs=xcb_ap[:, 0:n], start=True, stop=True),
                       reads=[WG, d["xcb"]], writes=[pi_])
                    d["pa"] = pa
                    d["pi"] = pi_
                for ct in cts:
                    d = B_[ct]
                    rr = SCRA()
                    ig = SCRA()
                    op("act", lambda e, rr=rr, pa=d["pa"], ct=ct: e.activation(out=rr.ap[:, 0:n], in_=pa.ap[:, 0:n], func=AF.Tanh,
                                                                          bias=LP.ap[:, 60 + 4 * k + ct:61 + 4 * k + ct], scale=0.5),
                       reads=[d["pa"], LP], writes=[rr])
                    op("act", lambda e, ig=ig, pi_=d["pi"], ct=ct: e.activation(out=ig.ap[:, 0:n], in_=pi_.ap[:, 0:n], func=AF.Tanh,
                                                                            bias=LP.ap[:, 80 + 4 * k + ct:81 + 4 * k + ct], scale=0.5),
                       reads=[d["pi"], LP], writes=[ig])
                    d["rr"] = rr
                    d["ig"] = ig
                for ct in cts:
                    d = B_[ct]
                    aa = SCRA()
                    mm_ = SCRA()
                    cs = LP.ap[:, 4 * k + ct:4 * k + ct + 1]
                    hcs = LP.ap[:, 20 + 4 * k + ct:20 + 4 * k + ct + 1]
                    op("act", lambda e, aa=aa, rr=d["rr"], hcs=hcs: e.activation(out=aa.ap[:, 0:n], in_=rr.ap[:, 0:n], func=AF.Exp,
                                                                             scale=hcs, bias=hcs), reads=[d["rr"], LP], writes=[aa])
                    op("act", lambda e, mm_=mm_, rr=d["rr"], cs=cs: e.activation(out=mm_.ap[:, 0:n], in_=rr.ap[:, 0:n], func=AF.Exp,
                                                                             scale=cs, bias=cs), reads=[d["rr"], LP], writes=[mm_])
                    op("dve", lambda e, mm_=mm_: e.tensor_scalar(out=mm_.ap[:, 0:n], in0=mm_.ap[:, 0:n], scalar1=-0.25, scalar2=0.25,
                                                                 op0=ALU.mult, op1=ALU.add), reads=[mm_], writes=[mm_])
                    op("dve", lambda e, ig=d["ig"], xc=d["xc"]: e.scalar_tensor_tensor(out=ig.ap[:, 0:n], in0=ig.ap[:, 0:n], scalar=1.0,
                                                                                   in1=xc.ap[:, 0:n], op0=ALU.add, op1=ALU.mult),
                       reads=[d["ig"], d["xc"]], writes=[d["ig"]])
                    d["aa"] = aa
                    d["mm"] = mm_
                for ct in cts:
                    d = B_[ct]
                    op("act", lambda e, mm_=d["mm"]: e.activation(out=mm_.ap[:, 0:n], in_=mm_.ap[:, 0:n], func=AF.Sqrt),
                       reads=[d["mm"]], writes=[d["mm"]])
                for ct in cts:
                    d = B_[ct]
                    op("dve", lambda e, ig=d["ig"], mm_=d["mm"]: e.tensor_tensor(out=ig.ap[:, 0:n], in0=ig.ap[:, 0:n], in1=mm_.ap[:, 0:n],
                                                                              op=ALU.mult), reads=[d["ig"], d["mm"]], writes=[d["ig"]])
                    if mini or j > 0:
                        init = CAR.ap[:, 56 + ct:57 + ct]
                    else:
                        init = CAR.ap[:, 32 + ct:33 + ct]
                    op("dve", lambda e, aa=d["aa"], ig=d["ig"], ct=ct, init=init: e.tensor_tensor_scan(
                        out=HSC3[:, ct, 0:n], data0=aa.ap[:, 0:n], data1=ig.ap[:, 0:n], initial=init,
                        op0=ALU.mult, op1=ALU.add), reads=[d["aa"], d["ig"], CAR], writes=[HSC])
            if not mini:
                op("dve", lambda e: e.tensor_copy(out=CAR.ap[:, 56:60], in_=HSC3[:, :, n - 1]), reads=[HSC], writes=[CAR])
            chk('A2')
            if k == 3:
                op("dve", lambda e: e.tensor_copy(out=HF3[:, :, own0 if not mini else CH:(own0 if not mini else CH) + n],
                                                   in_=HSC3[:, :, 0:n]), reads=[HSC], writes=[HF])
            if k <= 2 and (not mini) and j == NB - 1:
                for (st_o, u_i) in ((0, 2), (4, 3)):
                    op("dve", lambda e, st_o=st_o, u_i=u_i: e.tensor_scalar(
                        out=CAR.ap[:, st_o:st_o + 4], in0=CAR.ap[:, st_o:st_o + 4], scalar1=nflg(k, u_i), scalar2=None,
                        op0=ALU.mult), reads=[CAR, LP], writes=[CAR])
                    op("dve", lambda e, st_o=st_o, u_i=u_i: e.scalar_tensor_tensor(
                        out=CAR.ap[:, st_o:st_o + 4], in0=HSC3[:, :, n - 1], scalar=flg(k, u_i), in1=CAR.ap[:, st_o:st_o + 4],
                        op0=ALU.mult, op1=ALU.add), reads=[CAR, HSC, PP], writes=[CAR])
                for (h_o, u_i) in ((8, 2), (20, 3)):
                    hv = CAR.ap[:, h_o:h_o + 12].rearrange("p (c t) -> p c t", c=4)
                    op("dve", lambda e, hv=hv, u_i=u_i: e.tensor_scalar(
                        out=hv, in0=hv, scalar1=nflg(k, u_i), scalar2=None, op0=ALU.mult), reads=[CAR, LP], writes=[CAR])
                    op("dve", lambda e, hv=hv, u_i=u_i: e.scalar_tensor_tensor(
                        out=hv, in0=XR3[:, :, 512:515], scalar=flg(k, u_i), in1=hv, op0=ALU.mult, op1=ALU.add),
                       reads=[CAR, XR, PP], writes=[CAR])
            chk('A3')
            if mid is not None:
                mid()
            if do_kv:
                key0 = k * CH + j * 512
                pb = PS()

                def mmkv(e, pb=pb):
                    ins = None
                    for kc in range(8):
                        ins = e.matmul(pb.ap[:, 0:n], lhsT=WIN3[:, kc, 1280:1408], rhs=HT3[:, kc, 0:n],
                                       start=(kc == 0), stop=(kc == 7))
                    return ins
                op("pe", mmkv, reads=[WIN, HT], writes=[pb])
                cf = SCRA()
                sq = SCRA()
                sq_ap = sq.ap.bitcast(BF16)
                op("act", lambda e, cf=cf, pb=pb: e.activation(out=cf.ap[:, 0:n], in_=pb.ap[:, 0:n], func=AF.Copy),
                   reads=[pb], writes=[cf])
                op("act", lambda e, sq_ap=sq_ap, pb=pb: e.activation(out=sq_ap[:, 0:n], in_=pb.ap[:, 0:n], func=AF.Square),
                   reads=[pb], writes=[sq])
                rb = fm_rstd([(sq, sq_ap[:, 0:n], 128, 0)], 128, n, 128.0, SCRA)
                op("dve", lambda e, cf=cf, rb=rb: e.scalar_tensor_tensor(
                    out=CKVN.ap[:, key0:key0 + n], in0=cf.ap[:, 0:n], scalar=ppc("gkv"), in1=rb.ap[:, 0:n],
                    op0=ALU.mult, op1=ALU.mult), reads=[cf, rb, PP], writes=[CKVN])
                chk('A3a')
                pt = PS()
                prt = PS()

                def mmt(e, pt=pt, o=0):
                    ins = None
                    for kc in range(8):
                        ins = e.matmul(pt.ap[0:96, 0:n], lhsT=WKR3[:, kc, o:o + 96], rhs=HT3[:, kc, 0:n],
                                       start=(kc == 0), stop=(kc == 7))
                    return ins
                op("pe", lambda e: mmt(e, pt, 0), reads=[WKR, HT], writes=[pt])
                op("pe", lambda e: mmt(e, prt, 96), reads=[WKR, HT], writes=[prt])
                chk('A3b')
                (sb, s_ap), (cb_, c_ap) = rope_tables(posk[k, :, j * 512:j * 512 + n], n, SCRA)
                chk('A3c')
                tq = SCRA()
                tq_ap = tq.ap.bitcast(BF16)
                op("act", lambda e, pt=pt, tq_ap=tq_ap: e.activation(out=tq_ap[64:96, 0:n], in_=pt.ap[64:96, 0:n], func=AF.Square),
                   reads=[pt], writes=[tq])
                op("dve", lambda e, tq_ap=tq_ap: e.tensor_copy(out=ETS.ap[0:32, key0:key0 + n], in_=tq_ap[64:96, 0:n]),
                   reads=[tq], writes=[ETS])
                e1 = SCRA()
                e2 = SCRA()
                op("dve", lambda e, e1=e1, pt=pt, c_ap=c_ap: e.scalar_tensor_tensor(
                    out=e1.ap[64:96, 0:n], in0=pt.ap[64:96, 0:n], scalar=ppc("gk", 0, slice(64, 96)), in1=c_ap,
                    op0=ALU.mult, op1=ALU.mult), reads=[pt, cb_, PP, tq], writes=[e1])
                op("dve", lambda e, e2=e2, prt=prt, s_ap=s_ap: e.scalar_tensor_tensor(
                    out=e2.ap[64:96, 0:n], in0=prt.ap[64:96, 0:n], scalar=ppc("gkr", 0, slice(64, 96)), in1=s_ap,
                    op0=ALU.mult, op1=ALU.mult), reads=[prt, sb, PP], writes=[e2])
                op("dve", lambda e, e1=e1, e2=e2: e.tensor_tensor(out=ETS.ap[64:96, key0:key0 + n], in0=e1.ap[64:96, 0:n],
                                                                 in1=e2.ap[64:96, 0:n], op=ALU.add),
                   reads=[e1, e2], writes=[ETS])
            chk('A4')
            if do_own:
                for ct in range(4):
                    pb = PS()

                    def mmy(e, pb=pb, ct=ct):
                        ins = None
                        for kc in range(8):
                            ins = e.matmul(pb.ap[:, 0:n], lhsT=WIN3[:, kc, 512 + ct * 128:512 + (ct + 1) * 128],
                                           rhs=HT3[:, kc, 0:n], start=(kc == 0), stop=(kc == 7))
                        return ins
                    op("pe", mmy, reads=[WIN, HT], writes=[pb])
                    u = SCRA()
                    w = SCRA()
                    op("act", lambda e, u=u, pb=pb: e.activation(out=u.ap[:, 0:n], in_=pb.ap[:, 0:n], func=AF.Copy, scale=0.5),
                       reads=[pb], writes=[u])
                    op("act", lambda e, w=w, pb=pb: e.activation(out=w.ap[:, 0:n], in_=pb.ap[:, 0:n], func=AF.Square),
                       reads=[pb], writes=[w])
                    op("dve", lambda e, w=w: e.tensor_scalar(out=w.ap[:, 0:n], in0=w.ap[:, 0:n], scalar1=2.0 * 0.044715, scalar2=2.0,
                                                             op0=ALU.mult, op1=ALU.add), reads=[w], writes=[w])
                    op("dve", lambda e, w=w, u=u: e.tensor_tensor(out=w.ap[:, 0:n], in0=w.ap[:, 0:n], in1=u.ap[:, 0:n], op=ALU.mult),
                       reads=[w, u], writes=[w])
                    op("act", lambda e, w=w: e.activation(out=w.ap[:, 0:n], in_=w.ap[:, 0:n], func=AF.Tanh,
                                                          scale=0.7978845608028654), reads=[w], writes=[w])
                    op("dve", lambda e, w=w, u=u, ct=ct: e.scalar_tensor_tensor(out=GG3[:, ct, own0:own0 + n], in0=w.ap[:, 0:n],
                                                                             scalar=1.0, in1=u.ap[:, 0:n], op0=ALU.add, op1=ALU.mult),
                       reads=[w, u], writes=[GG])
                cfs = []
                sqs = []
                for c2 in range(2):
                    pb = PS()

                    def mmq(e, pb=pb, c2=c2):
                        ins = None
                        for kc in range(8):
                            ins = e.matmul(pb.ap[:, 0:n], lhsT=WIN3[:, kc, 1024 + c2 * 128:1024 + (c2 + 1) * 128],
                                           rhs=HT3[:, kc, 0:n], start=(kc == 0), stop=(kc == 7))
                        return ins
                    op("pe", mmq, reads=[WIN, HT], writes=[pb])
                    cf = SCRA()
                    sq = SCRA()
                    sq_ap = sq.ap.bitcast(BF16)
                    op("act", lambda e, cf=cf, pb=pb: e.activation(out=cf.ap[:, 0:n], in_=pb.ap[:, 0:n], func=AF.Copy),
                       reads=[pb], writes=[cf])
                    op("act", lambda e, sq_ap=sq_ap, pb=pb: e.activation(out=sq_ap[:, 0:n], in_=pb.ap[:, 0:n], func=AF.Square),
                       reads=[pb], writes=[sq])
                    cfs.append(cf)
                    sqs.append((sq, sq_ap[:, 0:n], 128, 0))
                rb = fm_rstd(sqs, 128, n, 256.0, SCRA)
                for c2 in range(2):
                    op("dve", lambda e, c2=c2, cf=cfs[c2], rb=rb: e.scalar_tensor_tensor(
                        out=CQN3[:, c2, own0:own0 + n], in0=cf.ap[:, 0:n], scalar=ppc("gqa", c2), in1=rb.ap[:, 0:n],
                        op0=ALU.mult, op1=ALU.mult), reads=[cf, rb, PP], writes=[CQN])
            if k == 4 and not mini:
                lo = CH - 512 * (j + 1)
                lru_combine(lambda ct: HF3[:, ct, lo:lo + n], HF, lambda ct: HSC3[:, ct, 0:n][:, ::-1], HSC, lo, n)

        def lru_combine(hf_ap, hf_buf, hb_ap, hb_buf, own0, n):
            los = []
            sqs = []
            for ct in range(4):
                lo_ = SCRA()
                op("dve", lambda e, lo_=lo_, ct=ct: e.tensor_tensor(out=lo_.ap[:, 0:n], in0=hf_ap(ct), in1=hb_ap(ct), op=ALU.add),
                   reads=[hf_buf, hb_buf], writes=[lo_])
                op("dve", lambda e, lo_=lo_, ct=ct: e.tensor_tensor(out=lo_.ap[:, 0:n], in0=lo_.ap[:, 0:n],
                                                                  in1=GG3[:, ct, own0:own0 + n], op=ALU.mult),
                   reads=[lo_, GG], writes=[lo_])
                sq = SCRA()
                sq_ap = sq.ap.bitcast(BF16)
                op("act", lambda e, sq_ap=sq_ap, lo_=lo_: e.activation(out=sq_ap[:, 0:n], in_=lo_.ap[:, 0:n], func=AF.Square),
                   reads=[lo_], writes=[sq])
                los.append(lo_)
                sqs.append((sq, sq_ap[:, 0:n], 128, 0))
            rb = fm_rstd(sqs, 128, n, 512.0, SCRA)
            for ct in range(4):
                op("dve", lambda e, ct=ct, lo_=los[ct], rb=rb: e.scalar_tensor_tensor(
                    out=MIXL3[:, ct, own0:own0 + n], in0=lo_.ap[:, 0:n], scalar=ppc("glru", ct), in1=rb.ap[:, 0:n],
                    op0=ALU.mult, op1=ALU.mult), reads=[lo_, rb, PP], writes=[MIXL])

        blks = []
        for k in range(5):
            for j in range(NB):
                blks.append((k, j, 512, False))
            if k >= 3:
                blks.append((k, NB, 1, True))
        phaseA_front(*blks[0], 0)
        phaseA_xr(*blks[0], 0)
        for bi, (k, j, n, mini) in enumerate(blks):
            mid = None
            if bi + 1 < len(blks):
                phaseA_front(*blks[bi + 1], (bi + 1) % 2)
                mid = (lambda nb=blks[bi + 1], hb_=(bi + 1) % 2: phaseA_xr(*nb, hb_))
            phaseA_block(k, j, n, mini, bi % 2, mid)
            if mini and k == 3:
                op("dve", lambda e: e.tensor_copy(out=HE.ap[:, 0:4], in_=HSC3[:, :, 0]), reads=[HSC], writes=[HE])
            if mini and k == 4:
                op("dve", lambda e: e.tensor_copy(out=HE.ap[:, 4:8], in_=HSC3[:, :, 0]), reads=[HSC], writes=[HE])
        chk('A6')
        HE3 = HE.ap[:, 8:16].rearrange("p (c t) -> p c t", c=4)
        op("dve", lambda e: e.tensor_tensor(out=HE3[:, :, 0], in0=HE.ap[:, 0:4], in1=CAR.ap[:, 52:56], op=ALU.add),
           reads=[HE, CAR], writes=[HE])
        op("dve", lambda e: e.tensor_tensor(out=HE3[:, :, 1], in0=HE.ap[:, 4:8], in1=CAR.ap[:, 48:52], op=ALU.add),
           reads=[HE, CAR], writes=[HE])
        ZERO = SCRA()
        op("pool", lambda e: e.memset(ZERO.ap[:, 0:8], 0.0), writes=[ZERO])
        lru_combine(lambda ct: HE3[:, ct, :], HE, lambda ct: ZERO.ap[:, 0:2], ZERO, CH, 2)
        if "mixl" in dbg_d:
            TMPD = A.f32(4 * NOWN, "tmpd")
            op("dve", lambda e: e.tensor_copy(out=TMPD.ap, in_=MIXL.ap), reads=[MIXL], writes=[TMPD])
            dump("mixl", TMPD, TMPD.ap)
        if "ckvn" in dbg_d:
            TMPD2 = A.f32(NKEY, "tmpd2")
            op("dve", lambda e: e.tensor_copy(out=TMPD2.ap, in_=CKVN.ap), reads=[CKVN], writes=[TMPD2])
            dump("ckvn", TMPD2, TMPD2.ap)
        if "ets" in dbg_d:
            TMPD3 = A.f32(NKEY, "tmpd3")
            op("dve", lambda e: e.tensor_copy(out=TMPD3.ap[0:96], in_=ETS.ap[0:96]), reads=[ETS], writes=[TMPD3])
            dump("ets", TMPD3, TMPD3.ap[0:96])

        S.barrier_all()
        A.release(mA)
        if stop == 'A':
            S.emit(nc)
            return nc

        WUQ = A.bf(2 * 768, "wuq")
        WUQ3 = WUQ.ap.rearrange("p (k c) -> p k c", k=2)
        for c2 in range(2):
            load(WUQ, WUQ3[:, c2, :], w_uq[c2 * 128:(c2 + 1) * 128, :])
        WUQR = A.bf(2 * 8 * 96, "wuqr")
        WUQR4 = WUQR.ap.rearrange("p (k h c) -> p k h c", k=2, h=8)
        WUQ4 = WUQ.ap.rearrange("p (k h c) -> p k h c", k=2, h=8)
        op("pool", lambda e: e.memset(WUQR.ap, 0.0), writes=[WUQR])
        for c2 in range(2):
            op("dve", lambda e, c2=c2: e.tensor_scalar(out=WUQR4[:, c2, :, 64:80], in0=WUQ4[:, c2, :, 80:96], scalar1=-1.0,
                                                       scalar2=None, op0=ALU.mult), reads=[WUQ], writes=[WUQR])
            op("dve", lambda e, c2=c2: e.tensor_copy(out=WUQR4[:, c2, :, 80:96], in_=WUQ4[:, c2, :, 64:80]),
               reads=[WUQ], writes=[WUQR])
        WUKV = A.bf(1024, "wukv")
        load(WUKV, WUKV.ap, w_ukv)
        ON = A.bf(8 * NOWN, "on")
        ON3 = ON.ap.rearrange("p (h t) -> p h t", h=8)
        TAB = A.bf(2 * NOWN, "qtab")
        mB = A.mark()
        KT = [A.bf(NKEY, "kt%d" % i) for i in range(2)]
        VV = [A.bf(NKT * 65, "v%d" % i) for i in range(2)]
        VV3 = [v.ap.rearrange("p (t c) -> p t c", c=65) for v in VV]
        QT = [A.bf(NOWN, "qt%d" % i) for i in range(2)]
        PT = [Buf(NT_H[i].ap[:, 0:512], NT_H[i].r) for i in range(2)] + [A.bf(512, "pt%d" % i) for i in range(2)]
        scrB = [Buf(NT_X[i].ap[:, 0:512], NT_X[i].r) for i in range(2)] + [A.f32(512, "scrB%d" % i) for i in range(6)]
        sbi = [0]

        def SCRB():
            b = scrB[sbi[0] % len(scrB)]
            sbi[0] += 1
            return b
        for v3, v in zip(VV3, VV):
            op("pool", lambda e, v3=v3: e.memset(v3[:, :, 64:65], 1.0), writes=[v])
        nqb = -(-NOWN // 512)
        half = NOWN // 2
        qb_base = half // nqb
        qb_sizes = [2 * (qb_base + (1 if i < half - qb_base * nqb else 0)) for i in range(nqb)]
        qblocks = [(sum(qb_sizes[:i]), qb_sizes[i]) for i in range(nqb)]
        for (q0, n) in qblocks:
            rope_tables(poso[:, q0:q0 + n], n, SCRB, out_buf=TAB, out_s=TAB.ap[64:96, q0:q0 + n],
                        out_c=TAB.ap[64:96, NOWN + q0:NOWN + q0 + n])
        pti = [0]

        RK = [A.f32(NKT, "rk%d" % i) for i in range(2)]
        SSK = psum[5]

        for kt_ in KT:
            op("dve", lambda e, kt_=kt_: e.tensor_copy(out=kt_.ap[64:96, :], in_=ETS.ap[64:96, :]), reads=[ETS], writes=[kt_])

        TSS = A.f32(NKT, "tss")

        def mm_tss(e):
            ins = None
            for t in range(NKT):
                ins = e.matmul(SSK.ap[:, t:t + 1], lhsT=ETS.ap[0:32, t * 128:(t + 1) * 128], rhs=ONESB.ap[0:32, 0:1],
                               start=True, stop=True)
            return ins
        op("pe", mm_tss, reads=[ETS, ONESB], writes=[SSK])
        op("dve", lambda e: e.tensor_copy(out=TSS.ap[:, 0:NKT], in_=SSK.ap[:, 0:NKT]), reads=[SSK], writes=[TSS])

        def kgen(h):
            kt = KT[h % 2]
            rk = RK[h % 2]
            pend_mms = []
            for kb in range(NKB):
                c0 = kb * 512
                pk = PS()
                op("pe", lambda e, pk=pk, c0=c0: e.matmul(pk.ap[0:64, :], lhsT=WUKV.ap[:, h * 128:h * 128 + 64],
                                                      rhs=CKVN.ap[:, c0:c0 + 512], start=True, stop=True),
                   reads=[WUKV, CKVN], writes=[pk])
                sq = SCRB()
                sq_ap = sq.ap.bitcast(BF16)
                op("act", lambda e, sq_ap=sq_ap, pk=pk: e.activation(out=sq_ap[0:64, 0:512], in_=pk.ap[0:64, :], func=AF.Square),
                   reads=[pk], writes=[sq])
                op("dve", lambda e, pk=pk, c0=c0: e.tensor_scalar(out=kt.ap[0:64, c0:c0 + 512], in0=pk.ap[0:64, :],
                                                                 scalar1=ppc("gk", 0, slice(0, 64)), scalar2=None, op0=ALU.mult),
                   reads=[pk, PP, sq], writes=[kt])

                def mms(e, sq_ap=sq_ap, kb=kb, c0=c0):
                    ins = None
                    for i in range(4):
                        t = kb * 4 + i
                        ins = e.matmul(SSK.ap[:, t:t + 1], lhsT=sq_ap[0:64, i * 128:(i + 1) * 128], rhs=ONESB.ap[0:64, 0:1],
                                       start=True, stop=True)
                    return ins
                if pend_mms:
                    pm, psq = pend_mms.pop(0)
                    op("pe", pm, reads=[psq, ETS, ONESB], writes=[SSK])
                pend_mms.append((mms, sq))
            while pend_mms:
                pm, psq = pend_mms.pop(0)
                op("pe", pm, reads=[psq, ETS, ONESB], writes=[SSK])
            op("dve", lambda e: e.scalar_tensor_tensor(out=rk.ap[:, 0:NKT], in0=SSK.ap[:, 0:NKT], scalar=96.0 * EPS,
                                                       in1=TSS.ap[:, 0:NKT], op0=ALU.add, op1=ALU.add),
               reads=[SSK, TSS], writes=[rk])
            op("act", lambda e: e.activation(out=rk.ap[:, 0:NKT], in_=rk.ap[:, 0:NKT], func=AF.Ln), reads=[rk], writes=[rk])
            op("act", lambda e: e.activation(out=rk.ap[:, 0:NKT], in_=rk.ap[:, 0:NKT], func=AF.Exp, scale=-0.5), reads=[rk], writes=[rk])

        def vgen(h):
            vv = VV[h % len(VV)]
            vv3 = VV3[h % len(VV)]
            for kb in range(NKB):
                c0 = kb * 512
                pv = PS()

                def mmv(e, pv=pv, c0=c0):
                    ins = None
                    for i in range(4):
                        ins = e.matmul(pv.ap[:, i * 64:(i + 1) * 64], lhsT=CKVN.ap[:, c0 + i * 128:c0 + (i + 1) * 128],
                                       rhs=WUKV.ap[:, h * 128 + 64:h * 128 + 128], start=True, stop=True)
                    return ins
                op("pe", mmv, reads=[CKVN, WUKV], writes=[pv])
                op("dve", lambda e, pv=pv, kb=kb: e.tensor_copy(out=vv3[:, kb * 4:(kb + 1) * 4, 0:64],
                                                             in_=pv.ap[:, 0:256].rearrange("p (t c) -> p t c", c=64)),
                   reads=[pv], writes=[vv])

        def do_head(h):
            kt = KT[h % 2]
            rk = RK[h % 2]
            vv = VV[h % len(VV)]
            vv3 = VV3[h % len(VV)]
            qt = QT[h % 2]
            qst = {}

            def qgenA(q0, n):
                pq = PS()
                pr = PS()

                def mmq(e):
                    ins = None
                    for c2 in range(2):
                        ins = e.matmul(pq.ap[0:96, 0:n], lhsT=WUQ3[:, c2, h * 96:(h + 1) * 96], rhs=CQN3[:, c2, q0:q0 + n],
                                       start=(c2 == 0), stop=(c2 == 1))
                    return ins

                def mmr(e):
                    ins = None
                    for c2 in range(2):
                        ins = e.matmul(pr.ap[0:96, 0:n], lhsT=WUQR4[:, c2, h, :], rhs=CQN3[:, c2, q0:q0 + n],
                                       start=(c2 == 0), stop=(c2 == 1))
                    return ins
                op("pe", mmq, reads=[WUQ, CQN], writes=[pq])
                op("pe", mmr, reads=[WUQR, CQN], writes=[pr])
                sq = SCRB()
                sq_ap = sq.ap.bitcast(BF16)
                op("act", lambda e: e.activation(out=sq_ap[0:96, 0:n], in_=pq.ap[0:96, 0:n], func=AF.Square),
                   reads=[pq], writes=[sq])
                e2 = SCRB()
                op("dve", lambda e: e.scalar_tensor_tensor(
                    out=e2.ap[64:96, 0:n], in0=pr.ap[64:96, 0:n], scalar=ppc("gqr", 0, slice(64, 96)),
                    in1=TAB.ap[64:96, q0:q0 + n], op0=ALU.mult, op1=ALU.mult), reads=[pr, TAB, PP], writes=[e2])
                qst[q0] = (pq, pr, sq, sq_ap, e2)

            def qgenB(q0, n):
                pq, pr, sq, sq_ap, e2 = qst[q0]
                rb = fm_rstd([(sq, sq_ap[0:96, 0:n], 96, 0)], 96, n, 96.0, SCRB, pbuf=pr)
                op("dve", lambda e: e.scalar_tensor_tensor(
                    out=qt.ap[0:64, q0:q0 + n], in0=pq.ap[0:64, 0:n], scalar=ppc("gq", 0, slice(0, 64)), in1=rb.ap[0:64, 0:n],
                    op0=ALU.mult, op1=ALU.mult), reads=[pq, rb, PP], writes=[qt])
                e1 = SCRB()
                op("dve", lambda e: e.scalar_tensor_tensor(
                    out=e1.ap[64:96, 0:n], in0=pq.ap[64:96, 0:n], scalar=ppc("gq", 0, slice(64, 96)),
                    in1=TAB.ap[64:96, NOWN + q0:NOWN + q0 + n], op0=ALU.mult, op1=ALU.mult), reads=[pq, TAB, PP, sq], writes=[e1])
                op("dve", lambda e: e.tensor_tensor(out=e1.ap[64:96, 0:n], in0=e1.ap[64:96, 0:n],
                                                    in1=e2.ap[64:96, 0:n], op=ALU.add),
                   reads=[e1, e2], writes=[e1])
                op("dve", lambda e: e.tensor_tensor(out=qt.ap[64:96, q0:q0 + n], in0=e1.ap[64:96, 0:n],
                                                    in1=rb.ap[64:96, 0:n], op=ALU.mult),
                   reads=[e1, rb], writes=[qt])
            qgenA(*qblocks[0])
            for qi in range(len(qblocks)):
                if qi + 1 < len(qblocks):
                    qgenA(*qblocks[qi + 1])
                qgenB(*qblocks[qi])
            scale = 96.0 ** -0.5
            fin_pending = []
            for (q0, n) in qblocks:
                po = PSACC()
                pend = []

                def qk(t, q0=q0, n=n):
                    pb = PS()
                    op("pe", lambda e, pb=pb, t=t: e.matmul(pb.ap[:, 0:n], lhsT=kt.ap[0:96, t * 128:(t + 1) * 128],
                                                        rhs=qt.ap[0:96, q0:q0 + n], start=True, stop=True),
                       reads=[kt, qt], writes=[pb])
                    return pb

                def expv(t, pb, q0=q0, n=n, po=po):
                    pt_ = PT[pti[0] % 4]
                    pti[0] += 1
                    op("act", lambda e, pb=pb, pt_=pt_, t=t: e.activation(out=pt_.ap[:, 0:n], in_=pb.ap[:, 0:n], func=AF.Exp,
                                                                     scale=rk.ap[:, t:t + 1]), reads=[pb, rk], writes=[pt_])
                    op("pe", lambda e, pt_=pt_, t=t: e.matmul(po.ap[0:65, 0:n], lhsT=vv3[:, t, :], rhs=pt_.ap[:, 0:n],
                                                          start=(t == 0), stop=(t == NKT - 1)),
                       reads=[vv, pt_], writes=[po])
                LOOK = 2
                DEFER = 8
                for t in range(NKT + LOOK):
                    if t < NKT:
                        pend.append((t, qk(t)))
                    if t >= LOOK:
                        tt, pb = pend.pop(0)
                        expv(tt, pb)
                    if t == DEFER and fin_pending:
                        fin_pending.pop(0)()
                rd = SCRB()
                op("dve", lambda e, rd=rd, po=po, n=n: e.reciprocal(out=rd.ap[64:65, 0:n], in_=po.ap[64:65, 0:n]),
                   reads=[po], writes=[rd])

                def fin(rd=rd, po=po, q0=q0, n=n):
                    pbc = PS()
                    op("pe", lambda e, pbc=pbc: e.matmul(pbc.ap[0:64, 0:n], lhsT=ONESF.ap[64:65, 0:64], rhs=rd.ap[64:65, 0:n],
                                                         start=True, stop=True), reads=[ONESF, rd], writes=[pbc])
                    oc = SCRB()
                    op("act", lambda e, oc=oc: e.activation(out=oc.ap[0:64, 0:n], in_=po.ap[0:64, 0:n], func=AF.Copy),
                       reads=[po, rd], writes=[oc])
                    op("dve", lambda e, oc=oc, pbc=pbc: e.tensor_tensor(out=ON3[0:64, h, q0:q0 + n], in0=oc.ap[0:64, 0:n],
                                                                         in1=pbc.ap[0:64, 0:n], op=ALU.mult),
                       reads=[oc, pbc], writes=[ON])
                fin_pending.append(fin)
            while fin_pending:
                fin_pending.pop(0)()
        chk('B0')
        kgen(0)
        chk('B1')
        vgen(0)
        chk('B2')
        for h_ in range(8):
            if h_ < 7:
                kgen(h_ + 1)
                if len(VV) > 1:
                    vgen(h_ + 1)
            do_head(h_)
            if h_ < 7 and len(VV) == 1:
                vgen(h_ + 1)
        for (q0, n) in qblocks:
            sqs = []
            for h in range(8):
                sq = SCRB()
                sq_ap = sq.ap.bitcast(BF16)
                op("act", lambda e, sq_ap=sq_ap, h=h, q0=q0, n=n: e.activation(out=sq_ap[0:64, 0:n], in_=ON3[0:64, h, q0:q0 + n],
                                                                            func=AF.Square), reads=[ON], writes=[sq])
                sqs.append((sq, sq_ap[0:64, 0:n], 64, 0))
            rb = fm_rstd(sqs, 64, n, 512.0, SCRB)
            for h in range(8):
                op("dve", lambda e, h=h, rb=rb, q0=q0, n=n: e.scalar_tensor_tensor(
                    out=ON3[0:64, h, q0:q0 + n], in0=ON3[0:64, h, q0:q0 + n], scalar=ppc("gmla", h, slice(0, 64)),
                    in1=rb.ap[0:64, 0:n], op0=ALU.mult, op1=ALU.mult), reads=[ON, rb, PP], writes=[ON])
        if "on" in dbg_d:
            TMPD4 = A.f32(8 * NOWN, "tmpd4")
            op("dve", lambda e: e.tensor_copy(out=TMPD4.ap[0:64], in_=ON.ap[0:64]), reads=[ON], writes=[TMPD4])
            dump("on", TMPD4, TMPD4.ap[0:64])
        S.barrier_all()
        A.release(mB)
        if stop == 'B':
            S.emit(nc)
            return nc
        NTL = CH // 128
        tiles = [(i * 128, 128) for i in range(NTL)] + [(CH, 2)]
        xres_start = A.top()
        XRES = A.f32((NTL + 1) * D, "xres")
        xres_end = A.top()
        XRES3 = XRES.ap.rearrange("p (t d) -> p t d", d=D)
        XR_res = [Res("xres%d" % i) for i in range(NTL + 1)]
        XT_ = [Buf(XRES3[:, i, :], XR_res[i]) for i in range(NTL + 1)]
        mC = A.mark()
        A.regs = [[ckvn_start, ets_end], [xres_end, ASZ]]
        WOL = A.bf(4 * D, "wol")
        WOL3 = WOL.ap.rearrange("p (c d) -> p c d", c=4)
        WOM = A.bf(8 * D, "wom")
        WOM3 = WOM.ap.rearrange("p (h d) -> p h d", h=8)
        load(WOL, WOL3, w_out[0:512, :].rearrange("(c p) d -> p c d", p=128))
        load(WOM, WOM3[0:64], w_out[512:1024, :].rearrange("(h p) d -> p h d", p=64))
        for ti, (o0, rows) in enumerate(tiles):
            xt = XT_[ti]
            src = xs[3, o0:o0 + rows, :] if ti < NTL else xh[0:2, :]
            load(xt, xt.ap[0:rows], src, eng="sp")
            for half in range(2):
                pb = PS()

                def mmo(e, pb=pb, o0=o0, rows=rows, half=half):
                    ins = None
                    for ct in range(4):
                        ins = e.matmul(pb.ap[0:rows, :], lhsT=MIXL3[:, ct, o0:o0 + rows],
                                       rhs=WOL3[:, ct, half * 512:(half + 1) * 512], start=(ct == 0), stop=False)
                    for h in range(8):
                        ins = e.matmul(pb.ap[0:rows, :], lhsT=ON3[0:64, h, o0:o0 + rows],
                                       rhs=WOM3[0:64, h, half * 512:(half + 1) * 512], start=False, stop=(h == 7))
                    return ins
                op("pe", mmo, reads=[MIXL, ON, WOL, WOM], writes=[pb])
                op("dve", lambda e, pb=pb, xt=xt, rows=rows, half=half: e.tensor_tensor(
                    out=xt.ap[0:rows, half * 512:(half + 1) * 512], in0=xt.ap[0:rows, half * 512:(half + 1) * 512],
                    in1=pb.ap[0:rows, :], op=ALU.add), reads=[xt, pb], writes=[xt])
        if "x1" in dbg_d:
            for ti in range(NTL):
                store(dbg_d["x1"][ti * 128:(ti + 1) * 128, :], XT_[ti], XT_[ti].ap)
        S.barrier_all()
        A.regs = [[mP, xres_start], [xres_end, ASZ]]
        if stop == 'C':
            S.emit(nc)
            return nc

        GB2 = A.f32(D, "gbc2")
        GB3 = A.f32(D, "gbc3")
        load(GB2, GB2.ap, gvec[1, :].partition_broadcast(128))
        load(GB3, GB3.ap, gvec[3, :].partition_broadcast(128))
        HNT = A.bf(8 * NOWN, "hnt")
        HNT3 = HNT.ap.rearrange("p (k t) -> p k t", k=8)
        mD = A.mark()
        WMQ = A.bf(8 * 512, "wmq")
        WMQ3 = WMQ.ap.rearrange("p (k c) -> p k c", k=8)
        load(WMQ, WMQ3, w_mq.rearrange("(k p) c -> p k c", p=128))
        WMO = A.bf(4 * D, "wmo")
        WMO3 = WMO.ap.rearrange("p (h d) -> p h d", h=4)
        load(WMO, WMO3, w_mo.rearrange("(h p) d -> p h d", p=128))
        H1T = A.bf(8 * 512, "h1t")
        H1T3 = H1T.ap.rearrange("p (k t) -> p k t", k=8)
        OMT = A.bf(4 * 512, "omt")
        OMT3 = OMT.ap.rearrange("p (h t) -> p h t", h=4)
        HNH = A.bf(8 * 2, "hnh")
        HNH3 = HNH.ap.rearrange("p (k t) -> p k t", k=8)
        scrD = [A.f32(512, "scrD%d" % i) for i in range(10)]
        sdi = [0]

        def SCRD():
            b = scrD[sdi[0] % len(scrD)]
            sdi[0] += 1
            return b
        mscale = 128.0 ** -0.5
        qbl = [(q0, min(512, CH - q0)) for q0 in range(0, CH, 512)] + [(CH, 2)]
        for (q0, n) in qbl:
            tl = [ti for ti, (o0, rows) in enumerate(tiles) if q0 <= o0 < q0 + n]
            for ti in tl:
                o0, rows = tiles[ti]
                norm_T(XT_[ti], XT_[ti].ap[0:rows], rows, H1T, H1T3, o0 - q0, gb=GB2)
            for h in range(4):
                pq = PS()

                def mmq2(e, pq=pq, h=h, n=n):
                    ins = None
                    for kc in range(8):
                        ins = e.matmul(pq.ap[:, 0:n], lhsT=WMQ3[:, kc, h * 128:(h + 1) * 128], rhs=H1T3[:, kc, 0:n],
                                       start=(kc == 0), stop=(kc == 7))
                    return ins
                op("pe", mmq2, reads=[WMQ, H1T], writes=[pq])
                sq = SCRD()
                sq_ap = sq.ap.bitcast(BF16)
                op("act", lambda e, sq_ap=sq_ap, pq=pq, n=n: e.activation(out=sq_ap[:, 0:n], in_=pq.ap[:, 0:n], func=AF.Square),
                   reads=[pq], writes=[sq])
                rb = fm_rstd([(sq, sq_ap[:, 0:n], 128, 0)], 128, n, 128.0, SCRD)
                qm = SCRD()
                qm_ap = qm.ap.bitcast(BF16)
                op("dve", lambda e, pq=pq, rb=rb, qm_ap=qm_ap, n=n: e.scalar_tensor_tensor(
                    out=qm_ap[:, 0:n], in0=pq.ap[:, 0:n], scalar=ppc("gmq"), in1=rb.ap[:, 0:n], op0=ALU.mult, op1=ALU.mult),
                   reads=[pq, rb, PP], writes=[qm])
                po = PS()
                pd = PS()
                for mt in range(2):
                    ps_ = PS()
                    op("pe", lambda e, ps_=ps_, mt=mt, h=h, qm_ap=qm_ap, n=n: e.matmul(
                        ps_.ap[:, 0:n], lhsT=KM3[:, h, mt * 128:(mt + 1) * 128], rhs=qm_ap[:, 0:n], start=True, stop=True),
                       reads=[KM, qm], writes=[ps_])
                    pt_ = SCRD()
                    pt_ap = pt_.ap.bitcast(BF16)
                    op("act", lambda e, ps_=ps_, pt_ap=pt_ap, n=n: e.activation(out=pt_ap[:, 0:n], in_=ps_.ap[:, 0:n], func=AF.Exp,
                                                                         scale=mscale), reads=[ps_], writes=[pt_])
                    op("pe", lambda e, po=po, mt=mt, h=h, pt_ap=pt_ap, n=n: e.matmul(
                        po.ap[:, 0:n], lhsT=VM3[:, mt, h * 128:(h + 1) * 128], rhs=pt_ap[:, 0:n], start=(mt == 0), stop=(mt == 1)),
                       reads=[VM, pt_], writes=[po])
                    op("pe", lambda e, pd=pd, mt=mt, pt_ap=pt_ap, n=n: e.matmul(
                        pd.ap[:, 0:n], lhsT=ONESB.ap[:, 0:128], rhs=pt_ap[:, 0:n], start=(mt == 0), stop=(mt == 1)),
                       reads=[ONESB, pt_], writes=[pd])
                rd = SCRD()
                op("dve", lambda e, rd=rd, pd=pd, n=n: e.reciprocal(out=rd.ap[:, 0:n], in_=pd.ap[:, 0:n]),
                   reads=[pd], writes=[rd])
                op("dve", lambda e, po=po, rd=rd, h=h, n=n: e.tensor_tensor(out=OMT3[:, h, 0:n], in0=po.ap[:, 0:n], in1=rd.ap[:, 0:n],
                                                                         op=ALU.mult), reads=[po, rd], writes=[OMT])
            def d_outproj(ti, q0=q0):
                o0, rows = tiles[ti]
                xt = XT_[ti]
                c0 = o0 - q0
                for half in range(2):
                    pb = PS()

                    def mmo2(e, pb=pb, c0=c0, rows=rows, half=half):
                        ins = None
                        for h in range(4):
                            ins = e.matmul(pb.ap[0:rows, :], lhsT=OMT3[:, h, c0:c0 + rows],
                                           rhs=WMO3[:, h, half * 512:(half + 1) * 512], start=(h == 0), stop=(h == 3))
                        return ins
                    op("pe", mmo2, reads=[OMT, WMO], writes=[pb])
                    op("dve", lambda e, pb=pb, xt=xt, rows=rows, half=half: e.tensor_tensor(
                        out=xt.ap[0:rows, half * 512:(half + 1) * 512], in0=xt.ap[0:rows, half * 512:(half + 1) * 512],
                        in1=pb.ap[0:rows, :], op=ALU.add), reads=[xt, pb], writes=[xt])

            def d_ffnnorm(ti):
                o0, rows = tiles[ti]
                xt = XT_[ti]
                if ti < NTL:
                    norm_T(xt, xt.ap[0:rows], rows, HNT, HNT3, o0 + 1, gb=GB3, evac_eng="act")
                else:
                    norm_T(xt, xt.ap[0:rows], rows, HNH, HNH3, 0, gb=GB3)
                    op("dve", lambda e: e.tensor_scalar(out=HNT3[:, :, CH + 1:CH + 2], in0=HNH3[:, :, 0:1], scalar1=ppc("msk", 0),
                                                        scalar2=None, op0=ALU.mult), reads=[HNH, PP], writes=[HNT])
                    op("dve", lambda e: e.tensor_scalar(out=HNT3[:, :, 0:1], in0=HNH3[:, :, 1:2], scalar1=ppc("msk", 1),
                                                        scalar2=None, op0=ALU.mult), reads=[HNH, PP], writes=[HNT])
            d_outproj(tl[0])
            for i_ in range(len(tl)):
                if i_ + 1 < len(tl):
                    d_outproj(tl[i_ + 1])
                d_ffnnorm(tl[i_])
        if "x2" in dbg_d:
            for ti in range(NTL):
                store(dbg_d["x2"][ti * 128:(ti + 1) * 128, :], XT_[ti], XT_[ti].ap)
        S.barrier_all()
        A.release(mD)
        if stop == 'D':
            S.emit(nc)
            return nc

        GP = 2
        NR = 22 // GP
        WUPG = [A.bf(8 * 2 * GP * 128, "wupg%d" % i) for i in range(2)]
        WUPG4 = [w.ap.rearrange("p (k s c) -> p k s c", k=8, s=2) for w in WUPG]
        WDNG = [A.bf(GP * D, "wdng%d" % i) for i in range(2)]
        WDNG3 = [w.ap.rearrange("p (j d) -> p j d", j=GP) for w in WDNG]
        ACTT = [A.bf(GP * CH, "actt%d" % i) for i in range(2)]
        ACTT3 = [a.ap.rearrange("p (j t) -> p j t", j=GP) for a in ACTT]
        scrE = [A.f32(512, "scrE%d" % i) for i in range(8)]
        sei = [0]

        def SCRE():
            b = scrE[sei[0] % len(scrE)]
            sei[0] += 1
            return b
        fwo = PPL["fw"][0]
        fbo = PPL["fb"][0]
        def do_round(r):
            wu = WUPG[r % 2]
            wu4 = WUPG4[r % 2]
            wd = WDNG[r % 2]
            wd3 = WDNG3[r % 2]
            at = ACTT[r % 2]
            at3 = ACTT3[r % 2]
            j0 = r * GP
            load(wu, wu4[:, :, 0, :], w_up[:, j0 * 128:(j0 + GP) * 128].rearrange("(k p) c -> p k c", p=128))
            load(wu, wu4[:, :, 1, :], w_up[:, DFF + j0 * 128:DFF + (j0 + GP) * 128].rearrange("(k p) c -> p k c", p=128))
            load(wd, wd3, w_dn[j0 * 128:(j0 + GP) * 128, :].rearrange("(j p) d -> p j d", p=128))
            for (t0, n) in WINS:
                for jj in range(GP):
                    cv = []
                    for s in range(2):
                        chn = (j0 + jj) + s * 22
                        pg = PS()

                        def mmu(e, pg=pg, s=s, jj=jj, t0=t0, n=n):
                            ins = None
                            for kc in range(8):
                                ins = e.matmul(pg.ap[:, 0:n + 2], lhsT=wu4[:, kc, s, jj * 128:(jj + 1) * 128],
                                               rhs=HNT3[:, kc, t0:t0 + n + 2], start=(kc == 0), stop=(kc == 7))
                            return ins
                        op("pe", mmu, reads=[wu, HNT], writes=[pg])
                        c_ = SCRE()
                        wsrc, wb = PP, fwo + 3 * chn
                        bsrc, bb = PP, fbo + chn
                        op("act", lambda e, c_=c_, pg=pg, wsrc=wsrc, wb=wb, bsrc=bsrc, bb=bb, n=n: e.activation(
                            out=c_.ap[:, 0:n], in_=pg.ap[:, 1:n + 1], func=AF.Identity,
                            scale=wsrc.ap[:, wb + 1:wb + 2], bias=bsrc.ap[:, bb:bb + 1]),
                           reads=[pg, wsrc, bsrc], writes=[c_])
                        for tp in (0, 2):
                            op("dve", lambda e, c_=c_, pg=pg, wsrc=wsrc, wb=wb, tp=tp, n=n: e.scalar_tensor_tensor(
                                out=c_.ap[:, 0:n], in0=pg.ap[:, tp:tp + n], scalar=wsrc.ap[:, wb + tp:wb + tp + 1],
                                in1=c_.ap[:, 0:n], op0=ALU.mult, op1=ALU.add), reads=[pg, wsrc, c_], writes=[c_])
                        cv.append(c_)
                    sg = SCRE()
                    op("act", lambda e, sg=sg, g_=cv[0], n=n: e.activation(out=sg.ap[:, 0:n], in_=g_.ap[:, 0:n], func=AF.Tanh, scale=0.5),
                       reads=[cv[0]], writes=[sg])
                    op("act", lambda e, sg=sg, n=n: e.activation(out=sg.ap[:, 0:n], in_=sg.ap[:, 0:n], func=AF.Identity, scale=0.5,
                                                                bias=POSH.ap[:, 0:1]), reads=[sg, POSH], writes=[sg])
                    op("dve", lambda e, g_=cv[0], u_=cv[1], n=n: e.tensor_tensor(out=u_.ap[:, 0:n], in0=g_.ap[:, 0:n], in1=u_.ap[:, 0:n],
                                                                                op=ALU.mult), reads=[cv[0], cv[1]], writes=[cv[1]])
                    op("dve", lambda e, sg=sg, u_=cv[1], jj=jj, t0=t0, n=n: e.tensor_tensor(
                        out=at3[:, jj, t0:t0 + n], in0=sg.ap[:, 0:n], in1=u_.ap[:, 0:n], op=ALU.mult),
                       reads=[sg, cv[1]], writes=[at])
            if not (r % 2 == 1 or r == NR - 1):
                return
            rds = [r - 1, r] if r % 2 == 1 else [r]
            srcs = [(ACTT3[rr % 2], WDNG3[rr % 2]) for rr in rds]
            rbufs = [ACTT[rr % 2] for rr in rds] + [WDNG[rr % 2] for rr in rds]
            nmm = GP * len(rds)
            for ti in range(NTL):
                o0, rows = tiles[ti]
                xt = XT_[ti]
                for half in range(2):
                    pb = PS()

                    def mmd(e, pb=pb, o0=o0, half=half):
                        ins = None
                        i_ = 0
                        for (a3_, w3_) in srcs:
                            for jj in range(GP):
                                ins = e.matmul(pb.ap[:, :], lhsT=a3_[:, jj, o0:o0 + 128], rhs=w3_[:, jj, half * 512:(half + 1) * 512],
                                               start=(i_ == 0), stop=(i_ == nmm - 1))
                                i_ += 1
                        return ins
                    op("pe", mmd, reads=rbufs, writes=[pb])
                    op("dve", lambda e, pb=pb, xt=xt, half=half: e.tensor_tensor(
                        out=xt.ap[:, half * 512:(half + 1) * 512], in0=xt.ap[:, half * 512:(half + 1) * 512],
                        in1=pb.ap[:, :], op=ALU.add), reads=[xt, pb], writes=[xt])
        for r_ in range(NR):
            do_round(r_)
        for ti in range(NTL):
            store(out_d[ti * 128:(ti + 1) * 128, :], XT_[ti], XT_[ti].ap)
        S.emit(nc)
    return nc


def _cols(v, rows=128):
    v = np.asarray(v, np.float32)
    return np.ascontiguousarray(v.reshape(-1, rows).T)


def prep_core(inp, b, c, CH):
    f32 = np.float32
    x = np.asarray(inp["x"][b], f32)
    pos = np.asarray(inp["positions"][b]).astype(np.int32)
    s0 = c * CH
    slots = [(k, False) for k in range(c)] + [(k, True) for k in range(3, c, -1)] + [(c, False), (c, True)]
    assert len(slots) == 5
    xs = np.stack([x[k * CH:(k + 1) * CH][::-1] if rev else x[k * CH:(k + 1) * CH] for k, rev in slots])
    pk = np.stack([pos[k * CH:(k + 1) * CH][::-1] if rev else pos[k * CH:(k + 1) * CH] for k, rev in slots[:4]])
    posk = np.ascontiguousarray(np.broadcast_to(pk[:, None, :], (4, 32, CH))).astype(np.int32)
    xh = np.zeros((2, D), f32)
    ph = np.zeros((2,), np.int32)
    msk = np.zeros((2,), f32)
    if c < 3:
        xh[0] = x[s0 + CH]; ph[0] = pos[s0 + CH]; msk[0] = 1.0
    if c > 0:
        xh[1] = x[s0 - 1]; ph[1] = pos[s0 - 1]; msk[1] = 1.0
    po = np.concatenate([pos[s0:s0 + CH], ph])
    poso = np.ascontiguousarray(np.broadcast_to(po[None, :], (32, CH + 2))).astype(np.int32)
    flg = np.zeros((5, 4), f32)
    for k in range(3):
        if k < c:
            flg[k] = [1, 0, 1, 0]
        else:
            flg[k] = [0, 1, 0, 1]
    flg[3] = [1, 0, 0, 0]
    flg[4] = [0, 1, 0, 0]
    pp = np.zeros((128, PPL["_n"]), f32)

    def put(name, arr):
        o, n = PPL[name]
        arr = np.asarray(arr, f32)
        if arr.ndim == 1:
            arr = np.broadcast_to(arr[None, :], (128, arr.shape[0]))
        assert arr.shape[1] == n, (name, arr.shape, n)
        pp[:arr.shape[0], o:o + n] = arr
    put("flg", flg.reshape(-1))
    put("msk", msk)
    cw = np.zeros((128, 5, 4, 4), f32)
    cb = np.zeros((128, 5, 4), f32); ba = np.zeros((128, 5, 4), f32); bi = np.zeros((128, 5, 4), f32); lam = np.zeros((128, 5, 4), f32)
    wa = np.zeros((5, 4, 128, 128), f32); wi = np.zeros((5, 4, 128, 128), f32)
    for k, (ck, rev) in enumerate(slots):
        d = 1 if rev else 0
        w = np.asarray(inp["lru_conv_w"][0, d], f32)
        if rev:
            w = w[::-1]
        for tap in range(4):
            cw[:, k, :, tap] = _cols(w[tap])
        cb[:, k, :] = _cols(inp["lru_conv_b"][0, d])
        ba[:, k, :] = _cols(inp["lru_b_a"][0, d])
        bi[:, k, :] = _cols(inp["lru_b_i"][0, d])
        lam[:, k, :] = _cols(inp["lru_lambda"][0, d])
        for ct in range(4):
            for half in range(2):
                blk = 2 * ct + half
                wa[k, ct, half * 64:(half + 1) * 64, half * 64:(half + 1) * 64] = inp["lru_w_a"][0, d, blk]
                wi[k, ct, half * 64:(half + 1) * 64, half * 64:(half + 1) * 64] = inp["lru_w_i"][0, d, blk]
    put("cw", cw.reshape(128, -1)); put("cb", cb.reshape(128, -1)); put("ba", ba.reshape(128, -1))
    put("bi", bi.reshape(128, -1)); put("lam", lam.reshape(128, -1))
    put("gqa", _cols(inp["q_a_norm"][0]))
    put("gkv", _cols(inp["kv_a_norm"][0]))

    def rotperm(g):
        g = np.asarray(g, f32)
        r = g.copy()
        r[64:80] = g[80:96]
        r[80:96] = g[64:80]
        return r
    gq = np.asarray(inp["mla_q_norm"][0], f32); gk = np.asarray(inp["mla_k_norm"][0], f32)
    for name, v in (("gq", gq), ("gqr", rotperm(gq)), ("gk", gk), ("gkr", rotperm(gk))):
        a = np.zeros((128, 1), f32); a[:96, 0] = v
        put(name, a)
    put("glru", _cols(inp["lru_out_norm"][0]))
    gm = np.zeros((128, 8), f32); gm[:64] = _cols(inp["mla_out_norm"][0], 64)
    put("gmla", gm)
    put("gmq", _cols(inp["mem_q_norm"][0])); put("gmk", _cols(inp["mem_k_norm"][0]))
    fw = np.zeros((128, 44, 3), f32)
    fcw = np.asarray(inp["ffn_conv_w"][0], f32)
    for tap in range(3):
        fw[:, :, tap] = _cols(fcw[tap])
    put("fw", fw.reshape(128, -1))
    put("fb", _cols(inp["ffn_conv_b"][0]))
    invf = np.zeros((128, 1), f32)
    inv = (10000.0 ** (-np.arange(0, 32, 2, dtype=np.float64) / 32.0)) / (2.0 * np.pi)
    for p in range(64, 96):
        invf[p, 0] = inv[(p - 64) % 16]
    put("invf", invf)
    gvec = np.stack([inp["attn_norm"][0], inp["mem_attn_norm"][0], inp["mem_norm"][0], inp["ffn_norm"][0]]).astype(f32)
    m = {
        "xs": np.ascontiguousarray(xs), "xh": xh, "posk": posk, "poso": poso, "pp": pp, "wa": wa, "wi": wi,
        "mem": np.ascontiguousarray(np.asarray(inp["mem"][b], f32)), "gvec": np.ascontiguousarray(gvec),
        "w_in": np.ascontiguousarray(inp["w_in"][0], dtype=f32), "w_uq": np.ascontiguousarray(inp["w_uq"][0], dtype=f32),
        "w_ukv": np.ascontiguousarray(inp["w_ukv"][0], dtype=f32), "w_out": np.ascontiguousarray(inp["w_out"][0], dtype=f32),
        "w_mem_q": np.ascontiguousarray(inp["w_mem_q"][0], dtype=f32),
        "w_mem_kv": np.ascontiguousarray(inp["w_mem_kv"][0], dtype=f32),
        "w_mem_o": np.ascontiguousarray(inp["w_mem_o"][0], dtype=f32),
        "w_up": np.ascontiguousarray(inp["w_up"][0], dtype=f32), "w_down": np.ascontiguousarray(inp["w_down"][0], dtype=f32),
    }
    return m


_NC_CACHE = {}


def run(inputs, dbg=None, cores=None, stop=None):
    inputs = {k: np.asarray(v) for k, v in inputs.items()}
    B, SEQ, _ = inputs["x"].shape
    CH = SEQ // 4
    key = (CH, repr(dbg), stop)
    if key not in _NC_CACHE:
        _NC_CACHE[key] = build(CH, dbg, stop)
    nc = _NC_CACHE[key]
    core_list = cores if cores is not None else [(b, c) for b in range(B) for c in range(4)]
    in_maps = [prep_core(inputs, b, c, CH) for (b, c) in core_list]
    res = run_bass_kernel_spmd(nc, in_maps, core_ids=list(range(len(core_list))), trace=bool(os.environ.get('KTRACE')))
    return res, core_list, CH


def kernel(**inputs):
    res, core_list, CH = run(inputs)
    B, SEQ, _ = np.asarray(inputs["x"]).shape
    out = np.zeros((B, SEQ, D), np.float32)
    for (b, c), r in zip(core_list, res.results):
        out[b, c * CH:(c + 1) * CH] = r["out"]
    return out
```

```python
import numpy as np
from contextlib import ExitStack
import concourse.bass as bass
import concourse.mybir as mybir
from concourse.bass_utils import run_bass_kernel_spmd

F32 = mybir.dt.float32
BF16 = mybir.dt.bfloat16
I32 = mybir.dt.int32
AF = mybir.ActivationFunctionType
ALU = mybir.AluOpType

import os
SAME_ENGINE_SYNC = os.environ.get('SAME_ENGINE_SYNC', '1') == '1'
EPS = 1e-6
D = 1024
DFF = 2816
NCH = 44
MEM = 256


class Res:
    __slots__ = ("name", "w", "r")

    def __init__(self, name=""):
        self.name = name
        self.w = None
        self.r = {}


class Sched:
    ENGS = ("pe", "act", "dve", "pool", "sp")

    def __init__(self):
        self.ops = {e: [] for e in self.ENGS}
        self.cnt = {e: 0 for e in self.ENGS}
        self.dcnt = {}
        self.seen = {e: {} for e in self.ENGS}

    def _need(self, eng, tok, waits):
        if tok is None:
            return
        key, val = tok
        if self.seen[eng].get(key, 0) >= val:
            return
        if val > waits.get(key, 0):
            waits[key] = val

    def op(self, eng, fn, reads=(), writes=(), dma=None):
        waits = {}
        for r in reads:
            self._need(eng, r.w, waits)
        for w in writes:
            self._need(eng, w.w, waits)
            for k, v in w.r.items():
                self._need(eng, (k, v), waits)
        if not SAME_ENGINE_SYNC:
            waits.pop(eng, None)
        for k, v in waits.items():
            self.seen[eng][k] = v
        if dma is None:
            self.cnt[eng] += 1
            tok = (eng, self.cnt[eng])
        else:
            self.dcnt[dma] = self.dcnt.get(dma, 0) + 16
            tok = (dma, self.dcnt[dma])
        self.ops[eng].append((list(waits.items()), fn, tok))
        for r in reads:
            if r.r.get(tok[0], 0) < tok[1]:
                r.r[tok[0]] = tok[1]
        for w in writes:
            w.w = tok
            w.r = {}
        return tok

    def barrier_all(self):
        for e in self.ENGS:
            waits = {}
            for e2 in self.ENGS:
                if e2 != e and self.cnt[e2] > self.seen[e].get(e2, 0):
                    waits[e2] = self.cnt[e2]
            for k, v in self.dcnt.items():
                if v > self.seen[e].get(k, 0):
                    waits[k] = v
            for k, v in waits.items():
                self.seen[e][k] = v
            if waits:
                self.ops[e].append((list(waits.items()), None, None))

    def emit(self, nc):
        keys = list(self.ENGS) + sorted(self.dcnt.keys())
        with ExitStack() as st:
            sems = {k: st.enter_context(nc.semaphore("s_" + k)) for k in keys}
            block = st.enter_context(nc.Block())
            engmap = {"pe": block.tensor, "act": block.scalar, "dve": block.vector,
                      "pool": block.gpsimd, "sp": block.sync}
            fin = {}
            for k in keys:
                v = self.cnt[k] if k in self.cnt else self.dcnt[k]
                if v > 0:
                    fin[k] = v
            self.ops["sp"].append((list(fin.items()), None, None))
            for e in self.ENGS:
                ops = self.ops[e]

                def body(engobj, ops=ops, e=e):
                    for waits, fn, tok in ops:
                        for k, v in waits:
                            engobj.wait_ge(sems[k], v)
                        if fn is None:
                            continue
                        ins = fn(engobj)
                        if tok[0] == e:
                            ins.then_inc(sems[e], 1)
                        else:
                            ins.then_inc(sems[tok[0]], 16)
                engmap[e](body)


class Buf:
    __slots__ = ("ap", "r")

    def __init__(self, ap, r):
        self.ap = ap
        self.r = r


class Arena:
    def __init__(self, t, size):
        self.t = t
        self.size = size
        self.regs = [[0, size]]

    def _take(self, n, name):
        for r in self.regs:
            if r[1] - r[0] >= n:
                o = r[0]
                r[0] += n
                return o
        raise AssertionError(("SBUF arena overflow", name, n, self.regs))

    def f32(self, n, name=""):
        o = self._take(n, name)
        return Buf(self.t[:, o:o + n], Res(name))

    def bf(self, n, name=""):
        m = (n + 1) // 2
        o = self._take(m, name)
        return Buf(self.t[:, o:o + m].bitcast(BF16)[:, 0:n], Res(name))

    def mark(self):
        return [list(r) for r in self.regs]

    def release(self, m):
        self.regs = [list(r) for r in m]

    def top(self):
        return self.regs[0][0]


def ffn_windows(CH):
    nw = -(-CH // 510)
    if CH % 512 == 0 and CH >= 512:
        nw = max(nw, 1)
    base = CH // nw
    rem = CH - base * nw
    sizes = [base + (1 if i < rem else 0) for i in range(nw)]
    starts = [sum(sizes[:i]) for i in range(nw)]
    return list(zip(starts, sizes))


def pp_layout():
    o = {}
    c = 0

    def add(name, n):
        nonlocal c
        o[name] = (c, n)
        c += n
    add("flg", 20)
    add("msk", 2)
    add("cw", 80)
    add("cb", 20)
    add("ba", 20)
    add("bi", 20)
    add("lam", 20)
    add("gqa", 2)
    add("gkv", 1)
    add("gq", 1)
    add("gqr", 1)
    add("gk", 1)
    add("gkr", 1)
    add("glru", 4)
    add("gmla", 8)
    add("gmq", 1)
    add("gmk", 1)
    add("fw", 132)
    add("fb", 44)
    add("invf", 1)
    o["_n"] = c
    return o


PPL = pp_layout()


class _Stop(Exception):
    pass


def build(CH, dbg=None, stop=None):
    holder = {}
    try:
        return _build(CH, dbg, stop, holder)
    except _Stop:
        return holder['nc']


def _build(CH, dbg, stop, holder):
    NB = CH // 512
    NKEY = 4 * CH
    NKT = NKEY // 128
    NKB = NKEY // 512
    NOWN = CH + 2
    WINS = ffn_windows(CH)
    nc = bass.Bass("TRN2", target_bir_lowering=False)
    holder["nc"] = nc

    def din(name, shape, dt=F32):
        return nc.dram_tensor(name, list(shape), dt, kind="ExternalInput").ap()

    xs = din("xs", [5, CH, D])
    xh = din("xh", [2, D])
    posk = din("posk", [4, 32, CH], I32)
    poso = din("poso", [32, NOWN], I32)
    pp_d = din("pp", [128, PPL["_n"]])
    wa_d = din("wa", [5, 4, 128, 128])
    wi_d = din("wi", [5, 4, 128, 128])
    mem_d = din("mem", [MEM, D])
    gvec = din("gvec", [4, D])
    w_in = din("w_in", [D, 1440])
    w_uq = din("w_uq", [256, 768])
    w_ukv = din("w_ukv", [128, 1024])
    w_out = din("w_out", [1024, D])
    w_mq = din("w_mem_q", [D, 512])
    w_mkv = din("w_mem_kv", [D, 1024])
    w_mo = din("w_mem_o", [512, D])
    w_up = din("w_up", [D, 2 * DFF])
    w_dn = din("w_down", [DFF, D])
    out_d = nc.dram_tensor("out", [CH, D], F32, kind="ExternalOutput").ap()
    dbg_d = {}
    if dbg:
        for name, shape in dbg.items():
            dbg_d[name] = nc.dram_tensor("dbg_" + name, list(shape), F32, kind="ExternalOutput").ap()

    S = Sched()
    with ExitStack() as st:
        ASZ = 52500
        arena_t = st.enter_context(nc.sbuf_tensor("arena", [128, ASZ], F32))
        A = Arena(arena_t, ASZ)
        psum_t = [st.enter_context(nc.psum_tensor("ps%d" % i, [128, 512], F32)) for i in range(8)]
        psum = [Buf(t[:, :], Res("ps%d" % i)) for i, t in enumerate(psum_t)]
        psi = [0]

        ps_pool = [[0, 1, 2, 3, 4]]

        def PS():
            pool = ps_pool[0]
            b = psum[pool[psi[0] % len(pool)]]
            psi[0] += 1
            return b
        psa = [0]

        def PSACC():
            b = psum[6 + psa[0] % 2]
            psa[0] += 1
            return b

        def chk(tag):
            if stop == tag:
                S.emit(nc)
                raise _Stop()

        def op(eng, fn, reads=(), writes=(), dma=None):
            return S.op(eng, fn, [b.r for b in reads], [b.r for b in writes], dma)

        dkeys = {}

        def dkey(buf, pre="k"):
            k = (pre, id(buf.r))
            if k not in dkeys:
                dkeys[k] = "%s%02d" % (pre, len(dkeys))
            return dkeys[k]

        def load(dst, dst_ap, src_ap, eng=None, key=None):
            if eng is None:
                eng = "pool" if dst_ap.dtype != src_ap.dtype else "sp"
            op(eng, lambda e: e.dma_start(out=dst_ap, in_=src_ap), reads=[], writes=[dst], dma=dkey(dst))

        def store(dst_ap, buf, src_ap):
            op("sp", lambda e: e.dma_start(out=dst_ap, in_=src_ap), reads=[buf], dma=dkey(buf, "s"))

        def dump(name, buf, ap, rows=None):
            if name in dbg_d:
                store(dbg_d[name], buf, ap)

        PP = A.f32(PPL["_n"], "pp")
        load(PP, PP.ap, pp_d)

        def ppc(name, i=0, rows=slice(0, 128)):
            o, n = PPL[name]
            return PP.ap[rows, o + i:o + i + 1]

        CONST = A.f32(8, "const")
        op("pool", lambda e: e.memset(CONST.ap[:, 0:1], EPS), writes=[CONST])
        op("pool", lambda e: e.memset(CONST.ap[:, 1:2], 1.0), writes=[CONST])
        c_eps = CONST.ap[:, 0:1]
        c_one = CONST.ap[:, 1:2]
        NEGH = A.f32(8, "negh")
        POSH = A.f32(8, "posh")
        op("pool", lambda e: e.memset(NEGH.ap, -0.5), writes=[NEGH])
        op("pool", lambda e: e.memset(POSH.ap, 0.5), writes=[POSH])
        IDF = A.f32(128, "identf")
        IDENT = A.bf(128, "ident")
        ONESB = A.bf(128, "onesb")
        ONESF = A.f32(128, "onesf")
        op("pool", lambda e: e.iota(IDF.ap, [[1, 128]], base=0, channel_multiplier=-1,
                                    allow_small_or_imprecise_dtypes=True), writes=[IDF])
        op("dve", lambda e: e.tensor_scalar(out=IDENT.ap, in0=IDF.ap, scalar1=0.0, scalar2=None,
                                            op0=ALU.is_equal), reads=[IDF], writes=[IDENT])
        op("pool", lambda e: e.memset(ONESB.ap, 1.0), writes=[ONESB])
        op("pool", lambda e: e.memset(ONESF.ap, 1.0), writes=[ONESF])
        LP = A.f32(192, "lruparams")
        lam_o = PPL["lam"][0]
        flg_o = PPL["flg"][0]
        op("act", lambda e: e.activation(out=LP.ap[:, 0:20], in_=PP.ap[:, lam_o:lam_o + 20], func=AF.Exp, scale=-1.0),
           reads=[PP], writes=[LP])
        op("act", lambda e: e.activation(out=LP.ap[:, 0:20], in_=LP.ap[:, 0:20], func=AF.Ln, scale=1.0, bias=c_one),
           reads=[LP, CONST], writes=[LP])
        op("dve", lambda e: e.tensor_scalar(out=LP.ap[:, 20:40], in0=LP.ap[:, 0:20], scalar1=-4.0, scalar2=None,
                                            op0=ALU.mult), reads=[LP], writes=[LP])
        op("dve", lambda e: e.tensor_scalar(out=LP.ap[:, 0:20], in0=LP.ap[:, 0:20], scalar1=-8.0, scalar2=None,
                                            op0=ALU.mult), reads=[LP], writes=[LP])
        op("dve", lambda e: e.tensor_scalar(out=LP.ap[:, 40:60], in0=PP.ap[:, flg_o:flg_o + 20], scalar1=-1.0,
                                            scalar2=1.0, op0=ALU.mult, op1=ALU.add), reads=[PP], writes=[LP])

        ba_o = PPL["ba"][0]
        bi_o = PPL["bi"][0]
        fw_o = PPL["fw"][0]
        fb_o = PPL["fb"][0]
        op("dve", lambda e: e.tensor_scalar(out=LP.ap[:, 60:80], in0=PP.ap[:, ba_o:ba_o + 20], scalar1=0.5, scalar2=None,
                                            op0=ALU.mult), reads=[PP], writes=[LP])
        op("dve", lambda e: e.tensor_scalar(out=LP.ap[:, 80:100], in0=PP.ap[:, bi_o:bi_o + 20], scalar1=0.5, scalar2=None,
                                            op0=ALU.mult), reads=[PP], writes=[LP])
        op("dve", lambda e: e.tensor_scalar(out=LP.ap[:, 100:166], in0=PP.ap[:, fw_o + 66:fw_o + 132], scalar1=0.5, scalar2=None,
                                            op0=ALU.mult), reads=[PP], writes=[LP])
        op("dve", lambda e: e.tensor_scalar(out=LP.ap[:, 166:188], in0=PP.ap[:, fb_o + 22:fb_o + 44], scalar1=0.5, scalar2=None,
                                            op0=ALU.mult), reads=[PP], writes=[LP])

        def flg(k, i):
            return PP.ap[:, flg_o + 4 * k + i:flg_o + 4 * k + i + 1]

        def nflg(k, i):
            return LP.ap[:, 40 + 4 * k + i:40 + 4 * k + i + 1]

        GBC = A.f32(D, "gbc")

        NT_X = [A.f32(D, "xt%d" % i) for i in range(2)]
        NT_J = A.bf(D, "junk")
        NT_H = [A.bf(D, "hb%d" % i) for i in range(2)]
        NT_Sl = [A.f32(2, "nstat%d" % i) for i in range(2)]
        nt_i = [0]

        def norm_T(xbuf, x_ap, n, dstT, dstT_ap3, col0, evac_eng="dve", gb=None):
            gb = gb or GBC
            i = nt_i[0] % 2
            nt_i[0] += 1
            hb = NT_H[i]
            NT_S = NT_Sl[i]
            ss = NT_S.ap[0:n, 0:1]
            rs = NT_S.ap[0:n, 1:2]
            op("act", lambda e: e.activation(out=NT_J.ap[0:n], in_=x_ap, func=AF.Square, accum_out=ss),
               reads=[xbuf], writes=[NT_J, NT_S])
            op("dve", lambda e: e.tensor_scalar(out=rs, in0=ss, scalar1=1.0 / D, scalar2=EPS, op0=ALU.mult, op1=ALU.add),
               reads=[NT_S], writes=[NT_S])
            op("pool", lambda e: e.tensor_tensor(out=rs, in0=rs, in1=NEGH.ap[0:n, 0:1], op=ALU.pow),
               reads=[NT_S, NEGH], writes=[NT_S])
            op("dve", lambda e: e.scalar_tensor_tensor(out=hb.ap[0:n], in0=x_ap, scalar=rs, in1=gb.ap[0:n],
                                                       op0=ALU.mult, op1=ALU.mult),
               reads=[xbuf, NT_S, gb], writes=[hb])
            pb = PS()
            pbf = pb.ap.bitcast(BF16)

            def tr(e):
                ins = None
                for kc in range(8):
                    ins = e.transpose(pbf[:, kc * 128:kc * 128 + n], hb.ap[0:n, kc * 128:(kc + 1) * 128],
                                      IDENT.ap[0:n, 0:n])
                return ins
            op("pe", tr, reads=[hb, IDENT], writes=[pb])
            src = pbf.rearrange("p (k t) -> p k t", k=8)[:, :, 0:n]
            dst = dstT_ap3[:, :, col0:col0 + n]
            if evac_eng == "act":
                op("act", lambda e: e.activation(out=dst, in_=src, func=AF.Copy), reads=[pb], writes=[dstT])
            else:
                op("dve", lambda e: e.tensor_copy(out=dst, in_=src), reads=[pb], writes=[dstT])

        def fm_rstd(sq_list, rows_out, n, dim, SCR, pbuf=None):
            pb = pbuf if pbuf is not None else PS()

            def mm(e):
                ins = None
                for i, (b, ap, rk, base) in enumerate(sq_list):
                    ins = e.matmul(pb.ap[0:rows_out, 0:n], lhsT=ONESB.ap[base:base + rk, 0:rows_out], rhs=ap,
                                   start=(i == 0), stop=(i == len(sq_list) - 1))
                return ins
            op("pe", mm, reads=[b for b, _, _, _ in sq_list] + [ONESB], writes=[pb])
            rb = SCR()
            op("act", lambda e: e.activation(out=rb.ap[0:rows_out, 0:n], in_=pb.ap[0:rows_out, 0:n], func=AF.Ln,
                                             scale=1.0 / dim, bias=c_eps[0:rows_out]),
               reads=[pb, CONST], writes=[rb])
            op("act", lambda e: e.activation(out=rb.ap[0:rows_out, 0:n], in_=rb.ap[0:rows_out, 0:n], func=AF.Exp,
                                             scale=-0.5), reads=[rb], writes=[rb])
            return rb

        POSB = [A.f32(512, "posb%d" % i) for i in range(2)]
        posb_i = [0]

        def rope_tables(pos_src_ap, n, SCR, out_buf=None, out_s=None, out_c=None):
            pi = POSB[posb_i[0] % 2]
            posb_i[0] += 1
            pi_ap = pi.ap.bitcast(I32)
            load(pi, pi_ap[64:96, 0:n], pos_src_ap, eng="sp")
            y = SCR()
            op("dve", lambda e: e.tensor_copy(out=y.ap[64:96, 0:n], in_=pi_ap[64:96, 0:n]), reads=[pi], writes=[y])
            y2 = SCR()
            yc = SCR()
            op("dve", lambda e: e.tensor_scalar(out=y2.ap[64:96, 0:n], in0=y.ap[64:96, 0:n],
                                                scalar1=ppc("invf", 0, slice(64, 96)), scalar2=None, op0=ALU.mult),
               reads=[y, PP], writes=[y2])
            op("dve", lambda e: e.tensor_scalar(out=yc.ap[64:96, 0:n], in0=y2.ap[64:96, 0:n], scalar1=0.25,
                                                scalar2=None, op0=ALU.add), reads=[y2], writes=[yc])
            res = []
            for yy, dst in ((y2, out_s), (yc, out_c)):
                ti = SCR()
                ti_ap = ti.ap.bitcast(I32)
                op("dve", lambda e, yy=yy, ti_ap=ti_ap: e.tensor_copy(out=ti_ap[64:96, 0:n], in_=yy.ap[64:96, 0:n]),
                   reads=[yy], writes=[ti])
                tf = SCR()
                op("dve", lambda e, ti_ap=ti_ap, tf=tf: e.tensor_copy(out=tf.ap[64:96, 0:n], in_=ti_ap[64:96, 0:n]),
                   reads=[ti], writes=[tf])
                op("dve", lambda e, yy=yy, tf=tf: e.tensor_tensor(out=yy.ap[64:96, 0:n], in0=yy.ap[64:96, 0:n],
                                                                  in1=tf.ap[64:96, 0:n], op=ALU.subtract),
                   reads=[yy, tf], writes=[yy])
                if dst is None:
                    o_b, o_ap = yy, yy.ap[64:96, 0:n]
                else:
                    o_b, o_ap = out_buf, dst
                op("act", lambda e, yy=yy, o_ap=o_ap: e.activation(out=o_ap, in_=yy.ap[64:96, 0:n], func=AF.Sin,
                                                                   scale=2.0 * np.pi * 0.999999),
                   reads=[yy], writes=[o_b])
                res.append((o_b, o_ap))
            return res

        KM = A.bf(4 * MEM, "km")
        KM3 = KM.ap.rearrange("p (h t) -> p h t", h=4)
        VM = A.bf(2 * 512, "vm")
        VM3 = VM.ap.rearrange("p (t c) -> p t c", t=2)
        m0 = A.mark()
        scr0 = [A.f32(512, "scr0%d" % i) for i in range(4)]
        s0i = [0]

        def SCRD():
            b = scr0[s0i[0] % len(scr0)]
            s0i[0] += 1
            return b
        if stop == '0a':
            S.emit(nc)
            return nc
        load(GBC, GBC.ap, gvec[2, :].partition_broadcast(128))
        mW = A.mark()
        WMKV = A.bf(8 * 1024, "wmkv")
        WMKV3 = WMKV.ap.rearrange("p (k c) -> p k c", k=8)
        load(WMKV, WMKV3, w_mkv.rearrange("(k p) c -> p k c", p=128))
        MEMT = A.bf(8 * MEM, "memt")
        MEMT3 = MEMT.ap.rearrange("p (k t) -> p k t", k=8)
        for mt in range(2):
            xb = NT_X[nt_i[0] % 2]
            load(xb, xb.ap, mem_d[mt * 128:(mt + 1) * 128, :], eng="sp")
            norm_T(xb, xb.ap, 128, MEMT, MEMT3, mt * 128)
        if stop == '0b':
            S.emit(nc)
            return nc
        for h in range(4):
            pk = PS()

            def mmk(e, pk=pk, h=h):
                ins = None
                for kc in range(8):
                    ins = e.matmul(pk.ap[:, 0:MEM], lhsT=WMKV3[:, kc, h * 128:(h + 1) * 128], rhs=MEMT3[:, kc, :],
                                   start=(kc == 0), stop=(kc == 7))
                return ins
            op("pe", mmk, reads=[WMKV, MEMT], writes=[pk])
            sq = SCRD()
            sq_ap = sq.ap.bitcast(BF16)
            op("act", lambda e, sq_ap=sq_ap, pk=pk: e.activation(out=sq_ap[:, 0:MEM], in_=pk.ap[:, 0:MEM], func=AF.Square),
               reads=[pk], writes=[sq])
            rb = fm_rstd([(sq, sq_ap[:, 0:MEM], 128, 0)], 128, MEM, 128.0, SCRD)
            op("dve", lambda e, pk=pk, rb=rb, h=h: e.scalar_tensor_tensor(
                out=KM3[:, h, :], in0=pk.ap[:, 0:MEM], scalar=ppc("gmk"), in1=rb.ap[:, 0:MEM], op0=ALU.mult, op1=ALU.mult),
               reads=[pk, rb, PP], writes=[KM])
        if stop == '0c':
            S.emit(nc)
            return nc
        for mt in range(2):
            pv = PS()

            def mmv2(e, pv=pv, mt=mt):
                ins = None
                for kc in range(8):
                    ins = e.matmul(pv.ap[:, :], lhsT=MEMT3[:, kc, mt * 128:(mt + 1) * 128], rhs=WMKV3[:, kc, 512:1024],
                                   start=(kc == 0), stop=(kc == 7))
                return ins
            op("pe", mmv2, reads=[WMKV, MEMT], writes=[pv])
            op("act", lambda e, pv=pv, mt=mt: e.activation(out=VM3[:, mt, :], in_=pv.ap[:, :], func=AF.Copy),
               reads=[pv], writes=[VM])
        if stop == '0d':
            S.emit(nc)
            return nc
        S.barrier_all()
        A.release(m0)
        mP = A.top()
        if stop == '0':
            S.emit(nc)
            return nc

        MIXL = A.bf(4 * NOWN, "mixl")
        MIXL3 = MIXL.ap.rearrange("p (c t) -> p c t", c=4)
        CQN = A.bf(2 * NOWN, "cqn")
        CQN3 = CQN.ap.rearrange("p (c t) -> p c t", c=2)
        ckvn_start = A.top()
        CKVN = A.bf(NKEY, "ckvn")
        ETS = A.bf(NKEY, "e_tsq")
        ets_end = A.top()
        mA = A.mark()

        WIN = A.bf(8 * 1440, "win")
        WIN3 = WIN.ap.rearrange("p (k c) -> p k c", k=8)
        for kc in range(8):
            load(WIN, WIN3[:, kc, :], w_in[kc * 128:(kc + 1) * 128, :])
        WKR = A.bf(8 * 192, "wkrpad")
        WKR3 = WKR.ap.rearrange("p (k c) -> p k c", k=8)
        op("pool", lambda e: e.memset(WKR.ap, 0.0), writes=[WKR])
        op("dve", lambda e: e.tensor_copy(out=WKR3[:, :, 64:96], in_=WIN3[:, :, 1408:1440]), reads=[WIN], writes=[WKR])
        op("dve", lambda e: e.tensor_scalar(out=WKR3[:, :, 160:176], in0=WIN3[:, :, 1424:1440], scalar1=-1.0,
                                            scalar2=None, op0=ALU.mult), reads=[WIN], writes=[WKR])
        op("dve", lambda e: e.tensor_copy(out=WKR3[:, :, 176:192], in_=WIN3[:, :, 1408:1424]), reads=[WIN], writes=[WKR])
        WGL = [A.bf(4 * 2 * 128, "wgates%d" % i) for i in range(2)]
        WGL4 = [w.ap.rearrange("p (c g o) -> p c g o", c=4, g=2) for w in WGL]
        HTL = [A.bf(8 * 512, "ht%d" % i) for i in range(2)]
        HTL3 = [h_.ap.rearrange("p (k t) -> p k t", k=8) for h_ in HTL]
        XR = A.f32(4 * 515, "xr")
        XR3 = XR.ap.rearrange("p (c t) -> p c t", c=4)
        HF = A.bf(4 * (CH + 1), "hf")
        HF3 = HF.ap.rearrange("p (c t) -> p c t", c=4)
        GG = A.bf(4 * NOWN, "gelu")
        GG3 = GG.ap.rearrange("p (c t) -> p c t", c=4)
        CAR = A.f32(64, "carry")
        op("pool", lambda e: e.memset(CAR.ap, 0.0), writes=[CAR])
        HSC = A.f32(4 * 512, "hscan")
        HSC3 = HSC.ap.rearrange("p (c t) -> p c t", c=4)
        HE = A.f32(16, "hextra")
        scrA = [A.f32(512, "scrA%d" % i) for i in range(12)]
        sai = [0]

        def SCRA():
            b = scrA[sai[0] % len(scrA)]
            sai[0] += 1
            return b

        load(GBC, GBC.ap, gvec[0, :].partition_broadcast(128))

        chk('A0')

        def lru_cols(k, ct, name):
            o, n = PPL[name]
            return PP.ap[:, o + 4 * k + ct:o + 4 * k + ct + 1]

        def phaseA_front(k, j, n, mini, hb):
            HT = HTL[hb]
            HT3 = HTL3[hb]
            if j == 0 and not mini:
                load(WGL[k % 2], WGL4[k % 2][:, :, 0, :], wa_d[k].rearrange("c i o -> i c o"))
                load(WGL[k % 2], WGL4[k % 2][:, :, 1, :], wi_d[k].rearrange("c i o -> i c o"))
            ntile = (n + 127) // 128
            for i in range(ntile):
                rows = min(128, n - i * 128)
                xb = NT_X[nt_i[0] % 2]
                if mini:
                    src = xh[0:1, :] if k == 3 else xh[1:2, :]
                else:
                    src = xs[k, j * 512 + i * 128:j * 512 + i * 128 + rows, :]
                load(xb, xb.ap[0:rows], src, eng="sp")
                norm_T(xb, xb.ap[0:rows], rows, HT, HT3, i * 128, evac_eng="dve")

        def phaseA_xr(k, j, n, mini, hb):
            HT = HTL[hb]
            HT3 = HTL3[hb]
            if j == 0 and not mini:
                op("dve", lambda e: e.tensor_scalar(out=CAR.ap[:, 36:48], in0=CAR.ap[:, 8:20], scalar1=flg(k, 0),
                                                    scalar2=None, op0=ALU.mult), reads=[CAR, PP], writes=[CAR])
                op("dve", lambda e: e.scalar_tensor_tensor(out=CAR.ap[:, 36:48], in0=CAR.ap[:, 20:32], scalar=flg(k, 1),
                                                           in1=CAR.ap[:, 36:48], op0=ALU.mult, op1=ALU.add),
                   reads=[CAR, PP], writes=[CAR])
                op("dve", lambda e: e.tensor_scalar(out=CAR.ap[:, 32:36], in0=CAR.ap[:, 0:4], scalar1=flg(k, 0),
                                                    scalar2=None, op0=ALU.mult), reads=[CAR, PP], writes=[CAR])
                op("dve", lambda e: e.scalar_tensor_tensor(out=CAR.ap[:, 32:36], in0=CAR.ap[:, 4:8], scalar=flg(k, 1),
                                                           in1=CAR.ap[:, 32:36], op0=ALU.mult, op1=ALU.add),
                   reads=[CAR, PP], writes=[CAR])
                if k == 3:
                    op("dve", lambda e: e.tensor_copy(out=CAR.ap[:, 48:52], in_=CAR.ap[:, 32:36]), reads=[CAR], writes=[CAR])
                if k == 4:
                    op("dve", lambda e: e.tensor_copy(out=CAR.ap[:, 52:56], in_=CAR.ap[:, 32:36]), reads=[CAR], writes=[CAR])
                op("dve", lambda e: e.tensor_copy(out=XR3[:, :, 0:3],
                                                  in_=CAR.ap[:, 36:48].rearrange("p (c t) -> p c t", c=4)),
                   reads=[CAR], writes=[XR])
            else:
                op("dve", lambda e: e.tensor_copy(out=XR3[:, :, 0:3], in_=XR3[:, :, 512:515]), reads=[XR], writes=[XR])
            for ct in range(4):
                pb = PS()

                def mm(e, pb=pb, ct=ct):
                    ins = None
                    for kc in range(8):
                        ins = e.matmul(pb.ap[:, 0:n], lhsT=WIN3[:, kc, ct * 128:(ct + 1) * 128], rhs=HT3[:, kc, 0:n],
                                       start=(kc == 0), stop=(kc == 7))
                    return ins
                op("pe", mm, reads=[WIN, HT], writes=[pb])
                op("act", lambda e, pb=pb, ct=ct: e.activation(out=XR3[:, ct, 3:3 + n], in_=pb.ap[:, 0:n], func=AF.Copy),
                   reads=[pb], writes=[XR])

        def phaseA_block(k, j, n, mini, hb, mid=None):
            HT = HTL[hb]
            HT3 = HTL3[hb]
            WG = WGL[k % 2]
            WG4 = WGL4[k % 2]
            do_kv = (k <= 3) and not mini
            do_own = (k == 3) or (k == 4 and mini)
            if mini:
                own0 = CH if k == 3 else CH + 1
            else:
                own0 = j * 512
            chk('A1')
            for pr in range(2):
                cts = (2 * pr, 2 * pr + 1)
                B_ = {}
                for ct in cts:
                    cwo = PPL["cw"][0] + (k * 4 + ct) * 4
                    xc = SCRA()
                    op("act", lambda e, xc=xc, ct=ct, cwo=cwo: e.activation(
                        out=xc.ap[:, 0:n], in_=XR3[:, ct, 3:3 + n], func=AF.Identity,
                        scale=PP.ap[:, cwo + 3:cwo + 4], bias=lru_cols(k, ct, "cb")), reads=[XR, PP], writes=[xc])
                    for tp in range(3):
                        op("dve", lambda e, xc=xc, ct=ct, cwo=cwo, tp=tp: e.scalar_tensor_tensor(
                            out=xc.ap[:, 0:n], in0=XR3[:, ct, tp:tp + n], scalar=PP.ap[:, cwo + tp:cwo + tp + 1],
                            in1=xc.ap[:, 0:n], op0=ALU.mult, op1=ALU.add), reads=[XR, PP, xc], writes=[xc])
                    xcb = SCRA()
                    xcb_ap = xcb.ap.bitcast(BF16)
                    op("dve", lambda e, xc=xc, xcb_ap=xcb_ap: e.tensor_copy(out=xcb_ap[:, 0:n], in_=xc.ap[:, 0:n]),
                       reads=[xc], writes=[xcb])
                    B_[ct] = dict(xc=xc, xcb=xcb, xcb_ap=xcb_ap)
                for ct in cts:
                    d = B_[ct]
                    pa = PS()
                    op("pe", lambda e, pa=pa, ct=ct, xcb_ap=d["xcb_ap"]: e.matmul(
                        pa.ap[:, 0:n], lhsT=WG4[:, ct, 0, :], rhs=xcb_ap[:, 0:n], start=True, stop=True),
                       reads=[WG, d["xcb"]], writes=[pa])
                    pi_ = PS()
                    op("pe", lambda e, pi_=pi_, ct=ct, xcb_ap=d["xcb_ap"]: e.matmul(
                        pi_.ap[:, 0:n], lhsT=WG4[:, ct, 1, :], rhs=xcb_ap[:, 0:n], start=True, stop=True),
                       reads=[WG, d["xcb"]], writes=[pi_])
                    d["pa"] = pa
                    d["pi"] = pi_
                for ct in cts:
                    d = B_[ct]
                    rr = SCRA()
                    ig = SCRA()
                    op("act", lambda e, rr=rr, pa=d["pa"], ct=ct: e.activation(out=rr.ap[:, 0:n], in_=pa.ap[:, 0:n], func=AF.Tanh,
                                                                          bias=LP.ap[:, 60 + 4 * k + ct:61 + 4 * k + ct], scale=0.5),
                       reads=[d["pa"], LP], writes=[rr])
                    op("act", lambda e, ig=ig, pi_=d["pi"], ct=ct: e.activation(out=ig.ap[:, 0:n], in_=pi_.ap[:, 0:n], func=AF.Tanh,
                                                                            bias=LP.ap[:, 80 + 4 * k + ct:81 + 4 * k + ct], scale=0.5),
                       reads=[d["pi"], LP], writes=[ig])
                    d["rr"] = rr
                    d["ig"] = ig
                for ct in cts:
                    d = B_[ct]
                    aa = SCRA()
                    mm_ = SCRA()
                    cs = LP.ap[:, 4 * k + ct:4 * k + ct + 1]
                    hcs = LP.ap[:, 20 + 4 * k + ct:20 + 4 * k + ct + 1]
                    op("act", lambda e, aa=aa, rr=d["rr"], hcs=hcs: e.activation(out=aa.ap[:, 0:n], in_=rr.ap[:, 0:n], func=AF.Exp,
                                                                             scale=hcs, bias=hcs), reads=[d["rr"], LP], writes=[aa])
                    op("act", lambda e, mm_=mm_, rr=d["rr"], cs=cs: e.activation(out=mm_.ap[:, 0:n], in_=rr.ap[:, 0:n], func=AF.Exp,
                                                                             scale=cs, bias=cs), reads=[d["rr"], LP], writes=[mm_])
                    op("dve", lambda e, mm_=mm_: e.tensor_scalar(out=mm_.ap[:, 0:n], in0=mm_.ap[:, 0:n], scalar1=-0.25, scalar2=0.25,
                                                                 op0=ALU.mult, op1=ALU.add), reads=[mm_], writes=[mm_])
                    op("dve", lambda e, ig=d["ig"], xc=d["xc"]: e.scalar_tensor_tensor(out=ig.ap[:, 0:n], in0=ig.ap[:, 0:n], scalar=1.0,
                                                                                   in1=xc.ap[:, 0:n], op0=ALU.add, op1=ALU.mult),
                       reads=[d["ig"], d["xc"]], writes=[d["ig"]])
                    d["aa"] = aa
                    d["mm"] = mm_
                for ct in cts:
                    d = B_[ct]
                    op("act", lambda e, mm_=d["mm"]: e.activation(out=mm_.ap[:, 0:n], in_=mm_.ap[:, 0:n], func=AF.Sqrt),
                       reads=[d["mm"]], writes=[d["mm"]])
                for ct in cts:
                    d = B_[ct]
                    op("dve", lambda e, ig=d["ig"], mm_=d["mm"]: e.tensor_tensor(out=ig.ap[:, 0:n], in0=ig.ap[:, 0:n], in1=mm_.ap[:, 0:n],
                                                                              op=ALU.mult), reads=[d["ig"], d["mm"]], writes=[d["ig"]])
                    if mini or j > 0:
                        init = CAR.ap[:, 56 + ct:57 + ct]
                    else:
                        init = CAR.ap[:, 32 + ct:33 + ct]
                    op("dve", lambda e, aa=d["aa"], ig=d["ig"], ct=ct, init=init: e.tensor_tensor_scan(
                        out=HSC3[:, ct, 0:n], data0=aa.ap[:, 0:n], data1=ig.ap[:, 0:n], initial=init,
                        op0=ALU.mult, op1=ALU.add), reads=[d["aa"], d["ig"], CAR], writes=[HSC])
            if not mini:
                op("dve", lambda e: e.tensor_copy(out=CAR.ap[:, 56:60], in_=HSC3[:, :, n - 1]), reads=[HSC], writes=[CAR])
            chk('A2')
            if k == 3:
                op("dve", lambda e: e.tensor_copy(out=HF3[:, :, own0 if not mini else CH:(own0 if not mini else CH) + n],
                                                   in_=HSC3[:, :, 0:n]), reads=[HSC], writes=[HF])
            if k <= 2 and (not mini) and j == NB - 1:
                for (st_o, u_i) in ((0, 2), (4, 3)):
                    op("dve", lambda e, st_o=st_o, u_i=u_i: e.tensor_scalar(
                        out=CAR.ap[:, st_o:st_o + 4], in0=CAR.ap[:, st_o:st_o + 4], scalar1=nflg(k, u_i), scalar2=None,
                        op0=ALU.mult), reads=[CAR, LP], writes=[CAR])
                    op("dve", lambda e, st_o=st_o, u_i=u_i: e.scalar_tensor_tensor(
                        out=CAR.ap[:, st_o:st_o + 4], in0=HSC3[:, :, n - 1], scalar=flg(k, u_i), in1=CAR.ap[:, st_o:st_o + 4],
                        op0=ALU.mult, op1=ALU.add), reads=[CAR, HSC, PP], writes=[CAR])
                for (h_o, u_i) in ((8, 2), (20, 3)):
                    hv = CAR.ap[:, h_o:h_o + 12].rearrange("p (c t) -> p c t", c=4)
                    op("dve", lambda e, hv=hv, u_i=u_i: e.tensor_scalar(
                        out=hv, in0=hv, scalar1=nflg(k, u_i), scalar2=None, op0=ALU.mult), reads=[CAR, LP], writes=[CAR])
                    op("dve", lambda e, hv=hv, u_i=u_i: e.scalar_tensor_tensor(
                        out=hv, in0=XR3[:, :, 512:515], scalar=flg(k, u_i), in1=hv, op0=ALU.mult, op1=ALU.add),
                       reads=[CAR, XR, PP], writes=[CAR])
            chk('A3')
            if mid is not None:
                mid()
            if do_kv:
                key0 = k * CH + j * 512
                pb = PS()

                def mmkv(e, pb=pb):
                    ins = None
                    for kc in range(8):
                        ins = e.matmul(pb.ap[:, 0:n], lhsT=WIN3[:, kc, 1280:1408], rhs=HT3[:, kc, 0:n],
                                       start=(kc == 0), stop=(kc == 7))
                    return ins
                op("pe", mmkv, reads=[WIN, HT], writes=[pb])
                cf = SCRA()
                sq = SCRA()
                sq_ap = sq.ap.bitcast(BF16)
                op("act", lambda e, cf=cf, pb=pb: e.activation(out=cf.ap[:, 0:n], in_=pb.ap[:, 0:n], func=AF.Copy),
                   reads=[pb], writes=[cf])
                op("act", lambda e, sq_ap=sq_ap, pb=pb: e.activation(out=sq_ap[:, 0:n], in_=pb.ap[:, 0:n], func=AF.Square),
                   reads=[pb], writes=[sq])
                rb = fm_rstd([(sq, sq_ap[:, 0:n], 128, 0)], 128, n, 128.0, SCRA)
                op("dve", lambda e, cf=cf, rb=rb: e.scalar_tensor_tensor(
                    out=CKVN.ap[:, key0:key0 + n], in0=cf.ap[:, 0:n], scalar=ppc("gkv"), in1=rb.ap[:, 0:n],
                    op0=ALU.mult, op1=ALU.mult), reads=[cf, rb, PP], writes=[CKVN])
                chk('A3a')
                pt = PS()
                prt = PS()

                def mmt(e, pt=pt, o=0):
                    ins = None
                    for kc in range(8):
                        ins = e.matmul(pt.ap[0:96, 0:n], lhsT=WKR3[:, kc, o:o + 96], rhs=HT3[:, kc, 0:n],
                                       start=(kc == 0), stop=(kc == 7))
                    return ins
                op("pe", lambda e: mmt(e, pt, 0), reads=[WKR, HT], writes=[pt])
                op("pe", lambda e: mmt(e, prt, 96), reads=[WKR, HT], writes=[prt])
                chk('A3b')
                (sb, s_ap), (cb_, c_ap) = rope_tables(posk[k, :, j * 512:j * 512 + n], n, SCRA)
                chk('A3c')
                tq = SCRA()
                tq_ap = tq.ap.bitcast(BF16)
                op("act", lambda e, pt=pt, tq_ap=tq_ap: e.activation(out=tq_ap[64:96, 0:n], in_=pt.ap[64:96, 0:n], func=AF.Square),
                   reads=[pt], writes=[tq])
                op("dve", lambda e, tq_ap=tq_ap: e.tensor_copy(out=ETS.ap[0:32, key0:key0 + n], in_=tq_ap[64:96, 0:n]),
                   reads=[tq], writes=[ETS])
                e1 = SCRA()
                e2 = SCRA()
                op("dve", lambda e, e1=e1, pt=pt, c_ap=c_ap: e.scalar_tensor_tensor(
                    out=e1.ap[64:96, 0:n], in0=pt.ap[64:96, 0:n], scalar=ppc("gk", 0, slice(64, 96)), in1=c_ap,
                    op0=ALU.mult, op1=ALU.mult), reads=[pt, cb_, PP, tq], writes=[e1])
                op("dve", lambda e, e2=e2, prt=prt, s_ap=s_ap: e.scalar_tensor_tensor(
                    out=e2.ap[64:96, 0:n], in0=prt.ap[64:96, 0:n], scalar=ppc("gkr", 0, slice(64, 96)), in1=s_ap,
                    op0=ALU.mult, op1=ALU.mult), reads=[prt, sb, PP], writes=[e2])
                op("dve", lambda e, e1=e1, e2=e2: e.tensor_tensor(out=ETS.ap[64:96, key0:key0 + n], in0=e1.ap[64:96, 0:n],
                                                                 in1=e2.ap[64:96, 0:n], op=ALU.add),
                   reads=[e1, e2], writes=[ETS])
            chk('A4')
            if do_own:
                for ct in range(4):
                    pb = PS()

                    def mmy(e, pb=pb, ct=ct):
                        ins = None
                        for kc in range(8):
                            ins = e.matmul(pb.ap[:, 0:n], lhsT=WIN3[:, kc, 512 + ct * 128:512 + (ct + 1) * 128],
                                           rhs=HT3[:, kc, 0:n], start=(kc == 0), stop=(kc == 7))
                        return ins
                    op("pe", mmy, reads=[WIN, HT], writes=[pb])
                    u = SCRA()
                    w = SCRA()
                    op("act", lambda e, u=u, pb=pb: e.activation(out=u.ap[:, 0:n], in_=pb.ap[:, 0:n], func=AF.Copy, scale=0.5),
                       reads=[pb], writes=[u])
                    op("act", lambda e, w=w, pb=pb: e.activation(out=w.ap[:, 0:n], in_=pb.ap[:, 0:n], func=AF.Square),
                       reads=[pb], writes=[w])
                    op("dve", lambda e, w=w: e.tensor_scalar(out=w.ap[:, 0:n], in0=w.ap[:, 0:n], scalar1=2.0 * 0.044715, scalar2=2.0,
                                                             op0=ALU.mult, op1=ALU.add), reads=[w], writes=[w])
                    op("dve", lambda e, w=w, u=u: e.tensor_tensor(out=w.ap[:, 0:n], in0=w.ap[:, 0:n], in1=u.ap[:, 0:n], op=ALU.mult),
                       reads=[w, u], writes=[w])
                    op("act", lambda e, w=w: e.activation(out=w.ap[:, 0:n], in_=w.ap[:, 0:n], func=AF.Tanh,
                                                          scale=0.7978845608028654), reads=[w], writes=[w])
                    op("dve", lambda e, w=w, u=u, ct=ct: e.scalar_tensor_tensor(out=GG3[:, ct, own0:own0 + n], in0=w.ap[:, 0:n],
                                                                             scalar=1.0, in1=u.ap[:, 0:n], op0=ALU.add, op1=ALU.mult),
                       reads=[w, u], writes=[GG])
                cfs = []
                sqs = []
                for c2 in range(2):
                    pb = PS()

                    def mmq(e, pb=pb, c2=c2):
                        ins = None
                        for kc in range(8):
                            ins = e.matmul(pb.ap[:, 0:n], lhsT=WIN3[:, kc, 1024 + c2 * 128:1024 + (c2 + 1) * 128],
                                           rhs=HT3[:, kc, 0:n], start=(kc == 0), stop=(kc == 7))
                        return ins
                    op("pe", mmq, reads=[WIN, HT], writes=[pb])
                    cf = SCRA()
                    sq = SCRA()
                    sq_ap = sq.ap.bitcast(BF16)
                    op("act", lambda e, cf=cf, pb=pb: e.activation(out=cf.ap[:, 0:n], in_=pb.ap[:, 0:n], func=AF.Copy),
                       reads=[pb], writes=[cf])
                    op("act", lambda e, sq_ap=sq_ap, pb=pb: e.activation(out=sq_ap[:, 0:n], in_=pb.ap[:, 0:n], func=AF.Square),
                       reads=[pb], writes=[sq])
                    cfs.append(cf)
                    sqs.append((sq, sq_ap[:, 0:n], 128, 0))
                rb = fm_rstd(sqs, 128, n, 256.0, SCRA)
                for c2 in range(2):
                    op("dve", lambda e, c2=c2, cf=cfs[c2], rb=rb: e.scalar_tensor_tensor(
                        out=CQN3[:, c2, own0:own0 + n], in0=cf.ap[:, 0:n], scalar=ppc("gqa", c2), in1=rb.ap[:, 0:n],
                        op0=ALU.mult, op1=ALU.mult), reads=[cf, rb, PP], writes=[CQN])
            if k == 4 and not mini:
                lo = CH - 512 * (j + 1)
                lru_combine(lambda ct: HF3[:, ct, lo:lo + n], HF, lambda ct: HSC3[:, ct, 0:n][:, ::-1], HSC, lo, n)

        def lru_combine(hf_ap, hf_buf, hb_ap, hb_buf, own0, n):
            los = []
            sqs = []
            for ct in range(4):
                lo_ = SCRA()
                op("dve", lambda e, lo_=lo_, ct=ct: e.tensor_tensor(out=lo_.ap[:, 0:n], in0=hf_ap(ct), in1=hb_ap(ct), op=ALU.add),
                   reads=[hf_buf, hb_buf], writes=[lo_])
                op("dve", lambda e, lo_=lo_, ct=ct: e.tensor_tensor(out=lo_.ap[:, 0:n], in0=lo_.ap[:, 0:n],
                                                                  in1=GG3[:, ct, own0:own0 + n], op=ALU.mult),
                   reads=[lo_, GG], writes=[lo_])
                sq = SCRA()
                sq_ap = sq.ap.bitcast(BF16)
                op("act", lambda e, sq_ap=sq_ap, lo_=lo_: e.activation(out=sq_ap[:, 0:n], in_=lo_.ap[:, 0:n], func=AF.Square),
                   reads=[lo_], writes=[sq])
                los.append(lo_)
                sqs.append((sq, sq_ap[:, 0:n], 128, 0))
            rb = fm_rstd(sqs, 128, n, 512.0, SCRA)
            for ct in range(4):
                op("dve", lambda e, ct=ct, lo_=los[ct], rb=rb: e.scalar_tensor_tensor(
                    out=MIXL3[:, ct, own0:own0 + n], in0=lo_.ap[:, 0:n], scalar=ppc("glru", ct), in1=rb.ap[:, 0:n],
                    op0=ALU.mult, op1=ALU.mult), reads=[lo_, rb, PP], writes=[MIXL])

        blks = []
        for k in range(5):
            for j in range(NB):
                blks.append((k, j, 512, False))
            if k >= 3:
                blks.append((k, NB, 1, True))
        phaseA_front(*blks[0], 0)
        phaseA_xr(*blks[0], 0)
        for bi, (k, j, n, mini) in enumerate(blks):
            mid = None
            if bi + 1 < len(blks):
                phaseA_front(*blks[bi + 1], (bi + 1) % 2)
                mid = (lambda nb=blks[bi + 1], hb_=(bi + 1) % 2: phaseA_xr(*nb, hb_))
            phaseA_block(k, j, n, mini, bi % 2, mid)
            if mini and k == 3:
                op("dve", lambda e: e.tensor_copy(out=HE.ap[:, 0:4], in_=HSC3[:, :, 0]), reads=[HSC], writes=[HE])
            if mini and k == 4:
                op("dve", lambda e: e.tensor_copy(out=HE.ap[:, 4:8], in_=HSC3[:, :, 0]), reads=[HSC], writes=[HE])
        chk('A6')
        HE3 = HE.ap[:, 8:16].rearrange("p (c t) -> p c t", c=4)
        op("dve", lambda e: e.tensor_tensor(out=HE3[:, :, 0], in0=HE.ap[:, 0:4], in1=CAR.ap[:, 52:56], op=ALU.add),
           reads=[HE, CAR], writes=[HE])
        op("dve", lambda e: e.tensor_tensor(out=HE3[:, :, 1], in0=HE.ap[:, 4:8], in1=CAR.ap[:, 48:52], op=ALU.add),
           reads=[HE, CAR], writes=[HE])
        ZERO = SCRA()
        op("pool", lambda e: e.memset(ZERO.ap[:, 0:8], 0.0), writes=[ZERO])
        lru_combine(lambda ct: HE3[:, ct, :], HE, lambda ct: ZERO.ap[:, 0:2], ZERO, CH, 2)
        if "mixl" in dbg_d:
            TMPD = A.f32(4 * NOWN, "tmpd")
            op("dve", lambda e: e.tensor_copy(out=TMPD.ap, in_=MIXL.ap), reads=[MIXL], writes=[TMPD])
            dump("mixl", TMPD, TMPD.ap)
        if "ckvn" in dbg_d:
            TMPD2 = A.f32(NKEY, "tmpd2")
            op("dve", lambda e: e.tensor_copy(out=TMPD2.ap, in_=CKVN.ap), reads=[CKVN], writes=[TMPD2])
            dump("ckvn", TMPD2, TMPD2.ap)
        if "ets" in dbg_d:
            TMPD3 = A.f32(NKEY, "tmpd3")
            op("dve", lambda e: e.tensor_copy(out=TMPD3.ap[0:96], in_=ETS.ap[0:96]), reads=[ETS], writes=[TMPD3])
            dump("ets", TMPD3, TMPD3.ap[0:96])

        S.barrier_all()
        A.release(mA)
        if stop == 'A':
            S.emit(nc)
            return nc

        WUQ = A.bf(2 * 768, "wuq")
        WUQ3 = WUQ.ap.rearrange("p (k c) -> p k c", k=2)
        for c2 in range(2):
            load(WUQ, WUQ3[:, c2, :], w_uq[c2 * 128:(c2 + 1) * 128, :])
        WUQR = A.bf(2 * 8 * 96, "wuqr")
        WUQR4 = WUQR.ap.rearrange("p (k h c) -> p k h c", k=2, h=8)
        WUQ4 = WUQ.ap.rearrange("p (k h c) -> p k h c", k=2, h=8)
        op("pool", lambda e: e.memset(WUQR.ap, 0.0), writes=[WUQR])
        for c2 in range(2):
            op("dve", lambda e, c2=c2: e.tensor_scalar(out=WUQR4[:, c2, :, 64:80], in0=WUQ4[:, c2, :, 80:96], scalar1=-1.0,
                                                       scalar2=None, op0=ALU.mult), reads=[WUQ], writes=[WUQR])
            op("dve", lambda e, c2=c2: e.tensor_copy(out=WUQR4[:, c2, :, 80:96], in_=WUQ4[:, c2, :, 64:80]),
               reads=[WUQ], writes=[WUQR])
        WUKV = A.bf(1024, "wukv")
        load(WUKV, WUKV.ap, w_ukv)
        ON = A.bf(8 * NOWN, "on")
        ON3 = ON.ap.rearrange("p (h t) -> p h t", h=8)
        TAB = A.bf(2 * NOWN, "qtab")
        mB = A.mark()
        KT = [A.bf(NKEY, "kt%d" % i) for i in range(2)]
        VV = [A.bf(NKT * 65, "v%d" % i) for i in range(2)]
        VV3 = [v.ap.rearrange("p (t c) -> p t c", c=65) for v in VV]
        QT = [A.bf(NOWN, "qt%d" % i) for i in range(2)]
        PT = [Buf(NT_H[i].ap[:, 0:512], NT_H[i].r) for i in range(2)] + [A.bf(512, "pt%d" % i) for i in range(2)]
        scrB = [Buf(NT_X[i].ap[:, 0:512], NT_X[i].r) for i in range(2)] + [A.f32(512, "scrB%d" % i) for i in range(6)]
        sbi = [0]

        def SCRB():
            b = scrB[sbi[0] % len(scrB)]
            sbi[0] += 1
            return b
        for v3, v in zip(VV3, VV):
            op("pool", lambda e, v3=v3: e.memset(v3[:, :, 64:65], 1.0), writes=[v])
        nqb = -(-NOWN // 512)
        half = NOWN // 2
        qb_base = half // nqb
        qb_sizes = [2 * (qb_base + (1 if i < half - qb_base * nqb else 0)) for i in range(nqb)]
        qblocks = [(sum(qb_sizes[:i]), qb_sizes[i]) for i in range(nqb)]
        for (q0, n) in qblocks:
            rope_tables(poso[:, q0:q0 + n], n, SCRB, out_buf=TAB, out_s=TAB.ap[64:96, q0:q0 + n],
                        out_c=TAB.ap[64:96, NOWN + q0:NOWN + q0 + n])
        pti = [0]

        RK = [A.f32(NKT, "rk%d" % i) for i in range(2)]
        SSK = psum[5]

        for kt_ in KT:
            op("dve", lambda e, kt_=kt_: e.tensor_copy(out=kt_.ap[64:96, :], in_=ETS.ap[64:96, :]), reads=[ETS], writes=[kt_])

        TSS = A.f32(NKT, "tss")

        def mm_tss(e):
            ins = None
            for t in range(NKT):
                ins = e.matmul(SSK.ap[:, t:t + 1], lhsT=ETS.ap[0:32, t * 128:(t + 1) * 128], rhs=ONESB.ap[0:32, 0:1],
                               start=True, stop=True)
            return ins
        op("pe", mm_tss, reads=[ETS, ONESB], writes=[SSK])
        op("dve", lambda e: e.tensor_copy(out=TSS.ap[:, 0:NKT], in_=SSK.ap[:, 0:NKT]), reads=[SSK], writes=[TSS])

        def kgen(h):
            kt = KT[h % 2]
            rk = RK[h % 2]
            pend_mms = []
            for kb in range(NKB):
                c0 = kb * 512
                pk = PS()
                op("pe", lambda e, pk=pk, c0=c0: e.matmul(pk.ap[0:64, :], lhsT=WUKV.ap[:, h * 128:h * 128 + 64],
                                                      rhs=CKVN.ap[:, c0:c0 + 512], start=True, stop=True),
                   reads=[WUKV, CKVN], writes=[pk])
                sq = SCRB()
                sq_ap = sq.ap.bitcast(BF16)
                op("act", lambda e, sq_ap=sq_ap, pk=pk: e.activation(out=sq_ap[0:64, 0:512], in_=pk.ap[0:64, :], func=AF.Square),
                   reads=[pk], writes=[sq])
                op("dve", lambda e, pk=pk, c0=c0: e.tensor_scalar(out=kt.ap[0:64, c0:c0 + 512], in0=pk.ap[0:64, :],
                                                                 scalar1=ppc("gk", 0, slice(0, 64)), scalar2=None, op0=ALU.mult),
                   reads=[pk, PP, sq], writes=[kt])

                def mms(e, sq_ap=sq_ap, kb=kb, c0=c0):
                    ins = None
                    for i in range(4):
                        t = kb * 4 + i
                        ins = e.matmul(SSK.ap[:, t:t + 1], lhsT=sq_ap[0:64, i * 128:(i + 1) * 128], rhs=ONESB.ap[0:64, 0:1],
                                       start=True, stop=True)
                    return ins
                if pend_mms:
                    pm, psq = pend_mms.pop(0)
                    op("pe", pm, reads=[psq, ETS, ONESB], writes=[SSK])
                pend_mms.append((mms, sq))
            while pend_mms:
                pm, psq = pend_mms.pop(0)
                op("pe", pm, reads=[psq, ETS, ONESB], writes=[SSK])
            op("dve", lambda e: e.scalar_tensor_tensor(out=rk.ap[:, 0:NKT], in0=SSK.ap[:, 0:NKT], scalar=96.0 * EPS,
                                                       in1=TSS.ap[:, 0:NKT], op0=ALU.add, op1=ALU.add),
               reads=[SSK, TSS], writes=[rk])
            nh = SCRB()
            op("pool", lambda e, nh=nh: e.memset(nh.ap[:, 0:NKT], -0.5), writes=[nh])
            op("pool", lambda e, nh=nh: e.tensor_tensor(out=rk.ap[:, 0:NKT], in0=rk.ap[:, 0:NKT], in1=nh.ap[:, 0:NKT], op=ALU.pow),
               reads=[rk, nh], writes=[rk])

        def vgen(h):
            vv = VV[h % len(VV)]
            vv3 = VV3[h % len(VV)]
            for kb in range(NKB):
                c0 = kb * 512
                pv = PS()

                def mmv(e, pv=pv, c0=c0):
                    ins = None
                    for i in range(4):
                        ins = e.matmul(pv.ap[:, i * 64:(i + 1) * 64], lhsT=CKVN.ap[:, c0 + i * 128:c0 + (i + 1) * 128],
                                       rhs=WUKV.ap[:, h * 128 + 64:h * 128 + 128], start=True, stop=True)
                    return ins
                op("pe", mmv, reads=[CKVN, WUKV], writes=[pv])
                op("dve", lambda e, pv=pv, kb=kb: e.tensor_copy(out=vv3[:, kb * 4:(kb + 1) * 4, 0:64],
                                                             in_=pv.ap[:, 0:256].rearrange("p (t c) -> p t c", c=64)),
                   reads=[pv], writes=[vv])

        def do_head(h):
            kt = KT[h % 2]
            rk = RK[h % 2]
            vv = VV[h % len(VV)]
            vv3 = VV3[h % len(VV)]
            qt = QT[h % 2]
            qst = {}

            def qgenA(q0, n):
                pq = PS()
                pr = PS()

                def mmq(e):
                    ins = None
                    for c2 in range(2):
                        ins = e.matmul(pq.ap[0:96, 0:n], lhsT=WUQ3[:, c2, h * 96:(h + 1) * 96], rhs=CQN3[:, c2, q0:q0 + n],
                                       start=(c2 == 0), stop=(c2 == 1))
                    return ins

                def mmr(e):
                    ins = None
                    for c2 in range(2):
                        ins = e.matmul(pr.ap[0:96, 0:n], lhsT=WUQR4[:, c2, h, :], rhs=CQN3[:, c2, q0:q0 + n],
                                       start=(c2 == 0), stop=(c2 == 1))
                    return ins
                op("pe", mmq, reads=[WUQ, CQN], writes=[pq])
                op("pe", mmr, reads=[WUQR, CQN], writes=[pr])
                sq = SCRB()
                sq_ap = sq.ap.bitcast(BF16)
                op("act", lambda e: e.activation(out=sq_ap[0:96, 0:n], in_=pq.ap[0:96, 0:n], func=AF.Square),
                   reads=[pq], writes=[sq])
                e2 = SCRB()
                op("dve", lambda e: e.scalar_tensor_tensor(
                    out=e2.ap[64:96, 0:n], in0=pr.ap[64:96, 0:n], scalar=ppc("gqr", 0, slice(64, 96)),
                    in1=TAB.ap[64:96, q0:q0 + n], op0=ALU.mult, op1=ALU.mult), reads=[pr, TAB, PP], writes=[e2])
                qst[q0] = (pq, pr, sq, sq_ap, e2)

            def qgenB(q0, n):
                pq, pr, sq, sq_ap, e2 = qst[q0]
                rb = fm_rstd([(sq, sq_ap[0:96, 0:n], 96, 0)], 96, n, 96.0, SCRB, pbuf=pr)
                op("dve", lambda e: e.scalar_tensor_tensor(
                    out=qt.ap[0:64, q0:q0 + n], in0=pq.ap[0:64, 0:n], scalar=ppc("gq", 0, slice(0, 64)), in1=rb.ap[0:64, 0:n],
                    op0=ALU.mult, op1=ALU.mult), reads=[pq, rb, PP], writes=[qt])
                e1 = SCRB()
                op("dve", lambda e: e.scalar_tensor_tensor(
                    out=e1.ap[64:96, 0:n], in0=pq.ap[64:96, 0:n], scalar=ppc("gq", 0, slice(64, 96)),
                    in1=TAB.ap[64:96, NOWN + q0:NOWN + q0 + n], op0=ALU.mult, op1=ALU.mult), reads=[pq, TAB, PP, sq], writes=[e1])
                op("dve", lambda e: e.tensor_tensor(out=e1.ap[64:96, 0:n], in0=e1.ap[64:96, 0:n],
                                                    in1=e2.ap[64:96, 0:n], op=ALU.add),
                   reads=[e1, e2], writes=[e1])
                op("dve", lambda e: e.tensor_tensor(out=qt.ap[64:96, q0:q0 + n], in0=e1.ap[64:96, 0:n],
                                                    in1=rb.ap[64:96, 0:n], op=ALU.mult),
                   reads=[e1, rb], writes=[qt])
            qgenA(*qblocks[0])
            for qi in range(len(qblocks)):
                if qi + 1 < len(qblocks):
                    qgenA(*qblocks[qi + 1])
                qgenB(*qblocks[qi])
            scale = 96.0 ** -0.5
            fin_pending = []
            for (q0, n) in qblocks:
                po = PSACC()
                pend = []

                def qk(t, q0=q0, n=n):
                    pb = PS()
                    op("pe", lambda e, pb=pb, t=t: e.matmul(pb.ap[:, 0:n], lhsT=kt.ap[0:96, t * 128:(t + 1) * 128],
                                                        rhs=qt.ap[0:96, q0:q0 + n], start=True, stop=True),
                       reads=[kt, qt], writes=[pb])
                    return pb

                def expv(t, pb, q0=q0, n=n, po=po):
                    pt_ = PT[pti[0] % 4]
                    pti[0] += 1
                    op("act", lambda e, pb=pb, pt_=pt_, t=t: e.activation(out=pt_.ap[:, 0:n], in_=pb.ap[:, 0:n], func=AF.Exp,
                                                                     scale=rk.ap[:, t:t + 1]), reads=[pb, rk], writes=[pt_])
                    op("pe", lambda e, pt_=pt_, t=t: e.matmul(po.ap[0:65, 0:n], lhsT=vv3[:, t, :], rhs=pt_.ap[:, 0:n],
                                                          start=(t == 0), stop=(t == NKT - 1)),
                       reads=[vv, pt_], writes=[po])
                LOOK = 2
                DEFER = 8
                for t in range(NKT + LOOK):
                    if t < NKT:
                        pend.append((t, qk(t)))
                    if t >= LOOK:
                        tt, pb = pend.pop(0)
                        expv(tt, pb)
                    if t == DEFER and fin_pending:
                        fin_pending.pop(0)()
                rd = SCRB()
                op("dve", lambda e, rd=rd, po=po, n=n: e.reciprocal(out=rd.ap[64:65, 0:n], in_=po.ap[64:65, 0:n]),
                   reads=[po], writes=[rd])

                def fin(rd=rd, po=po, q0=q0, n=n):
                    pbc = PS()
                    op("pe", lambda e, pbc=pbc: e.matmul(pbc.ap[0:64, 0:n], lhsT=ONESF.ap[64:65, 0:64], rhs=rd.ap[64:65, 0:n],
                                                         start=True, stop=True), reads=[ONESF, rd], writes=[pbc])
                    oc = SCRB()
                    op("act", lambda e, oc=oc: e.activation(out=oc.ap[0:64, 0:n], in_=po.ap[0:64, 0:n], func=AF.Copy),
                       reads=[po, rd], writes=[oc])
                    op("dve", lambda e, oc=oc, pbc=pbc: e.tensor_tensor(out=ON3[0:64, h, q0:q0 + n], in0=oc.ap[0:64, 0:n],
                                                                         in1=pbc.ap[0:64, 0:n], op=ALU.mult),
                       reads=[oc, pbc], writes=[ON])
                fin_pending.append(fin)
            while fin_pending:
                fin_pending.pop(0)()
        chk('B0')
        kgen(0)
        chk('B1')
        vgen(0)
        chk('B2')
        for h_ in range(8):
            if h_ < 7:
                kgen(h_ + 1)
                if len(VV) > 1:
                    vgen(h_ + 1)
            do_head(h_)
            if h_ < 7 and len(VV) == 1:
                vgen(h_ + 1)
        for (q0, n) in qblocks:
            sqs = []
            for h in range(8):
                sq = SCRB()
                sq_ap = sq.ap.bitcast(BF16)
                op("act", lambda e, sq_ap=sq_ap, h=h, q0=q0, n=n: e.activation(out=sq_ap[0:64, 0:n], in_=ON3[0:64, h, q0:q0 + n],
                                                                            func=AF.Square), reads=[ON], writes=[sq])
                sqs.append((sq, sq_ap[0:64, 0:n], 64, 0))
            rb = fm_rstd(sqs, 64, n, 512.0, SCRB)
            for h in range(8):
                op("dve", lambda e, h=h, rb=rb, q0=q0, n=n: e.scalar_tensor_tensor(
                    out=ON3[0:64, h, q0:q0 + n], in0=ON3[0:64, h, q0:q0 + n], scalar=ppc("gmla", h, slice(0, 64)),
                    in1=rb.ap[0:64, 0:n], op0=ALU.mult, op1=ALU.mult), reads=[ON, rb, PP], writes=[ON])
        if "on" in dbg_d:
            TMPD4 = A.f32(8 * NOWN, "tmpd4")
            op("dve", lambda e: e.tensor_copy(out=TMPD4.ap[0:64], in_=ON.ap[0:64]), reads=[ON], writes=[TMPD4])
            dump("on", TMPD4, TMPD4.ap[0:64])
        S.barrier_all()
        A.release(mB)
        if stop == 'B':
            S.emit(nc)
            return nc
        NTL = CH // 128
        tiles = [(i * 128, 128) for i in range(NTL)] + [(CH, 2)]
        xres_start = A.top()
        XRES = A.f32((NTL + 1) * D, "xres")
        xres_end = A.top()
        XRES3 = XRES.ap.rearrange("p (t d) -> p t d", d=D)
        XR_res = [Res("xres%d" % i) for i in range(NTL + 1)]
        XT_ = [Buf(XRES3[:, i, :], XR_res[i]) for i in range(NTL + 1)]
        mC = A.mark()
        A.regs = [[ckvn_start, ets_end], [xres_end, ASZ]]
        WOL = A.bf(4 * D, "wol")
        WOL3 = WOL.ap.rearrange("p (c d) -> p c d", c=4)
        WOM = A.bf(8 * D, "wom")
        WOM3 = WOM.ap.rearrange("p (h d) -> p h d", h=8)
        load(WOL, WOL3, w_out[0:512, :].rearrange("(c p) d -> p c d", p=128))
        load(WOM, WOM3[0:64], w_out[512:1024, :].rearrange("(h p) d -> p h d", p=64))
        for ti, (o0, rows) in enumerate(tiles):
            xt = XT_[ti]
            src = xs[3, o0:o0 + rows, :] if ti < NTL else xh[0:2, :]
            load(xt, xt.ap[0:rows], src, eng="sp")
            for half in range(2):
                pb = PS()

                def mmo(e, pb=pb, o0=o0, rows=rows, half=half):
                    ins = None
                    for ct in range(4):
                        ins = e.matmul(pb.ap[0:rows, :], lhsT=MIXL3[:, ct, o0:o0 + rows],
                                       rhs=WOL3[:, ct, half * 512:(half + 1) * 512], start=(ct == 0), stop=False)
                    for h in range(8):
                        ins = e.matmul(pb.ap[0:rows, :], lhsT=ON3[0:64, h, o0:o0 + rows],
                                       rhs=WOM3[0:64, h, half * 512:(half + 1) * 512], start=False, stop=(h == 7))
                    return ins
                op("pe", mmo, reads=[MIXL, ON, WOL, WOM], writes=[pb])
                op("dve", lambda e, pb=pb, xt=xt, rows=rows, half=half: e.tensor_tensor(
                    out=xt.ap[0:rows, half * 512:(half + 1) * 512], in0=xt.ap[0:rows, half * 512:(half + 1) * 512],
                    in1=pb.ap[0:rows, :], op=ALU.add), reads=[xt, pb], writes=[xt])
        if "x1" in dbg_d:
            for ti in range(NTL):
                store(dbg_d["x1"][ti * 128:(ti + 1) * 128, :], XT_[ti], XT_[ti].ap)
        S.barrier_all()
        A.regs = [[mP, xres_start], [xres_end, ASZ]]
        if stop == 'C':
            S.emit(nc)
            return nc

        GB2 = A.f32(D, "gbc2")
        GB3 = A.f32(D, "gbc3")
        load(GB2, GB2.ap, gvec[1, :].partition_broadcast(128))
        load(GB3, GB3.ap, gvec[3, :].partition_broadcast(128))
        HNT = A.bf(8 * NOWN, "hnt")
        HNT3 = HNT.ap.rearrange("p (k t) -> p k t", k=8)
        mD = A.mark()
        WMQ = A.bf(8 * 512, "wmq")
        WMQ3 = WMQ.ap.rearrange("p (k c) -> p k c", k=8)
        load(WMQ, WMQ3, w_mq.rearrange("(k p) c -> p k c", p=128))
        WMO = A.bf(4 * D, "wmo")
        WMO3 = WMO.ap.rearrange("p (h d) -> p h d", h=4)
        load(WMO, WMO3, w_mo.rearrange("(h p) d -> p h d", p=128))
        H1T = A.bf(8 * 512, "h1t")
        H1T3 = H1T.ap.rearrange("p (k t) -> p k t", k=8)
        OMT = A.bf(4 * 512, "omt")
        OMT3 = OMT.ap.rearrange("p (h t) -> p h t", h=4)
        HNH = A.bf(8 * 2, "hnh")
        HNH3 = HNH.ap.rearrange("p (k t) -> p k t", k=8)
        scrD = [A.f32(512, "scrD%d" % i) for i in range(10)]
        sdi = [0]

        def SCRD():
            b = scrD[sdi[0] % len(scrD)]
            sdi[0] += 1
            return b
        mscale = 128.0 ** -0.5
        qbl = [(q0, min(512, CH - q0)) for q0 in range(0, CH, 512)] + [(CH, 2)]
        for (q0, n) in qbl:
            tl = [ti for ti, (o0, rows) in enumerate(tiles) if q0 <= o0 < q0 + n]
            for ti in tl:
                o0, rows = tiles[ti]
                norm_T(XT_[ti], XT_[ti].ap[0:rows], rows, H1T, H1T3, o0 - q0, gb=GB2)
            dst = {}

            def d_headA(h, n=n):
                pq = PS()

                def mmq2(e):
                    ins = None
                    for kc in range(8):
                        ins = e.matmul(pq.ap[:, 0:n], lhsT=WMQ3[:, kc, h * 128:(h + 1) * 128], rhs=H1T3[:, kc, 0:n],
                                       start=(kc == 0), stop=(kc == 7))
                    return ins
                op("pe", mmq2, reads=[WMQ, H1T], writes=[pq])
                sq = SCRD()
                sq_ap = sq.ap.bitcast(BF16)
                op("act", lambda e: e.activation(out=sq_ap[:, 0:n], in_=pq.ap[:, 0:n], func=AF.Square),
                   reads=[pq], writes=[sq])
                dst[h] = (pq, sq, sq_ap)

            def d_headB(h, n=n):
                pq, sq, sq_ap = dst[h]
                rb = fm_rstd([(sq, sq_ap[:, 0:n], 128, 0)], 128, n, 128.0, SCRD)
                qm = SCRD()
                qm_ap = qm.ap.bitcast(BF16)
                op("dve", lambda e: e.scalar_tensor_tensor(
                    out=qm_ap[:, 0:n], in0=pq.ap[:, 0:n], scalar=ppc("gmq"), in1=rb.ap[:, 0:n], op0=ALU.mult, op1=ALU.mult),
                   reads=[pq, rb, PP], writes=[qm])
                po = psum[6]
                pd = psum[7]
                for mt in range(2):
                    ps_ = PS()
                    op("pe", lambda e, ps_=ps_, mt=mt: e.matmul(
                        ps_.ap[:, 0:n], lhsT=KM3[:, h, mt * 128:(mt + 1) * 128], rhs=qm_ap[:, 0:n], start=True, stop=True),
                       reads=[KM, qm], writes=[ps_])
                    pt_ = SCRD()
                    pt_ap = pt_.ap.bitcast(BF16)
                    op("act", lambda e, ps_=ps_, pt_ap=pt_ap: e.activation(out=pt_ap[:, 0:n], in_=ps_.ap[:, 0:n], func=AF.Exp,
                                                                       scale=mscale), reads=[ps_], writes=[pt_])
                    op("pe", lambda e, mt=mt, pt_ap=pt_ap: e.matmul(
                        po.ap[:, 0:n], lhsT=VM3[:, mt, h * 128:(h + 1) * 128], rhs=pt_ap[:, 0:n], start=(mt == 0), stop=(mt == 1)),
                       reads=[VM, pt_], writes=[po])
                    op("pe", lambda e, mt=mt, pt_ap=pt_ap: e.matmul(
                        pd.ap[:, 0:n], lhsT=ONESB.ap[:, 0:128], rhs=pt_ap[:, 0:n], start=(mt == 0), stop=(mt == 1)),
                       reads=[ONESB, pt_], writes=[pd])
                rd = SCRD()
                op("dve", lambda e: e.reciprocal(out=rd.ap[:, 0:n], in_=pd.ap[:, 0:n]),
                   reads=[pd], writes=[rd])
                op("dve", lambda e: e.tensor_tensor(out=OMT3[:, h, 0:n], in0=po.ap[:, 0:n], in1=rd.ap[:, 0:n],
                                                    op=ALU.mult), reads=[po, rd], writes=[OMT])
            ps_pool[0] = [0, 1, 2, 3, 4, 5]
            d_headA(0)
            for h_ in range(4):
                if h_ < 3:
                    d_headA(h_ + 1)
                d_headB(h_)
            ps_pool[0] = [0, 1, 2, 3, 4]
            def d_outproj(ti, q0=q0):
                o0, rows = tiles[ti]
                xt = XT_[ti]
                c0 = o0 - q0
                for half in range(2):
                    pb = PS()

                    def mmo2(e, pb=pb, c0=c0, rows=rows, half=half):
                        ins = None
                        for h in range(4):
                            ins = e.matmul(pb.ap[0:rows, :], lhsT=OMT3[:, h, c0:c0 + rows],
                                           rhs=WMO3[:, h, half * 512:(half + 1) * 512], start=(h == 0), stop=(h == 3))
                        return ins
                    op("pe", mmo2, reads=[OMT, WMO], writes=[pb])
                    op("dve", lambda e, pb=pb, xt=xt, rows=rows, half=half: e.tensor_tensor(
                        out=xt.ap[0:rows, half * 512:(half + 1) * 512], in0=xt.ap[0:rows, half * 512:(half + 1) * 512],
                        in1=pb.ap[0:rows, :], op=ALU.add), reads=[xt, pb], writes=[xt])

            def d_ffnnorm(ti):
                o0, rows = tiles[ti]
                xt = XT_[ti]
                if ti < NTL:
                    norm_T(xt, xt.ap[0:rows], rows, HNT, HNT3, o0 + 1, gb=GB3, evac_eng="act")
                else:
                    norm_T(xt, xt.ap[0:rows], rows, HNH, HNH3, 0, gb=GB3)
                    op("dve", lambda e: e.tensor_scalar(out=HNT3[:, :, CH + 1:CH + 2], in0=HNH3[:, :, 0:1], scalar1=ppc("msk", 0),
                                                        scalar2=None, op0=ALU.mult), reads=[HNH, PP], writes=[HNT])
                    op("dve", lambda e: e.tensor_scalar(out=HNT3[:, :, 0:1], in0=HNH3[:, :, 1:2], scalar1=ppc("msk", 1),
                                                        scalar2=None, op0=ALU.mult), reads=[HNH, PP], writes=[HNT])
            d_outproj(tl[0])
            for i_ in range(len(tl)):
                if i_ + 1 < len(tl):
                    d_outproj(tl[i_ + 1])
                d_ffnnorm(tl[i_])
        if "x2" in dbg_d:
            for ti in range(NTL):
                store(dbg_d["x2"][ti * 128:(ti + 1) * 128, :], XT_[ti], XT_[ti].ap)
        S.barrier_all()
        A.release(mD)
        if stop == 'D':
            S.emit(nc)
            return nc

        GP = 2
        NR = 22 // GP
        WUPG = [A.bf(8 * 2 * GP * 128, "wupg%d" % i) for i in range(2)]
        WUPG4 = [w.ap.rearrange("p (k s c) -> p k s c", k=8, s=2) for w in WUPG]
        WDNG = [A.bf(GP * D, "wdng%d" % i) for i in range(2)]
        WDNG3 = [w.ap.rearrange("p (j d) -> p j d", j=GP) for w in WDNG]
        ACTT = [A.bf(GP * CH, "actt%d" % i) for i in range(2)]
        ACTT3 = [a.ap.rearrange("p (j t) -> p j t", j=GP) for a in ACTT]
        scrE = [A.f32(512, "scrE%d" % i) for i in range(8)]
        sei = [0]

        def SCRE():
            b = scrE[sei[0] % len(scrE)]
            sei[0] += 1
            return b
        fwo = PPL["fw"][0]
        fbo = PPL["fb"][0]
        def do_round(r):
            wu = WUPG[r % 2]
            wu4 = WUPG4[r % 2]
            wd = WDNG[r % 2]
            wd3 = WDNG3[r % 2]
            at = ACTT[r % 2]
            at3 = ACTT3[r % 2]
            j0 = r * GP
            load(wu, wu4[:, :, 0, :], w_up[:, j0 * 128:(j0 + GP) * 128].rearrange("(k p) c -> p k c", p=128))
            load(wu, wu4[:, :, 1, :], w_up[:, DFF + j0 * 128:DFF + (j0 + GP) * 128].rearrange("(k p) c -> p k c", p=128))
            load(wd, wd3, w_dn[j0 * 128:(j0 + GP) * 128, :].rearrange("(j p) d -> p j d", p=128))
            for (t0, n) in WINS:
                for jj in range(GP):
                    cv = []
                    for s in range(2):
                        chn = (j0 + jj) + s * 22
                        pg = PS()

                        def mmu(e, pg=pg, s=s, jj=jj, t0=t0, n=n):
                            ins = None
                            for kc in range(8):
                                ins = e.matmul(pg.ap[:, 0:n + 2], lhsT=wu4[:, kc, s, jj * 128:(jj + 1) * 128],
                                               rhs=HNT3[:, kc, t0:t0 + n + 2], start=(kc == 0), stop=(kc == 7))
                            return ins
                        op("pe", mmu, reads=[wu, HNT], writes=[pg])
                        c_ = SCRE()
                        wsrc, wb = PP, fwo + 3 * chn
                        bsrc, bb = PP, fbo + chn
                        op("act", lambda e, c_=c_, pg=pg, wsrc=wsrc, wb=wb, bsrc=bsrc, bb=bb, n=n: e.activation(
                            out=c_.ap[:, 0:n], in_=pg.ap[:, 1:n + 1], func=AF.Identity,
                            scale=wsrc.ap[:, wb + 1:wb + 2], bias=bsrc.ap[:, bb:bb + 1]),
                           reads=[pg, wsrc, bsrc], writes=[c_])
                        for tp in (0, 2):
                            op("dve", lambda e, c_=c_, pg=pg, wsrc=wsrc, wb=wb, tp=tp, n=n: e.scalar_tensor_tensor(
                                out=c_.ap[:, 0:n], in0=pg.ap[:, tp:tp + n], scalar=wsrc.ap[:, wb + tp:wb + tp + 1],
                                in1=c_.ap[:, 0:n], op0=ALU.mult, op1=ALU.add), reads=[pg, wsrc, c_], writes=[c_])
                        cv.append(c_)
                    sg = SCRE()
                    op("act", lambda e, sg=sg, g_=cv[0], n=n: e.activation(out=sg.ap[:, 0:n], in_=g_.ap[:, 0:n], func=AF.Tanh, scale=0.5),
                       reads=[cv[0]], writes=[sg])
                    op("act", lambda e, sg=sg, n=n: e.activation(out=sg.ap[:, 0:n], in_=sg.ap[:, 0:n], func=AF.Identity, scale=0.5,
                                                                bias=POSH.ap[:, 0:1]), reads=[sg, POSH], writes=[sg])
                    op("dve", lambda e, g_=cv[0], u_=cv[1], n=n: e.tensor_tensor(out=u_.ap[:, 0:n], in0=g_.ap[:, 0:n], in1=u_.ap[:, 0:n],
                                                                                op=ALU.mult), reads=[cv[0], cv[1]], writes=[cv[1]])
                    op("dve", lambda e, sg=sg, u_=cv[1], jj=jj, t0=t0, n=n: e.tensor_tensor(
                        out=at3[:, jj, t0:t0 + n], in0=sg.ap[:, 0:n], in1=u_.ap[:, 0:n], op=ALU.mult),
                       reads=[sg, cv[1]], writes=[at])
            if not (r % 2 == 1 or r == NR - 1):
                return
            rds = [r - 1, r] if r % 2 == 1 else [r]
            srcs = [(ACTT3[rr % 2], WDNG3[rr % 2]) for rr in rds]
            rbufs = [ACTT[rr % 2] for rr in rds] + [WDNG[rr % 2] for rr in rds]
            nmm = GP * len(rds)
            for ti in range(NTL):
                o0, rows = tiles[ti]
                xt = XT_[ti]
                for half in range(2):
                    pb = PS()

                    def mmd(e, pb=pb, o0=o0, half=half):
                        ins = None
                        i_ = 0
                        for (a3_, w3_) in srcs:
                            for jj in range(GP):
                                ins = e.matmul(pb.ap[:, :], lhsT=a3_[:, jj, o0:o0 + 128], rhs=w3_[:, jj, half * 512:(half + 1) * 512],
                                               start=(i_ == 0), stop=(i_ == nmm - 1))
                                i_ += 1
                        return ins
                    op("pe", mmd, reads=rbufs, writes=[pb])
                    op("dve", lambda e, pb=pb, xt=xt, half=half: e.tensor_tensor(
                        out=xt.ap[:, half * 512:(half + 1) * 512], in0=xt.ap[:, half * 512:(half + 1) * 512],
                        in1=pb.ap[:, :], op=ALU.add), reads=[xt, pb], writes=[xt])
        for r_ in range(NR):
            do_round(r_)
        for ti in range(NTL):
            store(out_d[ti * 128:(ti + 1) * 128, :], XT_[ti], XT_[ti].ap)
        S.emit(nc)
    return nc


def _cols(v, rows=128):
    v = np.asarray(v, np.float32)
    return np.ascontiguousarray(v.reshape(-1, rows).T)


def prep_core(inp, b, c, CH):
    f32 = np.float32
    x = np.asarray(inp["x"][b], f32)
    pos = np.asarray(inp["positions"][b]).astype(np.int32)
    s0 = c * CH
    slots = [(k, False) for k in range(c)] + [(k, True) for k in range(3, c, -1)] + [(c, False), (c, True)]
    assert len(slots) == 5
    xs = np.stack([x[k * CH:(k + 1) * CH][::-1] if rev else x[k * CH:(k + 1) * CH] for k, rev in slots])
    pk = np.stack([pos[k * CH:(k + 1) * CH][::-1] if rev else pos[k * CH:(k + 1) * CH] for k, rev in slots[:4]])
    posk = np.ascontiguousarray(np.broadcast_to(pk[:, None, :], (4, 32, CH))).astype(np.int32)
    xh = np.zeros((2, D), f32)
    ph = np.zeros((2,), np.int32)
    msk = np.zeros((2,), f32)
    if c < 3:
        xh[0] = x[s0 + CH]; ph[0] = pos[s0 + CH]; msk[0] = 1.0
    if c > 0:
        xh[1] = x[s0 - 1]; ph[1] = pos[s0 - 1]; msk[1] = 1.0
    po = np.concatenate([pos[s0:s0 + CH], ph])
    poso = np.ascontiguousarray(np.broadcast_to(po[None, :], (32, CH + 2))).astype(np.int32)
    flg = np.zeros((5, 4), f32)
    for k in range(3):
        if k < c:
            flg[k] = [1, 0, 1, 0]
        else:
            flg[k] = [0, 1, 0, 1]
    flg[3] = [1, 0, 0, 0]
    flg[4] = [0, 1, 0, 0]
    pp = np.zeros((128, PPL["_n"]), f32)

    def put(name, arr):
        o, n = PPL[name]
        arr = np.asarray(arr, f32)
        if arr.ndim == 1:
            arr = np.broadcast_to(arr[None, :], (128, arr.shape[0]))
        assert arr.shape[1] == n, (name, arr.shape, n)
        pp[:arr.shape[0], o:o + n] = arr
    put("flg", flg.reshape(-1))
    put("msk", msk)
    cw = np.zeros((128, 5, 4, 4), f32)
    cb = np.zeros((128, 5, 4), f32); ba = np.zeros((128, 5, 4), f32); bi = np.zeros((128, 5, 4), f32); lam = np.zeros((128, 5, 4), f32)
    wa = np.zeros((5, 4, 128, 128), f32); wi = np.zeros((5, 4, 128, 128), f32)
    for k, (ck, rev) in enumerate(slots):
        d = 1 if rev else 0
        w = np.asarray(inp["lru_conv_w"][0, d], f32)
        if rev:
            w = w[::-1]
        for tap in range(4):
            cw[:, k, :, tap] = _cols(w[tap])
        cb[:, k, :] = _cols(inp["lru_conv_b"][0, d])
        ba[:, k, :] = _cols(inp["lru_b_a"][0, d])
        bi[:, k, :] = _cols(inp["lru_b_i"][0, d])
        lam[:, k, :] = _cols(inp["lru_lambda"][0, d])
        for ct in range(4):
            for half in range(2):
                blk = 2 * ct + half
                wa[k, ct, half * 64:(half + 1) * 64, half * 64:(half + 1) * 64] = inp["lru_w_a"][0, d, blk]
                wi[k, ct, half * 64:(half + 1) * 64, half * 64:(half + 1) * 64] = inp["lru_w_i"][0, d, blk]
    put("cw", cw.reshape(128, -1)); put("cb", cb.reshape(128, -1)); put("ba", ba.reshape(128, -1))
    put("bi", bi.reshape(128, -1)); put("lam", lam.reshape(128, -1))
    put("gqa", _cols(inp["q_a_norm"][0]))
    put("gkv", _cols(inp["kv_a_norm"][0]))

    def rotperm(g):
        g = np.asarray(g, f32)
        r = g.copy()
        r[64:80] = g[80:96]
        r[80:96] = g[64:80]
        return r
    gq = np.asarray(inp["mla_q_norm"][0], f32); gk = np.asarray(inp["mla_k_norm"][0], f32)
    for name, v in (("gq", gq), ("gqr", rotperm(gq)), ("gk", gk), ("gkr", rotperm(gk))):
        a = np.zeros((128, 1), f32); a[:96, 0] = v
        put(name, a)
    put("glru", _cols(inp["lru_out_norm"][0]))
    gm = np.zeros((128, 8), f32); gm[:64] = _cols(inp["mla_out_norm"][0], 64)
    put("gmla", gm)
    put("gmq", _cols(inp["mem_q_norm"][0])); put("gmk", _cols(inp["mem_k_norm"][0]))
    fw = np.zeros((128, 44, 3), f32)
    fcw = np.asarray(inp["ffn_conv_w"][0], f32)
    for tap in range(3):
        fw[:, :, tap] = _cols(fcw[tap])
    put("fw", fw.reshape(128, -1))
    put("fb", _cols(inp["ffn_conv_b"][0]))
    invf = np.zeros((128, 1), f32)
    inv = (10000.0 ** (-np.arange(0, 32, 2, dtype=np.float64) / 32.0)) / (2.0 * np.pi)
    for p in range(64, 96):
        invf[p, 0] = inv[(p - 64) % 16]
    put("invf", invf)
    gvec = np.stack([inp["attn_norm"][0], inp["mem_attn_norm"][0], inp["mem_norm"][0], inp["ffn_norm"][0]]).astype(f32)
    m = {
        "xs": np.ascontiguousarray(xs), "xh": xh, "posk": posk, "poso": poso, "pp": pp, "wa": wa, "wi": wi,
        "mem": np.ascontiguousarray(np.asarray(inp["mem"][b], f32)), "gvec": np.ascontiguousarray(gvec),
        "w_in": np.ascontiguousarray(inp["w_in"][0], dtype=f32), "w_uq": np.ascontiguousarray(inp["w_uq"][0], dtype=f32),
        "w_ukv": np.ascontiguousarray(inp["w_ukv"][0], dtype=f32), "w_out": np.ascontiguousarray(inp["w_out"][0], dtype=f32),
        "w_mem_q": np.ascontiguousarray(inp["w_mem_q"][0], dtype=f32),
        "w_mem_kv": np.ascontiguousarray(inp["w_mem_kv"][0], dtype=f32),
        "w_mem_o": np.ascontiguousarray(inp["w_mem_o"][0], dtype=f32),
        "w_up": np.ascontiguousarray(inp["w_up"][0], dtype=f32), "w_down": np.ascontiguousarray(inp["w_down"][0], dtype=f32),
    }
    return m


_NC_CACHE = {}


def run(inputs, dbg=None, cores=None, stop=None):
    inputs = {k: np.asarray(v) for k, v in inputs.items()}
    B, SEQ, _ = inputs["x"].shape
    CH = SEQ // 4
    key = (CH, repr(dbg), stop)
    if key not in _NC_CACHE:
        _NC_CACHE[key] = build(CH, dbg, stop)
    nc = _NC_CACHE[key]
    core_list = cores if cores is not None else [(b, c) for b in range(B) for c in range(4)]
    in_maps = [prep_core(inputs, b, c, CH) for (b, c) in core_list]
    res = run_bass_kernel_spmd(nc, in_maps, core_ids=list(range(len(core_list))), trace=bool(os.environ.get('KTRACE')))
    return res, core_list, CH


def kernel(**inputs):
    res, core_list, CH = run(inputs)
    B, SEQ, _ = np.asarray(inputs["x"]).shape
    out = np.zeros((B, SEQ, D), np.float32)
    for (b, c), r in zip(core_list, res.results):
        out[b, c * CH:(c + 1) * CH] = r["out"]
    return out
```

```python
import numpy as np
from contextlib import ExitStack
import concourse.bass as bass
import concourse.mybir as mybir
from concourse.bass_utils import run_bass_kernel_spmd

F32 = mybir.dt.float32
BF16 = mybir.dt.bfloat16
I32 = mybir.dt.int32
AF = mybir.ActivationFunctionType
ALU = mybir.AluOpType

import os
SAME_ENGINE_SYNC = os.environ.get('SAME_ENGINE_SYNC', '1') == '1'
EPS = 1e-6
D = 1024
DFF = 2816
NCH = 44
MEM = 256


class Res:
    __slots__ = ("name", "w", "r")

    def __init__(self, name=""):
        self.name = name
        self.w = None
        self.r = {}


class Sched:
    ENGS = ("pe", "act", "dve", "pool", "sp")

    def __init__(self):
        self.ops = {e: [] for e in self.ENGS}
        self.cnt = {e: 0 for e in self.ENGS}
        self.dcnt = {}
        self.seen = {e: {} for e in self.ENGS}

    def _need(self, eng, tok, waits):
        if tok is None:
            return
        key, val = tok
        if self.seen[eng].get(key, 0) >= val:
            return
        if val > waits.get(key, 0):
            waits[key] = val

    def op(self, eng, fn, reads=(), writes=(), dma=None):
        waits = {}
        for r in reads:
            self._need(eng, r.w, waits)
        for w in writes:
            self._need(eng, w.w, waits)
            for k, v in w.r.items():
                self._need(eng, (k, v), waits)
        if not SAME_ENGINE_SYNC:
            waits.pop(eng, None)
        for k, v in waits.items():
            self.seen[eng][k] = v
        if dma is None:
            self.cnt[eng] += 1
            tok = (eng, self.cnt[eng])
        else:
            self.dcnt[dma] = self.dcnt.get(dma, 0) + 16
            tok = (dma, self.dcnt[dma])
        self.ops[eng].append((list(waits.items()), fn, tok))
        for r in reads:
            if r.r.get(tok[0], 0) < tok[1]:
                r.r[tok[0]] = tok[1]
        for w in writes:
            w.w = tok
            w.r = {}
        return tok

    def barrier_all(self):
        for e in self.ENGS:
            waits = {}
            for e2 in self.ENGS:
                if e2 != e and self.cnt[e2] > self.seen[e].get(e2, 0):
                    waits[e2] = self.cnt[e2]
            for k, v in self.dcnt.items():
                if v > self.seen[e].get(k, 0):
                    waits[k] = v
            for k, v in waits.items():
                self.seen[e][k] = v
            if waits:
                self.ops[e].append((list(waits.items()), None, None))

    def emit(self, nc):
        keys = list(self.ENGS) + sorted(self.dcnt.keys())
        with ExitStack() as st:
            sems = {k: st.enter_context(nc.semaphore("s_" + k)) for k in keys}
            block = st.enter_context(nc.Block())
            engmap = {"pe": block.tensor, "act": block.scalar, "dve": block.vector,
                      "pool": block.gpsimd, "sp": block.sync}
            fin = {}
            for k in keys:
                v = self.cnt[k] if k in self.cnt else self.dcnt[k]
                if v > 0:
                    fin[k] = v
            self.ops["sp"].append((list(fin.items()), None, None))
            for e in self.ENGS:
                ops = self.ops[e]

                def body(engobj, ops=ops, e=e):
                    for waits, fn, tok in ops:
                        for k, v in waits:
                            engobj.wait_ge(sems[k], v)
                        if fn is None:
                            continue
                        ins = fn(engobj)
                        if tok[0] == e:
                            ins.then_inc(sems[e], 1)
                        else:
                            ins.then_inc(sems[tok[0]], 16)
                engmap[e](body)


class Buf:
    __slots__ = ("ap", "r")

    def __init__(self, ap, r):
        self.ap = ap
        self.r = r


class Arena:
    def __init__(self, t, size):
        self.t = t
        self.size = size
        self.regs = [[0, size]]

    def _take(self, n, name):
        for r in self.regs:
            if r[1] - r[0] >= n:
                o = r[0]
                r[0] += n
                return o
        raise AssertionError(("SBUF arena overflow", name, n, self.regs))

    def f32(self, n, name=""):
        o = self._take(n, name)
        return Buf(self.t[:, o:o + n], Res(name))

    def bf(self, n, name=""):
        m = (n + 1) // 2
        o = self._take(m, name)
        return Buf(self.t[:, o:o + m].bitcast(BF16)[:, 0:n], Res(name))

    def mark(self):
        return [list(r) for r in self.regs]

    def release(self, m):
        self.regs = [list(r) for r in m]

    def top(self):
        return self.regs[0][0]


def ffn_windows(CH):
    nw = -(-CH // 510)
    if CH % 512 == 0 and CH >= 512:
        nw = max(nw, 1)
    base = CH // nw
    rem = CH - base * nw
    sizes = [base + (1 if i < rem else 0) for i in range(nw)]
    starts = [sum(sizes[:i]) for i in range(nw)]
    return list(zip(starts, sizes))


def pp_layout():
    o = {}
    c = 0

    def add(name, n):
        nonlocal c
        o[name] = (c, n)
        c += n
    add("flg", 20)
    add("msk", 2)
    add("cw", 80)
    add("cb", 20)
    add("ba", 20)
    add("bi", 20)
    add("lam", 20)
    add("gqa", 2)
    add("gkv", 1)
    add("gq", 1)
    add("gqr", 1)
    add("gk", 1)
    add("gkr", 1)
    add("glru", 4)
    add("gmla", 8)
    add("gmq", 1)
    add("gmk", 1)
    add("fw", 132)
    add("fb", 44)
    add("invf", 1)
    o["_n"] = c
    return o


PPL = pp_layout()


class _Stop(Exception):
    pass


def build(CH, dbg=None, stop=None):
    holder = {}
    try:
        return _build(CH, dbg, stop, holder)
    except _Stop:
        return holder['nc']


def _build(CH, dbg, stop, holder):
    NB = CH // 512
    NKEY = 4 * CH
    NKT = NKEY // 128
    NKB = NKEY // 512
    NOWN = CH + 2
    WINS = ffn_windows(CH)
    nc = bass.Bass("TRN2", target_bir_lowering=False)
    holder["nc"] = nc

    def din(name, shape, dt=F32):
        return nc.dram_tensor(name, list(shape), dt, kind="ExternalInput").ap()

    xs = din("xs", [5, CH, D])
    xh = din("xh", [2, D])
    posk = din("posk", [4, 32, CH], I32)
    poso = din("poso", [32, NOWN], I32)
    pp_d = din("pp", [128, PPL["_n"]])
    wa_d = din("wa", [5, 4, 128, 128])
    wi_d = din("wi", [5, 4, 128, 128])
    mem_d = din("mem", [MEM, D])
    gvec = din("gvec", [4, D])
    w_in = din("w_in", [D, 1440])
    w_uq = din("w_uq", [256, 768])
    w_ukv = din("w_ukv", [128, 1024])
    w_out = din("w_out", [1024, D])
    w_mq = din("w_mem_q", [D, 512])
    w_mkv = din("w_mem_kv", [D, 1024])
    w_mo = din("w_mem_o", [512, D])
    w_up = din("w_up", [D, 2 * DFF])
    w_dn = din("w_down", [DFF, D])
    out_d = nc.dram_tensor("out", [CH, D], F32, kind="ExternalOutput").ap()
    dbg_d = {}
    if dbg:
        for name, shape in dbg.items():
            dbg_d[name] = nc.dram_tensor("dbg_" + name, list(shape), F32, kind="ExternalOutput").ap()

    S = Sched()
    with ExitStack() as st:
        ASZ = 52500
        arena_t = st.enter_context(nc.sbuf_tensor("arena", [128, ASZ], F32))
        A = Arena(arena_t, ASZ)
        psum_t = [st.enter_context(nc.psum_tensor("ps%d" % i, [128, 512], F32)) for i in range(8)]
        psum = [Buf(t[:, :], Res("ps%d" % i)) for i, t in enumerate(psum_t)]
        psi = [0]

        ps_pool = [[0, 1, 2, 3, 4]]

        def PS():
            pool = ps_pool[0]
            b = psum[pool[psi[0] % len(pool)]]
            psi[0] += 1
            return b
        psa = [0]

        def PSACC():
            b = psum[6 + psa[0] % 2]
            psa[0] += 1
            return b

        def chk(tag):
            if stop == tag:
                S.emit(nc)
                raise _Stop()

        def op(eng, fn, reads=(), writes=(), dma=None):
            return S.op(eng, fn, [b.r for b in reads], [b.r for b in writes], dma)

        dkeys = {}

        def dkey(buf, pre="k"):
            k = (pre, id(buf.r))
            if k not in dkeys:
                dkeys[k] = "%s%02d" % (pre, len(dkeys))
            return dkeys[k]

        def load(dst, dst_ap, src_ap, eng=None, key=None):
            if eng is None:
                eng = "pool" if dst_ap.dtype != src_ap.dtype else "sp"
            op(eng, lambda e: e.dma_start(out=dst_ap, in_=src_ap), reads=[], writes=[dst], dma=dkey(dst))

        def store(dst_ap, buf, src_ap):
            op("sp", lambda e: e.dma_start(out=dst_ap, in_=src_ap), reads=[buf], dma=dkey(buf, "s"))

        def dump(name, buf, ap, rows=None):
            if name in dbg_d:
                store(dbg_d[name], buf, ap)

        PP = A.f32(PPL["_n"], "pp")
        load(PP, PP.ap, pp_d)

        def ppc(name, i=0, rows=slice(0, 128)):
            o, n = PPL[name]
            return PP.ap[rows, o + i:o + i + 1]

        CONST = A.f32(8, "const")
        op("pool", lambda e: e.memset(CONST.ap[:, 0:1], EPS), writes=[CONST])
        op("pool", lambda e: e.memset(CONST.ap[:, 1:2], 1.0), writes=[CONST])
        c_eps = CONST.ap[:, 0:1]
        c_one = CONST.ap[:, 1:2]
        NEGH = A.f32(8, "negh")
        POSH = A.f32(8, "posh")
        op("pool", lambda e: e.memset(NEGH.ap, -0.5), writes=[NEGH])
        op("pool", lambda e: e.memset(POSH.ap, 0.5), writes=[POSH])
        IDF = A.f32(128, "identf")
        IDENT = A.bf(128, "ident")
        ONESB = A.bf(128, "onesb")
        ONESF = A.f32(128, "onesf")
        op("pool", lambda e: e.iota(IDF.ap, [[1, 128]], base=0, channel_multiplier=-1,
                                    allow_small_or_imprecise_dtypes=True), writes=[IDF])
        op("dve", lambda e: e.tensor_scalar(out=IDENT.ap, in0=IDF.ap, scalar1=0.0, scalar2=None,
                                            op0=ALU.is_equal), reads=[IDF], writes=[IDENT])
        op("pool", lambda e: e.memset(ONESB.ap, 1.0), writes=[ONESB])
        op("pool", lambda e: e.memset(ONESF.ap, 1.0), writes=[ONESF])
        LP = A.f32(192, "lruparams")
        lam_o = PPL["lam"][0]
        flg_o = PPL["flg"][0]
        op("act", lambda e: e.activation(out=LP.ap[:, 0:20], in_=PP.ap[:, lam_o:lam_o + 20], func=AF.Exp, scale=-1.0),
           reads=[PP], writes=[LP])
        op("act", lambda e: e.activation(out=LP.ap[:, 0:20], in_=LP.ap[:, 0:20], func=AF.Ln, scale=1.0, bias=c_one),
           reads=[LP, CONST], writes=[LP])
        op("dve", lambda e: e.tensor_scalar(out=LP.ap[:, 20:40], in0=LP.ap[:, 0:20], scalar1=-4.0, scalar2=None,
                                            op0=ALU.mult), reads=[LP], writes=[LP])
        op("dve", lambda e: e.tensor_scalar(out=LP.ap[:, 0:20], in0=LP.ap[:, 0:20], scalar1=-8.0, scalar2=None,
                                            op0=ALU.mult), reads=[LP], writes=[LP])
        op("dve", lambda e: e.tensor_scalar(out=LP.ap[:, 40:60], in0=PP.ap[:, flg_o:flg_o + 20], scalar1=-1.0,
                                            scalar2=1.0, op0=ALU.mult, op1=ALU.add), reads=[PP], writes=[LP])

        ba_o = PPL["ba"][0]
        bi_o = PPL["bi"][0]
        fw_o = PPL["fw"][0]
        fb_o = PPL["fb"][0]
        op("dve", lambda e: e.tensor_scalar(out=LP.ap[:, 60:80], in0=PP.ap[:, ba_o:ba_o + 20], scalar1=0.5, scalar2=None,
                                            op0=ALU.mult), reads=[PP], writes=[LP])
        op("dve", lambda e: e.tensor_scalar(out=LP.ap[:, 80:100], in0=PP.ap[:, bi_o:bi_o + 20], scalar1=0.5, scalar2=None,
                                            op0=ALU.mult), reads=[PP], writes=[LP])
        op("dve", lambda e: e.tensor_scalar(out=LP.ap[:, 100:166], in0=PP.ap[:, fw_o + 66:fw_o + 132], scalar1=0.5, scalar2=None,
                                            op0=ALU.mult), reads=[PP], writes=[LP])
        op("dve", lambda e: e.tensor_scalar(out=LP.ap[:, 166:188], in0=PP.ap[:, fb_o + 22:fb_o + 44], scalar1=0.5, scalar2=None,
                                            op0=ALU.mult), reads=[PP], writes=[LP])

        def flg(k, i):
            return PP.ap[:, flg_o + 4 * k + i:flg_o + 4 * k + i + 1]

        def nflg(k, i):
            return LP.ap[:, 40 + 4 * k + i:40 + 4 * k + i + 1]

        GBC = A.f32(D, "gbc")

        NT_X = [A.f32(D, "xt%d" % i) for i in range(2)]
        NT_J = A.bf(D, "junk")
        NT_H = [A.bf(D, "hb%d" % i) for i in range(2)]
        NT_Sl = [A.f32(2, "nstat%d" % i) for i in range(2)]
        nt_i = [0]

        def norm_T(xbuf, x_ap, n, dstT, dstT_ap3, col0, evac_eng="dve", gb=None):
            gb = gb or GBC
            i = nt_i[0] % 2
            nt_i[0] += 1
            hb = NT_H[i]
            NT_S = NT_Sl[i]
            ss = NT_S.ap[0:n, 0:1]
            rs = NT_S.ap[0:n, 1:2]
            op("act", lambda e: e.activation(out=NT_J.ap[0:n], in_=x_ap, func=AF.Square, accum_out=ss),
               reads=[xbuf], writes=[NT_J, NT_S])
            op("dve", lambda e: e.tensor_scalar(out=rs, in0=ss, scalar1=1.0 / D, scalar2=EPS, op0=ALU.mult, op1=ALU.add),
               reads=[NT_S], writes=[NT_S])
            op("pool", lambda e: e.tensor_tensor(out=rs, in0=rs, in1=NEGH.ap[0:n, 0:1], op=ALU.pow),
               reads=[NT_S, NEGH], writes=[NT_S])
            op("dve", lambda e: e.scalar_tensor_tensor(out=hb.ap[0:n], in0=x_ap, scalar=rs, in1=gb.ap[0:n],
                                                       op0=ALU.mult, op1=ALU.mult),
               reads=[xbuf, NT_S, gb], writes=[hb])
            pb = PS()
            pbf = pb.ap.bitcast(BF16)

            def tr(e):
                ins = None
                for kc in range(8):
                    ins = e.transpose(pbf[:, kc * 128:kc * 128 + n], hb.ap[0:n, kc * 128:(kc + 1) * 128],
                                      IDENT.ap[0:n, 0:n])
                return ins
            op("pe", tr, reads=[hb, IDENT], writes=[pb])
            src = pbf.rearrange("p (k t) -> p k t", k=8)[:, :, 0:n]
            dst = dstT_ap3[:, :, col0:col0 + n]
            if evac_eng == "act":
                op("act", lambda e: e.activation(out=dst, in_=src, func=AF.Copy), reads=[pb], writes=[dstT])
            else:
                op("dve", lambda e: e.tensor_copy(out=dst, in_=src), reads=[pb], writes=[dstT])

        def fm_rstd(sq_list, rows_out, n, dim, SCR, pbuf=None):
            pb = pbuf if pbuf is not None else PS()

            def mm(e):
                ins = None
                for i, (b, ap, rk, base) in enumerate(sq_list):
                    ins = e.matmul(pb.ap[0:rows_out, 0:n], lhsT=ONESB.ap[base:base + rk, 0:rows_out], rhs=ap,
                                   start=(i == 0), stop=(i == len(sq_list) - 1))
                return ins
            op("pe", mm, reads=[b for b, _, _, _ in sq_list] + [ONESB], writes=[pb])
            rb = SCR()
            op("act", lambda e: e.activation(out=rb.ap[0:rows_out, 0:n], in_=pb.ap[0:rows_out, 0:n], func=AF.Ln,
                                             scale=1.0 / dim, bias=c_eps[0:rows_out]),
               reads=[pb, CONST], writes=[rb])
            op("act", lambda e: e.activation(out=rb.ap[0:rows_out, 0:n], in_=rb.ap[0:rows_out, 0:n], func=AF.Exp,
                                             scale=-0.5), reads=[rb], writes=[rb])
            return rb

        POSB = [A.f32(512, "posb%d" % i) for i in range(2)]
        posb_i = [0]

        def rope_tables(pos_src_ap, n, SCR, out_buf=None, out_s=None, out_c=None):
            pi = POSB[posb_i[0] % 2]
            posb_i[0] += 1
            pi_ap = pi.ap.bitcast(I32)
            load(pi, pi_ap[64:96, 0:n], pos_src_ap, eng="sp")
            y = SCR()
            op("dve", lambda e: e.tensor_copy(out=y.ap[64:96, 0:n], in_=pi_ap[64:96, 0:n]), reads=[pi], writes=[y])
            y2 = SCR()
            yc = SCR()
            op("dve", lambda e: e.tensor_scalar(out=y2.ap[64:96, 0:n], in0=y.ap[64:96, 0:n],
                                                scalar1=ppc("invf", 0, slice(64, 96)), scalar2=None, op0=ALU.mult),
               reads=[y, PP], writes=[y2])
            op("dve", lambda e: e.tensor_scalar(out=yc.ap[64:96, 0:n], in0=y2.ap[64:96, 0:n], scalar1=0.25,
                                                scalar2=None, op0=ALU.add), reads=[y2], writes=[yc])
            res = []
            for yy, dst in ((y2, out_s), (yc, out_c)):
                ti = SCR()
                ti_ap = ti.ap.bitcast(I32)
                op("dve", lambda e, yy=yy, ti_ap=ti_ap: e.tensor_copy(out=ti_ap[64:96, 0:n], in_=yy.ap[64:96, 0:n]),
                   reads=[yy], writes=[ti])
                tf = SCR()
                op("dve", lambda e, ti_ap=ti_ap, tf=tf: e.tensor_copy(out=tf.ap[64:96, 0:n], in_=ti_ap[64:96, 0:n]),
                   reads=[ti], writes=[tf])
                op("dve", lambda e, yy=yy, tf=tf: e.tensor_tensor(out=yy.ap[64:96, 0:n], in0=yy.ap[64:96, 0:n],
                                                                  in1=tf.ap[64:96, 0:n], op=ALU.subtract),
                   reads=[yy, tf], writes=[yy])
                if dst is None:
                    o_b, o_ap = yy, yy.ap[64:96, 0:n]
                else:
                    o_b, o_ap = out_buf, dst
                op("act", lambda e, yy=yy, o_ap=o_ap: e.activation(out=o_ap, in_=yy.ap[64:96, 0:n], func=AF.Sin,
                                                                   scale=2.0 * np.pi * 0.999999),
                   reads=[yy], writes=[o_b])
                res.append((o_b, o_ap))
            return res

        KM = A.bf(4 * MEM, "km")
        KM3 = KM.ap.rearrange("p (h t) -> p h t", h=4)
        VM = A.bf(2 * 512, "vm")
        VM3 = VM.ap.rearrange("p (t c) -> p t c", t=2)
        m0 = A.mark()
        scr0 = [A.f32(512, "scr0%d" % i) for i in range(4)]
        s0i = [0]

        def SCRD():
            b = scr0[s0i[0] % len(scr0)]
            s0i[0] += 1
            return b
        if stop == '0a':
            S.emit(nc)
            return nc
        load(GBC, GBC.ap, gvec[2, :].partition_broadcast(128))
        mW = A.mark()
        WMKV = A.bf(8 * 1024, "wmkv")
        WMKV3 = WMKV.ap.rearrange("p (k c) -> p k c", k=8)
        load(WMKV, WMKV3, w_mkv.rearrange("(k p) c -> p k c", p=128))
        MEMT = A.bf(8 * MEM, "memt")
        MEMT3 = MEMT.ap.rearrange("p (k t) -> p k t", k=8)
        for mt in range(2):
            xb = NT_X[nt_i[0] % 2]
            load(xb, xb.ap, mem_d[mt * 128:(mt + 1) * 128, :], eng="sp")
            norm_T(xb, xb.ap, 128, MEMT, MEMT3, mt * 128)
        if stop == '0b':
            S.emit(nc)
            return nc
        for h in range(4):
            pk = PS()

            def mmk(e, pk=pk, h=h):
                ins = None
                for kc in range(8):
                    ins = e.matmul(pk.ap[:, 0:MEM], lhsT=WMKV3[:, kc, h * 128:(h + 1) * 128], rhs=MEMT3[:, kc, :],
                                   start=(kc == 0), stop=(kc == 7))
                return ins
            op("pe", mmk, reads=[WMKV, MEMT], writes=[pk])
            sq = SCRD()
            sq_ap = sq.ap.bitcast(BF16)
            op("act", lambda e, sq_ap=sq_ap, pk=pk: e.activation(out=sq_ap[:, 0:MEM], in_=pk.ap[:, 0:MEM], func=AF.Square),
               reads=[pk], writes=[sq])
            rb = fm_rstd([(sq, sq_ap[:, 0:MEM], 128, 0)], 128, MEM, 128.0, SCRD)
            op("dve", lambda e, pk=pk, rb=rb, h=h: e.scalar_tensor_tensor(
                out=KM3[:, h, :], in0=pk.ap[:, 0:MEM], scalar=ppc("gmk"), in1=rb.ap[:, 0:MEM], op0=ALU.mult, op1=ALU.mult),
               reads=[pk, rb, PP], writes=[KM])
        if stop == '0c':
            S.emit(nc)
            return nc
        for mt in range(2):
            pv = PS()

            def mmv2(e, pv=pv, mt=mt):
                ins = None
                for kc in range(8):
                    ins = e.matmul(pv.ap[:, :], lhsT=MEMT3[:, kc, mt * 128:(mt + 1) * 128], rhs=WMKV3[:, kc, 512:1024],
                                   start=(kc == 0), stop=(kc == 7))
                return ins
            op("pe", mmv2, reads=[WMKV, MEMT], writes=[pv])
            op("act", lambda e, pv=pv, mt=mt: e.activation(out=VM3[:, mt, :], in_=pv.ap[:, :], func=AF.Copy),
               reads=[pv], writes=[VM])
        if stop == '0d':
            S.emit(nc)
            return nc
        S.barrier_all()
        A.release(m0)
        mP = A.top()
        if stop == '0':
            S.emit(nc)
            return nc

        MIXL = A.bf(4 * NOWN, "mixl")
        MIXL3 = MIXL.ap.rearrange("p (c t) -> p c t", c=4)
        CQN = A.bf(2 * NOWN, "cqn")
        CQN3 = CQN.ap.rearrange("p (c t) -> p c t", c=2)
        ckvn_start = A.top()
        CKVN = A.bf(NKEY, "ckvn")
        ETS = A.bf(NKEY, "e_tsq")
        ets_end = A.top()
        mA = A.mark()

        WIN = A.bf(8 * 1440, "win")
        WIN3 = WIN.ap.rearrange("p (k c) -> p k c", k=8)
        for kc in range(8):
            load(WIN, WIN3[:, kc, :], w_in[kc * 128:(kc + 1) * 128, :])
        WKR = A.bf(8 * 192, "wkrpad")
        WKR3 = WKR.ap.rearrange("p (k c) -> p k c", k=8)
        op("pool", lambda e: e.memset(WKR.ap, 0.0), writes=[WKR])
        op("dve", lambda e: e.tensor_copy(out=WKR3[:, :, 64:96], in_=WIN3[:, :, 1408:1440]), reads=[WIN], writes=[WKR])
        op("dve", lambda e: e.tensor_scalar(out=WKR3[:, :, 160:176], in0=WIN3[:, :, 1424:1440], scalar1=-1.0,
                                            scalar2=None, op0=ALU.mult), reads=[WIN], writes=[WKR])
        op("dve", lambda e: e.tensor_copy(out=WKR3[:, :, 176:192], in_=WIN3[:, :, 1408:1424]), reads=[WIN], writes=[WKR])
        WGL = [A.bf(4 * 2 * 128, "wgates%d" % i) for i in range(2)]
        WGL4 = [w.ap.rearrange("p (c g o) -> p c g o", c=4, g=2) for w in WGL]
        HTL = [A.bf(8 * 512, "ht%d" % i) for i in range(2)]
        HTL3 = [h_.ap.rearrange("p (k t) -> p k t", k=8) for h_ in HTL]
        XR = A.f32(4 * 515, "xr")
        XR3 = XR.ap.rearrange("p (c t) -> p c t", c=4)
        HF = A.bf(4 * (CH + 1), "hf")
        HF3 = HF.ap.rearrange("p (c t) -> p c t", c=4)
        GG = A.bf(4 * NOWN, "gelu")
        GG3 = GG.ap.rearrange("p (c t) -> p c t", c=4)
        CAR = A.f32(64, "carry")
        op("pool", lambda e: e.memset(CAR.ap, 0.0), writes=[CAR])
        HSC = A.f32(4 * 512, "hscan")
        HSC3 = HSC.ap.rearrange("p (c t) -> p c t", c=4)
        HE = A.f32(16, "hextra")
        scrA = [A.f32(512, "scrA%d" % i) for i in range(12)]
        sai = [0]

        def SCRA():
            b = scrA[sai[0] % len(scrA)]
            sai[0] += 1
            return b

        load(GBC, GBC.ap, gvec[0, :].partition_broadcast(128))

        chk('A0')

        def lru_cols(k, ct, name):
            o, n = PPL[name]
            return PP.ap[:, o + 4 * k + ct:o + 4 * k + ct + 1]

        def phaseA_front(k, j, n, mini, hb):
            HT = HTL[hb]
            HT3 = HTL3[hb]
            if j == 0 and not mini:
                load(WGL[k % 2], WGL4[k % 2][:, :, 0, :], wa_d[k].rearrange("c i o -> i c o"))
                load(WGL[k % 2], WGL4[k % 2][:, :, 1, :], wi_d[k].rearrange("c i o -> i c o"))
            ntile = (n + 127) // 128
            for i in range(ntile):
                rows = min(128, n - i * 128)
                xb = NT_X[nt_i[0] % 2]
                if mini:
                    src = xh[0:1, :] if k == 3 else xh[1:2, :]
                else:
                    src = xs[k, j * 512 + i * 128:j * 512 + i * 128 + rows, :]
                load(xb, xb.ap[0:rows], src, eng="sp")
                norm_T(xb, xb.ap[0:rows], rows, HT, HT3, i * 128, evac_eng="dve")

        def phaseA_xr(k, j, n, mini, hb):
            HT = HTL[hb]
            HT3 = HTL3[hb]
            if j == 0 and not mini:
                op("dve", lambda e: e.tensor_scalar(out=CAR.ap[:, 36:48], in0=CAR.ap[:, 8:20], scalar1=flg(k, 0),
                                                    scalar2=None, op0=ALU.mult), reads=[CAR, PP], writes=[CAR])
                op("dve", lambda e: e.scalar_tensor_tensor(out=CAR.ap[:, 36:48], in0=CAR.ap[:, 20:32], scalar=flg(k, 1),
                                                           in1=CAR.ap[:, 36:48], op0=ALU.mult, op1=ALU.add),
                   reads=[CAR, PP], writes=[CAR])
                op("dve", lambda e: e.tensor_scalar(out=CAR.ap[:, 32:36], in0=CAR.ap[:, 0:4], scalar1=flg(k, 0),
                                                    scalar2=None, op0=ALU.mult), reads=[CAR, PP], writes=[CAR])
                op("dve", lambda e: e.scalar_tensor_tensor(out=CAR.ap[:, 32:36], in0=CAR.ap[:, 4:8], scalar=flg(k, 1),
                                                           in1=CAR.ap[:, 32:36], op0=ALU.mult, op1=ALU.add),
                   reads=[CAR, PP], writes=[CAR])
                if k == 3:
                    op("dve", lambda e: e.tensor_copy(out=CAR.ap[:, 48:52], in_=CAR.ap[:, 32:36]), reads=[CAR], writes=[CAR])
                if k == 4:
                    op("dve", lambda e: e.tensor_copy(out=CAR.ap[:, 52:56], in_=CAR.ap[:, 32:36]), reads=[CAR], writes=[CAR])
                op("dve", lambda e: e.tensor_copy(out=XR3[:, :, 0:3],
                                                  in_=CAR.ap[:, 36:48].rearrange("p (c t) -> p c t", c=4)),
                   reads=[CAR], writes=[XR])
            else:
                op("dve", lambda e: e.tensor_copy(out=XR3[:, :, 0:3], in_=XR3[:, :, 512:515]), reads=[XR], writes=[XR])
            for ct in range(4):
                pb = PS()

                def mm(e, pb=pb, ct=ct):
                    ins = None
                    for kc in range(8):
                        ins = e.matmul(pb.ap[:, 0:n], lhsT=WIN3[:, kc, ct * 128:(ct + 1) * 128], rhs=HT3[:, kc, 0:n],
                                       start=(kc == 0), stop=(kc == 7))
                    return ins
                op("pe", mm, reads=[WIN, HT], writes=[pb])
                op("act", lambda e, pb=pb, ct=ct: e.activation(out=XR3[:, ct, 3:3 + n], in_=pb.ap[:, 0:n], func=AF.Copy),
                   reads=[pb], writes=[XR])

        def phaseA_block(k, j, n, mini, hb, mid=None):
            HT = HTL[hb]
            HT3 = HTL3[hb]
            WG = WGL[k % 2]
            WG4 = WGL4[k % 2]
            do_kv = (k <= 3) and not mini
            do_own = (k == 3) or (k == 4 and mini)
            if mini:
                own0 = CH if k == 3 else CH + 1
            else:
                own0 = j * 512
            chk('A1')
            for pr in range(2):
                cts = (2 * pr, 2 * pr + 1)
                B_ = {}
                for ct in cts:
                    cwo = PPL["cw"][0] + (k * 4 + ct) * 4
                    xc = SCRA()
                    op("act", lambda e, xc=xc, ct=ct, cwo=cwo: e.activation(
                        out=xc.ap[:, 0:n], in_=XR3[:, ct, 3:3 + n], func=AF.Identity,
                        scale=PP.ap[:, cwo + 3:cwo + 4], bias=lru_cols(k, ct, "cb")), reads=[XR, PP], writes=[xc])
                    for tp in range(3):
                        op("dve", lambda e, xc=xc, ct=ct, cwo=cwo, tp=tp: e.scalar_tensor_tensor(
                            out=xc.ap[:, 0:n], in0=XR3[:, ct, tp:tp + n], scalar=PP.ap[:, cwo + tp:cwo + tp + 1],
                            in1=xc.ap[:, 0:n], op0=ALU.mult, op1=ALU.add), reads=[XR, PP, xc], writes=[xc])
                    xcb = SCRA()
                    xcb_ap = xcb.ap.bitcast(BF16)
                    op("dve", lambda e, xc=xc, xcb_ap=xcb_ap: e.tensor_copy(out=xcb_ap[:, 0:n], in_=xc.ap[:, 0:n]),
                       reads=[xc], writes=[xcb])
                    B_[ct] = dict(xc=xc, xcb=xcb, xcb_ap=xcb_ap)
                for ct in cts:
                    d = B_[ct]
                    pa = PS()
                    op("pe", lambda e, pa=pa, ct=ct, xcb_ap=d["xcb_ap"]: e.matmul(
                        pa.ap[:, 0:n], lhsT=WG4[:, ct, 0, :], rhs=xcb_ap[:, 0:n], start=True, stop=True),
                       reads=[WG, d["xcb"]], writes=[pa])
                    pi_ = PS()
                    op("pe", lambda e, pi_=pi_, ct=ct, xcb_ap=d["xcb_ap"]: e.matmul(
                        pi_.ap[:, 0:n], lhsT=WG4[:, ct, 1, :], rhs=xcb_ap[:, 0:n], start=True, stop=True),
                       reads=[WG, d["xcb"]], writes=[pi_])
                    d["pa"] = pa
                    d["pi"] = pi_
                for ct in cts:
                    d = B_[ct]
                    rr = SCRA()
                    ig = SCRA()
                    op("act", lambda e, rr=rr, pa=d["pa"], ct=ct: e.activation(out=rr.ap[:, 0:n], in_=pa.ap[:, 0:n], func=AF.Tanh,
                                                                          bias=LP.ap[:, 60 + 4 * k + ct:61 + 4 * k + ct], scale=0.5),
                       reads=[d["pa"], LP], writes=[rr])
                    op("act", lambda e, ig=ig, pi_=d["pi"], ct=ct: e.activation(out=ig.ap[:, 0:n], in_=pi_.ap[:, 0:n], func=AF.Tanh,
                                                                            bias=LP.ap[:, 80 + 4 * k + ct:81 + 4 * k + ct], scale=0.5),
                       reads=[d["pi"], LP], writes=[ig])
                    d["rr"] = rr
                    d["ig"] = ig
                for ct in cts:
                    d = B_[ct]
                    aa = SCRA()
                    mm_ = SCRA()
                    cs = LP.ap[:, 4 * k + ct:4 * k + ct + 1]
                    hcs = LP.ap[:, 20 + 4 * k + ct:20 + 4 * k + ct + 1]
                    op("act", lambda e, aa=aa, rr=d["rr"], hcs=hcs: e.activation(out=aa.ap[:, 0:n], in_=rr.ap[:, 0:n], func=AF.Exp,
                                                                             scale=hcs, bias=hcs), reads=[d["rr"], LP], writes=[aa])
                    op("act", lambda e, mm_=mm_, rr=d["rr"], cs=cs: e.activation(out=mm_.ap[:, 0:n], in_=rr.ap[:, 0:n], func=AF.Exp,
                                                                             scale=cs, bias=cs), reads=[d["rr"], LP], writes=[mm_])
                    op("dve", lambda e, mm_=mm_: e.tensor_scalar(out=mm_.ap[:, 0:n], in0=mm_.ap[:, 0:n], scalar1=-0.25, scalar2=0.25,
                                                                 op0=ALU.mult, op1=ALU.add), reads=[mm_], writes=[mm_])
                    op("dve", lambda e, ig=d["ig"], xc=d["xc"]: e.scalar_tensor_tensor(out=ig.ap[:, 0:n], in0=ig.ap[:, 0:n], scalar=1.0,
                                                                                   in1=xc.ap[:, 0:n], op0=ALU.add, op1=ALU.mult),
                       reads=[d["ig"], d["xc"]], writes=[d["ig"]])
                    d["aa"] = aa
                    d["mm"] = mm_
                for ct in cts:
                    d = B_[ct]
                    op("act", lambda e, mm_=d["mm"]: e.activation(out=mm_.ap[:, 0:n], in_=mm_.ap[:, 0:n], func=AF.Sqrt),
                       reads=[d["mm"]], writes=[d["mm"]])
                for ct in cts:
                    d = B_[ct]
                    op("dve", lambda e, ig=d["ig"], mm_=d["mm"]: e.tensor_tensor(out=ig.ap[:, 0:n], in0=ig.ap[:, 0:n], in1=mm_.ap[:, 0:n],
                                                                              op=ALU.mult), reads=[d["ig"], d["mm"]], writes=[d["ig"]])
                    if mini or j > 0:
                        init = CAR.ap[:, 56 + ct:57 + ct]
                    else:
                        init = CAR.ap[:, 32 + ct:33 + ct]
                    op("dve", lambda e, aa=d["aa"], ig=d["ig"], ct=ct, init=init: e.tensor_tensor_scan(
                        out=HSC3[:, ct, 0:n], data0=aa.ap[:, 0:n], data1=ig.ap[:, 0:n], initial=init,
                        op0=ALU.mult, op1=ALU.add), reads=[d["aa"], d["ig"], CAR], writes=[HSC])
            if not mini:
                op("dve", lambda e: e.tensor_copy(out=CAR.ap[:, 56:60], in_=HSC3[:, :, n - 1]), reads=[HSC], writes=[CAR])
            chk('A2')
            if k == 3:
                op("dve", lambda e: e.tensor_copy(out=HF3[:, :, own0 if not mini else CH:(own0 if not mini else CH) + n],
                                                   in_=HSC3[:, :, 0:n]), reads=[HSC], writes=[HF])
            if k <= 2 and (not mini) and j == NB - 1:
                for (st_o, u_i) in ((0, 2), (4, 3)):
                    op("dve", lambda e, st_o=st_o, u_i=u_i: e.tensor_scalar(
                        out=CAR.ap[:, st_o:st_o + 4], in0=CAR.ap[:, st_o:st_o + 4], scalar1=nflg(k, u_i), scalar2=None,
                        op0=ALU.mult), reads=[CAR, LP], writes=[CAR])
                    op("dve", lambda e, st_o=st_o, u_i=u_i: e.scalar_tensor_tensor(
                        out=CAR.ap[:, st_o:st_o + 4], in0=HSC3[:, :, n - 1], scalar=flg(k, u_i), in1=CAR.ap[:, st_o:st_o + 4],
                        op0=ALU.mult, op1=ALU.add), reads=[CAR, HSC, PP], writes=[CAR])
                for (h_o, u_i) in ((8, 2), (20, 3)):
                    hv = CAR.ap[:, h_o:h_o + 12].rearrange("p (c t) -> p c t", c=4)
                    op("dve", lambda e, hv=hv, u_i=u_i: e.tensor_scalar(
                        out=hv, in0=hv, scalar1=nflg(k, u_i), scalar2=None, op0=ALU.mult), reads=[CAR, LP], writes=[CAR])
                    op("dve", lambda e, hv=hv, u_i=u_i: e.scalar_tensor_tensor(
                        out=hv, in0=XR3[:, :, 512:515], scalar=flg(k, u_i), in1=hv, op0=ALU.mult, op1=ALU.add),
                       reads=[CAR, XR, PP], writes=[CAR])
            chk('A3')
            if mid is not None:
                mid()
            if do_kv:
                key0 = k * CH + j * 512
                pb = PS()

                def mmkv(e, pb=pb):
                    ins = None
                    for kc in range(8):
                        ins = e.matmul(pb.ap[:, 0:n], lhsT=WIN3[:, kc, 1280:1408], rhs=HT3[:, kc, 0:n],
                                       start=(kc == 0), stop=(kc == 7))
                    return ins
                op("pe", mmkv, reads=[WIN, HT], writes=[pb])
                cf = SCRA()
                sq = SCRA()
                sq_ap = sq.ap.bitcast(BF16)
                op("act", lambda e, cf=cf, pb=pb: e.activation(out=cf.ap[:, 0:n], in_=pb.ap[:, 0:n], func=AF.Copy),
                   reads=[pb], writes=[cf])
                op("act", lambda e, sq_ap=sq_ap, pb=pb: e.activation(out=sq_ap[:, 0:n], in_=pb.ap[:, 0:n], func=AF.Square),
                   reads=[pb], writes=[sq])
                rb = fm_rstd([(sq, sq_ap[:, 0:n], 128, 0)], 128, n, 128.0, SCRA)
                op("dve", lambda e, cf=cf, rb=rb: e.scalar_tensor_tensor(
                    out=CKVN.ap[:, key0:key0 + n], in0=cf.ap[:, 0:n], scalar=ppc("gkv"), in1=rb.ap[:, 0:n],
                    op0=ALU.mult, op1=ALU.mult), reads=[cf, rb, PP], writes=[CKVN])
                chk('A3a')
                pt = PS()
                prt = PS()

                def mmt(e, pt=pt, o=0):
                    ins = None
                    for kc in range(8):
                        ins = e.matmul(pt.ap[0:96, 0:n], lhsT=WKR3[:, kc, o:o + 96], rhs=HT3[:, kc, 0:n],
                                       start=(kc == 0), stop=(kc == 7))
                    return ins
                op("pe", lambda e: mmt(e, pt, 0), reads=[WKR, HT], writes=[pt])
                op("pe", lambda e: mmt(e, prt, 96), reads=[WKR, HT], writes=[prt])
                chk('A3b')
                (sb, s_ap), (cb_, c_ap) = rope_tables(posk[k, :, j * 512:j * 512 + n], n, SCRA)
                chk('A3c')
                tq = SCRA()
                tq_ap = tq.ap.bitcast(BF16)
                op("act", lambda e, pt=pt, tq_ap=tq_ap: e.activation(out=tq_ap[64:96, 0:n], in_=pt.ap[64:96, 0:n], func=AF.Square),
                   reads=[pt], writes=[tq])
                op("dve", lambda e, tq_ap=tq_ap: e.tensor_copy(out=ETS.ap[0:32, key0:key0 + n], in_=tq_ap[64:96, 0:n]),
                   reads=[tq], writes=[ETS])
                e1 = SCRA()
                e2 = SCRA()
                op("dve", lambda e, e1=e1, pt=pt, c_ap=c_ap: e.scalar_tensor_tensor(
                    out=e1.ap[64:96, 0:n], in0=pt.ap[64:96, 0:n], scalar=ppc("gk", 0, slice(64, 96)), in1=c_ap,
                    op0=ALU.mult, op1=ALU.mult), reads=[pt, cb_, PP, tq], writes=[e1])
                op("dve", lambda e, e2=e2, prt=prt, s_ap=s_ap: e.scalar_tensor_tensor(
                    out=e2.ap[64:96, 0:n], in0=prt.ap[64:96, 0:n], scalar=ppc("gkr", 0, slice(64, 96)), in1=s_ap,
                    op0=ALU.mult, op1=ALU.mult), reads=[prt, sb, PP], writes=[e2])
                op("dve", lambda e, e1=e1, e2=e2: e.tensor_tensor(out=ETS.ap[64:96, key0:key0 + n], in0=e1.ap[64:96, 0:n],
                                                                 in1=e2.ap[64:96, 0:n], op=ALU.add),
                   reads=[e1, e2], writes=[ETS])
            chk('A4')
            if do_own:
                for ct in range(4):
                    pb = PS()

                    def mmy(e, pb=pb, ct=ct):
                        ins = None
                        for kc in range(8):
                            ins = e.matmul(pb.ap[:, 0:n], lhsT=WIN3[:, kc, 512 + ct * 128:512 + (ct + 1) * 128],
                                           rhs=HT3[:, kc, 0:n], start=(kc == 0), stop=(kc == 7))
                        return ins
                    op("pe", mmy, reads=[WIN, HT], writes=[pb])
                    u = SCRA()
                    w = SCRA()
                    op("act", lambda e, u=u, pb=pb: e.activation(out=u.ap[:, 0:n], in_=pb.ap[:, 0:n], func=AF.Copy, scale=0.5),
                       reads=[pb], writes=[u])
                    op("act", lambda e, w=w, pb=pb: e.activation(out=w.ap[:, 0:n], in_=pb.ap[:, 0:n], func=AF.Square),
                       reads=[pb], writes=[w])
                    op("dve", lambda e, w=w: e.tensor_scalar(out=w.ap[:, 0:n], in0=w.ap[:, 0:n], scalar1=2.0 * 0.044715, scalar2=2.0,
                                                             op0=ALU.mult, op1=ALU.add), reads=[w], writes=[w])
                    op("dve", lambda e, w=w, u=u: e.tensor_tensor(out=w.ap[:, 0:n], in0=w.ap[:, 0:n], in1=u.ap[:, 0:n], op=ALU.mult),
                       reads=[w, u], writes=[w])
                    op("act", lambda e, w=w: e.activation(out=w.ap[:, 0:n], in_=w.ap[:, 0:n], func=AF.Tanh,
                                                          scale=0.7978845608028654), reads=[w], writes=[w])
                    op("dve", lambda e, w=w, u=u, ct=ct: e.scalar_tensor_tensor(out=GG3[:, ct, own0:own0 + n], in0=w.ap[:, 0:n],
                                                                             scalar=1.0, in1=u.ap[:, 0:n], op0=ALU.add, op1=ALU.mult),
                       reads=[w, u], writes=[GG])
                cfs = []
                sqs = []
                for c2 in range(2):
                    pb = PS()

                    def mmq(e, pb=pb, c2=c2):
                        ins = None
                        for kc in range(8):
                            ins = e.matmul(pb.ap[:, 0:n], lhsT=WIN3[:, kc, 1024 + c2 * 128:1024 + (c2 + 1) * 128],
                                           rhs=HT3[:, kc, 0:n], start=(kc == 0), stop=(kc == 7))
                        return ins
                    op("pe", mmq, reads=[WIN, HT], writes=[pb])
                    cf = SCRA()
                    sq = SCRA()
                    sq_ap = sq.ap.bitcast(BF16)
                    op("act", lambda e, cf=cf, pb=pb: e.activation(out=cf.ap[:, 0:n], in_=pb.ap[:, 0:n], func=AF.Copy),
                       reads=[pb], writes=[cf])
                    op("act", lambda e, sq_ap=sq_ap, pb=pb: e.activation(out=sq_ap[:, 0:n], in_=pb.ap[:, 0:n], func=AF.Square),
                       reads=[pb], writes=[sq])
                    cfs.append(cf)
                    sqs.append((sq, sq_ap[:, 0:n], 128, 0))
                rb = fm_rstd(sqs, 128, n, 256.0, SCRA)
                for c2 in range(2):
                    op("dve", lambda e, c2=c2, cf=cfs[c2], rb=rb: e.scalar_tensor_tensor(
                        out=CQN3[:, c2, own0:own0 + n], in0=cf.ap[:, 0:n], scalar=ppc("gqa", c2), in1=rb.ap[:, 0:n],
                        op0=ALU.mult, op1=ALU.mult), reads=[cf, rb, PP], writes=[CQN])
            if k == 4 and not mini:
                lo = CH - 512 * (j + 1)
                lru_combine(lambda ct: HF3[:, ct, lo:lo + n], HF, lambda ct: HSC3[:, ct, 0:n][:, ::-1], HSC, lo, n)

        def lru_combine(hf_ap, hf_buf, hb_ap, hb_buf, own0, n):
            los = []
            sqs = []
            for ct in range(4):
                lo_ = SCRA()
                op("dve", lambda e, lo_=lo_, ct=ct: e.tensor_tensor(out=lo_.ap[:, 0:n], in0=hf_ap(ct), in1=hb_ap(ct), op=ALU.add),
                   reads=[hf_buf, hb_buf], writes=[lo_])
                op("dve", lambda e, lo_=lo_, ct=ct: e.tensor_tensor(out=lo_.ap[:, 0:n], in0=lo_.ap[:, 0:n],
                                                                  in1=GG3[:, ct, own0:own0 + n], op=ALU.mult),
                   reads=[lo_, GG], writes=[lo_])
                sq = SCRA()
                sq_ap = sq.ap.bitcast(BF16)
                op("act", lambda e, sq_ap=sq_ap, lo_=lo_: e.activation(out=sq_ap[:, 0:n], in_=lo_.ap[:, 0:n], func=AF.Square),
                   reads=[lo_], writes=[sq])
                los.append(lo_)
                sqs.append((sq, sq_ap[:, 0:n], 128, 0))
            rb = fm_rstd(sqs, 128, n, 512.0, SCRA)
            for ct in range(4):
                op("dve", lambda e, ct=ct, lo_=los[ct], rb=rb: e.scalar_tensor_tensor(
                    out=MIXL3[:, ct, own0:own0 + n], in0=lo_.ap[:, 0:n], scalar=ppc("glru", ct), in1=rb.ap[:, 0:n],
                    op0=ALU.mult, op1=ALU.mult), reads=[lo_, rb, PP], writes=[MIXL])

        blks = []
        for k in range(5):
            for j in range(NB):
                blks.append((k, j, 512, False))
            if k >= 3:
                blks.append((k, NB, 1, True))
        phaseA_front(*blks[0], 0)
        phaseA_xr(*blks[0], 0)
        for bi, (k, j, n, mini) in enumerate(blks):
            mid = None
            if bi + 1 < len(blks):
                phaseA_front(*blks[bi + 1], (bi + 1) % 2)
                mid = (lambda nb=blks[bi + 1], hb_=(bi + 1) % 2: phaseA_xr(*nb, hb_))
            phaseA_block(k, j, n, mini, bi % 2, mid)
            if mini and k == 3:
                op("dve", lambda e: e.tensor_copy(out=HE.ap[:, 0:4], in_=HSC3[:, :, 0]), reads=[HSC], writes=[HE])
            if mini and k == 4:
                op("dve", lambda e: e.tensor_copy(out=HE.ap[:, 4:8], in_=HSC3[:, :, 0]), reads=[HSC], writes=[HE])
        chk('A6')
        HE3 = HE.ap[:, 8:16].rearrange("p (c t) -> p c t", c=4)
        op("dve", lambda e: e.tensor_tensor(out=HE3[:, :, 0], in0=HE.ap[:, 0:4], in1=CAR.ap[:, 52:56], op=ALU.add),
           reads=[HE, CAR], writes=[HE])
        op("dve", lambda e: e.tensor_tensor(out=HE3[:, :, 1], in0=HE.ap[:, 4:8], in1=CAR.ap[:, 48:52], op=ALU.add),
           reads=[HE, CAR], writes=[HE])
        ZERO = SCRA()
        op("pool", lambda e: e.memset(ZERO.ap[:, 0:8], 0.0), writes=[ZERO])
        lru_combine(lambda ct: HE3[:, ct, :], HE, lambda ct: ZERO.ap[:, 0:2], ZERO, CH, 2)
        if "mixl" in dbg_d:
            TMPD = A.f32(4 * NOWN, "tmpd")
            op("dve", lambda e: e.tensor_copy(out=TMPD.ap, in_=MIXL.ap), reads=[MIXL], writes=[TMPD])
            dump("mixl", TMPD, TMPD.ap)
        if "ckvn" in dbg_d:
            TMPD2 = A.f32(NKEY, "tmpd2")
            op("dve", lambda e: e.tensor_copy(out=TMPD2.ap, in_=CKVN.ap), reads=[CKVN], writes=[TMPD2])
            dump("ckvn", TMPD2, TMPD2.ap)
        if "ets" in dbg_d:
            TMPD3 = A.f32(NKEY, "tmpd3")
            op("dve", lambda e: e.tensor_copy(out=TMPD3.ap[0:96], in_=ETS.ap[0:96]), reads=[ETS], writes=[TMPD3])
            dump("ets", TMPD3, TMPD3.ap[0:96])

        S.barrier_all()
        A.release(mA)
        if stop == 'A':
            S.emit(nc)
            return nc

        WUQ = A.bf(2 * 768, "wuq")
        WUQ3 = WUQ.ap.rearrange("p (k c) -> p k c", k=2)
        for c2 in range(2):
            load(WUQ, WUQ3[:, c2, :], w_uq[c2 * 128:(c2 + 1) * 128, :])
        WUQR = A.bf(2 * 8 * 96, "wuqr")
        WUQR4 = WUQR.ap.rearrange("p (k h c) -> p k h c", k=2, h=8)
        WUQ4 = WUQ.ap.rearrange("p (k h c) -> p k h c", k=2, h=8)
        op("pool", lambda e: e.memset(WUQR.ap, 0.0), writes=[WUQR])
        for c2 in range(2):
            op("dve", lambda e, c2=c2: e.tensor_scalar(out=WUQR4[:, c2, :, 64:80], in0=WUQ4[:, c2, :, 80:96], scalar1=-1.0,
                                                       scalar2=None, op0=ALU.mult), reads=[WUQ], writes=[WUQR])
            op("dve", lambda e, c2=c2: e.tensor_copy(out=WUQR4[:, c2, :, 80:96], in_=WUQ4[:, c2, :, 64:80]),
               reads=[WUQ], writes=[WUQR])
        WUKV = A.bf(1024, "wukv")
        load(WUKV, WUKV.ap, w_ukv)
        ON = A.bf(8 * NOWN, "on")
        ON3 = ON.ap.rearrange("p (h t) -> p h t", h=8)
        TAB = A.bf(2 * NOWN, "qtab")
        mB = A.mark()
        KT = [A.bf(NKEY, "kt%d" % i) for i in range(2)]
        VV = [A.bf(NKT * 65, "v%d" % i) for i in range(2)]
        VV3 = [v.ap.rearrange("p (t c) -> p t c", c=65) for v in VV]
        QT = [A.bf(NOWN, "qt%d" % i) for i in range(2)]
        PT = [Buf(NT_H[i].ap[:, 0:512], NT_H[i].r) for i in range(2)] + [A.bf(512, "pt%d" % i) for i in range(2)]
        scrB = [Buf(NT_X[i].ap[:, 0:512], NT_X[i].r) for i in range(2)] + [A.f32(512, "scrB%d" % i) for i in range(6)]
        sbi = [0]

        def SCRB():
            b = scrB[sbi[0] % len(scrB)]
            sbi[0] += 1
            return b
        for v3, v in zip(VV3, VV):
            op("pool", lambda e, v3=v3: e.memset(v3[:, :, 64:65], 1.0), writes=[v])
        nqb = -(-NOWN // 512)
        half = NOWN // 2
        qb_base = half // nqb
        qb_sizes = [2 * (qb_base + (1 if i < half - qb_base * nqb else 0)) for i in range(nqb)]
        qblocks = [(sum(qb_sizes[:i]), qb_sizes[i]) for i in range(nqb)]
        for (q0, n) in qblocks:
            rope_tables(poso[:, q0:q0 + n], n, SCRB, out_buf=TAB, out_s=TAB.ap[64:96, q0:q0 + n],
                        out_c=TAB.ap[64:96, NOWN + q0:NOWN + q0 + n])
        pti = [0]

        RK = [A.f32(NKT, "rk%d" % i) for i in range(2)]
        SSK = psum[5]

        for kt_ in KT:
            op("dve", lambda e, kt_=kt_: e.tensor_copy(out=kt_.ap[64:96, :], in_=ETS.ap[64:96, :]), reads=[ETS], writes=[kt_])

        TSS = A.f32(NKT, "tss")

        def mm_tss(e):
            ins = None
            for t in range(NKT):
                ins = e.matmul(SSK.ap[:, t:t + 1], lhsT=ETS.ap[0:32, t * 128:(t + 1) * 128], rhs=ONESB.ap[0:32, 0:1],
                               start=True, stop=True)
            return ins
        op("pe", mm_tss, reads=[ETS, ONESB], writes=[SSK])
        op("dve", lambda e: e.tensor_copy(out=TSS.ap[:, 0:NKT], in_=SSK.ap[:, 0:NKT]), reads=[SSK], writes=[TSS])

        def kgen(h):
            kt = KT[h % 2]
            rk = RK[h % 2]
            pend_mms = []
            for kb in range(NKB):
                c0 = kb * 512
                pk = PS()
                op("pe", lambda e, pk=pk, c0=c0: e.matmul(pk.ap[0:64, :], lhsT=WUKV.ap[:, h * 128:h * 128 + 64],
                                                      rhs=CKVN.ap[:, c0:c0 + 512], start=True, stop=True),
                   reads=[WUKV, CKVN], writes=[pk])
                sq = SCRB()
                sq_ap = sq.ap.bitcast(BF16)
                op("act", lambda e, sq_ap=sq_ap, pk=pk: e.activation(out=sq_ap[0:64, 0:512], in_=pk.ap[0:64, :], func=AF.Square),
                   reads=[pk], writes=[sq])
                op("dve", lambda e, pk=pk, c0=c0: e.tensor_scalar(out=kt.ap[0:64, c0:c0 + 512], in0=pk.ap[0:64, :],
                                                                 scalar1=ppc("gk", 0, slice(0, 64)), scalar2=None, op0=ALU.mult),
                   reads=[pk, PP, sq], writes=[kt])

                def mms(e, sq_ap=sq_ap, kb=kb, c0=c0):
                    ins = None
                    for i in range(4):
                        t = kb * 4 + i
                        ins = e.matmul(SSK.ap[:, t:t + 1], lhsT=sq_ap[0:64, i * 128:(i + 1) * 128], rhs=ONESB.ap[0:64, 0:1],
                                       start=True, stop=True)
                    return ins
                if pend_mms:
                    pm, psq = pend_mms.pop(0)
                    op("pe", pm, reads=[psq, ETS, ONESB], writes=[SSK])
                pend_mms.append((mms, sq))
            while pend_mms:
                pm, psq = pend_mms.pop(0)
                op("pe", pm, reads=[psq, ETS, ONESB], writes=[SSK])
            op("dve", lambda e: e.scalar_tensor_tensor(out=rk.ap[:, 0:NKT], in0=SSK.ap[:, 0:NKT], scalar=96.0 * EPS,
                                                       in1=TSS.ap[:, 0:NKT], op0=ALU.add, op1=ALU.add),
               reads=[SSK, TSS], writes=[rk])
            nh = SCRB()
            op("pool", lambda e, nh=nh: e.memset(nh.ap[:, 0:NKT], -0.5), writes=[nh])
            op("pool", lambda e, nh=nh: e.tensor_tensor(out=rk.ap[:, 0:NKT], in0=rk.ap[:, 0:NKT], in1=nh.ap[:, 0:NKT], op=ALU.pow),
               reads=[rk, nh], writes=[rk])

        def vgen(h):
            vv = VV[h % len(VV)]
            vv3 = VV3[h % len(VV)]
            for kb in range(NKB):
                c0 = kb * 512
                pv = PS()

                def mmv(e, pv=pv, c0=c0):
                    ins = None
                    for i in range(4):
                        ins = e.matmul(pv.ap[:, i * 64:(i + 1) * 64], lhsT=CKVN.ap[:, c0 + i * 128:c0 + (i + 1) * 128],
                                       rhs=WUKV.ap[:, h * 128 + 64:h * 128 + 128], start=True, stop=True)
                    return ins
                op("pe", mmv, reads=[CKVN, WUKV], writes=[pv])
                op("dve", lambda e, pv=pv, kb=kb: e.tensor_copy(out=vv3[:, kb * 4:(kb + 1) * 4, 0:64],
                                                             in_=pv.ap[:, 0:256].rearrange("p (t c) -> p t c", c=64)),
                   reads=[pv], writes=[vv])

        def do_head(h):
            kt = KT[h % 2]
            rk = RK[h % 2]
            vv = VV[h % len(VV)]
            vv3 = VV3[h % len(VV)]
            qt = QT[h % 2]
            qst = {}

            def qgenA(q0, n):
                pq = PS()
                pr = PS()

                def mmq(e):
                    ins = None
                    for c2 in range(2):
                        ins = e.matmul(pq.ap[0:96, 0:n], lhsT=WUQ3[:, c2, h * 96:(h + 1) * 96], rhs=CQN3[:, c2, q0:q0 + n],
                                       start=(c2 == 0), stop=(c2 == 1))
                    return ins

                def mmr(e):
                    ins = None
                    for c2 in range(2):
                        ins = e.matmul(pr.ap[0:96, 0:n], lhsT=WUQR4[:, c2, h, :], rhs=CQN3[:, c2, q0:q0 + n],
                                       start=(c2 == 0), stop=(c2 == 1))
                    return ins
                op("pe", mmq, reads=[WUQ, CQN], writes=[pq])
                op("pe", mmr, reads=[WUQR, CQN], writes=[pr])
                sq = SCRB()
                sq_ap = sq.ap.bitcast(BF16)
                op("act", lambda e: e.activation(out=sq_ap[0:96, 0:n], in_=pq.ap[0:96, 0:n], func=AF.Square),
                   reads=[pq], writes=[sq])
                e2 = SCRB()
                op("dve", lambda e: e.scalar_tensor_tensor(
                    out=e2.ap[64:96, 0:n], in0=pr.ap[64:96, 0:n], scalar=ppc("gqr", 0, slice(64, 96)),
                    in1=TAB.ap[64:96, q0:q0 + n], op0=ALU.mult, op1=ALU.mult), reads=[pr, TAB, PP], writes=[e2])
                qst[q0] = (pq, pr, sq, sq_ap, e2)

            def qgenB(q0, n):
                pq, pr, sq, sq_ap, e2 = qst[q0]
                rb = fm_rstd([(sq, sq_ap[0:96, 0:n], 96, 0)], 96, n, 96.0, SCRB, pbuf=pr)
                op("dve", lambda e: e.scalar_tensor_tensor(
                    out=qt.ap[0:64, q0:q0 + n], in0=pq.ap[0:64, 0:n], scalar=ppc("gq", 0, slice(0, 64)), in1=rb.ap[0:64, 0:n],
                    op0=ALU.mult, op1=ALU.mult), reads=[pq, rb, PP], writes=[qt])
                e1 = SCRB()
                op("dve", lambda e: e.scalar_tensor_tensor(
                    out=e1.ap[64:96, 0:n], in0=pq.ap[64:96, 0:n], scalar=ppc("gq", 0, slice(64, 96)),
                    in1=TAB.ap[64:96, NOWN + q0:NOWN + q0 + n], op0=ALU.mult, op1=ALU.mult), reads=[pq, TAB, PP, sq], writes=[e1])
                op("dve", lambda e: e.tensor_tensor(out=e1.ap[64:96, 0:n], in0=e1.ap[64:96, 0:n],
                                                    in1=e2.ap[64:96, 0:n], op=ALU.add),
                   reads=[e1, e2], writes=[e1])
                op("dve", lambda e: e.tensor_tensor(out=qt.ap[64:96, q0:q0 + n], in0=e1.ap[64:96, 0:n],
                                                    in1=rb.ap[64:96, 0:n], op=ALU.mult),
                   reads=[e1, rb], writes=[qt])
            qgenA(*qblocks[0])
            for qi in range(len(qblocks)):
                if qi + 1 < len(qblocks):
                    qgenA(*qblocks[qi + 1])
                qgenB(*qblocks[qi])
            scale = 96.0 ** -0.5
            fin_pending = []
            for (q0, n) in qblocks:
                po = PSACC()
                pend = []

                def qk(t, q0=q0, n=n):
                    pb = PS()
                    op("pe", lambda e, pb=pb, t=t: e.matmul(pb.ap[:, 0:n], lhsT=kt.ap[0:96, t * 128:(t + 1) * 128],
                                                        rhs=qt.ap[0:96, q0:q0 + n], start=True, stop=True),
                       reads=[kt, qt], writes=[pb])
                    return pb

                def expv(t, pb, q0=q0, n=n, po=po):
                    pt_ = PT[pti[0] % 4]
                    pti[0] += 1
                    op("act", lambda e, pb=pb, pt_=pt_, t=t: e.activation(out=pt_.ap[:, 0:n], in_=pb.ap[:, 0:n], func=AF.Exp,
                                                                     scale=rk.ap[:, t:t + 1]), reads=[pb, rk], writes=[pt_])
                    op("pe", lambda e, pt_=pt_, t=t: e.matmul(po.ap[0:65, 0:n], lhsT=vv3[:, t, :], rhs=pt_.ap[:, 0:n],
                                                          start=(t == 0), stop=(t == NKT - 1)),
                       reads=[vv, pt_], writes=[po])
                LOOK = 2
                DEFER = 8
                for t in range(NKT + LOOK):
                    if t < NKT:
                        pend.append((t, qk(t)))
                    if t >= LOOK:
                        tt, pb = pend.pop(0)
                        expv(tt, pb)
                    if t == DEFER and fin_pending:
                        fin_pending.pop(0)()
                rd = SCRB()
                op("dve", lambda e, rd=rd, po=po, n=n: e.reciprocal(out=rd.ap[64:65, 0:n], in_=po.ap[64:65, 0:n]),
                   reads=[po], writes=[rd])

                def fin(rd=rd, po=po, q0=q0, n=n):
                    pbc = PS()
                    op("pe", lambda e, pbc=pbc: e.matmul(pbc.ap[0:64, 0:n], lhsT=ONESF.ap[64:65, 0:64], rhs=rd.ap[64:65, 0:n],
                                                         start=True, stop=True), reads=[ONESF, rd], writes=[pbc])
                    oc = SCRB()
                    op("act", lambda e, oc=oc: e.activation(out=oc.ap[0:64, 0:n], in_=po.ap[0:64, 0:n], func=AF.Copy),
                       reads=[po, rd], writes=[oc])
                    op("dve", lambda e, oc=oc, pbc=pbc: e.tensor_tensor(out=ON3[0:64, h, q0:q0 + n], in0=oc.ap[0:64, 0:n],
                                                                         in1=pbc.ap[0:64, 0:n], op=ALU.mult),
                       reads=[oc, pbc], writes=[ON])
                fin_pending.append(fin)
            while fin_pending:
                fin_pending.pop(0)()
        chk('B0')
        kgen(0)
        chk('B1')
        vgen(0)
        chk('B2')
        for h_ in range(8):
            if h_ < 7:
                kgen(h_ + 1)
                if len(VV) > 1:
                    vgen(h_ + 1)
            do_head(h_)
            if h_ < 7 and len(VV) == 1:
                vgen(h_ + 1)
        for (q0, n) in qblocks:
            sqs = []
            for h in range(8):
                sq = SCRB()
                sq_ap = sq.ap.bitcast(BF16)
                op("act", lambda e, sq_ap=sq_ap, h=h, q0=q0, n=n: e.activation(out=sq_ap[0:64, 0:n], in_=ON3[0:64, h, q0:q0 + n],
                                                                            func=AF.Square), reads=[ON], writes=[sq])
                sqs.append((sq, sq_ap[0:64, 0:n], 64, 0))
            rb = fm_rstd(sqs, 64, n, 512.0, SCRB)
            for h in range(8):
                op("dve", lambda e, h=h, rb=rb, q0=q0, n=n: e.scalar_tensor_tensor(
                    out=ON3[0:64, h, q0:q0 + n], in0=ON3[0:64, h, q0:q0 + n], scalar=ppc("gmla", h, slice(0, 64)),
                    in1=rb.ap[0:64, 0:n], op0=ALU.mult, op1=ALU.mult), reads=[ON, rb, PP], writes=[ON])
        if "on" in dbg_d:
            TMPD4 = A.f32(8 * NOWN, "tmpd4")
            op("dve", lambda e: e.tensor_copy(out=TMPD4.ap[0:64], in_=ON.ap[0:64]), reads=[ON], writes=[TMPD4])
            dump("on", TMPD4, TMPD4.ap[0:64])
        S.barrier_all()
        A.release(mB)
        if stop == 'B':
            S.emit(nc)
            return nc
        NTL = CH // 128
        tiles = [(i * 128, 128) for i in range(NTL)] + [(CH, 2)]
        xres_start = A.top()
        XRES = A.f32((NTL + 1) * D, "xres")
        xres_end = A.top()
        XRES3 = XRES.ap.rearrange("p (t d) -> p t d", d=D)
        XR_res = [Res("xres%d" % i) for i in range(NTL + 1)]
        XT_ = [Buf(XRES3[:, i, :], XR_res[i]) for i in range(NTL + 1)]
        mC = A.mark()
        A.regs = [[ckvn_start, ets_end], [xres_end, ASZ]]
        WOL = A.bf(4 * D, "wol")
        WOL3 = WOL.ap.rearrange("p (c d) -> p c d", c=4)
        WOM = A.bf(8 * D, "wom")
        WOM3 = WOM.ap.rearrange("p (h d) -> p h d", h=8)
        load(WOL, WOL3, w_out[0:512, :].rearrange("(c p) d -> p c d", p=128))
        load(WOM, WOM3[0:64], w_out[512:1024, :].rearrange("(h p) d -> p h d", p=64))
        for ti, (o0, rows) in enumerate(tiles):
            xt = XT_[ti]
            src = xs[3, o0:o0 + rows, :] if ti < NTL else xh[0:2, :]
            load(xt, xt.ap[0:rows], src, eng="sp")
            for half in range(2):
                pb = PS()

                def mmo(e, pb=pb, o0=o0, rows=rows, half=half):
                    ins = None
                    for ct in range(4):
                        ins = e.matmul(pb.ap[0:rows, :], lhsT=MIXL3[:, ct, o0:o0 + rows],
                                       rhs=WOL3[:, ct, half * 512:(half + 1) * 512], start=(ct == 0), stop=False)
                    for h in range(8):
                        ins = e.matmul(pb.ap[0:rows, :], lhsT=ON3[0:64, h, o0:o0 + rows],
                                       rhs=WOM3[0:64, h, half * 512:(half + 1) * 512], start=False, stop=(h == 7))
                    return ins
                op("pe", mmo, reads=[MIXL, ON, WOL, WOM], writes=[pb])
                op("dve", lambda e, pb=pb, xt=xt, rows=rows, half=half: e.tensor_tensor(
                    out=xt.ap[0:rows, half * 512:(half + 1) * 512], in0=xt.ap[0:rows, half * 512:(half + 1) * 512],
                    in1=pb.ap[0:rows, :], op=ALU.add), reads=[xt, pb], writes=[xt])
        if "x1" in dbg_d:
            for ti in range(NTL):
                store(dbg_d["x1"][ti * 128:(ti + 1) * 128, :], XT_[ti], XT_[ti].ap)
        S.barrier_all()
        A.regs = [[mP, xres_start], [xres_end, ASZ]]
        if stop == 'C':
            S.emit(nc)
            return nc

        GB2 = A.f32(D, "gbc2")
        GB3 = A.f32(D, "gbc3")
        load(GB2, GB2.ap, gvec[1, :].partition_broadcast(128))
        load(GB3, GB3.ap, gvec[3, :].partition_broadcast(128))
        HNT = A.bf(8 * NOWN, "hnt")
        HNT3 = HNT.ap.rearrange("p (k t) -> p k t", k=8)
        mD = A.mark()
        WMQ = A.bf(8 * 512, "wmq")
        WMQ3 = WMQ.ap.rearrange("p (k c) -> p k c", k=8)
        load(WMQ, WMQ3, w_mq.rearrange("(k p) c -> p k c", p=128))
        WMO = A.bf(4 * D, "wmo")
        WMO3 = WMO.ap.rearrange("p (h d) -> p h d", h=4)
        load(WMO, WMO3, w_mo.rearrange("(h p) d -> p h d", p=128))
        H1T = A.bf(8 * 512, "h1t")
        H1T3 = H1T.ap.rearrange("p (k t) -> p k t", k=8)
        OMT = A.bf(4 * 512, "omt")
        OMT3 = OMT.ap.rearrange("p (h t) -> p h t", h=4)
        HNH = A.bf(8 * 2, "hnh")
        HNH3 = HNH.ap.rearrange("p (k t) -> p k t", k=8)
        scrD = [A.f32(512, "scrD%d" % i) for i in range(10)]
        sdi = [0]

        def SCRD():
            b = scrD[sdi[0] % len(scrD)]
            sdi[0] += 1
            return b
        mscale = 128.0 ** -0.5
        qbl = [(q0, min(512, CH - q0)) for q0 in range(0, CH, 512)] + [(CH, 2)]
        for (q0, n) in qbl:
            tl = [ti for ti, (o0, rows) in enumerate(tiles) if q0 <= o0 < q0 + n]
            for ti in tl:
                o0, rows = tiles[ti]
                norm_T(XT_[ti], XT_[ti].ap[0:rows], rows, H1T, H1T3, o0 - q0, gb=GB2)
            dst = {}

            def d_headA(h, n=n):
                pq = PS()

                def mmq2(e):
                    ins = None
                    for kc in range(8):
                        ins = e.matmul(pq.ap[:, 0:n], lhsT=WMQ3[:, kc, h * 128:(h + 1) * 128], rhs=H1T3[:, kc, 0:n],
                                       start=(kc == 0), stop=(kc == 7))
                    return ins
                op("pe", mmq2, reads=[WMQ, H1T], writes=[pq])
                sq = SCRD()
                sq_ap = sq.ap.bitcast(BF16)
                op("act", lambda e: e.activation(out=sq_ap[:, 0:n], in_=pq.ap[:, 0:n], func=AF.Square),
                   reads=[pq], writes=[sq])
                dst[h] = (pq, sq, sq_ap)

            def d_headB(h, n=n):
                pq, sq, sq_ap = dst[h]
                rb = fm_rstd([(sq, sq_ap[:, 0:n], 128, 0)], 128, n, 128.0, SCRD)
                qm = SCRD()
                qm_ap = qm.ap.bitcast(BF16)
                op("dve", lambda e: e.scalar_tensor_tensor(
                    out=qm_ap[:, 0:n], in0=pq.ap[:, 0:n], scalar=ppc("gmq"), in1=rb.ap[:, 0:n], op0=ALU.mult, op1=ALU.mult),
                   reads=[pq, rb, PP], writes=[qm])
                po = psum[6]
                pd = psum[7]
                for mt in range(2):
                    ps_ = PS()
                    op("pe", lambda e, ps_=ps_, mt=mt: e.matmul(
                        ps_.ap[:, 0:n], lhsT=KM3[:, h, mt * 128:(mt + 1) * 128], rhs=qm_ap[:, 0:n], start=True, stop=True),
                       reads=[KM, qm], writes=[ps_])
                    pt_ = SCRD()
                    pt_ap = pt_.ap.bitcast(BF16)
                    op("act", lambda e, ps_=ps_, pt_ap=pt_ap: e.activation(out=pt_ap[:, 0:n], in_=ps_.ap[:, 0:n], func=AF.Exp,
                                                                       scale=mscale), reads=[ps_], writes=[pt_])
                    op("pe", lambda e, mt=mt, pt_ap=pt_ap: e.matmul(
                        po.ap[:, 0:n], lhsT=VM3[:, mt, h * 128:(h + 1) * 128], rhs=pt_ap[:, 0:n], start=(mt == 0), stop=(mt == 1)),
                       reads=[VM, pt_], writes=[po])
                    op("pe", lambda e, mt=mt, pt_ap=pt_ap: e.matmul(
                        pd.ap[:, 0:n], lhsT=ONESB.ap[:, 0:128], rhs=pt_ap[:, 0:n], start=(mt == 0), stop=(mt == 1)),
                       reads=[ONESB, pt_], writes=[pd])
                rd = SCRD()
                op("dve", lambda e: e.reciprocal(out=rd.ap[:, 0:n], in_=pd.ap[:, 0:n]),
                   reads=[pd], writes=[rd])
                op("dve", lambda e: e.tensor_tensor(out=OMT3[:, h, 0:n], in0=po.ap[:, 0:n], in1=rd.ap[:, 0:n],
                                                    op=ALU.mult), reads=[po, rd], writes=[OMT])
            ps_pool[0] = [0, 1, 2, 3, 4, 5]
            d_headA(0)
            for h_ in range(4):
                if h_ < 3:
                    d_headA(h_ + 1)
                d_headB(h_)
            ps_pool[0] = [0, 1, 2, 3, 4]
            def d_outproj(ti, q0=q0):
                o0, rows = tiles[ti]
                xt = XT_[ti]
                c0 = o0 - q0
                for half in range(2):
                    pb = PS()

                    def mmo2(e, pb=pb, c0=c0, rows=rows, half=half):
                        ins = None
                        for h in range(4):
                            ins = e.matmul(pb.ap[0:rows, :], lhsT=OMT3[:, h, c0:c0 + rows],
                                           rhs=WMO3[:, h, half * 512:(half + 1) * 512], start=(h == 0), stop=(h == 3))
                        return ins
                    op("pe", mmo2, reads=[OMT, WMO], writes=[pb])
                    op("dve", lambda e, pb=pb, xt=xt, rows=rows, half=half: e.tensor_tensor(
                        out=xt.ap[0:rows, half * 512:(half + 1) * 512], in0=xt.ap[0:rows, half * 512:(half + 1) * 512],
                        in1=pb.ap[0:rows, :], op=ALU.add), reads=[xt, pb], writes=[xt])

            def d_ffnnorm(ti):
                o0, rows = tiles[ti]
                xt = XT_[ti]
                if ti < NTL:
                    norm_T(xt, xt.ap[0:rows], rows, HNT, HNT3, o0 + 1, gb=GB3, evac_eng="act")
                else:
                    norm_T(xt, xt.ap[0:rows], rows, HNH, HNH3, 0, gb=GB3)
                    op("dve", lambda e: e.tensor_scalar(out=HNT3[:, :, CH + 1:CH + 2], in0=HNH3[:, :, 0:1], scalar1=ppc("msk", 0),
                                                        scalar2=None, op0=ALU.mult), reads=[HNH, PP], writes=[HNT])
                    op("dve", lambda e: e.tensor_scalar(out=HNT3[:, :, 0:1], in0=HNH3[:, :, 1:2], scalar1=ppc("msk", 1),
                                                        scalar2=None, op0=ALU.mult), reads=[HNH, PP], writes=[HNT])
            d_outproj(tl[0])
            for i_ in range(len(tl)):
                if i_ + 1 < len(tl):
                    d_outproj(tl[i_ + 1])
                d_ffnnorm(tl[i_])
        if "x2" in dbg_d:
            for ti in range(NTL):
                store(dbg_d["x2"][ti * 128:(ti + 1) * 128, :], XT_[ti], XT_[ti].ap)
        S.barrier_all()
        A.release(mD)
        if stop == 'D':
            S.emit(nc)
            return nc

        GP = 2
        NR = 22 // GP
        WUPG = [A.bf(8 * 2 * GP * 128, "wupg%d" % i) for i in range(2)]
        WUPG4 = [w.ap.rearrange("p (k s c) -> p k s c", k=8, s=2) for w in WUPG]
        WDNG = [A.bf(GP * D, "wdng%d" % i) for i in range(2)]
        WDNG3 = [w.ap.rearrange("p (j d) -> p j d", j=GP) for w in WDNG]
        ACTT = [A.bf(GP * CH, "actt%d" % i) for i in range(2)]
        ACTT3 = [a.ap.rearrange("p (j t) -> p j t", j=GP) for a in ACTT]
        scrE = [A.f32(512, "scrE%d" % i) for i in range(8)]
        sei = [0]

        def SCRE():
            b = scrE[sei[0] % len(scrE)]
            sei[0] += 1
            return b
        fwo = PPL["fw"][0]
        fbo = PPL["fb"][0]
        def do_round(r):
            wu = WUPG[r % 2]
            wu4 = WUPG4[r % 2]
            wd = WDNG[r % 2]
            wd3 = WDNG3[r % 2]
            at = ACTT[r % 2]
            at3 = ACTT3[r % 2]
            j0 = r * GP
            load(wu, wu4[:, :, 0, :], w_up[:, j0 * 128:(j0 + GP) * 128].rearrange("(k p) c -> p k c", p=128))
            load(wu, wu4[:, :, 1, :], w_up[:, DFF + j0 * 128:DFF + (j0 + GP) * 128].rearrange("(k p) c -> p k c", p=128))
            load(wd, wd3, w_dn[j0 * 128:(j0 + GP) * 128, :].rearrange("(j p) d -> p j d", p=128))
            for (t0, n) in WINS:
                for jj in range(GP):
                    cv = []
                    for s in range(2):
                        chn = (j0 + jj) + s * 22
                        pg = PS()

                        def mmu(e, pg=pg, s=s, jj=jj, t0=t0, n=n):
                            ins = None
                            for kc in range(8):
                                ins = e.matmul(pg.ap[:, 0:n + 2], lhsT=wu4[:, kc, s, jj * 128:(jj + 1) * 128],
                                               rhs=HNT3[:, kc, t0:t0 + n + 2], start=(kc == 0), stop=(kc == 7))
                            return ins
                        op("pe", mmu, reads=[wu, HNT], writes=[pg])
                        c_ = SCRE()
                        wsrc, wb = PP, fwo + 3 * chn
                        bsrc, bb = PP, fbo + chn
                        op("act", lambda e, c_=c_, pg=pg, wsrc=wsrc, wb=wb, bsrc=bsrc, bb=bb, n=n: e.activation(
                            out=c_.ap[:, 0:n], in_=pg.ap[:, 1:n + 1], func=AF.Identity,
                            scale=wsrc.ap[:, wb + 1:wb + 2], bias=bsrc.ap[:, bb:bb + 1]),
                           reads=[pg, wsrc, bsrc], writes=[c_])
                        for tp in (0, 2):
                            op("dve", lambda e, c_=c_, pg=pg, wsrc=wsrc, wb=wb, tp=tp, n=n: e.scalar_tensor_tensor(
                                out=c_.ap[:, 0:n], in0=pg.ap[:, tp:tp + n], scalar=wsrc.ap[:, wb + tp:wb + tp + 1],
                                in1=c_.ap[:, 0:n], op0=ALU.mult, op1=ALU.add), reads=[pg, wsrc, c_], writes=[c_])
                        cv.append(c_)
                    sg = SCRE()
                    op("act", lambda e, sg=sg, g_=cv[0], n=n: e.activation(out=sg.ap[:, 0:n], in_=g_.ap[:, 0:n], func=AF.Silu),
                       reads=[cv[0]], writes=[sg])
                    op("dve", lambda e, sg=sg, u_=cv[1], jj=jj, t0=t0, n=n: e.tensor_tensor(
                        out=at3[:, jj, t0:t0 + n], in0=sg.ap[:, 0:n], in1=u_.ap[:, 0:n], op=ALU.mult),
                       reads=[sg, cv[1]], writes=[at])
            if not (r % 2 == 1 or r == NR - 1):
                return
            rds = [r - 1, r] if r % 2 == 1 else [r]
            srcs = [(ACTT3[rr % 2], WDNG3[rr % 2]) for rr in rds]
            rbufs = [ACTT[rr % 2] for rr in rds] + [WDNG[rr % 2] for rr in rds]
            nmm = GP * len(rds)
            for ti in range(NTL):
                o0, rows = tiles[ti]
                xt = XT_[ti]
                for half in range(2):
                    pb = PS()

                    def mmd(e, pb=pb, o0=o0, half=half):
                        ins = None
                        i_ = 0
                        for (a3_, w3_) in srcs:
                            for jj in range(GP):
                                ins = e.matmul(pb.ap[:, :], lhsT=a3_[:, jj, o0:o0 + 128], rhs=w3_[:, jj, half * 512:(half + 1) * 512],
                                               start=(i_ == 0), stop=(i_ == nmm - 1))
                                i_ += 1
                        return ins
                    op("pe", mmd, reads=rbufs, writes=[pb])
                    op("dve", lambda e, pb=pb, xt=xt, half=half: e.tensor_tensor(
                        out=xt.ap[:, half * 512:(half + 1) * 512], in0=xt.ap[:, half * 512:(half + 1) * 512],
                        in1=pb.ap[:, :], op=ALU.add), reads=[xt, pb], writes=[xt])
        for r_ in range(NR):
            do_round(r_)
        for ti in range(NTL):
            store(out_d[ti * 128:(ti + 1) * 128, :], XT_[ti], XT_[ti].ap)
        S.emit(nc)
    return nc


def _cols(v, rows=128):
    v = np.asarray(v, np.float32)
    return np.ascontiguousarray(v.reshape(-1, rows).T)


def prep_core(inp, b, c, CH):
    f32 = np.float32
    x = np.asarray(inp["x"][b], f32)
    pos = np.asarray(inp["positions"][b]).astype(np.int32)
    s0 = c * CH
    slots = [(k, False) for k in range(c)] + [(k, True) for k in range(3, c, -1)] + [(c, False), (c, True)]
    assert len(slots) == 5
    xs = np.stack([x[k * CH:(k + 1) * CH][::-1] if rev else x[k * CH:(k + 1) * CH] for k, rev in slots])
    pk = np.stack([pos[k * CH:(k + 1) * CH][::-1] if rev else pos[k * CH:(k + 1) * CH] for k, rev in slots[:4]])
    posk = np.ascontiguousarray(np.broadcast_to(pk[:, None, :], (4, 32, CH))).astype(np.int32)
    xh = np.zeros((2, D), f32)
    ph = np.zeros((2,), np.int32)
    msk = np.zeros((2,), f32)
    if c < 3:
        xh[0] = x[s0 + CH]; ph[0] = pos[s0 + CH]; msk[0] = 1.0
    if c > 0:
        xh[1] = x[s0 - 1]; ph[1] = pos[s0 - 1]; msk[1] = 1.0
    po = np.concatenate([pos[s0:s0 + CH], ph])
    poso = np.ascontiguousarray(np.broadcast_to(po[None, :], (32, CH + 2))).astype(np.int32)
    flg = np.zeros((5, 4), f32)
    for k in range(3):
        if k < c:
            flg[k] = [1, 0, 1, 0]
        else:
            flg[k] = [0, 1, 0, 1]
    flg[3] = [1, 0, 0, 0]
    flg[4] = [0, 1, 0, 0]
    pp = np.zeros((128, PPL["_n"]), f32)

    def put(name, arr):
        o, n = PPL[name]
        arr = np.asarray(arr, f32)
        if arr.ndim == 1:
            arr = np.broadcast_to(arr[None, :], (128, arr.shape[0]))
        assert arr.shape[1] == n, (name, arr.shape, n)
        pp[:arr.shape[0], o:o + n] = arr
    put("flg", flg.reshape(-1))
    put("msk", msk)
    cw = np.zeros((128, 5, 4, 4), f32)
    cb = np.zeros((128, 5, 4), f32); ba = np.zeros((128, 5, 4), f32); bi = np.zeros((128, 5, 4), f32); lam = np.zeros((128, 5, 4), f32)
    wa = np.zeros((5, 4, 128, 128), f32); wi = np.zeros((5, 4, 128, 128), f32)
    for k, (ck, rev) in enumerate(slots):
        d = 1 if rev else 0
        w = np.asarray(inp["lru_conv_w"][0, d], f32)
        if rev:
            w = w[::-1]
        for tap in range(4):
            cw[:, k, :, tap] = _cols(w[tap])
        cb[:, k, :] = _cols(inp["lru_conv_b"][0, d])
        ba[:, k, :] = _cols(inp["lru_b_a"][0, d])
        bi[:, k, :] = _cols(inp["lru_b_i"][0, d])
        lam[:, k, :] = _cols(inp["lru_lambda"][0, d])
        for ct in range(4):
            for half in range(2):
                blk = 2 * ct + half
                wa[k, ct, half * 64:(half + 1) * 64, half * 64:(half + 1) * 64] = inp["lru_w_a"][0, d, blk]
                wi[k, ct, half * 64:(half + 1) * 64, half * 64:(half + 1) * 64] = inp["lru_w_i"][0, d, blk]
    put("cw", cw.reshape(128, -1)); put("cb", cb.reshape(128, -1)); put("ba", ba.reshape(128, -1))
    put("bi", bi.reshape(128, -1)); put("lam", lam.reshape(128, -1))
    put("gqa", _cols(inp["q_a_norm"][0]))
    put("gkv", _cols(inp["kv_a_norm"][0]))

    def rotperm(g):
        g = np.asarray(g, f32)
        r = g.copy()
        r[64:80] = g[80:96]
        r[80:96] = g[64:80]
        return r
    gq = np.asarray(inp["mla_q_norm"][0], f32); gk = np.asarray(inp["mla_k_norm"][0], f32)
    for name, v in (("gq", gq), ("gqr", rotperm(gq)), ("gk", gk), ("gkr", rotperm(gk))):
        a = np.zeros((128, 1), f32); a[:96, 0] = v
        put(name, a)
    put("glru", _cols(inp["lru_out_norm"][0]))
    gm = np.zeros((128, 8), f32); gm[:64] = _cols(inp["mla_out_norm"][0], 64)
    put("gmla", gm)
    put("gmq", _cols(inp["mem_q_norm"][0])); put("gmk", _cols(inp["mem_k_norm"][0]))
    fw = np.zeros((128, 44, 3), f32)
    fcw = np.asarray(inp["ffn_conv_w"][0], f32)
    for tap in range(3):
        fw[:, :, tap] = _cols(fcw[tap])
    put("fw", fw.reshape(128, -1))
    put("fb", _cols(inp["ffn_conv_b"][0]))
    invf = np.zeros((128, 1), f32)
    inv = (10000.0 ** (-np.arange(0, 32, 2, dtype=np.float64) / 32.0)) / (2.0 * np.pi)
    for p in range(64, 96):
        invf[p, 0] = inv[(p - 64) % 16]
    put("invf", invf)
    gvec = np.stack([inp["attn_norm"][0], inp["mem_attn_norm"][0], inp["mem_norm"][0], inp["ffn_norm"][0]]).astype(f32)
    m = {
        "xs": np.ascontiguousarray(xs), "xh": xh, "posk": posk, "poso": poso, "pp": pp, "wa": wa, "wi": wi,
        "mem": np.ascontiguousarray(np.asarray(inp["mem"][b], f32)), "gvec": np.ascontiguousarray(gvec),
        "w_in": np.ascontiguousarray(inp["w_in"][0], dtype=f32), "w_uq": np.ascontiguousarray(inp["w_uq"][0], dtype=f32),
        "w_ukv": np.ascontiguousarray(inp["w_ukv"][0], dtype=f32), "w_out": np.ascontiguousarray(inp["w_out"][0], dtype=f32),
        "w_mem_q": np.ascontiguousarray(inp["w_mem_q"][0], dtype=f32),
        "w_mem_kv": np.ascontiguousarray(inp["w_mem_kv"][0], dtype=f32),
        "w_mem_o": np.ascontiguousarray(inp["w_mem_o"][0], dtype=f32),
        "w_up": np.ascontiguousarray(inp["w_up"][0], dtype=f32), "w_down": np.ascontiguousarray(inp["w_down"][0], dtype=f32),
    }
    return m


_NC_CACHE = {}


def run(inputs, dbg=None, cores=None, stop=None):
    inputs = {k: np.asarray(v) for k, v in inputs.items()}
    B, SEQ, _ = inputs["x"].shape
    CH = SEQ // 4
    key = (CH, repr(dbg), stop)
    if key not in _NC_CACHE:
        _NC_CACHE[key] = build(CH, dbg, stop)
    nc = _NC_CACHE[key]
    core_list = cores if cores is not None else [(b, c) for b in range(B) for c in range(4)]
    in_maps = [prep_core(inputs, b, c, CH) for (b, c) in core_list]
    res = run_bass_kernel_spmd(nc, in_maps, core_ids=list(range(len(core_list))), trace=bool(os.environ.get('KTRACE')))
    return res, core_list, CH


def kernel(**inputs):
    res, core_list, CH = run(inputs)
    B, SEQ, _ = np.asarray(inputs["x"]).shape
    out = np.zeros((B, SEQ, D), np.float32)
    for (b, c), r in zip(core_list, res.results):
        out[b, c * CH:(c + 1) * CH] = r["out"]
    return out
```
